# Optimizing a Trainium2 kernel written in Bass

```python
import math
import jax, jax.numpy as jnp
from jax import lax
import numpy as np

D_MODEL = 2048
BATCH = 4
SEQ = 2048
DEPTH = 1
DEC_BATCH = 128
DEC_SEQ = 8
PAST_LEN = 16384
PAGE_SIZE = 128

MIX_WIDTH = D_MODEL
H_A = 8
DK_A = MIX_WIDTH // 2 // H_A
DV_A = MIX_WIDTH // 2 // H_A
H_B = 8
DK_B = MIX_WIDTH // 2 // H_B
DV_B = MIX_WIDTH // 2 // H_B
CONV_W = 4
C_CONV = 2 * H_B * DK_B + H_B * DV_B
D_FF = 4 * D_MODEL
CHUNK = 64
EPS = 1e-6
COL_SIZES = (H_A * DK_A, H_A * DK_A, H_A * DV_A, H_A, H_A, H_A * DV_A,
             H_B * DK_B, H_B * DK_B, H_B * DV_B, H_B, H_B, H_B * DV_B)
P_IN = sum(COL_SIZES)

kernel_name = "hymba_mlstm_gdn_adaln_decoder_step"


def _rmsnorm(x, g):
    x32 = x.astype(jnp.float32)
    y = x32 * lax.rsqrt(jnp.mean(x32 * x32, axis=-1, keepdims=True) + EPS)
    return y.astype(x.dtype) * g


def _head_rmsnorm(h, g):
    B, T, H, E = h.shape
    y = h * lax.rsqrt(jnp.mean(h * h, axis=-1, keepdims=True) + EPS)
    return (y * g.astype(jnp.float32).reshape(H, E)).reshape(B, T, H * E)


def _l2norm(x):
    return x * lax.rsqrt(jnp.sum(x * x, axis=-1, keepdims=True) + EPS)


def _modulate(h, shift, scale):
    return h * (1.0 + scale[:, None, :]) + shift[:, None, :]


def _split_cols(p):
    idx = []
    acc = 0
    for s in COL_SIZES[:-1]:
        acc += s
        idx.append(acc)
    return jnp.split(p, idx, axis=-1)


def _to_chunks(a, L):
    B, T, H = a.shape[:3]
    rest = a.shape[3:]
    a = a.reshape((B, T // L, L, H) + rest)
    return a.transpose((1, 0, 3, 2) + tuple(range(4, a.ndim)))


def _from_chunks(a):
    N, B, H, L, E = a.shape
    return a.transpose(1, 0, 3, 2, 4).reshape(B, N * L, H, E)


def _causal_conv(u, buf, w):
    T = u.shape[1]
    full = jnp.concatenate([buf.astype(u.dtype), u], axis=1)
    out = full[:, 0:T] * w[0]
    for j in range(1, CONV_W):
        out = out + full[:, j:j + T] * w[j]
    return out, full[:, -(CONV_W - 1):]


def _mlstm_scan(q, k, v, li, lf, C0, n0, m0):
    T = q.shape[1]
    L = math.gcd(T, CHUNK)
    xs = (_to_chunks(q, L), _to_chunks(k, L), _to_chunks(v, L),
          _to_chunks(li, L), _to_chunks(lf, L))
    causal = jnp.tril(jnp.ones((L, L), dtype=bool))

    def step(carry, xc):
        C, n, m = carry
        qc, kc, vc, lic, lfc = xc
        b = jnp.cumsum(lfc, axis=-1)
        d = jnp.where(causal, b[..., :, None] - b[..., None, :] + lic[..., None, :], -jnp.inf)
        inter = b + m[..., None]
        m_t = jnp.maximum(inter, jnp.max(d, axis=-1))
        w = jnp.exp(d - m_t[..., None])
        dec = jnp.exp(inter - m_t)
        s = jnp.einsum('bhtd,bhsd->bhts', qc, kc) * w
        numer = dec[..., None] * jnp.einsum('bhtd,bhde->bhte', qc, C) + jnp.einsum('bhts,bhse->bhte', s, vc)
        dd = dec * jnp.einsum('bhtd,bhd->bht', qc, n) + jnp.sum(s, axis=-1)
        h = numer / jnp.maximum(jnp.abs(dd), jnp.exp(-m_t))[..., None]
        m_new = m_t[..., -1]
        wl = jnp.exp(b[..., -1:] - b + lic - m_new[..., None])
        dC = jnp.exp(b[..., -1] + m - m_new)
        C_new = dC[..., None, None] * C + jnp.einsum('bhs,bhsd,bhse->bhde', wl, kc, vc)
        n_new = dC[..., None] * n + jnp.einsum('bhs,bhsd->bhd', wl, kc)
        return (C_new, n_new, m_new), h

    (C, n, m), hs = lax.scan(step, (C0, n0, m0), xs)
    return _from_chunks(hs), C, n, m


def _gdn_scan(q, k, v, logg, beta, S0):
    T = q.shape[1]
    L = math.gcd(T, CHUNK)
    xs = (_to_chunks(q, L), _to_chunks(k, L), _to_chunks(v, L),
          _to_chunks(logg, L), _to_chunks(beta, L))
    incl = jnp.tril(jnp.ones((L, L), dtype=bool))
    strict = jnp.tril(jnp.ones((L, L), dtype=bool), -1)
    eye = jnp.eye(L, dtype=jnp.float32)

    def step(S, xc):
        qc, kc, vc, gc, bc = xc
        G = jnp.cumsum(gc, axis=-1)
        decay = jnp.exp(jnp.where(incl, G[..., :, None] - G[..., None, :], -jnp.inf))
        kk = jnp.einsum('bhtd,bhsd->bhts', kc, kc)
        a = jnp.where(strict, bc[..., None] * decay * kk, 0.0)
        gam = jnp.exp(G)
        rhs = bc[..., None] * (vc - gam[..., None] * jnp.einsum('bhtd,bhde->bhte', kc, S))
        u = lax.linalg.triangular_solve(a + eye, rhs, left_side=True, lower=True, unit_diagonal=True)
        qk = jnp.einsum('bhtd,bhsd->bhts', qc, kc) * decay
        o = gam[..., None] * jnp.einsum('bhtd,bhde->bhte', qc, S) + jnp.einsum('bhts,bhse->bhte', qk, u)
        S_new = jnp.exp(G[..., -1])[..., None, None] * S + jnp.einsum(
            'bhs,bhsd,bhse->bhde', jnp.exp(G[..., -1:] - G), kc, u)
        return S_new, o

    S, os_ = lax.scan(step, S0, xs)
    return _from_chunks(os_), S


def _layer(x, c, C0, n0, m0, S0, conv0, ada_w, ada_b, norm1, w_in, gate_b, mlstm_g,
           conv_w, A_log, dt_bias, gdn_g, w_out, norm2, w_up, w_down):
    f32 = jnp.float32
    B, T, _ = x.shape
    mod = jax.nn.silu(c) @ ada_w + ada_b
    sh1, sc1, g1, sh2, sc2, g2 = jnp.split(mod, 6, axis=-1)
    h = _modulate(_rmsnorm(x, norm1), sh1, sc1)
    proj = h @ w_in
    qa, ka, va, ia, fa, oa, qb, kb, vb, ab, bb, zb = _split_cols(proj)

    qa = qa.astype(f32).reshape(B, T, H_A, DK_A)
    ka = ka.astype(f32).reshape(B, T, H_A, DK_A) * (DK_A ** -0.5)
    va = va.astype(f32).reshape(B, T, H_A, DV_A)
    gb = gate_b.astype(f32)
    li = ia.astype(f32) + gb[:H_A]
    lf = jax.nn.log_sigmoid(fa.astype(f32) + gb[H_A:])
    ha, C1, n1, m1 = _mlstm_scan(qa, ka, va, li, lf, C0.astype(f32), n0.astype(f32), m0.astype(f32))
    ha = _head_rmsnorm(ha, mlstm_g) * jax.nn.sigmoid(oa.astype(f32))

    qkv = jnp.concatenate([qb, kb, vb], axis=-1)
    qkv, conv1 = _causal_conv(qkv, conv0, conv_w)
    qkv = jax.nn.silu(qkv.astype(f32))
    qb, kb, vb = jnp.split(qkv, [H_B * DK_B, 2 * H_B * DK_B], axis=-1)
    qb = _l2norm(qb.reshape(B, T, H_B, DK_B)) * (DK_B ** -0.5)
    kb = _l2norm(kb.reshape(B, T, H_B, DK_B))
    vb = vb.reshape(B, T, H_B, DV_B)
    logg = -jnp.exp(A_log.astype(f32)) * jax.nn.softplus(ab.astype(f32) + dt_bias.astype(f32))
    beta = jax.nn.sigmoid(bb.astype(f32))
    hb, S1 = _gdn_scan(qb, kb, vb, logg, beta, S0.astype(f32))
    hb = _head_rmsnorm(hb, gdn_g) * jax.nn.silu(zb.astype(f32))

    mix = jnp.concatenate([ha, hb], axis=-1).astype(x.dtype) @ w_out
    x = x + g1[:, None, :] * mix
    h2 = _modulate(_rmsnorm(x, norm2), sh2, sc2)
    x = x + g2[:, None, :] * (jnp.square(jax.nn.relu(h2 @ w_up)) @ w_down)
    return x, C1, n1, m1, S1, conv1


def _final(x, c, w, b, g):
    sh, sc = jnp.split(jax.nn.silu(c) @ w + b, 2, axis=-1)
    return _modulate(_rmsnorm(x, g), sh, sc)


def setup_inputs(seed: int = 0) -> dict:
    key = jax.random.key(seed)
    ks = jax.random.split(key, 32)
    nrm = jax.random.normal
    D = D_MODEL
    f_bias = jnp.linspace(3.0, 6.0, H_A)[None, :] + 0.1 * nrm(ks[14], (DEPTH, H_A))
    i_bias = 0.1 * nrm(ks[15], (DEPTH, H_A))
    dt = jnp.exp(jax.random.uniform(ks[18], (DEPTH, H_B), minval=math.log(1e-3), maxval=math.log(1e-1)))
    return {
        "x_prompt": nrm(ks[0], (BATCH, SEQ, D)),
        "x_sample": nrm(ks[1], (DEC_BATCH, DEC_SEQ, D)),
        "state_mlstm_C": nrm(ks[2], (DEPTH, DEC_BATCH, H_A, DK_A, DV_A)) * DK_A ** -0.5,
        "state_mlstm_n": nrm(ks[3], (DEPTH, DEC_BATCH, H_A, DK_A)) * 0.5,
        "state_mlstm_m": nrm(ks[4], (DEPTH, DEC_BATCH, H_A)),
        "state_gdn_S": nrm(ks[5], (DEPTH, DEC_BATCH, H_B, DK_B, DV_B)) * DK_B ** -0.5,
        "state_gdn_conv": nrm(ks[6], (DEPTH, DEC_BATCH, CONV_W - 1, C_CONV)),
        "c_prompt": nrm(ks[7], (BATCH, D)),
        "c_sample": nrm(ks[8], (DEC_BATCH, D)),
        "ada_w": nrm(ks[9], (DEPTH, D, 6 * D)) * (0.5 * D ** -0.5),
        "ada_b": nrm(ks[10], (DEPTH, 6 * D)) * 0.02,
        "norm1": 1.0 + 0.01 * nrm(ks[11], (DEPTH, D)),
        "w_in": nrm(ks[12], (DEPTH, D, P_IN)) * D ** -0.5,
        "mlstm_gate_bias": jnp.concatenate([i_bias, f_bias], axis=-1),
        "mlstm_norm": 1.0 + 0.01 * nrm(ks[13], (DEPTH, H_A * DV_A)),
        "gdn_conv_w": nrm(ks[16], (DEPTH, CONV_W, C_CONV)) * CONV_W ** -0.5,
        "gdn_A_log": jnp.log(jax.random.uniform(ks[17], (DEPTH, H_B), minval=1.0, maxval=16.0)),
        "gdn_dt_bias": dt + jnp.log(-jnp.expm1(-dt)),
        "gdn_norm": 1.0 + 0.01 * nrm(ks[19], (DEPTH, H_B * DV_B)),
        "w_out": nrm(ks[20], (DEPTH, MIX_WIDTH, D)) * MIX_WIDTH ** -0.5,
        "norm2": 1.0 + 0.01 * nrm(ks[21], (DEPTH, D)),
        "w_up": nrm(ks[22], (DEPTH, D, D_FF)) * D ** -0.5,
        "w_down": nrm(ks[23], (DEPTH, D_FF, D)) * D_FF ** -0.5,
        "ada_final_w": nrm(ks[24], (D, 2 * D)) * (0.5 * D ** -0.5),
        "ada_final_b": nrm(ks[25], (2 * D,)) * 0.02,
        "norm_final": 1.0 + 0.01 * nrm(ks[26], (D,)),
    }


def reference(x_prompt, x_sample, state_mlstm_C, state_mlstm_n, state_mlstm_m, state_gdn_S,
              state_gdn_conv, c_prompt, c_sample, ada_w, ada_b, norm1, w_in, mlstm_gate_bias,
              mlstm_norm, gdn_conv_w, gdn_A_log, gdn_dt_bias, gdn_norm, w_out, norm2, w_up,
              w_down, ada_final_w, ada_final_b, norm_final):
    f32 = jnp.float32
    Bp = x_prompt.shape[0]
    xp, xs = x_prompt, x_sample
    pC, pn, pm, pS, pconv = [], [], [], [], []
    sC, sn, sm, sS, sconv = [], [], [], [], []
    for l in range(DEPTH):
        w_l = (ada_w[l], ada_b[l], norm1[l], w_in[l], mlstm_gate_bias[l], mlstm_norm[l],
               gdn_conv_w[l], gdn_A_log[l], gdn_dt_bias[l], gdn_norm[l], w_out[l], norm2[l],
               w_up[l], w_down[l])
        xp, C1, n1, m1, S1, cv1 = _layer(
            xp, c_prompt,
            jnp.zeros((Bp, H_A, DK_A, DV_A), f32), jnp.zeros((Bp, H_A, DK_A), f32),
            jnp.zeros((Bp, H_A), f32), jnp.zeros((Bp, H_B, DK_B, DV_B), f32),
            jnp.zeros((Bp, CONV_W - 1, C_CONV), xp.dtype), *w_l)
        pC.append(C1); pn.append(n1); pm.append(m1); pS.append(S1); pconv.append(cv1)
        xs, C2, n2, m2, S2, cv2 = _layer(
            xs, c_sample, state_mlstm_C[l], state_mlstm_n[l], state_mlstm_m[l],
            state_gdn_S[l], state_gdn_conv[l], *w_l)
        sC.append(C2); sn.append(n2); sm.append(m2); sS.append(S2); sconv.append(cv2)
    y_prompt = _final(xp, c_prompt, ada_final_w, ada_final_b, norm_final)
    y_sample = _final(xs, c_sample, ada_final_w, ada_final_b, norm_final)
    return (y_prompt, y_sample,
            jnp.stack(pC), jnp.stack(pn), jnp.stack(pm), jnp.stack(pS), jnp.stack(pconv),
            jnp.stack(sC), jnp.stack(sn), jnp.stack(sm), jnp.stack(sS), jnp.stack(sconv))
```

```python
import numpy as np
import concourse.bass as bass
import concourse.mybir as mybir
from concourse.bass_utils import run_bass_kernel_spmd

F32 = mybir.dt.float32
BF16 = mybir.dt.bfloat16
AF = mybir.ActivationFunctionType
ALU = mybir.AluOpType
AX = mybir.AxisListType

D = 2048
KD = 16
NH = 8
DH = 128
T0 = 1024
T1 = 1024
TS = 128
NSEQ = 16
LS = 8
NTOK = T0 + T1 + TS
NQ = T1 + TS
NSUB = NTOK // 128
NSUBQ = NQ // 128
DFF = 8192
EPS = 1e-6
NEG = -1.0e30
WIN_COLS = 5120 + 4096 + 32

ENGS = ("tensor", "vector", "scalar", "gpsimd", "sync")


class Buf:
    __slots__ = ("name", "w", "r")

    def __init__(self, name=""):
        self.name = name
        self.w = None
        self.r = {}


class TL:
    def __init__(self, t, name=""):
        self.t = t
        self.buf = Buf(name)

    def __getitem__(self, k):
        return self.t[k]


class Prog:
    def __init__(self, nc):
        self.nc = nc
        self.sems = {}
        self.cnt = {}
        self.ops = {e: [] for e in ENGS}
        self.seen = {e: {} for e in ENGS}
        for e in ENGS:
            self.sems[e] = nc.alloc_semaphore(name="s_" + e)
            self.cnt[e] = 0
        self.dkeys = []
        self.nins = 0

    def key(self, name):
        k = "d_" + name
        if k not in self.sems:
            self.sems[k] = self.nc.alloc_semaphore(name="s" + k)
            self.cnt[k] = 0
            self.dkeys.append(k)
        return k

    def _deps(self, eng, reads, writes):
        deps = {}

        def add(kv):
            if kv is None:
                return
            k, v = kv
            if k == eng and eng == "tensor":
                return
            if deps.get(k, 0) < v:
                deps[k] = v
        for b in reads:
            add(b.buf.w)
        for b in writes:
            add(b.buf.w)
            for kv in b.buf.r.items():
                add(kv)
        out = []
        seen = self.seen[eng]
        for k, v in deps.items():
            if seen.get(k, 0) >= v:
                continue
            seen[k] = v
            out.append((k, v))
        return out

    def capture(self):
        self.cap = []
        return self.cap

    def end_capture(self):
        c = self.cap
        self.cap = None
        return c

    def replay(self, lists):
        lists = [l for l in lists if l]
        if not lists:
            return
        n = max(len(l) for l in lists)
        pos = [0] * len(lists)
        for i in range(1, n + 1):
            for j, l in enumerate(lists):
                tgt = (i * len(l) + n - 1) // n
                while pos[j] < tgt:
                    it = l[pos[j]]
                    pos[j] += 1
                    if it[0] == "op":
                        self.op(*it[1:])
                    else:
                        self.dma(*it[1:])

    def _emit_item(self, it):
        if it[0] == "op":
            self.op(*it[1:])
        else:
            self.dma(*it[1:])

    def replay_pipe(self, items, depth, burst=2):
        active = []
        nxt = 0
        while nxt < len(items) or active:
            while nxt < len(items) and len(active) < depth and (
                    not active or active[-1][1] >= max(1, len(active[-1][0]) // depth)):
                active.append([items[nxt], 0])
                nxt += 1
            for a in list(active):
                for _ in range(burst):
                    if a[1] < len(a[0]):
                        self._emit_item(a[0][a[1]])
                        a[1] += 1
                if a[1] >= len(a[0]):
                    active.remove(a)

    @staticmethod
    def _fl(lst):
        out = []
        for b in lst:
            if hasattr(b, "kids"):
                out.extend(b.kids)
            else:
                out.append(b)
        return out

    def op(self, eng, fn, r=(), w=()):
        if getattr(self, "cap", None) is not None:
            self.cap.append(("op", eng, fn, tuple(r), tuple(w)))
            return
        r, w = self._fl(r), self._fl(w)
        waits = self._deps(eng, r, w)
        self.cnt[eng] += 1
        v = self.cnt[eng]
        self.ops[eng].append((waits, fn, eng, 1))
        for b in r:
            if b.buf.r.get(eng, 0) < v:
                b.buf.r[eng] = v
        for b in w:
            b.buf.w = (eng, v)
            b.buf.r = {}
        self.nins += 1

    def V(self, fn, r=(), w=()):
        self.op("vector", fn, r, w)

    def A(self, fn, r=(), w=()):
        self.op("scalar", fn, r, w)

    def G(self, fn, r=(), w=()):
        self.op("gpsimd", fn, r, w)

    def T(self, fn, r=(), w=()):
        self.op("tensor", fn, r, w)

    def dma(self, eng, key, fns, r=(), w=()):
        if not isinstance(fns, (list, tuple)):
            fns = [fns]
        if getattr(self, "cap", None) is not None:
            self.cap.append(("dma", eng, key, fns, tuple(r), tuple(w)))
            return
        r, w = self._fl(r), self._fl(w)
        key = self.key(key) if not key.startswith("d_") else key
        waits = self._deps(eng, r, w)
        for i, fn in enumerate(fns):
            self.cnt[key] += 16
            self.ops[eng].append((waits if i == 0 else [], fn, key, 16))
        v = self.cnt[key]
        for b in r:
            if b.buf.r.get(key, 0) < v:
                b.buf.r[key] = v
        for b in w:
            b.buf.w = (key, v)
            b.buf.r = {}
        self.nins += len(fns)

    def barrier(self):
        for e in ENGS:
            waits = []
            for k, v in self.cnt.items():
                if k == e or v == 0:
                    continue
                if self.seen[e].get(k, 0) >= v:
                    continue
                self.seen[e][k] = v
                waits.append((k, v))
            self.ops[e].append((waits, None, None, 0))

    def flush(self, final=False):
        nc = self.nc
        sems = self.sems
        ops = self.ops

        def run(e, name):
            for waits, fn, key, inc in ops[name]:
                for k, v in waits:
                    e.wait_ge(sems[k], v)
                if fn is not None:
                    fn(e).then_inc(sems[key], inc)

        with nc.Block() as block:
            @block.tensor
            def _(e):
                run(e, "tensor")

            @block.vector
            def _(e):
                run(e, "vector")

            @block.scalar
            def _(e):
                run(e, "scalar")

            @block.gpsimd
            def _(e):
                run(e, "gpsimd")

            @block.sync
            def _(e):
                run(e, "sync")
        self.ops = {e: [] for e in ENGS}


def _const_layout():
    off = {}
    c = 0
    for name, n in (("ident", 128), ("ones", 128), ("neg128", 128), ("uinc128", 128), ("ustr128", 128),
                    ("sel128", 128), ("bd32", 128), ("o64", 128), ("offL128", 128), ("neg8", 8), ("uinc8", 8), ("ustr8", 8), ("sel8", 128),
                    ("sign", 16)):
        off[name] = (c, n)
        c += n
    return off, c


CO, NCONST = _const_layout()


def _build_consts():
    a = np.zeros((128, NCONST), np.float32)

    def put(name, m):
        o, n = CO[name]
        a[: m.shape[0], o:o + m.shape[1]] = m
    put("ident", np.eye(128, dtype=np.float32))
    put("ones", np.ones((128, 128), np.float32))
    for L, sfx in ((128, "128"), (8, "8")):
        t = np.arange(L)
        neg = np.where(t[None, :] <= t[:, None], 0.0, NEG).astype(np.float32)
        uinc = (t[None, :] >= t[:, None]).astype(np.float32)
        ustr = (t[None, :] > t[:, None]).astype(np.float32)
        sel = np.zeros((L, 128), np.float32)
        sel[L - 1, :] = 1.0
        put("neg" + sfx, neg)
        put("uinc" + sfx, uinc)
        put("ustr" + sfx, ustr)
        put("sel" + sfx, sel)
    bd32 = np.zeros((128, 128), np.float32)
    bd64 = np.zeros((128, 128), np.float32)
    for i in range(0, 128, 32):
        bd32[i:i + 32, i:i + 32] = 1.0
    for i in range(0, 128, 64):
        bd64[i:i + 64, i:i + 64] = 1.0
    put("bd32", bd32)
    put("o64", bd64 - bd32)
    ofl = np.zeros((128, 128), np.float32)
    ofl[64:, :64] = 1.0
    put("offL128", ofl)
    sg = np.ones((128, 16), np.float32)
    sg[:, 0:8] = -1.0
    put("sign", sg)
    return a


class Builder:
    def __init__(self, debug=False, phases=(0, 1, 2, 3)):
        self.debug = debug
        self.phases = phases
        nc = bass.Bass("TRN2", target_bir_lowering=False)
        self.nc = nc
        self.P = Prog(nc)
        self.ins = {}
        self.outs = {}
        self.psum = TLBank(nc)

    def din(self, name, shape, dt=F32):
        t = self.nc.dram_tensor(name, list(shape), dt, kind="ExternalInput").ap()
        self.ins[name] = t
        return t

    def dout(self, name, shape, dt=F32):
        t = self.nc.dram_tensor(name, list(shape), dt, kind="ExternalOutput").ap()
        self.outs[name] = t
        return t

    def dscr(self, name, shape, dt):
        kind = "ExternalOutput" if self.debug else "Internal"
        t = self.nc.dram_tensor(name, list(shape), dt, kind=kind).ap()
        if self.debug:
            self.outs[name] = t
        return t


class TLBank:
    def __init__(self, nc):
        self.t = nc.alloc_psum_tensor("psum_all", [128, 4096], F32)
        self.banks = [TL(None, "bank%d" % i) for i in range(8)]
        self.ptr = 0

        self.ids = list(range(8))

    def sub(self, ids):
        o = TLBank.__new__(TLBank)
        o.t, o.banks, o.ptr, o.ids = self.t, self.banks, 0, list(ids)
        return o

    def one(self):
        i = self.ids[self.ptr]
        self.ptr = (self.ptr + 1) % len(self.ids)
        return i

    def pair(self):
        if self.ptr % 2:
            self.ptr = (self.ptr + 1) % len(self.ids)
        i = self.ids[self.ptr]
        self.ptr = (self.ptr + 2) % len(self.ids)
        return i

    def f32(self, i, n=1):
        return self.t[:, i * 512:(i + n) * 512]

    def bf(self, i, n=1):
        return self.t[:, i * 512:(i + n) * 512].bitcast(BF16)


def _mm(P, out, lhsT, rhs, start, stop, r, w):
    P.T(lambda e: e.matmul(out, lhsT=lhsT, rhs=rhs, start=start, stop=stop), r, w)


def _tr(P, out, in_, ident, r, w):
    P.T(lambda e: e.transpose(out=out, in_=in_, identity=ident), r, w)


def _act(P, out, in_, func, r, w, bias=None, scale=None, accum=None):
    kw = {}
    if bias is not None:
        kw["bias"] = bias
    if scale is not None:
        kw["scale"] = scale
    if accum is not None:
        kw["accum_out"] = accum
    P.A(lambda e: e.activation(out=out, in_=in_, func=func, **kw), r, w)


def _tt(P, eng, out, in0, in1, op, r, w):
    P.op(eng, lambda e: e.tensor_tensor(out=out, in0=in0, in1=in1, op=op), r, w)


def _ts(P, eng, out, in0, s1, op0, r, w, s2=None, op1=None):
    if op1 is None:
        P.op(eng, lambda e: e.tensor_single_scalar(out=out, in_=in0, scalar=s1, op=op0), r, w)
    else:
        P.op(eng, lambda e: e.tensor_scalar(out=out, in0=in0, scalar1=s1, scalar2=s2, op0=op0, op1=op1), r, w)


def _stt(P, eng, out, in0, scalar, in1, op0, op1, r, w):
    P.op(eng, lambda e: e.scalar_tensor_tensor(out=out, in0=in0, scalar=scalar, in1=in1, op0=op0, op1=op1), r, w)


def _red(P, eng, out, in_, op, r, w):
    P.op(eng, lambda e: e.tensor_reduce(out=out, in_=in_, axis=AX.X, op=op), r, w)


def _cp(P, eng, out, in_, r, w):
    if eng == "scalar":
        P.A(lambda e: e.activation(out=out, in_=in_, func=AF.Copy), r, w)
    else:
        P.op(eng, lambda e: e.tensor_copy(out=out, in_=in_), r, w)


def _ms(P, eng, ap, val, w):
    P.op(eng, lambda e: e.memset(ap, val), (), w)


def _ld(P, eng, key, out, in_, r=(), w=()):
    P.dma(eng, key, lambda e: e.dma_start(out=out, in_=in_), r, w)


class Ctx:
    def __init__(self, nc):
        self.nc = nc
        self.guards = []

    def sb(self, name, shape, dt):
        g = self.nc.sbuf_tensor(name, list(shape), dt)
        t = g.__enter__()
        self.guards.append(g)
        return TL(t, name)

    def close(self):
        for g in reversed(self.guards):
            g.__exit__(None, None, None)
        self.guards = []


def build(debug=False, phases=(0, 1, 2, 3)):
    B = Builder(debug, phases)
    nc, P, PS = B.nc, B.P, B.psum
    bankb = PS.banks

    x_d = B.din("x", [NTOK, D])
    c_d = B.din("c", [17, D])
    flag_d = B.din("flag", [128, 1])
    consts_d = B.din("consts", [128, NCONST])
    adaw_d = B.din("ada_w", [D, 16384])
    adab_d = B.din("ada_b", [1, 16384])
    win_d = B.din("w_in", [D, WIN_COLS])
    wout_d = B.din("w_out", [D, D])
    wup_d = B.din("w_up", [D, DFF])
    wdn_d = B.din("w_down", [DFF, D])
    norms_d = B.din("norms", [3, D])
    hnorm_d = B.din("hnorm", [2, 1024])
    gvec_d = B.din("gvec", [1, 32])
    convw_d = B.din("conv_w", [128, 24 * 4])
    sC_d = B.din("sC", [NSEQ, NH, DH, DH])
    sn_d = B.din("sn_t", [NSEQ, DH, NH])
    sm_d = B.din("sm", [NSEQ, NH])
    sS_d = B.din("sS", [NSEQ, NH, DH, DH])
    scv_d = B.din("sconv_t", [128, 24 * NSEQ * 3])

    y_d = B.dout("y", [NQ, D])
    pC_d = B.dout("pC", [NH, DH, DH])
    pn_d = B.dout("pn_t", [DH, NH])
    pm_d = B.dout("pm", [1, NH])
    pS_d = B.dout("pS", [NH, DH, DH])
    pcv_d = B.dout("pconv_t", [128, 24 * 3])
    oC_d = B.dout("oC", [NSEQ, NH, DH, DH])
    on_d = B.dout("on_t", [NSEQ, DH, NH])
    om_d = B.dout("om", [NSEQ, NH])
    oS_d = B.dout("oS", [NSEQ, NH, DH, DH])
    ocv_d = B.dout("oconv_t", [128, 24 * NSEQ * 3])

    modd = B.dscr("modd", [17, 16384], F32)
    s_qa = B.dscr("s_qa", [NH, DH, NQ], BF16)
    s_kaT = B.dscr("s_kaT", [NH, DH, NQ], BF16)
    s_qb = B.dscr("s_qb", [NH, DH, NTOK], BF16)
    s_kb = B.dscr("s_kb", [NH, DH, NTOK], BF16)
    s_vb = B.dscr("s_vb", [NH, DH, NTOK], BF16)
    s_ka = B.dscr("s_ka", [NTOK, 1024], BF16)
    s_va = B.dscr("s_va", [NTOK, 1024], BF16)
    s_oa = B.dscr("s_oa", [NQ, 1024], BF16)
    s_zb = B.dscr("s_zb", [NQ, 1024], BF16)
    s_gt = B.dscr("s_gt", [NTOK, 32], F32)

    G = Ctx(nc)
    cst = G.sb("consts_sb", [128, NCONST], F32)
    identb = G.sb("identb", [128, 128], BF16)
    onesb = G.sb("onesb", [128, 128], BF16)
    flag = G.sb("flag_sb", [128, 1], F32)
    _ld(P, "sync", "g0", cst[:], consts_d, w=[cst])
    _ld(P, "sync", "g1", flag[:], flag_d, w=[flag])

    def C(name, rows=128):
        o, n = CO[name]
        return cst[0:rows, o:o + n]
    _cp(P, "vector", identb[:], C("ident"), [cst], [identb])
    _cp(P, "vector", onesb[:], C("ones"), [cst], [onesb])

    cT = G.sb("cT_sb", [128, 16 * 17], BF16)
    if 0 in phases:
        _phase0(B, P, PS, cst, C, c_d, adaw_d, adab_d, modd, cT)
        P.barrier()
        P.flush()
    if 1 in phases:
        _phase1(B, P, PS, cst, C, identb, onesb, flag, x_d, norms_d, modd, win_d,
                dict(qa=s_qa, kaT=s_kaT, qb=s_qb, kb=s_kb, vb=s_vb, ka=s_ka, va=s_va, oa=s_oa, zb=s_zb, gt=s_gt),
                gvec_d, convw_d, scv_d, pcv_d, ocv_d, cT, adaw_d, adab_d)
    S = dict(qa=s_qa, kaT=s_kaT, qb=s_qb, kb=s_kb, vb=s_vb, ka=s_ka, va=s_va, oa=s_oa, zb=s_zb, gt=s_gt)
    G2 = Ctx(nc)
    mixT = G2.sb("mixT", [128, 16, NQ], BF16)
    if debug:
        mixd = B.dout("mix_dbg", [128, 16 * NQ], BF16)
    if 2 in phases:
        io = dict(sC=sC_d, sn=sn_d, sm=sm_d, sS=sS_d, scv=scv_d, pC=pC_d, pn=pn_d, pm=pm_d, pS=pS_d, pcv=pcv_d,
                  oC=oC_d, on=on_d, om=om_d, oS=oS_d, ocv=ocv_d)
        sc = Scan(B, P, PS, cst, C, identb, onesb, flag, mixT, S, hnorm_d, gvec_d, convw_d, io)
        sc.run()
        if debug:
            _ld(P, "sync", "dbgm", mixd, mixT[:, :, :].rearrange("p k t -> p (k t)"), r=[mixT])
        P.barrier()
        P.flush()
        sc.X.close()
    if 3 in phases:
        _phase3(B, P, PS, cst, C, identb, mixT, x_d, norms_d, modd, wout_d, wup_d, wdn_d, y_d)
    P.barrier()
    P.flush()
    return B


NADA0 = 8


def _ada_block(P, PS, jb, slot, key, bt, bkey, m, mkey, cT, adaw_d, adab_d, modd):
    cols = slice(jb * 512, (jb + 1) * 512)
    _ld(P, "gpsimd", key, slot[:], adaw_d[:, cols].rearrange("(k p) c -> p k c", p=128), w=[slot])
    _ld(P, "sync", bkey, bt[:], adab_d[0:1, cols].partition_broadcast(17), w=[bt])
    bk = PS.one()
    for k in range(16):
        _mm(P, PS.f32(bk)[0:17, :], cT[:, k * 17:(k + 1) * 17], slot[:, k, :], k == 0, k == 15,
            [cT, slot], [PS.banks[bk]])
    _tt(P, "vector", m[:], PS.f32(bk)[0:17, :], bt[:], ALU.add, [PS.banks[bk], bt], [m])
    _ld(P, "sync", mkey, modd[:, cols], m[:], r=[m])


def _phase0(B, P, PS, cst, C, c_d, adaw_d, adab_d, modd, cT):
    nc = B.nc
    X = Ctx(nc)
    c_sb = X.sb("p0_c", [17, D], F32)
    wr = [X.sb("p0_w%d" % i, [128, 16, 512], BF16) for i in range(3)]
    bias = [X.sb("p0_b%d" % i, [17, 512], F32) for i in range(2)]
    mo = [X.sb("p0_m%d" % i, [17, 512], F32) for i in range(2)]
    _ld(P, "sync", "p0c", c_sb[:], c_d, w=[c_sb])
    _act(P, c_sb[:], c_sb[:], AF.Silu, [c_sb], [c_sb])
    bk = PS.one()
    for k in range(16):
        _tr(P, PS.f32(bk)[:, k * 17:(k + 1) * 17], c_sb[:, k * 128:(k + 1) * 128], C("ident", 17)[:, 0:17],
            [c_sb, cst], [PS.banks[bk]])
    _cp(P, "vector", cT[:], PS.f32(bk)[:, 0:272], [PS.banks[bk]], [cT])
    for jb in range(NADA0):
        _ada_block(P, PS, jb, wr[jb % 3], "wr%d" % (jb % 3), bias[jb % 2], "p0b%d" % (jb % 2), mo[jb % 2],
                   "p0m%d" % (jb % 2), cT, adaw_d, adab_d, modd)
    X.close()


def _mod_tiles(P, modd, col0, tp, ts, key):
    _ld(P, "sync", key + "p", tp[:], modd[0:1, col0:col0 + D].partition_broadcast(128), w=[tp])
    fns = []
    for b in range(NSEQ):
        fns.append(lambda e, b=b: e.dma_start(out=ts[8 * b:8 * b + 8, :],
                                              in_=modd[1 + b:2 + b, col0:col0 + D].partition_broadcast(8)))
    P.dma("sync", key + "s", fns, w=[ts])


def _phase1(B, P, PS, cst, C, identb, onesb, flag, x_d, norms_d, modd, win_d, S, gvec_d, convw_d, scv_d, pcv_d, ocv_d,
            cT, adaw_d, adab_d):
    nc = B.nc
    bank = PS.banks
    X = Ctx(nc)
    hT = X.sb("hT", [128, 16, NTOK], BF16)
    wr = [X.sb("p1_w%d" % i, [128, 16, 512], BF16) for i in range(3)]

    XA = Ctx(nc)
    xt = [XA.sb("p1_x%d" % i, [128, D], F32) for i in range(2)]
    tmpf = XA.sb("p1_tmp", [128, D], F32)
    hb = [XA.sb("p1_hb%d" % i, [128, D], BF16) for i in range(2)]
    Ap = XA.sb("p1_Ap", [128, D], F32)
    Bp = XA.sb("p1_Bp", [128, D], F32)
    As = XA.sb("p1_As", [128, D], F32)
    Bs = XA.sb("p1_Bs", [128, D], F32)
    st = [XA.sb("p1_st%d" % i, [128, 4], F32) for i in range(2)]
    _ld(P, "sync", "p1n", tmpf[:], norms_d[0:1, :].partition_broadcast(128), w=[tmpf])
    _mod_tiles(P, modd, 2048, Ap, As, "p1A")
    _mod_tiles(P, modd, 0, Bp, Bs, "p1B")
    for A_ in (Ap, As):
        _stt(P, "vector", A_[:], A_[:], 1.0, tmpf[:], ALU.add, ALU.mult, [A_, tmpf], [A_])
    wq = []

    def wload(jb):
        slot = wr[jb % 3]
        ncol = 512 if jb < 18 else 32
        _ld(P, "gpsimd", "wr%d" % (jb % 3), slot[:, :, 0:ncol],
            win_d[:, jb * 512:jb * 512 + ncol].rearrange("(k p) c -> p k c", p=128), w=[slot])
    for jb in range(3):
        wload(jb)
    for i in range(NSUB):
        x_ = xt[i % 2]
        h_ = hb[i % 2]
        s_ = st[i % 2]
        _ld(P, "sync", "p1x%d" % (i % 2), x_[:], x_d[i * 128:(i + 1) * 128, :], w=[x_])
        _act(P, h_[:], x_[:], AF.Square, [x_], [h_, s_], accum=s_[:, 0:1])
        _act(P, s_[:, 1:2], s_[:, 0:1], AF.Ln, [s_], [s_], bias=EPS, scale=1.0 / D)
        _act(P, s_[:, 2:3], s_[:, 1:2], AF.Exp, [s_], [s_], scale=-0.5)
        A_, B_ = (Ap, Bp) if i < 16 else (As, Bs)
        _stt(P, "vector", tmpf[:], x_[:], s_[:, 2:3], A_[:], ALU.mult, ALU.mult, [x_, s_, A_], [tmpf])
        _tt(P, "vector", h_[:], tmpf[:], B_[:], ALU.add, [tmpf, B_], [h_])
        for half in range(2):
            bk = PS.one()
            for kk in range(8):
                k = half * 8 + kk
                _tr(P, PS.bf(bk)[:, kk * 128:(kk + 1) * 128], h_[:, k * 128:(k + 1) * 128], identb[:],
                    [h_, identb], [PS.banks[bk]])
            _cp(P, "scalar" if half else "vector", hT[:, half * 8:(half + 1) * 8, i * 128:(i + 1) * 128],
                PS.bf(bk).rearrange("p (k t) -> p k t", k=8), [PS.banks[bk]], [hT])
    P.barrier()
    P.flush()
    XA.close()

    XB = Ctx(nc)
    stg = [XB.sb("p1_stg%d" % i, [128, 512], BF16) for i in range(4)]
    NU = 4
    stg = stg + [XB.sb("p1_stg%d" % i, [128, 512], BF16) for i in range(4, 8)]
    pds = [XB.sb("p1_pd%d" % i, [128, 3 + 512], F32) for i in range(NU)]
    accs = [XB.sb("p1_acc%d" % i, [128, 512], F32) for i in range(NU)]
    efs = [XB.sb("p1_ef%d" % i, [128, 512], F32) for i in range(NU)]
    sqs = [XB.sb("p1_sq%d" % i, [128, 512], BF16) for i in range(NU)]
    rvs = accs
    cvs = XB.sb("p1_cvs", [128, 24 * NSEQ * 3], F32)
    pcvt = XB.sb("p1_pcvt", [128, 72], F32)
    convw = XB.sb("p1_convw", [128, 96], F32)
    gbr = XB.sb("p1_gbr", [128, 32], F32)
    biasrow = XB.sb("p1_biasrow", [128, 16], F32)
    coefrow = XB.sb("p1_coefrow", [128, 16], F32)
    graw = [XB.sb("p1_graw%d" % i, [128, 32], F32) for i in range(2)]
    GPt = [XB.sb("p1_GP%d" % i, [128, 32], F32) for i in range(2)]
    gt1 = XB.sb("p1_gt1", [128, 16], F32)
    gt2 = XB.sb("p1_gt2", [128, 16], F32)
    gt3 = XB.sb("p1_gt3", [128, 16], F32)
    gt4 = XB.sb("p1_gt4", [128, 8], F32)
    aw = [XB.sb("p1_aw%d" % i, [128, 16, 512], BF16) for i in range(2)]
    abias = [XB.sb("p1_ab%d" % i, [17, 512], F32) for i in range(2)]
    amo = [XB.sb("p1_am%d" % i, [17, 512], F32) for i in range(2)]
    ada_next = [NADA0]

    def ada_item():
        jb_ = ada_next[0]
        if jb_ >= 32:
            return None
        ada_next[0] += 1
        P.capture()
        _ada_block(P, PS, jb_, aw[jb_ % 2], "aw%d" % (jb_ % 2), abias[jb_ % 2], "p1ab%d" % (jb_ % 2), amo[jb_ % 2],
                   "p1am%d" % (jb_ % 2), cT, adaw_d, adab_d, modd)
        return P.end_capture()
    _ld(P, "sync", "p1cv", cvs[:], scv_d, w=[cvs])
    _ld(P, "sync", "p1cw", convw[:], convw_d, w=[convw])
    _ld(P, "sync", "p1gb", gbr[:], gvec_d.partition_broadcast(128), w=[gbr])
    _cp(P, "vector", biasrow[:, 0:8], gbr[:, 8:16], [gbr], [biasrow])
    _cp(P, "vector", biasrow[:, 8:16], gbr[:, 24:32], [gbr], [biasrow])
    _ms(P, "vector", coefrow[:], -1.0, [coefrow])
    _act(P, coefrow[:, 8:16], gbr[:, 16:24], AF.Exp, [gbr, coefrow], [coefrow])
    _ts(P, "vector", coefrow[:, 8:16], coefrow[:, 8:16], -1.0, ALU.mult, [coefrow], [coefrow])

    ev = [0]

    def evac(out, in_, bank_, dst, func=AF.Copy, scale=None):
        if func == AF.Copy and scale is None and (ev[0] % 2 == 0):
            _cp(P, "vector", out, in_, [bank_], [dst])
        else:
            _act(P, out, in_, func, [bank_], [dst], scale=scale)
        ev[0] += 1

    sidx = [0]

    def store(ap_dst, sg, nt):
        _ld(P, "sync", "p1s%d" % ((sidx[0] - 1) % 8), ap_dst, sg[:, 0:nt], r=[sg])

    def next_stg():
        sg = stg[sidx[0] % 8]
        sidx[0] += 1
        return sg

    ucnt = [0]

    def gdn_unit(name, ty, h, t0, nt, bk, first, do_store, prev):
        blk = 8 * ty + h
        u = ucnt[0]
        ucnt[0] += 1
        pd, acc, ef, sq = pds[u % NU], accs[u % NU], efs[u % NU], sqs[u % NU]
        rv = ef
        sample = t0 == T0 + T1
        cw = convw[:, blk * 4:(blk + 1) * 4]
        if sample:
            pv = pd[:, 0:NSEQ * 11].rearrange("p (s l) -> p s l", s=NSEQ)
            cv3 = cvs[:, blk * 48:(blk + 1) * 48].rearrange("p (s j) -> p s j", s=NSEQ)
            _cp(P, "scalar", pv[:, :, 0:3], cv3, [cvs, pd], [pd])
            _cp(P, "scalar", pv[:, :, 3:11], PS.f32(bk)[:, 0:nt].rearrange("p (s l) -> p s l", s=NSEQ),
                [bank[bk], pd], [pd])
            _cp(P, "scalar", cv3, pv[:, :, 8:11], [pd, cvs], [cvs])
            a_ = acc[:, 0:nt].rearrange("p (s l) -> p s l", s=NSEQ)

            def tap(j):
                return pv[:, :, j:j + LS]
        else:
            if first:
                _ms(P, "vector", pd[:, 0:3], 0.0, [pd])
            else:
                ppd, pnt = prev
                if t0 == T0:
                    _ts(P, "vector", pd[:, 0:3], ppd[:, pnt:pnt + 3], flag[:, 0:1], ALU.mult, [ppd, flag, pd], [pd])
                else:
                    _cp(P, "scalar", pd[:, 0:3], ppd[:, pnt:pnt + 3], [ppd, pd], [pd])
            _cp(P, "scalar", pd[:, 3:3 + nt], PS.f32(bk)[:, 0:nt], [bank[bk], pd], [pd])
            if t0 + nt == T0 + T1:
                _cp(P, "scalar", pcvt[:, blk * 3:(blk + 1) * 3], pd[:, nt:nt + 3], [pd, pcvt], [pcvt])
            a_ = acc[:, 0:nt]

            def tap(j):
                return pd[:, j:j + nt]
        if not do_store:
            return (pd, nt)
        _ts(P, "vector", a_, tap(0), cw[:, 0:1], ALU.mult, [pd, convw, acc], [acc])
        for j in range(1, 4):
            _stt(P, "vector", a_, tap(j), cw[:, j:j + 1], a_, ALU.mult, ALU.add, [pd, convw, acc], [acc])
        _act(P, ef[:, 0:nt], acc[:, 0:nt], AF.Exp, [acc, ef], [ef], scale=-1.0)
        _act(P, ef[:, 0:nt], ef[:, 0:nt], AF.Ln, [ef], [ef], bias=1.0)
        _act(P, ef[:, 0:nt], ef[:, 0:nt], AF.Exp, [ef], [ef], scale=-1.0)
        sg = next_stg()
        if ty == 2:
            _tt(P, "vector", sg[:, 0:nt], acc[:, 0:nt], ef[:, 0:nt], ALU.mult, [acc, ef, sg], [sg])
        else:
            _tt(P, "vector", acc[:, 0:nt], acc[:, 0:nt], ef[:, 0:nt], ALU.mult, [acc, ef], [acc])
            _act(P, sq[:, 0:nt], acc[:, 0:nt], AF.Square, [acc, sq], [sq])
            b2 = PS.one()
            _mm(P, PS.f32(b2)[:, 0:nt], onesb[:, :], sq[:, 0:nt], True, True, [onesb, sq], [bank[b2]])
            _act(P, rv[:, 0:nt], PS.f32(b2)[:, 0:nt], AF.Ln, [bank[b2], rv], [rv], bias=EPS)
            _act(P, rv[:, 0:nt], rv[:, 0:nt], AF.Exp, [rv], [rv], scale=-0.5)
            if ty == 0:
                _stt(P, "vector", sg[:, 0:nt], acc[:, 0:nt], DH ** -0.5, rv[:, 0:nt], ALU.mult, ALU.mult,
                     [acc, rv, sg], [sg])
            else:
                _tt(P, "vector", sg[:, 0:nt], acc[:, 0:nt], rv[:, 0:nt], ALU.mult, [acc, rv, sg], [sg])
        store(S[name][h, :, t0:t0 + nt], sg, nt)
        return (pd, nt)

    def gate_unit(i, bk):
        gr, GP = graw[i % 2], GPt[i % 2]
        _cp(P, "vector", gr[:], PS.f32(bk)[:, 0:32], [bank[bk], gr], [gr])
        _tt(P, "vector", GP[:, 0:8], gr[:, 0:8], gbr[:, 0:8], ALU.add, [gr, gbr, GP], [GP])
        _tt(P, "vector", gt1[:], gr[:, 8:24], biasrow[:], ALU.add, [gr, biasrow, gt1], [gt1])
        _tt(P, "vector", gt1[:], gt1[:], C("sign"), ALU.mult, [gt1, cst], [gt1])
        _act(P, gt2[:], gt1[:], AF.Abs, [gt1, gt2], [gt2])
        _act(P, gt2[:], gt2[:], AF.Exp, [gt2], [gt2], scale=-1.0)
        _act(P, gt2[:], gt2[:], AF.Ln, [gt2], [gt2], bias=1.0)
        _ts(P, "vector", gt3[:], gt1[:], 0.0, ALU.max, [gt1, gt3], [gt3])
        _tt(P, "vector", gt3[:], gt3[:], gt2[:], ALU.add, [gt3, gt2], [gt3])
        _tt(P, "vector", GP[:, 8:24], gt3[:], coefrow[:], ALU.mult, [gt3, coefrow, GP], [GP])
        _act(P, gt4[:], gr[:, 24:32], AF.Exp, [gr, gt4], [gt4], scale=-1.0)
        _ts(P, "vector", gt4[:], gt4[:], 1.0, ALU.add, [gt4], [gt4])
        P.V(lambda e: e.reciprocal(out=GP[:, 24:32], in_=gt4[:]), [gt4, GP], [GP])
        _ld(P, "sync", "p1g%d" % (i % 2), S["gt"][i * 128:(i + 1) * 128, :], GP[:], r=[GP])

    fm_types = [("qa", 0), ("kaT", 0), ("qb", 1), ("kb", 2), ("vb", 2)]
    tiles_q = [(1024, 512), (1536, 512), (2048, 128)]
    tiles_all = [(0, 512), (512, 512)] + tiles_q
    items = []
    for jb in range(19):
        slot = wr[jb % 3]
        if jb == 10:
            while True:
                it_ = ada_item()
                if it_ is None:
                    break
                items.append(it_)
            P.replay_pipe(items, 3)
            items = []
        if jb >= 3:
            if jb < 10:
                P.capture()
                wload(jb)
                items.append(P.end_capture())
            else:
                wload(jb)
        if jb < 10:
            name, mode = fm_types[jb // 2]
            tl = {0: tiles_q, 1: [(512, 512)] + tiles_q, 2: tiles_all}[mode]
            for sbk in range(4):
                h = (jb % 2) * 4 + sbk
                prev = None
                for ti, (t0, nt) in enumerate(tl):
                    P.capture()
                    bk = PS.one()
                    for k in range(16):
                        _mm(P, PS.f32(bk)[:, 0:nt], slot[:, k, sbk * 128:(sbk + 1) * 128], hT[:, k, t0:t0 + nt],
                            k == 0, k == 15, [slot, hT], [PS.banks[bk]])
                    if name in ("qa", "kaT"):
                        sg = next_stg()
                        evac(sg[:, 0:nt], PS.f32(bk)[:, 0:nt], PS.banks[bk], sg,
                             scale=(DH ** -0.5 if name == "kaT" else None))
                        store(S[name][h, :, t0 - T0:t0 - T0 + nt], sg, nt)
                    else:
                        ty = {"qb": 0, "kb": 1, "vb": 2}[name]
                        prev = gdn_unit(name, ty, h, t0, nt, bk, ti == 0, not (name == "qb" and t0 < T0), prev)
                    items.append(P.end_capture())
                    if len(items) % 5 == 0:
                        it_ = ada_item()
                        if it_ is not None:
                            items.append(it_)
        elif jb < 18:
            name = ("ka", "va", "oa", "zb")[(jb - 10) // 2]
            c0 = ((jb - 10) % 2) * 512
            qonly = name in ("oa", "zb")
            for i in range(NSUB):
                if qonly and i < 8:
                    continue
                bk = PS.one()
                for k in range(16):
                    _mm(P, PS.f32(bk)[:, :], hT[:, k, i * 128:(i + 1) * 128], slot[:, k, :], k == 0, k == 15,
                        [slot, hT], [PS.banks[bk]])
                sg = next_stg()
                if name == "ka":
                    evac(sg[:], PS.f32(bk), PS.banks[bk], sg, scale=DH ** -0.5)
                elif name == "va":
                    evac(sg[:], PS.f32(bk), PS.banks[bk], sg)
                elif name == "oa":
                    evac(sg[:], PS.f32(bk), PS.banks[bk], sg, func=AF.Sigmoid)
                else:
                    evac(sg[:], PS.f32(bk), PS.banks[bk], sg, func=AF.Silu)
                r0 = i * 128 - (T0 if qonly else 0)
                store(S[name][r0:r0 + 128, c0:c0 + 512], sg, 512)
        else:
            for i in range(NSUB):
                bk = PS.one()
                for k in range(16):
                    _mm(P, PS.f32(bk)[:, 0:32], hT[:, k, i * 128:(i + 1) * 128], slot[:, k, 0:32], k == 0, k == 15,
                        [slot, hT], [PS.banks[bk]])
                gate_unit(i, bk)
    _ld(P, "sync", "p1pc", pcv_d, pcvt[:, :], r=[pcvt])
    _ld(P, "sync", "p1oc", ocv_d, cvs[:, :], r=[cvs])
    P.barrier()
    P.flush()
    XB.close()
    X.close()


def _shared_inputs(inp):
    w_in = np.asarray(inp["w_in"][0])
    o = np.cumsum([0, 1024, 1024, 1024, 8, 8, 1024, 1024, 1024, 1024, 8, 8, 1024])
    seg = {n: w_in[:, o[i]:o[i + 1]] for i, n in enumerate(
        ["qa", "ka", "va", "ia", "fa", "oa", "qb", "kb", "vb", "ab", "bb", "zb"])}
    w_in_r = np.ascontiguousarray(np.concatenate(
        [seg[n] for n in ("qa", "ka", "qb", "kb", "vb", "ka", "va", "oa", "zb", "ia", "fa", "ab", "bb")], axis=1))
    sh = {
        "consts": _build_consts(),
        "ada_w": np.ascontiguousarray(np.concatenate([inp["ada_w"][0], inp["ada_final_w"]], axis=1)),
        "ada_b": np.ascontiguousarray(np.concatenate([inp["ada_b"][0], inp["ada_final_b"]])[None, :]),
        "w_in": w_in_r,
        "w_out": np.ascontiguousarray(inp["w_out"][0]),
        "w_up": np.ascontiguousarray(inp["w_up"][0]),
        "w_down": np.ascontiguousarray(inp["w_down"][0]),
        "norms": np.ascontiguousarray(np.stack([inp["norm1"][0], inp["norm2"][0], inp["norm_final"]])),
        "hnorm": np.ascontiguousarray(np.stack([inp["mlstm_norm"][0], inp["gdn_norm"][0]])),
        "gvec": np.ascontiguousarray(np.concatenate(
            [inp["mlstm_gate_bias"][0], inp["gdn_A_log"][0], inp["gdn_dt_bias"][0]])[None, :]),
        "conv_w": np.ascontiguousarray(
            np.asarray(inp["gdn_conv_w"][0]).T.reshape(24, 128, 4).transpose(1, 0, 2).reshape(128, 96)),
    }
    return {k: np.asarray(v, np.float32) for k, v in sh.items()}


def _core_inputs(inp, c, sh):
    b, half = c // 2, c % 2
    sl = slice(c * NSEQ, (c + 1) * NSEQ)
    xp = inp["x_prompt"][b]
    m = dict(sh)
    m["x"] = np.ascontiguousarray(np.concatenate(
        [xp[0:T0], xp[half * T1:(half + 1) * T1], inp["x_sample"][sl].reshape(TS, D)], axis=0), np.float32)
    m["c"] = np.ascontiguousarray(np.concatenate([inp["c_prompt"][b:b + 1], inp["c_sample"][sl]], axis=0), np.float32)
    m["flag"] = np.full((128, 1), float(half), np.float32)
    m["sC"] = np.ascontiguousarray(inp["state_mlstm_C"][0, sl], np.float32)
    m["sn_t"] = np.ascontiguousarray(np.asarray(inp["state_mlstm_n"][0, sl]).transpose(0, 2, 1), np.float32)
    m["sm"] = np.ascontiguousarray(inp["state_mlstm_m"][0, sl], np.float32)
    m["sS"] = np.ascontiguousarray(inp["state_gdn_S"][0, sl], np.float32)
    cv = np.asarray(inp["state_gdn_conv"][0, sl])
    m["sconv_t"] = np.ascontiguousarray(
        cv.transpose(2, 0, 1).reshape(24, 128, NSEQ, 3).transpose(1, 0, 2, 3).reshape(128, 24 * NSEQ * 3), np.float32)
    return m


LP = 128
NLEV = {8: 2, 64: 5, 128: 6}


def _v3(ap2d, n):
    return ap2d.rearrange("p (h x) -> p h x", h=NH)


class TLV:
    def __init__(self, t, c0, c1, name=""):
        self.t, self.c0, self.c1 = t, c0, c1
        self.buf = Buf(name)

    def __getitem__(self, k):
        rows, cols = k
        a = 0 if cols.start is None else cols.start
        b = (self.c1 - self.c0) if cols.stop is None else cols.stop
        return self.t[rows, self.c0 + a:self.c0 + b]


class TLP:
    def __init__(self, t, kids):
        self.t, self.kids = t, kids

    def __getitem__(self, k):
        return self.t[k]


class Grp:
    pass


class Scan:
    NG = 2

    def __init__(self, B, P, PS, cst, C, identb, onesb, flag, mixT, S, hnorm_d, gvec_d, convw_d, io):
        self.B, self.P, self.PS, self.cst, self.C = B, P, PS, cst, C
        self.identb, self.onesb, self.flag, self.mixT, self.S, self.io = identb, onesb, flag, mixT, S, io
        nc = B.nc
        X = self.X = Ctx(nc)
        sb = X.sb
        NG = self.NG
        nh = NH // NG
        self.gA = sb("gA", [128, 1024], F32)
        self.gB = sb("gB", [128, 1024], F32)
        _ld(P, "sync", "s2a", self.gA[:], hnorm_d[0:1, :].partition_broadcast(128), w=[self.gA])
        _ld(P, "sync", "s2b", self.gB[:], hnorm_d[1:2, :].partition_broadcast(128), w=[self.gB])
        self.fm = [dict(qaT=sb("qaT%d" % i, [128, 8, 128], BF16), kaT=sb("kaT%d" % i, [128, 8, 128], BF16),
                        q=sb("qpost%d" % i, [128, 8, 128], BF16), k=sb("kpost%d" % i, [128, 8, 128], BF16),
                        v=sb("vpost%d" % i, [128, 8, 128], BF16)) for i in range(2)]
        self.tm = [dict(ka=sb("ka_t%d" % i, [128, 1024], BF16), va=sb("va_t%d" % i, [128, 1024], BF16),
                        oa=sb("oa_t%d" % i, [128, 1024], BF16), zb=sb("zb_t%d" % i, [128, 1024], BF16),
                        GP=sb("GP%d" % i, [128, 32], F32)) for i in range(2)]
        specA_big = (("dgx", F32), ("R", F32), ("sloc", BF16), ("sTsb", BF16), ("nl", F32), ("wlk", BF16), ("dCs", F32),
                     ("h1", F32), ("hnb", BF16), ("gs", BF16), ("Cst", F32), ("Cbf", BF16))
        specB_big = (("dG", F32), ("NZ", F32), ("qg", BF16), ("wT", F32), ("dTi", F32), ("P0", BF16), ("P1", BF16),
                     ("PT0", BF16), ("PT1", BF16), ("Tacc", BF16), ("qkT", BF16), ("Mf", BF16), ("MoT", BF16), ("MoT2", BF16),
                     ("kbg", BF16), ("kdec", BF16), ("vbt", BF16), ("WT", BF16), ("U0", F32), ("u", BF16),
                     ("ob", BF16), ("gz", BF16), ("Sst", F32), ("Sbf", BF16))

        def smallA(n):
            return (("sm1", 2 * n), ("g", n), ("cm", n), ("rows", n), ("gl", n), ("dns", n), ("mx", 2 * n), ("t12", 2 * n),
                    ("fd", 2 * n), ("mt", n), ("en", n), ("dd", n), ("d2", n), ("a12", 2 * n), ("ssq", n), ("sm2", 2 * n),
                    ("dC", n))

        def smallB(n):
            return (("gsm", 2 * n), ("gsmall", 2 * n), ("gLe", n), ("ssq2", n))
        parents = {}
        for kind, spec in (("a", specA_big), ("b", specB_big)):
            for n, dt in spec:
                parents[(kind, n)] = sb("P%s_%s" % (kind, n), [128, 1024], dt)
        nstP = sb("Pa_nst", [128, NH], F32)
        nbfP = sb("Pa_nbf", [128, NH], BF16)
        mprevP = sb("Pa_mprev", [128, NH], F32)
        self.ga, self.gb = [], []
        for gi in range(NG):
            for kind in ("a", "b"):
                g = Grp()
                g.h0, g.nh, g.kind = gi * nh, nh, kind
                base = (0 if kind == "a" else 4) + 2 * gi
                g.PS = PS.sub([base, base + 1])
                t = "%s%d_" % (kind, gi)
                g.W = {}
                for n, dt in (specA_big if kind == "a" else specB_big):
                    g.W[n] = TLV(parents[(kind, n)].t, gi * nh * 128, (gi + 1) * nh * 128, t + n)
                for n, wd in (smallA(nh) if kind == "a" else smallB(nh)):
                    g.W[n] = sb(t + n, [128, wd], F32)
                if kind == "a":
                    g.Cst, g.Cbf = g.W["Cst"], g.W["Cbf"]
                    g.nst = TLV(nstP.t, gi * nh, (gi + 1) * nh, t + "nst")
                    g.nbf = TLV(nbfP.t, gi * nh, (gi + 1) * nh, t + "nbf")
                    g.mprev = TLV(mprevP.t, gi * nh, (gi + 1) * nh, t + "mprev")
                else:
                    g.Sst, g.Sbf = g.W["Sst"], g.W["Sbf"]
                    g.W["dB"] = g.W["dG"]
                    g.W["gre"] = g.W["wT"]
                    g.W["osb"] = g.W["dG"]
                (self.ga if kind == "a" else self.gb).append(g)
        self.alt = dict(Cst=sb("alt_Cst", [128, 1024], F32), Sst=sb("alt_Sst", [128, 1024], F32),
                        nst=sb("alt_nst", [128, NH], F32), mprev=sb("alt_mprev", [128, NH], F32))
        self.gaS, self.gbS = Grp(), Grp()
        for g, kind, spec, small, grps in ((self.gaS, "a", specA_big, smallA, self.ga), (self.gbS, "b", specB_big, smallB, self.gb)):
            g.h0, g.nh, g.kind = 0, NH, kind
            g.PS = PS.sub([0, 1, 2, 3] if kind == "a" else [4, 5, 6, 7])
            g.W = {}
            for n, dt in spec:
                g.W[n] = TLP(parents[(kind, n)].t, [gg.W[n] for gg in grps])
            for n, wd in small(NH):
                g.W[n] = sb("S%s_%s" % (kind, n), [128, wd], F32)
            if kind == "a":
                g.Cst, g.Cbf = g.W["Cst"], g.W["Cbf"]
                g.nst = TLP(nstP.t, [gg.nst for gg in grps])
                g.nbf = TLP(nbfP.t, [gg.nbf for gg in grps])
                g.mprev = TLP(mprevP.t, [gg.mprev for gg in grps])
            else:
                g.Sst, g.Sbf = g.W["Sst"], g.W["Sbf"]
                g.W["dB"] = g.W["dG"]
                g.W["gre"] = g.W["wT"]
                g.W["osb"] = g.W["dG"]

    def mlstm_chunk(self, g, L, c0, q0, full, fm, tm):
        P, PS, C, W, cst = self.P, g.PS, self.C, g.W, self.cst
        bank = PS.banks
        h0, nh = g.h0, g.nh
        cn = str(L)
        GP = tm["GP"]
        lf = GP[0:L, 8 + h0:8 + h0 + nh]
        li = GP[0:L, h0:h0 + nh]
        HL = nh * L
        assert HL <= 512

        wide = nh * 128 > 512

        def m2():
            return PS.pair() if wide else PS.one()

        def f2(b_):
            return PS.f32(b_, 2) if wide else PS.f32(b_)

        def bl2(b_):
            return [bank[b_], bank[b_ + 1]] if wide else [bank[b_]]

        def hb2(b_, h_):
            return bank[b_ + (h_ * 128) // 512]
        identb, onesb = self.identb, self.onesb
        qaT, kaT, ka, va, oa = fm["qaT"], fm["kaT"], tm["ka"], tm["va"], tm["oa"]
        gc = slice(h0 * 128, (h0 + nh) * 128)

        def v3(ap2d):
            return ap2d.rearrange("p (h x) -> p h x", h=nh)

        def bcl(ap, n):
            return ap[:, :, None].to_broadcast([L, nh, n])
        bs = PS.one()
        _mm(P, PS.f32(bs)[0:L, 0:nh], C("uinc" + cn, L), lf, True, True, [cst, GP], [bank[bs]])
        _mm(P, PS.f32(bs)[:, nh:2 * nh], C("ones", L)[:, 0:128], lf, True, True, [cst, GP], [bank[bs]])
        sm1 = W["sm1"]
        _cp(P, "scalar", sm1[:, :], PS.f32(bs)[:, 0:2 * nh], [bank[bs]], [sm1])
        gg = W["g"]
        _tt(P, "vector", gg[0:L, :], li, sm1[0:L, 0:nh], ALU.subtract, [GP, sm1], [gg])
        dgx = W["dgx"]
        _tt(P, "gpsimd", v3(dgx[0:L, 0:HL]), C("ident", L)[:, None, 0:L].to_broadcast([L, nh, L]),
            bcl(gg[0:L, :], L), ALU.mult, [cst, gg], [dgx])
        br = PS.one()
        _mm(P, PS.f32(br)[0:L, 0:HL], C("ones", L)[:, 0:L], dgx[0:L, 0:HL], True, True, [cst, dgx], [bank[br]])
        R = W["R"]
        R3 = v3(R[0:L, 0:HL])
        _tt(P, "vector", R3, v3(PS.f32(br)[0:L, 0:HL]), C("neg" + cn, L)[:, None, :].to_broadcast([L, nh, L]),
            ALU.add, [bank[br], cst], [R])
        cm = W["cm"]
        _red(P, "vector", cm[0:L, :], R3, ALU.max, [R], [cm])
        _tt(P, "vector", R3, R3, bcl(cm[0:L, :], L), ALU.subtract, [R, cm], [R])
        _act(P, R[0:L, 0:HL], R[0:L, 0:HL], AF.Exp, [R], [R])
        if full:
            bq = PS.one()
            for h in range(nh):
                _mm(P, PS.f32(bq)[0:L, h * L:(h + 1) * L], qaT[:, h0 + h, c0:c0 + L], kaT[:, h0 + h, c0:c0 + L],
                    True, True, [qaT, kaT], [bank[bq]])
            sloc = W["sloc"]
            _tt(P, "vector", sloc[0:L, 0:HL], PS.f32(bq)[0:L, 0:HL], R[0:L, 0:HL], ALU.mult, [bank[bq], R], [sloc])
            rows = W["rows"]
            _red(P, "vector", rows[0:L, :], v3(sloc[0:L, 0:HL]), ALU.add, [sloc], [rows])
            bt = PS.one()
            for h in range(nh):
                _tr(P, PS.bf(bt)[0:L, h * L:(h + 1) * L], sloc[0:L, h * L:(h + 1) * L], identb[0:L, 0:L],
                    [sloc, identb], [bank[bt]])
            sTsb = W["sTsb"]
            _cp(P, "scalar", sTsb[0:L, 0:HL], PS.bf(bt)[0:L, 0:HL], [bank[bt]], [sTsb])
            bn = m2()
            for h in range(nh):
                _mm(P, f2(bn)[0:L, h * 128:(h + 1) * 128], sTsb[0:L, h * L:(h + 1) * L],
                    va[0:L, (h0 + h) * 128:(h0 + h + 1) * 128], True, True, [sTsb, va], [hb2(bn, h)])
            nl = W["nl"]
            _cp(P, "scalar", nl[0:L, :], f2(bn)[0:L, :], bl2(bn), [nl])
            gs = W["gs"]
            _tt(P, "gpsimd", gs[0:L, :], oa[0:L, gc], self.gA[0:L, gc], ALU.mult, [oa, self.gA], [gs])
        b2 = PS.one()
        _mm(P, PS.f32(b2)[0:L, 0:nh], C("sel" + cn, L)[:, 0:L], cm[0:L, :], True, True, [cst, cm], [bank[b2]])
        gl = W["gl"]
        _tt(P, "vector", gl[0:L, :], gg[0:L, :], PS.f32(b2)[0:L, 0:nh], ALU.subtract, [gg, bank[b2]], [gl])
        _act(P, gl[0:L, :], gl[0:L, :], AF.Exp, [gl], [gl])
        wlk = W["wlk"]
        _tt(P, "gpsimd", v3(wlk[0:L, :]), v3(ka[0:L, gc]), bcl(gl[0:L, :], 128), ALU.mult, [ka, gl], [wlk])
        bd = m2()
        for h in range(nh):
            _mm(P, f2(bd)[:, h * 128:(h + 1) * 128], wlk[0:L, h * 128:(h + 1) * 128],
                va[0:L, (h0 + h) * 128:(h0 + h + 1) * 128], True, True, [wlk, va], [hb2(bd, h)])
        dCs, dns = W["dCs"], W["dns"]
        _cp(P, "scalar", dCs[:, :], f2(bd)[:, :], bl2(bd), [dCs])
        b3 = PS.one()
        for h in range(nh):
            _mm(P, PS.f32(b3)[:, h:h + 1], wlk[0:L, h * 128:(h + 1) * 128], onesb[0:L, 0:1], True, True,
                [wlk, onesb], [bank[b3]])
        _cp(P, "vector", dns[:, :], PS.f32(b3)[:, 0:nh], [bank[b3]], [dns])
        mprev, Cst, Cbf, nst, nbf = g.mprev, g.Cst, g.Cbf, g.nst, g.nbf
        mx, t12, fd = W["mx"], W["t12"], W["fd"]
        _tt(P, "vector", mx[0:L, 0:nh], mprev[0:L, :], cm[0:L, :], ALU.max, [mprev, cm], [mx])
        _tt(P, "vector", t12[0:L, 0:nh], cm[0:L, :], mx[0:L, 0:nh], ALU.subtract, [cm, mx], [t12])
        _tt(P, "vector", t12[0:L, nh:2 * nh], mprev[0:L, :], mx[0:L, 0:nh], ALU.subtract, [mprev, mx, t12], [t12])
        _act(P, fd[0:L, :], t12[0:L, :], AF.Exp, [t12], [fd])
        _cp(P, "vector", mx[0:L, nh:2 * nh], fd[0:L, 0:nh], [fd, mx], [mx])
        if full:
            bc_ = m2()
            for h in range(nh):
                _mm(P, f2(bc_)[0:L, h * 128:(h + 1) * 128], qaT[:, h0 + h, c0:c0 + L], Cbf[:, h * 128:(h + 1) * 128],
                    True, True, [qaT, Cbf], [hb2(bc_, h)])
            mt, en, dd, d2, a12 = W["mt"], W["en"], W["dd"], W["d2"], W["a12"]
            h1, nl, ssq, hnb = W["h1"], W["nl"], W["ssq"], W["hnb"]
            _tt(P, "vector", mt[0:L, :], sm1[0:L, 0:nh], mx[0:L, 0:nh], ALU.add, [sm1, mx], [mt])
            _act(P, en[0:L, :], mt[0:L, :], AF.Exp, [mt], [en], scale=-1.0)
            _cp(P, "scalar", h1[0:L, :], f2(bc_)[0:L, :], bl2(bc_), [h1])
            b4 = PS.one()
            for h in range(nh):
                _mm(P, PS.f32(b4)[0:L, h:h + 1], qaT[:, h0 + h, c0:c0 + L], nbf[:, h:h + 1], True, True,
                    [qaT, nbf], [bank[b4]])
            _tt(P, "vector", dd[0:L, :], fd[0:L, nh:2 * nh], PS.f32(b4)[0:L, 0:nh], ALU.mult, [fd, bank[b4]], [dd])
            _tt(P, "vector", d2[0:L, :], fd[0:L, 0:nh], W["rows"][0:L, :], ALU.mult, [fd, W["rows"]], [d2])
            _tt(P, "vector", dd[0:L, :], dd[0:L, :], d2[0:L, :], ALU.add, [dd, d2], [dd])
            _act(P, dd[0:L, :], dd[0:L, :], AF.Abs, [dd], [dd])
            _tt(P, "vector", dd[0:L, :], dd[0:L, :], en[0:L, :], ALU.max, [dd, en], [dd])
            P.V(lambda e: e.reciprocal(out=dd[0:L, :], in_=dd[0:L, :]), [dd], [dd])
            _tt(P, "vector", a12[0:L, 0:nh], fd[0:L, nh:2 * nh], dd[0:L, :], ALU.mult, [fd, dd], [a12])
            _tt(P, "vector", a12[0:L, nh:2 * nh], fd[0:L, 0:nh], dd[0:L, :], ALU.mult, [fd, dd, a12], [a12])
            _tt(P, "vector", v3(h1[0:L, :]), v3(h1[0:L, :]), bcl(a12[0:L, 0:nh], 128), ALU.mult, [h1, a12], [h1])
            _tt(P, "gpsimd", v3(nl[0:L, :]), v3(nl[0:L, :]), bcl(a12[0:L, nh:2 * nh], 128), ALU.mult, [nl, a12], [nl])
            _tt(P, "vector", h1[0:L, :], h1[0:L, :], nl[0:L, :], ALU.add, [h1, nl], [h1])
            _act(P, nl[0:L, :], h1[0:L, :], AF.Square, [h1, nl], [nl])
            _red(P, "vector", ssq[0:L, :], v3(nl[0:L, :]), ALU.add, [nl], [ssq])
            _act(P, ssq[0:L, :], ssq[0:L, :], AF.Ln, [ssq], [ssq], bias=EPS, scale=1.0 / DH)
            _act(P, ssq[0:L, :], ssq[0:L, :], AF.Exp, [ssq], [ssq], scale=-0.5)
            _tt(P, "vector", v3(h1[0:L, :]), v3(h1[0:L, :]), bcl(ssq[0:L, :], 128), ALU.mult, [h1, ssq], [h1])
            _tt(P, "gpsimd", hnb[0:L, :], h1[0:L, :], W["gs"][0:L, :], ALU.mult, [h1, W["gs"]], [hnb])
            bh = PS.one()
            for h in range(nh):
                _tr(P, PS.bf(bh)[:, h * L:(h + 1) * L], hnb[0:L, h * 128:(h + 1) * 128], identb[0:L, 0:L],
                    [hnb, identb], [bank[bh]])
            _cp(P, "scalar", self.mixT[:, h0:h0 + nh, q0:q0 + L], PS.bf(bh)[:, 0:HL].rearrange("p (h t) -> p h t", h=nh),
                [bank[bh]], [self.mixT])
        b5 = PS.one()
        _mm(P, PS.f32(b5)[:, 0:2 * nh], C("sel" + cn, L)[:, 0:128], mx[0:L, 0:2 * nh], True, True, [cst, mx], [bank[b5]])
        sm2, dC = W["sm2"], W["dC"]
        _cp(P, "scalar", sm2[:, :], PS.f32(b5)[:, 0:2 * nh], [bank[b5]], [sm2])
        _tt(P, "vector", dC[:, :], mprev[:, :], sm2[:, 0:nh], ALU.subtract, [mprev, sm2], [dC])
        _act(P, dC[:, :], dC[:, :], AF.Exp, [dC], [dC])

        def bc128(ap):
            return ap[:, :, None].to_broadcast([128, nh, 128])
        _tt(P, "vector", v3(Cst[:, :]), v3(Cst[:, :]), bc128(dC[:, :]), ALU.mult, [Cst, dC], [Cst])
        _tt(P, "gpsimd", v3(dCs[:, :]), v3(dCs[:, :]), bc128(sm2[:, nh:2 * nh]), ALU.mult, [dCs, sm2], [dCs])
        _tt(P, "vector", Cst[:, :], Cst[:, :], dCs[:, :], ALU.add, [Cst, dCs], [Cst])
        _tt(P, "vector", nst[:, :], nst[:, :], dC[:, :], ALU.mult, [nst, dC], [nst])
        _tt(P, "vector", dns[:, :], dns[:, :], sm2[:, nh:2 * nh], ALU.mult, [dns, sm2], [dns])
        _tt(P, "vector", nst[:, :], nst[:, :], dns[:, :], ALU.add, [nst, dns], [nst])
        _cp(P, "scalar", Cbf[:, :], Cst[:, :], [Cst], [Cbf])
        _cp(P, "vector", nbf[:, :], nst[:, :], [nst], [nbf])
        _tt(P, "vector", mprev[:, :], sm1[:, nh:2 * nh], sm2[:, 0:nh], ALU.add, [sm1, sm2, mprev], [mprev])

    def gdn_chunk(self, g, L, c0, q0, full, fm, tm):
        P, PS, C, W, cst = self.P, g.PS, self.C, g.W, self.cst
        bank = PS.banks
        h0, nh = g.h0, g.nh
        cn = str(L)
        GP = tm["GP"]
        zb = tm["zb"]
        logg = GP[0:L, 16 + h0:16 + h0 + nh]
        beta = GP[0:L, 24 + h0:24 + h0 + nh]
        HL = nh * L
        assert HL <= 512

        wide = nh * 128 > 512

        def m2():
            return PS.pair() if wide else PS.one()

        def f2(b_):
            return PS.f32(b_, 2) if wide else PS.f32(b_)

        def bl2(b_):
            return [bank[b_], bank[b_ + 1]] if wide else [bank[b_]]

        def hb2(b_, h_):
            return bank[b_ + (h_ * 128) // 512]
        identb = self.identb
        qpost, kpost, vpost = fm["q"], fm["k"], fm["v"]
        Sst, Sbf = g.Sst, g.Sbf
        gc = slice(h0 * 128, (h0 + nh) * 128)

        def v3(ap2d):
            return ap2d.rearrange("p (h x) -> p h x", h=nh)

        def bcl(ap, n):
            return ap[:, :, None].to_broadcast([L, nh, n])
        identL = C("ident", L)[:, None, 0:L].to_broadcast([L, nh, L])
        bs = PS.one()
        _mm(P, PS.f32(bs)[0:L, 0:nh], C("uinc" + cn, L), logg, True, True, [cst, GP], [bank[bs]])
        _mm(P, PS.f32(bs)[:, nh:2 * nh], C("ones", L)[:, 0:128], logg, True, True, [cst, GP], [bank[bs]])
        gsm = W["gsm"]
        _cp(P, "scalar", gsm[:, :], PS.f32(bs)[:, 0:2 * nh], [bank[bs]], [gsm])
        Gt = gsm[0:L, 0:nh]
        dG = W["dG"]
        _tt(P, "gpsimd", v3(dG[0:L, 0:HL]), identL, bcl(Gt, L), ALU.mult, [cst, gsm], [dG])
        bg = PS.one()
        _mm(P, PS.f32(bg)[:, 0:HL], C("ones", L)[:, 0:128], dG[0:L, 0:HL], True, True, [cst, dG], [bank[bg]])
        NZ = W["NZ"]
        _tt(P, "vector", v3(NZ[0:L, 0:HL]), v3(PS.f32(bg)[0:L, 0:HL]), bcl(Gt, L), ALU.subtract, [bank[bg], gsm], [NZ])
        _ts(P, "vector", NZ[0:L, 0:HL], NZ[0:L, 0:HL], 0.0, ALU.min, [NZ], [NZ])
        _act(P, NZ[0:L, 0:HL], NZ[0:L, 0:HL], AF.Exp, [NZ], [NZ])
        if full:
            gre, qg = W["gre"], W["qg"]
            _act(P, gre[:, 0:HL], PS.f32(bg)[:, 0:HL], AF.Exp, [bank[bg]], [gre])
            _tt(P, "vector", v3(qg[:, 0:HL]), qpost[:, h0:h0 + nh, c0:c0 + L], v3(gre[:, 0:HL]), ALU.mult,
                [qpost, gre], [qg])
        dB = W["dB"]
        _tt(P, "gpsimd", v3(dB[0:L, 0:HL]), identL, bcl(beta, L), ALU.mult, [cst, GP, dB], [dB])
        bb_ = PS.one()
        _mm(P, PS.f32(bb_)[0:L, 0:HL], C("ones", L)[:, 0:L], dB[0:L, 0:HL], True, True, [cst, dB], [bank[bb_]])
        wT = W["wT"]
        _tt(P, "gpsimd", v3(wT[0:L, 0:HL]), v3(NZ[0:L, 0:HL]),
            C("ustr" + cn, L)[:, None, :].to_broadcast([L, nh, L]), ALU.mult, [NZ, cst, wT], [wT])
        _tt(P, "vector", wT[0:L, 0:HL], wT[0:L, 0:HL], PS.f32(bb_)[0:L, 0:HL], ALU.mult, [wT, bank[bb_]], [wT])
        if full:
            dTi = W["dTi"]
            _tt(P, "gpsimd", v3(dTi[0:L, 0:HL]), v3(NZ[0:L, 0:HL]),
                C("uinc" + cn, L)[:, None, :].to_broadcast([L, nh, L]), ALU.mult, [NZ, cst], [dTi])
        bk = PS.one()
        for h in range(nh):
            _mm(P, PS.f32(bk)[0:L, h * L:(h + 1) * L], kpost[:, h0 + h, c0:c0 + L], kpost[:, h0 + h, c0:c0 + L],
                True, True, [kpost], [bank[bk]])
        blocked = (L == 128)
        Pc, PTc, Pn_, PTn_ = W["P0"], W["PT0"], W["P1"], W["PT1"]
        Mf = W["Mf"] if blocked else Pc
        _stt(P, "vector", Mf[0:L, 0:HL], PS.f32(bk)[0:L, 0:HL], -1.0, wT[0:L, 0:HL], ALU.mult, ALU.mult,
             [bank[bk], wT], [Mf])
        if full:
            bq = PS.one()
            for h in range(nh):
                _mm(P, PS.f32(bq)[0:L, h * L:(h + 1) * L], kpost[:, h0 + h, c0:c0 + L], qpost[:, h0 + h, c0:c0 + L],
                    True, True, [kpost, qpost], [bank[bq]])
            qkT = W["qkT"]
            _tt(P, "vector", qkT[0:L, 0:HL], PS.f32(bq)[0:L, 0:HL], W["dTi"][0:L, 0:HL], ALU.mult,
                [bank[bq], W["dTi"]], [qkT])
        bt = PS.one()
        for h in range(nh):
            _tr(P, PS.bf(bt)[0:L, h * L:(h + 1) * L], Mf[0:L, h * L:(h + 1) * L], identb[0:L, 0:L],
                [Mf, identb], [bank[bt]])
        if blocked:
            bdm = C("bd32", L)[:, None, :].to_broadcast([L, nh, L])
            MoT, MoT2 = W["MoT"], W["MoT2"]
            _tt(P, "gpsimd", v3(Pc[0:L, 0:HL]), v3(Mf[0:L, 0:HL]), bdm, ALU.mult, [Mf, cst], [Pc])
            _tt(P, "vector", v3(PTc[0:L, 0:HL]), v3(PS.bf(bt)[0:L, 0:HL]), bdm, ALU.mult, [bank[bt], cst], [PTc])
            _tt(P, "vector", v3(MoT[0:L, 0:HL]), v3(PS.bf(bt)[0:L, 0:HL]),
                C("o64", L)[:, None, :].to_broadcast([L, nh, L]), ALU.mult, [bank[bt], cst], [MoT])
            _tt(P, "vector", v3(MoT2[0:L, 0:HL]), v3(PS.bf(bt)[0:L, 0:HL]),
                C("offL128", L)[:, None, :].to_broadcast([L, nh, L]), ALU.mult, [bank[bt], cst], [MoT2])
        else:
            _cp(P, "scalar", PTc[0:L, 0:HL], PS.bf(bt)[0:L, 0:HL], [bank[bt]], [PTc])
        Tacc = W["Tacc"]
        _tt(P, "gpsimd", v3(Tacc[0:L, 0:HL]), v3(Pc[0:L, 0:HL]), identL, ALU.add, [Pc, cst], [Tacc])
        nlev = 4 if blocked else NLEV[L]
        for lev in range(1, nlev + 1):
            b1 = PS.one()
            for h in range(nh):
                sl = slice(h * L, (h + 1) * L)
                _mm(P, PS.f32(b1)[0:L, sl], Pc[0:L, sl], PTc[0:L, sl], True, True, [Pc, PTc], [bank[b1]])
            _cp(P, "scalar", PTn_[0:L, 0:HL], PS.f32(b1)[0:L, 0:HL], [bank[b1]], [PTn_])
            if lev < nlev:
                b2 = PS.one()
                for h in range(nh):
                    sl = slice(h * L, (h + 1) * L)
                    _mm(P, PS.f32(b2)[0:L, sl], PTc[0:L, sl], Pc[0:L, sl], True, True, [Pc, PTc], [bank[b2]])
                _cp(P, "vector", Pn_[0:L, 0:HL], PS.f32(b2)[0:L, 0:HL], [bank[b2]], [Pn_])
            b3 = PS.one()
            for h in range(nh):
                sl = slice(h * L, (h + 1) * L)
                _mm(P, PS.f32(b3)[0:L, sl], PTn_[0:L, sl], Tacc[0:L, sl], True, True, [PTn_, Tacc], [bank[b3]])
            _tt(P, "vector", Tacc[0:L, 0:HL], Tacc[0:L, 0:HL], PS.f32(b3)[0:L, 0:HL], ALU.add, [Tacc, bank[b3]], [Tacc])
            Pc, PTc, Pn_, PTn_ = Pn_, PTn_, Pc, PTc
        if blocked:
            TbT, Xt = Pn_, PTn_
            for Mo in (W["MoT"], W["MoT2"]):
                b7 = PS.one()
                for h in range(nh):
                    sl = slice(h * L, (h + 1) * L)
                    _tr(P, PS.bf(b7)[0:L, sl], Tacc[0:L, sl], identb[0:L, 0:L], [Tacc, identb], [bank[b7]])
                _cp(P, "scalar", TbT[0:L, 0:HL], PS.bf(b7)[0:L, 0:HL], [bank[b7]], [TbT])
                b8 = PS.one()
                for h in range(nh):
                    sl = slice(h * L, (h + 1) * L)
                    _mm(P, PS.f32(b8)[0:L, sl], Mo[0:L, sl], Tacc[0:L, sl], True, True, [Mo, Tacc], [bank[b8]])
                _cp(P, "scalar", Xt[0:L, 0:HL], PS.f32(b8)[0:L, 0:HL], [bank[b8]], [Xt])
                b9 = PS.one()
                for h in range(nh):
                    sl = slice(h * L, (h + 1) * L)
                    _mm(P, PS.f32(b9)[0:L, sl], TbT[0:L, sl], Xt[0:L, sl], True, True, [TbT, Xt], [bank[b9]])
                _tt(P, "vector", Tacc[0:L, 0:HL], Tacc[0:L, 0:HL], PS.f32(b9)[0:L, 0:HL], ALU.add, [Tacc, bank[b9]], [Tacc])
        gsl, gLe = W["gsmall"], W["gLe"]
        _act(P, gsl[0:L, 0:nh], Gt, AF.Exp, [gsm], [gsl])
        _tt(P, "vector", gsl[0:L, 0:nh], gsl[0:L, 0:nh], beta, ALU.mult, [gsl, GP], [gsl])
        _tt(P, "vector", gsl[0:L, nh:2 * nh], gsm[0:L, nh:2 * nh], Gt, ALU.subtract, [gsm, gsl], [gsl])
        _act(P, gsl[0:L, nh:2 * nh], gsl[0:L, nh:2 * nh], AF.Exp, [gsl], [gsl])
        _act(P, gLe[:, :], gsm[:, nh:2 * nh], AF.Exp, [gsm], [gLe])
        kbg, kdec, vbt = W["kbg"], W["kdec"], W["vbt"]
        bkt = PS.one()
        for h in range(nh):
            _tr(P, PS.bf(bkt)[0:L, h * 128:(h + 1) * 128], kpost[:, h0 + h, c0:c0 + L], identb[:, :], [kpost, identb],
                [bank[bkt]])
        _tt(P, "vector", v3(kbg[0:L, :]), v3(PS.bf(bkt)[0:L, 0:nh * 128]), bcl(gsl[0:L, 0:nh], 128), ALU.mult,
            [bank[bkt], gsl], [kbg])
        _tt(P, "vector", v3(kdec[0:L, :]), v3(PS.bf(bkt)[0:L, 0:nh * 128]), bcl(gsl[0:L, nh:2 * nh], 128), ALU.mult,
            [bank[bkt], gsl], [kdec])
        bvt = PS.one()
        for h in range(nh):
            _tr(P, PS.bf(bvt)[0:L, h * 128:(h + 1) * 128], vpost[:, h0 + h, c0:c0 + L], identb[:, :], [vpost, identb],
                [bank[bvt]])
        _tt(P, "vector", v3(vbt[0:L, :]), v3(PS.bf(bvt)[0:L, 0:nh * 128]), bcl(beta, 128), ALU.mult,
            [bank[bvt], GP], [vbt])
        bw = PS.one()
        for h in range(nh):
            _mm(P, PS.f32(bw)[:, h * L:(h + 1) * L], kbg[0:L, h * 128:(h + 1) * 128], Tacc[0:L, h * L:(h + 1) * L],
                True, True, [kbg, Tacc], [bank[bw]])
        WT = W["WT"]
        _cp(P, "scalar", WT[:, 0:HL], PS.f32(bw)[:, 0:HL], [bank[bw]], [WT])
        bu = m2()
        for h in range(nh):
            _mm(P, f2(bu)[0:L, h * 128:(h + 1) * 128], Tacc[0:L, h * L:(h + 1) * L],
                vbt[0:L, h * 128:(h + 1) * 128], True, True, [Tacc, vbt], [hb2(bu, h)])
        U0 = W["U0"]
        _cp(P, "scalar", U0[0:L, :], f2(bu)[0:L, :], bl2(bu), [U0])
        if full:
            gz = W["gz"]
            _tt(P, "gpsimd", gz[0:L, :], zb[0:L, gc], self.gB[0:L, gc], ALU.mult, [zb, self.gB], [gz])
        bpu = m2()
        for h in range(nh):
            _mm(P, f2(bpu)[0:L, h * 128:(h + 1) * 128], WT[:, h * L:(h + 1) * L], Sbf[:, h * 128:(h + 1) * 128],
                True, True, [WT, Sbf], [hb2(bpu, h)])
        u = W["u"]
        _tt(P, "vector", u[0:L, :], U0[0:L, :], f2(bpu)[0:L, :], ALU.subtract, [U0] + bl2(bpu), [u])
        if full:
            bo = m2()
            for h in range(nh):
                o_ = f2(bo)[0:L, h * 128:(h + 1) * 128]
                _mm(P, o_, W["qg"][:, h * L:(h + 1) * L], Sbf[:, h * 128:(h + 1) * 128], True, False,
                    [W["qg"], Sbf], [hb2(bo, h)])
                _mm(P, o_, W["qkT"][0:L, h * L:(h + 1) * L], u[0:L, h * 128:(h + 1) * 128], False, True,
                    [W["qkT"], u], [hb2(bo, h)])
            osb = W["osb"]
            _cp(P, "scalar", osb[0:L, :], f2(bo)[0:L, :], bl2(bo) + [osb], [osb])
        bss = m2()
        for h in range(nh):
            _mm(P, f2(bss)[:, h * 128:(h + 1) * 128], kdec[0:L, h * 128:(h + 1) * 128],
                u[0:L, h * 128:(h + 1) * 128], True, True, [kdec, u], [hb2(bss, h)])
        _tt(P, "vector", v3(Sst[:, :]), v3(Sst[:, :]), gLe[:, :, None].to_broadcast([128, nh, 128]),
            ALU.mult, [Sst, gLe], [Sst])
        _tt(P, "vector", Sst[:, :], Sst[:, :], f2(bss)[:, :], ALU.add, [Sst] + bl2(bss), [Sst])
        _cp(P, "scalar", Sbf[:, :], Sst[:, :], [Sst], [Sbf])
        if full:
            ob, ssq = W["ob"], W["ssq2"]
            _act(P, U0[0:L, :], osb[0:L, :], AF.Square, [osb, U0], [U0])
            _red(P, "vector", ssq[0:L, :], v3(U0[0:L, :]), ALU.add, [U0], [ssq])
            _act(P, ssq[0:L, :], ssq[0:L, :], AF.Ln, [ssq], [ssq], bias=EPS, scale=1.0 / DH)
            _act(P, ssq[0:L, :], ssq[0:L, :], AF.Exp, [ssq], [ssq], scale=-0.5)
            _tt(P, "vector", v3(osb[0:L, :]), v3(osb[0:L, :]), bcl(ssq[0:L, :], 128), ALU.mult, [osb, ssq], [osb])
            _tt(P, "gpsimd", ob[0:L, :], osb[0:L, :], W["gz"][0:L, :], ALU.mult, [osb, W["gz"]], [ob])
            bh = PS.one()
            for h in range(nh):
                _tr(P, PS.bf(bh)[:, h * L:(h + 1) * L], ob[0:L, h * 128:(h + 1) * 128], identb[0:L, 0:L],
                    [ob, identb], [bank[bh]])
            _cp(P, "scalar", self.mixT[:, 8 + h0:8 + h0 + nh, q0:q0 + L],
                PS.bf(bh)[:, 0:HL].rearrange("p (h t) -> p h t", h=nh), [bank[bh]], [self.mixT])

    def _refresh_bf(self):
        P, a, b = self.P, self.gaS, self.gbS
        _cp(P, "scalar", a.Cbf[:, :], a.Cst[:, :], [a.Cst], [a.Cbf])
        _cp(P, "vector", a.nbf[:, :], a.nst[:, :], [a.nst], [a.nbf])
        _cp(P, "scalar", b.Sbf[:, :], b.Sst[:, :], [b.Sst], [b.Sbf])

    def _state_tiles(self):
        return [self.gaS.Cst, self.gaS.nst, self.gaS.mprev, self.gbS.Sst]

    def _use_set(self, i):
        a, b = self.gaS, self.gbS
        if not hasattr(self, "_set0"):
            self._set0 = dict(Cst=a.Cst, Sst=b.Sst, nst=a.nst, mprev=a.mprev)
        st = self._set0 if i == 0 else self.alt
        a.Cst, a.nst, a.mprev, b.Sst = st["Cst"], st["nst"], st["mprev"], st["Sst"]
        a.W["Cst"], b.W["Sst"] = st["Cst"], st["Sst"]

    def load_state(self, j):
        P, io, a, b = self.P, self.io, self.gaS, self.gbS
        pairs = [
            (a.Cst[:, :].rearrange("p (h x) -> p h x", h=NH), io["sC"][j].rearrange("h d e -> d h e")),
            (b.Sst[:, :].rearrange("p (h x) -> p h x", h=NH), io["sS"][j].rearrange("h d e -> d h e")),
            (a.nst[:, :], io["sn"][j]),
            (a.mprev[:, :], io["sm"][j:j + 1, :].partition_broadcast(128)),
        ]
        P.dma("sync", "s2ld", [(lambda e, o=o, i=i: e.dma_start(out=o, in_=i)) for o, i in pairs], w=self._state_tiles())

    def store_state(self, dC, dS, dn, dm):
        P, a, b = self.P, self.gaS, self.gbS
        pairs = [
            (dC.rearrange("h d e -> d h e"), a.Cst[:, :].rearrange("p (h x) -> p h x", h=NH)),
            (dS.rearrange("h d e -> d h e"), b.Sst[:, :].rearrange("p (h x) -> p h x", h=NH)),
            (dn, a.nst[:, :]),
            (dm, a.mprev[0:1, :]),
        ]
        P.dma("sync", "s2st", [(lambda e, o=o, i=i: e.dma_start(out=o, in_=i)) for o, i in pairs], r=self._state_tiles())

    def tm_load(self, L, r0, full, tm):
        P, S = self.P, self.S
        fns = [
            lambda e: e.dma_start(out=tm["ka"][0:L, :], in_=S["ka"][r0:r0 + L, :]),
            lambda e: e.dma_start(out=tm["va"][0:L, :], in_=S["va"][r0:r0 + L, :]),
            lambda e: e.dma_start(out=tm["GP"][0:L, :], in_=S["gt"][r0:r0 + L, :]),
        ]
        wl_ = [tm["ka"], tm["va"], tm["GP"]]
        if full:
            rq = r0 - T0
            fns += [
                lambda e: e.dma_start(out=tm["oa"][0:L, :], in_=S["oa"][rq:rq + L, :]),
                lambda e: e.dma_start(out=tm["zb"][0:L, :], in_=S["zb"][rq:rq + L, :]),
            ]
            wl_ += [tm["oa"], tm["zb"]]
        P.dma("sync", "s2t%d" % self.tm.index(tm), fns, w=wl_)

    def fm_load(self, sc, fm):
        P, S = self.P, self.S
        t0 = sc * 128
        full = sc >= 8
        fns = [
            lambda e: e.dma_start(out=fm["k"][:, :, :], in_=S["kb"][:, :, t0:t0 + 128].rearrange("h d t -> d h t")),
            lambda e: e.dma_start(out=fm["v"][:, :, :], in_=S["vb"][:, :, t0:t0 + 128].rearrange("h d t -> d h t")),
        ]
        wl_ = [fm["k"], fm["v"]]
        if full:
            tq = t0 - T0
            fns += [
                lambda e: e.dma_start(out=fm["q"][:, :, :], in_=S["qb"][:, :, t0:t0 + 128].rearrange("h d t -> d h t")),
                lambda e: e.dma_start(out=fm["qaT"][:, :, :], in_=S["qa"][:, :, tq:tq + 128].rearrange("h d t -> d h t")),
                lambda e: e.dma_start(out=fm["kaT"][:, :, :], in_=S["kaT"][:, :, tq:tq + 128].rearrange("h d t -> d h t")),
            ]
            wl_ += [fm["q"], fm["qaT"], fm["kaT"]]
        P.dma("sync", "s2f%d" % self.fm.index(fm), fns, w=wl_)

    def run(self):
        P, S, io = self.P, self.S, self.io
        for t in self._state_tiles():
            _ms(P, "gpsimd", t[:, :], 0.0, [t])
        self._refresh_bf()
        chunks = []
        for sc in range(NSUB):
            sample = sc == NSUB - 1
            L = LS if sample else LP
            for ch in range(128 // L):
                chunks.append((sc, ch, L, sample))
        self.fm_load(0, self.fm[0])
        self.tm_load(chunks[0][2], 0, False, self.tm[0])
        for idx, (sc, ch, L, sample) in enumerate(chunks):
            full = sc >= 8
            c0 = ch * L
            r0 = sc * 128 + c0
            q0 = r0 - T0
            fm, tm = self.fm[sc % 2], self.tm[idx % 2]
            if idx + 1 < len(chunks):
                nsc, nch, nL, _ = chunks[idx + 1]
                if nsc != sc:
                    self.fm_load(nsc, self.fm[nsc % 2])
                self.tm_load(nL, nsc * 128 + nch * nL, nsc >= 8, self.tm[(idx + 1) % 2])
            if sample:
                if ch == 0:
                    self._use_set(0)
                    self.load_state(0)
                if ch + 1 < 128 // L:
                    self._use_set((ch + 1) % 2)
                    self.load_state(ch + 1)
                self._use_set(ch % 2)
                self._refresh_bf()
            lists = []
            for g in ([self.gaS] if sample else self.ga):
                P.capture()
                self.mlstm_chunk(g, L, c0, q0, full, fm, tm)
                lists.append(P.end_capture())
            for g in ([self.gbS] if sample else self.gb):
                P.capture()
                self.gdn_chunk(g, L, c0, q0, full, fm, tm)
                lists.append(P.end_capture())
            P.replay(lists)
            if sample:
                self.store_state(io["oC"][ch], io["oS"][ch], io["on"][ch], io["om"][ch:ch + 1, :])
            if sc == 7 and ch == 128 // L - 1:
                for t in self._state_tiles():
                    _ts(P, "vector", t[:, :], t[:, :], self.flag[:, 0:1], ALU.mult, [t, self.flag], [t])
                self._refresh_bf()
            if sc == 15 and ch == 128 // L - 1:
                self.store_state(io["pC"], io["pS"], io["pn"], io["pm"])


def _phase3(B, P, PS, cst, C, identb, mixT, x_d, norms_d, modd, wout_d, wup_d, wdn_d, y_d):
    nc = B.nc
    bank = PS.banks
    X = Ctx(nc)
    x1 = X.sb("x1", [128, NSUBQ, D], F32)
    x1s = [TL(x1.t, "x1_%d" % i) for i in range(NSUBQ)]
    wr = [X.sb("p3_w%d" % i, [128, 16, 512], BF16) for i in range(3)]
    mt0 = X.sb("p3_mt0", [128, D], F32)
    mt1 = X.sb("p3_mt1", [128, D], F32)
    st = [X.sb("p3_st%d" % i, [128, 4], F32) for i in range(2)]
    wi = [0]

    def wslot():
        s = wr[wi[0] % 3]
        k = "wr%d" % (wi[0] % 3)
        wi[0] += 1
        return s, k

    def mtile(i):
        return mt0 if i < 8 else mt1

    for i in range(NSUBQ):
        _ld(P, "sync", "p3x%d" % (i % 3), x1[:, i, :], x_d[T0 + i * 128:T0 + (i + 1) * 128, :], w=[x1s[i]])

    XA = Ctx(nc)
    tmpf = XA.sb("p3_tmp", [128, D], F32)
    hb = XA.sb("p3_hb", [128, D], BF16)
    sgA = [XA.sb("p3_sgA%d" % i, [128, 512], F32) for i in range(2)]
    _mod_tiles(P, modd, 4096, mt0, mt1, "p3G")
    ne = 0
    for cb in range(4):
        slot, key = wslot()
        _ld(P, "gpsimd", key, slot[:], wout_d[:, cb * 512:(cb + 1) * 512].rearrange("(k p) c -> p k c", p=128), w=[slot])
        for i in range(NSUBQ):
            bk = PS.one()
            for k in range(16):
                _mm(P, PS.f32(bk)[:, :], mixT[:, k, i * 128:(i + 1) * 128], slot[:, k, :], k == 0, k == 15,
                    [mixT, slot], [bank[bk]])
            sg = sgA[ne % 2]
            ne += 1
            _tt(P, "vector", sg[:], PS.f32(bk)[:, :], mtile(i)[:, cb * 512:(cb + 1) * 512], ALU.mult,
                [bank[bk], mtile(i)], [sg])
            _tt(P, "vector", x1[:, i, cb * 512:(cb + 1) * 512], x1[:, i, cb * 512:(cb + 1) * 512], sg[:], ALU.add,
                [x1s[i], sg], [x1s[i]])

    def norm_tiles(col_sc, col_sh, nrow, first):
        _ld(P, "sync", "p3n", tmpf[:], norms_d[nrow:nrow + 1, :].partition_broadcast(128), w=[tmpf])
        if first:
            _ld(P, "sync", "p3m0", mt0[:], modd[0:1, col_sc:col_sc + D].partition_broadcast(128), w=[mt0])
            _ld(P, "sync", "p3m1", mt1[:], modd[0:1, col_sh:col_sh + D].partition_broadcast(128), w=[mt1])
        else:
            for t, col, key in ((mt0, col_sc, "p3m0"), (mt1, col_sh, "p3m1")):
                fns = []
                for b in range(NSEQ):
                    fns.append(lambda e, b=b, t=t, col=col: e.dma_start(
                        out=t[8 * b:8 * b + 8, :], in_=modd[1 + b:2 + b, col:col + D].partition_broadcast(8)))
                P.dma("sync", key, fns, w=[t])
        _stt(P, "vector", mt0[:], mt0[:], 1.0, tmpf[:], ALU.add, ALU.mult, [mt0, tmpf], [mt0])

    def norm_sub(i, out_ap, out_tl, junk):
        s_ = st[i % 2]
        _act(P, junk[:], x1[:, i, :], AF.Square, [x1s[i]], [junk, s_], accum=s_[:, 0:1])
        _act(P, s_[:, 1:2], s_[:, 0:1], AF.Ln, [s_], [s_], bias=EPS, scale=1.0 / D)
        _act(P, s_[:, 2:3], s_[:, 1:2], AF.Exp, [s_], [s_], scale=-0.5)
        _stt(P, "vector", tmpf[:], x1[:, i, :], s_[:, 2:3], mt0[:], ALU.mult, ALU.mult, [x1s[i], s_, mt0], [tmpf])
        _tt(P, "vector", out_ap, tmpf[:], mt1[:], ALU.add, [tmpf, mt1], [out_tl])

    h2T = mixT
    for i in range(NSUBQ):
        if i == 0:
            norm_tiles(8192, 6144, 1, True)
        if i == 8:
            norm_tiles(8192, 6144, 1, False)
        norm_sub(i, hb[:], hb, hb)
        for half in range(2):
            bk = PS.one()
            for kk in range(8):
                k = half * 8 + kk
                _tr(P, PS.bf(bk)[:, kk * 128:(kk + 1) * 128], hb[:, k * 128:(k + 1) * 128], identb[:],
                    [hb, identb], [bank[bk]])
            _cp(P, "scalar" if half else "vector", h2T[:, half * 8:(half + 1) * 8, i * 128:(i + 1) * 128],
                PS.bf(bk).rearrange("p (k t) -> p k t", k=8), [bank[bk]], [h2T])
    P.barrier()
    P.flush()
    XA.close()

    XB = Ctx(nc)
    actT = XB.sb("actT", [128, 8, NQ], BF16)
    sgB = [XB.sb("p3_sgB%d" % i, [128, 512], F32) for i in range(2)]
    rl = [XB.sb("p3_rl%d" % i, [128, 512], F32) for i in range(2)]
    _mod_tiles(P, modd, 10240, mt0, mt1, "p3G")
    ttiles = [(0, 512), (512, 512), (1024, 128)]
    ne = 0
    nr = 0
    for fb in range(8):
        for half in range(2):
            slot, key = wslot()
            c0 = fb * 1024 + half * 512
            _ld(P, "gpsimd", key, slot[:], wup_d[:, c0:c0 + 512].rearrange("(k p) c -> p k c", p=128), w=[slot])
            for sbk in range(4):
                for (t0, nt) in ttiles:
                    bk = PS.one()
                    for k in range(16):
                        _mm(P, PS.f32(bk)[:, 0:nt], slot[:, k, sbk * 128:(sbk + 1) * 128], h2T[:, k, t0:t0 + nt],
                            k == 0, k == 15, [slot, h2T], [bank[bk]])
                    r_ = rl[nr % 2]
                    nr += 1
                    _act(P, r_[:, 0:nt], PS.f32(bk)[:, 0:nt], AF.Relu, [bank[bk]], [r_])
                    _tt(P, "vector", actT[:, half * 4 + sbk, t0:t0 + nt], r_[:, 0:nt], r_[:, 0:nt], ALU.mult,
                        [r_], [actT])
        sd = []
        for half in range(2):
            slot, key = wslot()
            r0 = fb * 1024 + half * 512
            sv = slot[:, :, :].rearrange("p k c -> p (k c)").rearrange("p (s c) -> p s c", s=4)
            _ld(P, "gpsimd", key, sv, wdn_d[r0:r0 + 512, :].rearrange("(s p) c -> p s c", p=128), w=[slot])
            sd.append((slot, sv))
        for i in range(NSUBQ):
            for cb in range(4):
                bk = PS.one()
                for s8 in range(8):
                    slot, sv = sd[s8 // 4]
                    _mm(P, PS.f32(bk)[:, :], actT[:, s8, i * 128:(i + 1) * 128], sv[:, s8 % 4, cb * 512:(cb + 1) * 512],
                        s8 == 0, s8 == 7, [actT, slot], [bank[bk]])
                sg = sgB[ne % 2]
                ne += 1
                _tt(P, "vector", sg[:], PS.f32(bk)[:, :], mtile(i)[:, cb * 512:(cb + 1) * 512], ALU.mult,
                    [bank[bk], mtile(i)], [sg])
                _tt(P, "vector", x1[:, i, cb * 512:(cb + 1) * 512], x1[:, i, cb * 512:(cb + 1) * 512], sg[:], ALU.add,
                    [x1s[i], sg], [x1s[i]])
    P.barrier()
    P.flush()
    XB.close()

    XC = Ctx(nc)
    tmpf = XC.sb("p3c_tmp", [128, D], F32)
    yo = [XC.sb("p3c_y%d" % i, [128, D], F32) for i in range(2)]
    junk = XC.sb("p3c_junk", [128, D], BF16)
    for i in range(NSUBQ):
        if i == 0:
            norm_tiles(14336, 12288, 2, True)
        if i == 8:
            norm_tiles(14336, 12288, 2, False)
        y_ = yo[i % 2]
        norm_sub(i, y_[:], y_, junk)
        _ld(P, "sync", "p3y%d" % (i % 2), y_d[i * 128:(i + 1) * 128, :], y_[:], r=[y_])
    P.barrier()
    P.flush()
    XC.close()
    X.close()


def kernel(**inputs):
    inp = {k: np.asarray(v) for k, v in inputs.items()}
    B = build(debug=False)
    sh = _shared_inputs(inp)
    in_maps = []
    for c in range(8):
        m = _core_inputs(inp, c, sh)
        in_maps.append({k: v for k, v in m.items() if k in B.ins})
    res = run_bass_kernel_spmd(B.nc, in_maps, core_ids=list(range(8)))
    r = [{k: np.asarray(v) for k, v in rr.items()} for rr in res.results]

    y_prompt = np.empty((4, 2048, D), np.float32)
    y_sample = np.empty((128, LS, D), np.float32)
    pC = np.empty((1, 4, NH, DH, DH), np.float32)
    pn = np.empty((1, 4, NH, DH), np.float32)
    pm = np.empty((1, 4, NH), np.float32)
    pS = np.empty((1, 4, NH, DH, DH), np.float32)
    pconv = np.empty((1, 4, 3, 3072), np.float32)
    sC = np.empty((1, 128, NH, DH, DH), np.float32)
    sn = np.empty((1, 128, NH, DH), np.float32)
    sm = np.empty((1, 128, NH), np.float32)
    sS = np.empty((1, 128, NH, DH, DH), np.float32)
    sconv = np.empty((1, 128, 3, 3072), np.float32)
    for c in range(8):
        b, half = c // 2, c % 2
        sl = slice(c * NSEQ, (c + 1) * NSEQ)
        o = r[c]
        y_prompt[b, half * T1:(half + 1) * T1] = o["y"][:T1]
        y_sample[sl] = o["y"][T1:].reshape(NSEQ, LS, D)
        if half == 1:
            pC[0, b] = o["pC"]
            pn[0, b] = o["pn_t"].T
            pm[0, b] = o["pm"][0]
            pS[0, b] = o["pS"]
            pconv[0, b] = o["pconv_t"].reshape(128, 24, 3).transpose(2, 1, 0).reshape(3, 3072)
        sC[0, sl] = o["oC"]
        sn[0, sl] = o["on_t"].transpose(0, 2, 1)
        sm[0, sl] = o["om"]
        sS[0, sl] = o["oS"]
        sconv[0, sl] = o["oconv_t"].reshape(128, 24, NSEQ, 3).transpose(2, 3, 1, 0).reshape(NSEQ, 3, 3072)
    return (y_prompt, y_sample, pC, pn, pm, pS, pconv, sC, sn, sm, sS, sconv)
```

```python
import numpy as np
import concourse.bass as bass
import concourse.mybir as mybir
from concourse.bass_utils import run_bass_kernel_spmd

F32 = mybir.dt.float32
BF16 = mybir.dt.bfloat16
AF = mybir.ActivationFunctionType
ALU = mybir.AluOpType
AX = mybir.AxisListType

D = 2048
KD = 16
NH = 8
DH = 128
T0 = 1024
T1 = 1024
TS = 128
NSEQ = 16
LS = 8
NTOK = T0 + T1 + TS
NQ = T1 + TS
NSUB = NTOK // 128
NSUBQ = NQ // 128
DFF = 8192
EPS = 1e-6
NEG = -1.0e30
WIN_COLS = 5120 + 4096 + 32

ENGS = ("tensor", "vector", "scalar", "gpsimd", "sync")


class Buf:
    __slots__ = ("name", "w", "r")

    def __init__(self, name=""):
        self.name = name
        self.w = None
        self.r = {}


class TL:
    def __init__(self, t, name=""):
        self.t = t
        self.buf = Buf(name)

    def __getitem__(self, k):
        return self.t[k]


class Prog:
    def __init__(self, nc):
        self.nc = nc
        self.sems = {}
        self.cnt = {}
        self.ops = {e: [] for e in ENGS}
        self.seen = {e: {} for e in ENGS}
        for e in ENGS:
            self.sems[e] = nc.alloc_semaphore(name="s_" + e)
            self.cnt[e] = 0
        self.dkeys = []
        self.nins = 0

    def key(self, name):
        k = "d_" + name
        if k not in self.sems:
            self.sems[k] = self.nc.alloc_semaphore(name="s" + k)
            self.cnt[k] = 0
            self.dkeys.append(k)
        return k

    def _deps(self, eng, reads, writes):
        deps = {}

        def add(kv):
            if kv is None:
                return
            k, v = kv
            if k == eng and eng == "tensor":
                return
            if deps.get(k, 0) < v:
                deps[k] = v
        for b in reads:
            add(b.buf.w)
        for b in writes:
            add(b.buf.w)
            for kv in b.buf.r.items():
                add(kv)
        out = []
        seen = self.seen[eng]
        for k, v in deps.items():
            if seen.get(k, 0) >= v:
                continue
            seen[k] = v
            out.append((k, v))
        return out

    def capture(self):
        self.cap = []
        return self.cap

    def end_capture(self):
        c = self.cap
        self.cap = None
        return c

    def replay(self, lists):
        lists = [l for l in lists if l]
        if not lists:
            return
        n = max(len(l) for l in lists)
        pos = [0] * len(lists)
        for i in range(1, n + 1):
            for j, l in enumerate(lists):
                tgt = (i * len(l) + n - 1) // n
                while pos[j] < tgt:
                    it = l[pos[j]]
                    pos[j] += 1
                    if it[0] == "op":
                        self.op(*it[1:])
                    else:
                        self.dma(*it[1:])

    def _emit_item(self, it):
        if it[0] == "op":
            self.op(*it[1:])
        else:
            self.dma(*it[1:])

    def replay_pipe(self, items, depth, burst=2):
        active = []
        nxt = 0

        def prefix(it):
            n = 0
            while n < len(it) and it[n][0] == "op" and it[n][1] == "tensor":
                n += 1
            return n

        def rw(ops):
            rs, ws = set(), set()
            for o in ops:
                for b_ in self._fl(o[-2]):
                    rs.add(id(b_.buf))
                for b_ in self._fl(o[-1]):
                    ws.add(id(b_.buf))
            return rs, ws

        def conflict(it):
            r1, w1 = rw(it)
            for a_ in active:
                r2, w2 = rw(a_[0][a_[1]:])
                if (w1 & (r2 | w2)) or (r1 & w2):
                    return True
            return False
        while nxt < len(items) or active:
            while nxt < len(items) and len(active) < depth and (
                    not active or (active[-1][1] - active[-1][2]) >= max(1, (len(active[-1][0]) - active[-1][2]) // depth)):
                it = items[nxt]
                if active and conflict(it):
                    break
                nxt += 1
                npre = prefix(it)
                for i in range(npre):
                    self._emit_item(it[i])
                if npre < len(it):
                    active.append([it, npre, npre])
            for a in list(active):
                for _ in range(burst):
                    if a[1] < len(a[0]):
                        self._emit_item(a[0][a[1]])
                        a[1] += 1
                if a[1] >= len(a[0]):
                    active.remove(a)

    @staticmethod
    def _fl(lst):
        out = []
        for b in lst:
            if hasattr(b, "kids"):
                out.extend(b.kids)
            else:
                out.append(b)
        return out

    def op(self, eng, fn, r=(), w=()):
        if getattr(self, "cap", None) is not None:
            self.cap.append(("op", eng, fn, tuple(r), tuple(w)))
            return
        r, w = self._fl(r), self._fl(w)
        waits = self._deps(eng, r, w)
        self.cnt[eng] += 1
        v = self.cnt[eng]
        self.ops[eng].append((waits, fn, eng, 1))
        for b in r:
            if b.buf.r.get(eng, 0) < v:
                b.buf.r[eng] = v
        for b in w:
            b.buf.w = (eng, v)
            b.buf.r = {}
        self.nins += 1

    def V(self, fn, r=(), w=()):
        self.op("vector", fn, r, w)

    def A(self, fn, r=(), w=()):
        self.op("scalar", fn, r, w)

    def G(self, fn, r=(), w=()):
        self.op("gpsimd", fn, r, w)

    def T(self, fn, r=(), w=()):
        self.op("tensor", fn, r, w)

    def dma(self, eng, key, fns, r=(), w=()):
        if not isinstance(fns, (list, tuple)):
            fns = [fns]
        if getattr(self, "cap", None) is not None:
            self.cap.append(("dma", eng, key, fns, tuple(r), tuple(w)))
            return
        r, w = self._fl(r), self._fl(w)
        key = self.key(key) if not key.startswith("d_") else key
        waits = self._deps(eng, r, w)
        for i, fn in enumerate(fns):
            self.cnt[key] += 16
            self.ops[eng].append((waits if i == 0 else [], fn, key, 16))
        v = self.cnt[key]
        for b in r:
            if b.buf.r.get(key, 0) < v:
                b.buf.r[key] = v
        for b in w:
            b.buf.w = (key, v)
            b.buf.r = {}
        self.nins += len(fns)

    def barrier(self):
        for e in ENGS:
            waits = []
            for k, v in self.cnt.items():
                if k == e or v == 0:
                    continue
                if self.seen[e].get(k, 0) >= v:
                    continue
                self.seen[e][k] = v
                waits.append((k, v))
            self.ops[e].append((waits, None, None, 0))

    def flush(self, final=False):
        nc = self.nc
        sems = self.sems
        ops = self.ops

        def run(e, name):
            for waits, fn, key, inc in ops[name]:
                for k, v in waits:
                    e.wait_ge(sems[k], v)
                if fn is not None:
                    fn(e).then_inc(sems[key], inc)

        with nc.Block() as block:
            @block.tensor
            def _(e):
                run(e, "tensor")

            @block.vector
            def _(e):
                run(e, "vector")

            @block.scalar
            def _(e):
                run(e, "scalar")

            @block.gpsimd
            def _(e):
                run(e, "gpsimd")

            @block.sync
            def _(e):
                run(e, "sync")
        self.ops = {e: [] for e in ENGS}


def _const_layout():
    off = {}
    c = 0
    for name, n in (("ident", 128), ("ones", 128), ("neg128", 128), ("uinc128", 128), ("ustr128", 128),
                    ("sel128", 128), ("bd32", 128), ("o64", 128), ("offL128", 128), ("neg8", 8), ("uinc8", 8), ("ustr8", 8), ("sel8", 128),
                    ("sign", 16)):
        off[name] = (c, n)
        c += n
    return off, c


CO, NCONST = _const_layout()


def _build_consts():
    a = np.zeros((128, NCONST), np.float32)

    def put(name, m):
        o, n = CO[name]
        a[: m.shape[0], o:o + m.shape[1]] = m
    put("ident", np.eye(128, dtype=np.float32))
    put("ones", np.ones((128, 128), np.float32))
    for L, sfx in ((128, "128"), (8, "8")):
        t = np.arange(L)
        neg = np.where(t[None, :] <= t[:, None], 0.0, NEG).astype(np.float32)
        uinc = (t[None, :] >= t[:, None]).astype(np.float32)
        ustr = (t[None, :] > t[:, None]).astype(np.float32)
        sel = np.zeros((L, 128), np.float32)
        sel[L - 1, :] = 1.0
        put("neg" + sfx, neg)
        put("uinc" + sfx, uinc)
        put("ustr" + sfx, ustr)
        put("sel" + sfx, sel)
    bd32 = np.zeros((128, 128), np.float32)
    bd64 = np.zeros((128, 128), np.float32)
    for i in range(0, 128, 32):
        bd32[i:i + 32, i:i + 32] = 1.0
    for i in range(0, 128, 64):
        bd64[i:i + 64, i:i + 64] = 1.0
    put("bd32", bd32)
    put("o64", bd64 - bd32)
    ofl = np.zeros((128, 128), np.float32)
    ofl[64:, :64] = 1.0
    put("offL128", ofl)
    sg = np.ones((128, 16), np.float32)
    sg[:, 0:8] = -1.0
    put("sign", sg)
    return a


class Builder:
    def __init__(self, debug=False, phases=(0, 1, 2, 3)):
        self.debug = debug
        self.phases = phases
        nc = bass.Bass("TRN2", target_bir_lowering=False)
        self.nc = nc
        self.P = Prog(nc)
        self.ins = {}
        self.outs = {}
        self.psum = TLBank(nc)

    def din(self, name, shape, dt=F32):
        t = self.nc.dram_tensor(name, list(shape), dt, kind="ExternalInput").ap()
        self.ins[name] = t
        return t

    def dout(self, name, shape, dt=F32):
        t = self.nc.dram_tensor(name, list(shape), dt, kind="ExternalOutput").ap()
        self.outs[name] = t
        return t

    def dscr(self, name, shape, dt):
        kind = "ExternalOutput" if self.debug else "Internal"
        t = self.nc.dram_tensor(name, list(shape), dt, kind=kind).ap()
        if self.debug:
            self.outs[name] = t
        return t


class TLBank:
    def __init__(self, nc):
        self.t = nc.alloc_psum_tensor("psum_all", [128, 4096], F32)
        self.banks = [TL(None, "bank%d" % i) for i in range(8)]
        self.ptr = 0

        self.ids = list(range(8))

    def sub(self, ids):
        o = TLBank.__new__(TLBank)
        o.t, o.banks, o.ptr, o.ids = self.t, self.banks, 0, list(ids)
        return o

    def one(self):
        i = self.ids[self.ptr]
        self.ptr = (self.ptr + 1) % len(self.ids)
        return i

    def pair(self):
        if self.ptr % 2:
            self.ptr = (self.ptr + 1) % len(self.ids)
        i = self.ids[self.ptr]
        self.ptr = (self.ptr + 2) % len(self.ids)
        return i

    def f32(self, i, n=1):
        return self.t[:, i * 512:(i + n) * 512]

    def bf(self, i, n=1):
        return self.t[:, i * 512:(i + n) * 512].bitcast(BF16)


def _mm(P, out, lhsT, rhs, start, stop, r, w):
    P.T(lambda e: e.matmul(out, lhsT=lhsT, rhs=rhs, start=start, stop=stop), r, w)


def _tr(P, out, in_, ident, r, w):
    P.T(lambda e: e.transpose(out=out, in_=in_, identity=ident), r, w)


def _act(P, out, in_, func, r, w, bias=None, scale=None, accum=None):
    kw = {}
    if bias is not None:
        kw["bias"] = bias
    if scale is not None:
        kw["scale"] = scale
    if accum is not None:
        kw["accum_out"] = accum
    P.A(lambda e: e.activation(out=out, in_=in_, func=func, **kw), r, w)


def _tt(P, eng, out, in0, in1, op, r, w):
    P.op(eng, lambda e: e.tensor_tensor(out=out, in0=in0, in1=in1, op=op), r, w)


def _ts(P, eng, out, in0, s1, op0, r, w, s2=None, op1=None):
    if op1 is None:
        P.op(eng, lambda e: e.tensor_single_scalar(out=out, in_=in0, scalar=s1, op=op0), r, w)
    else:
        P.op(eng, lambda e: e.tensor_scalar(out=out, in0=in0, scalar1=s1, scalar2=s2, op0=op0, op1=op1), r, w)


def _stt(P, eng, out, in0, scalar, in1, op0, op1, r, w):
    P.op(eng, lambda e: e.scalar_tensor_tensor(out=out, in0=in0, scalar=scalar, in1=in1, op0=op0, op1=op1), r, w)


def _red(P, eng, out, in_, op, r, w):
    P.op(eng, lambda e: e.tensor_reduce(out=out, in_=in_, axis=AX.X, op=op), r, w)


def _cp(P, eng, out, in_, r, w):
    if eng == "scalar":
        P.A(lambda e: e.activation(out=out, in_=in_, func=AF.Copy), r, w)
    else:
        P.op(eng, lambda e: e.tensor_copy(out=out, in_=in_), r, w)


def _ms(P, eng, ap, val, w):
    P.op(eng, lambda e: e.memset(ap, val), (), w)


def _ld(P, eng, key, out, in_, r=(), w=()):
    P.dma(eng, key, lambda e: e.dma_start(out=out, in_=in_), r, w)


class Ctx:
    def __init__(self, nc):
        self.nc = nc
        self.guards = []

    def sb(self, name, shape, dt):
        g = self.nc.sbuf_tensor(name, list(shape), dt)
        t = g.__enter__()
        self.guards.append(g)
        return TL(t, name)

    def close(self):
        for g in reversed(self.guards):
            g.__exit__(None, None, None)
        self.guards = []


def build(debug=False, phases=(0, 1, 2, 3)):
    B = Builder(debug, phases)
    nc, P, PS = B.nc, B.P, B.psum
    bankb = PS.banks

    x_d = B.din("x", [NTOK, D])
    c_d = B.din("c", [17, D])
    flag_d = B.din("flag", [128, 1])
    consts_d = B.din("consts", [128, NCONST])
    adaw_d = B.din("ada_w", [D, 16384])
    adab_d = B.din("ada_b", [1, 16384])
    win_d = B.din("w_in", [D, WIN_COLS])
    wout_d = B.din("w_out", [D, D])
    wup_d = B.din("w_up", [D, DFF])
    wdn_d = B.din("w_down", [DFF, D])
    norms_d = B.din("norms", [3, D])
    hnorm_d = B.din("hnorm", [2, 1024])
    gvec_d = B.din("gvec", [1, 32])
    convw_d = B.din("conv_w", [128, 24 * 4])
    sC_d = B.din("sC", [NSEQ, NH, DH, DH])
    sn_d = B.din("sn_t", [NSEQ, DH, NH])
    sm_d = B.din("sm", [NSEQ, NH])
    sS_d = B.din("sS", [NSEQ, NH, DH, DH])
    scv_d = B.din("sconv_t", [128, 24 * NSEQ * 3])

    y_d = B.dout("y", [NQ, D])
    pC_d = B.dout("pC", [NH, DH, DH])
    pn_d = B.dout("pn_t", [DH, NH])
    pm_d = B.dout("pm", [1, NH])
    pS_d = B.dout("pS", [NH, DH, DH])
    pcv_d = B.dout("pconv_t", [128, 24 * 3])
    oC_d = B.dout("oC", [NSEQ, NH, DH, DH])
    on_d = B.dout("on_t", [NSEQ, DH, NH])
    om_d = B.dout("om", [NSEQ, NH])
    oS_d = B.dout("oS", [NSEQ, NH, DH, DH])
    ocv_d = B.dout("oconv_t", [128, 24 * NSEQ * 3])

    modd = B.dscr("modd", [17, 16384], F32)
    s_qa = B.dscr("s_qa", [NH, DH, NQ], BF16)
    s_kaT = B.dscr("s_kaT", [NH, DH, NQ], BF16)
    s_qb = B.dscr("s_qb", [NH, DH, NTOK], BF16)
    s_kb = B.dscr("s_kb", [NH, DH, NTOK], BF16)
    s_vb = B.dscr("s_vb", [NH, DH, NTOK], BF16)
    s_ka = B.dscr("s_ka", [NTOK, 1024], BF16)
    s_va = B.dscr("s_va", [NTOK, 1024], BF16)
    s_oa = B.dscr("s_oa", [NQ, 1024], BF16)
    s_zb = B.dscr("s_zb", [NQ, 1024], BF16)
    s_gt = B.dscr("s_gt", [NTOK, 32], F32)

    G = Ctx(nc)
    cst = G.sb("consts_sb", [128, NCONST], F32)
    identb = G.sb("identb", [128, 128], BF16)
    onesb = G.sb("onesb", [128, 128], BF16)
    flag = G.sb("flag_sb", [128, 1], F32)
    _ld(P, "sync", "g0", cst[:], consts_d, w=[cst])
    _ld(P, "sync", "g1", flag[:], flag_d, w=[flag])

    def C(name, rows=128):
        o, n = CO[name]
        return cst[0:rows, o:o + n]
    _cp(P, "vector", identb[:], C("ident"), [cst], [identb])
    _cp(P, "vector", onesb[:], C("ones"), [cst], [onesb])

    cT = G.sb("cT_sb", [128, 16 * 17], BF16)
    if 0 in phases:
        _phase0(B, P, PS, cst, C, c_d, adaw_d, adab_d, modd, cT)
        P.barrier()
        P.flush()
    if 1 in phases:
        _phase1(B, P, PS, cst, C, identb, onesb, flag, x_d, norms_d, modd, win_d,
                dict(qa=s_qa, kaT=s_kaT, qb=s_qb, kb=s_kb, vb=s_vb, ka=s_ka, va=s_va, oa=s_oa, zb=s_zb, gt=s_gt),
                gvec_d, convw_d, scv_d, pcv_d, ocv_d, cT, adaw_d, adab_d)
    S = dict(qa=s_qa, kaT=s_kaT, qb=s_qb, kb=s_kb, vb=s_vb, ka=s_ka, va=s_va, oa=s_oa, zb=s_zb, gt=s_gt)
    G2 = Ctx(nc)
    mixT = G2.sb("mixT", [128, 16, NQ], BF16)
    if debug:
        mixd = B.dout("mix_dbg", [128, 16 * NQ], BF16)
    if 2 in phases:
        io = dict(sC=sC_d, sn=sn_d, sm=sm_d, sS=sS_d, scv=scv_d, pC=pC_d, pn=pn_d, pm=pm_d, pS=pS_d, pcv=pcv_d,
                  oC=oC_d, on=on_d, om=om_d, oS=oS_d, ocv=ocv_d)
        sc = Scan(B, P, PS, cst, C, identb, onesb, flag, mixT, S, hnorm_d, gvec_d, convw_d, io)
        sc.run()
        if debug:
            _ld(P, "sync", "dbgm", mixd, mixT[:, :, :].rearrange("p k t -> p (k t)"), r=[mixT])
        P.barrier()
        P.flush()
        sc.X.close()
    if 3 in phases:
        _phase3(B, P, PS, cst, C, identb, mixT, x_d, norms_d, modd, wout_d, wup_d, wdn_d, y_d)
    P.barrier()
    P.flush()
    return B


NADA0 = 8


def _ada_block(P, PS, jb, slot, key, bt, bkey, m, mkey, cT, adaw_d, adab_d, modd):
    cols = slice(jb * 512, (jb + 1) * 512)
    _ld(P, "gpsimd", key, slot[:], adaw_d[:, cols].rearrange("(k p) c -> p k c", p=128), w=[slot])
    _ld(P, "sync", bkey, bt[:], adab_d[0:1, cols].partition_broadcast(17), w=[bt])
    bk = PS.one()
    for k in range(16):
        _mm(P, PS.f32(bk)[0:17, :], cT[:, k * 17:(k + 1) * 17], slot[:, k, :], k == 0, k == 15,
            [cT, slot], [PS.banks[bk]])
    _tt(P, "vector", m[:], PS.f32(bk)[0:17, :], bt[:], ALU.add, [PS.banks[bk], bt], [m])
    _ld(P, "sync", mkey, modd[:, cols], m[:], r=[m])


def _phase0(B, P, PS, cst, C, c_d, adaw_d, adab_d, modd, cT):
    nc = B.nc
    X = Ctx(nc)
    c_sb = X.sb("p0_c", [17, D], F32)
    wr = [X.sb("p0_w%d" % i, [128, 16, 512], BF16) for i in range(3)]
    bias = [X.sb("p0_b%d" % i, [17, 512], F32) for i in range(2)]
    mo = [X.sb("p0_m%d" % i, [17, 512], F32) for i in range(2)]
    _ld(P, "sync", "p0c", c_sb[:], c_d, w=[c_sb])
    _act(P, c_sb[:], c_sb[:], AF.Silu, [c_sb], [c_sb])
    bk = PS.one()
    for k in range(16):
        _tr(P, PS.f32(bk)[:, k * 17:(k + 1) * 17], c_sb[:, k * 128:(k + 1) * 128], C("ident", 17)[:, 0:17],
            [c_sb, cst], [PS.banks[bk]])
    _cp(P, "vector", cT[:], PS.f32(bk)[:, 0:272], [PS.banks[bk]], [cT])
    for jb in range(NADA0):
        _ada_block(P, PS, jb, wr[jb % 3], "wr%d" % (jb % 3), bias[jb % 2], "p0b%d" % (jb % 2), mo[jb % 2],
                   "p0m%d" % (jb % 2), cT, adaw_d, adab_d, modd)
    X.close()


def _mod_tiles(P, modd, col0, tp, ts, key):
    _ld(P, "sync", key + "p", tp[:], modd[0:1, col0:col0 + D].partition_broadcast(128), w=[tp])
    fns = []
    for b in range(NSEQ):
        fns.append(lambda e, b=b: e.dma_start(out=ts[8 * b:8 * b + 8, :],
                                              in_=modd[1 + b:2 + b, col0:col0 + D].partition_broadcast(8)))
    P.dma("sync", key + "s", fns, w=[ts])


def _phase1(B, P, PS, cst, C, identb, onesb, flag, x_d, norms_d, modd, win_d, S, gvec_d, convw_d, scv_d, pcv_d, ocv_d,
            cT, adaw_d, adab_d):
    nc = B.nc
    bank = PS.banks
    X = Ctx(nc)
    hT = X.sb("hT", [128, 16, NTOK], BF16)
    wr = [X.sb("p1_w%d" % i, [128, 16, 512], BF16) for i in range(3)]

    XA = Ctx(nc)
    xt = [XA.sb("p1_x%d" % i, [128, D], F32) for i in range(2)]
    tmpf = XA.sb("p1_tmp", [128, D], F32)
    hb = [XA.sb("p1_hb%d" % i, [128, D], BF16) for i in range(2)]
    Ap = XA.sb("p1_Ap", [128, D], F32)
    Bp = XA.sb("p1_Bp", [128, D], F32)
    As = XA.sb("p1_As", [128, D], F32)
    Bs = XA.sb("p1_Bs", [128, D], F32)
    st = [XA.sb("p1_st%d" % i, [128, 4], F32) for i in range(2)]
    _ld(P, "sync", "p1n", tmpf[:], norms_d[0:1, :].partition_broadcast(128), w=[tmpf])
    _mod_tiles(P, modd, 2048, Ap, As, "p1A")
    _mod_tiles(P, modd, 0, Bp, Bs, "p1B")
    for A_ in (Ap, As):
        _stt(P, "vector", A_[:], A_[:], 1.0, tmpf[:], ALU.add, ALU.mult, [A_, tmpf], [A_])
    wq = []

    def wload(jb):
        slot = wr[jb % 3]
        ncol = 512 if jb < 18 else 32
        _ld(P, "gpsimd", "wr%d" % (jb % 3), slot[:, :, 0:ncol],
            win_d[:, jb * 512:jb * 512 + ncol].rearrange("(k p) c -> p k c", p=128), w=[slot])
    for jb in range(3):
        wload(jb)
    for i in range(NSUB):
        x_ = xt[i % 2]
        h_ = hb[i % 2]
        s_ = st[i % 2]
        _ld(P, "sync", "p1x%d" % (i % 2), x_[:], x_d[i * 128:(i + 1) * 128, :], w=[x_])
        _act(P, h_[:], x_[:], AF.Square, [x_], [h_, s_], accum=s_[:, 0:1])
        _act(P, s_[:, 1:2], s_[:, 0:1], AF.Ln, [s_], [s_], bias=EPS, scale=1.0 / D)
        _act(P, s_[:, 2:3], s_[:, 1:2], AF.Exp, [s_], [s_], scale=-0.5)
        A_, B_ = (Ap, Bp) if i < 16 else (As, Bs)
        _stt(P, "vector", tmpf[:], x_[:], s_[:, 2:3], A_[:], ALU.mult, ALU.mult, [x_, s_, A_], [tmpf])
        _tt(P, "vector", h_[:], tmpf[:], B_[:], ALU.add, [tmpf, B_], [h_])
        for half in range(2):
            bk = PS.one()
            for kk in range(8):
                k = half * 8 + kk
                _tr(P, PS.bf(bk)[:, kk * 128:(kk + 1) * 128], h_[:, k * 128:(k + 1) * 128], identb[:],
                    [h_, identb], [PS.banks[bk]])
            _cp(P, "scalar" if half else "vector", hT[:, half * 8:(half + 1) * 8, i * 128:(i + 1) * 128],
                PS.bf(bk).rearrange("p (k t) -> p k t", k=8), [PS.banks[bk]], [hT])
    P.barrier()
    P.flush()
    XA.close()

    XB = Ctx(nc)
    stg = [XB.sb("p1_stg%d" % i, [128, 512], BF16) for i in range(4)]
    NU = 4
    stg = stg + [XB.sb("p1_stg%d" % i, [128, 512], BF16) for i in range(4, 8)]
    pds = [XB.sb("p1_pd%d" % i, [128, 3 + 512], F32) for i in range(NU)]
    accs = [XB.sb("p1_acc%d" % i, [128, 512], F32) for i in range(NU)]
    efs = [XB.sb("p1_ef%d" % i, [128, 512], F32) for i in range(NU)]
    sqs = [XB.sb("p1_sq%d" % i, [128, 512], BF16) for i in range(NU)]
    rvs = accs
    cvs = XB.sb("p1_cvs", [128, 24 * NSEQ * 3], F32)
    pcvt = XB.sb("p1_pcvt", [128, 72], F32)
    convw = XB.sb("p1_convw", [128, 96], F32)
    gbr = XB.sb("p1_gbr", [128, 32], F32)
    biasrow = XB.sb("p1_biasrow", [128, 16], F32)
    coefrow = XB.sb("p1_coefrow", [128, 16], F32)
    graw = [XB.sb("p1_graw%d" % i, [128, 32], F32) for i in range(2)]
    GPt = [XB.sb("p1_GP%d" % i, [128, 32], F32) for i in range(2)]
    gt1 = XB.sb("p1_gt1", [128, 16], F32)
    gt2 = XB.sb("p1_gt2", [128, 16], F32)
    gt3 = XB.sb("p1_gt3", [128, 16], F32)
    gt4 = XB.sb("p1_gt4", [128, 8], F32)
    aw = [XB.sb("p1_aw%d" % i, [128, 16, 512], BF16) for i in range(2)]
    abias = [XB.sb("p1_ab%d" % i, [17, 512], F32) for i in range(2)]
    amo = [XB.sb("p1_am%d" % i, [17, 512], F32) for i in range(2)]
    ada_next = [NADA0]

    def ada_item():
        jb_ = ada_next[0]
        if jb_ >= 32:
            return None
        ada_next[0] += 1
        P.capture()
        _ada_block(P, PS, jb_, aw[jb_ % 2], "aw%d" % (jb_ % 2), abias[jb_ % 2], "p1ab%d" % (jb_ % 2), amo[jb_ % 2],
                   "p1am%d" % (jb_ % 2), cT, adaw_d, adab_d, modd)
        return P.end_capture()
    _ld(P, "sync", "p1cv", cvs[:], scv_d, w=[cvs])
    _ld(P, "sync", "p1cw", convw[:], convw_d, w=[convw])
    _ld(P, "sync", "p1gb", gbr[:], gvec_d.partition_broadcast(128), w=[gbr])
    _cp(P, "vector", biasrow[:, 0:8], gbr[:, 8:16], [gbr], [biasrow])
    _cp(P, "vector", biasrow[:, 8:16], gbr[:, 24:32], [gbr], [biasrow])
    _ms(P, "vector", coefrow[:], -1.0, [coefrow])
    _act(P, coefrow[:, 8:16], gbr[:, 16:24], AF.Exp, [gbr, coefrow], [coefrow])
    _ts(P, "vector", coefrow[:, 8:16], coefrow[:, 8:16], -1.0, ALU.mult, [coefrow], [coefrow])

    ev = [0]

    def evac(out, in_, bank_, dst, func=AF.Copy, scale=None):
        if func == AF.Copy and scale is None and (ev[0] % 2 == 0):
            _cp(P, "vector", out, in_, [bank_], [dst])
        else:
            _act(P, out, in_, func, [bank_], [dst], scale=scale)
        ev[0] += 1

    sidx = [0]

    def store(ap_dst, sg, nt):
        _ld(P, "sync", "p1s%d" % ((sidx[0] - 1) % 8), ap_dst, sg[:, 0:nt], r=[sg])

    def next_stg():
        sg = stg[sidx[0] % 8]
        sidx[0] += 1
        return sg

    ucnt = [0]

    def gdn_unit(name, ty, h, t0, nt, bk, first, do_store, prev):
        blk = 8 * ty + h
        u = ucnt[0]
        ucnt[0] += 1
        pd, acc, ef, sq = pds[u % NU], accs[u % NU], efs[u % NU], sqs[u % NU]
        rv = ef
        sample = t0 == T0 + T1
        cw = convw[:, blk * 4:(blk + 1) * 4]
        if sample:
            pv = pd[:, 0:NSEQ * 11].rearrange("p (s l) -> p s l", s=NSEQ)
            cv3 = cvs[:, blk * 48:(blk + 1) * 48].rearrange("p (s j) -> p s j", s=NSEQ)
            _cp(P, "scalar", pv[:, :, 0:3], cv3, [cvs, pd], [pd])
            _cp(P, "scalar", pv[:, :, 3:11], PS.f32(bk)[:, 0:nt].rearrange("p (s l) -> p s l", s=NSEQ),
                [bank[bk], pd], [pd])
            _cp(P, "scalar", cv3, pv[:, :, 8:11], [pd, cvs], [cvs])
            a_ = acc[:, 0:nt].rearrange("p (s l) -> p s l", s=NSEQ)

            def tap(j):
                return pv[:, :, j:j + LS]
        else:
            if first:
                _ms(P, "vector", pd[:, 0:3], 0.0, [pd])
            else:
                ppd, pnt = prev
                if t0 == T0:
                    _ts(P, "vector", pd[:, 0:3], ppd[:, pnt:pnt + 3], flag[:, 0:1], ALU.mult, [ppd, flag, pd], [pd])
                else:
                    _cp(P, "scalar", pd[:, 0:3], ppd[:, pnt:pnt + 3], [ppd, pd], [pd])
            _cp(P, "scalar", pd[:, 3:3 + nt], PS.f32(bk)[:, 0:nt], [bank[bk], pd], [pd])
            if t0 + nt == T0 + T1:
                _cp(P, "scalar", pcvt[:, blk * 3:(blk + 1) * 3], pd[:, nt:nt + 3], [pd, pcvt], [pcvt])
            a_ = acc[:, 0:nt]

            def tap(j):
                return pd[:, j:j + nt]
        if not do_store:
            return (pd, nt)
        _ts(P, "vector", a_, tap(0), cw[:, 0:1], ALU.mult, [pd, convw, acc], [acc])
        for j in range(1, 4):
            _stt(P, "vector", a_, tap(j), cw[:, j:j + 1], a_, ALU.mult, ALU.add, [pd, convw, acc], [acc])
        _act(P, ef[:, 0:nt], acc[:, 0:nt], AF.Exp, [acc, ef], [ef], scale=-1.0)
        _act(P, ef[:, 0:nt], ef[:, 0:nt], AF.Ln, [ef], [ef], bias=1.0)
        _act(P, ef[:, 0:nt], ef[:, 0:nt], AF.Exp, [ef], [ef], scale=-1.0)
        sg = next_stg()
        if ty == 2:
            _tt(P, "vector", sg[:, 0:nt], acc[:, 0:nt], ef[:, 0:nt], ALU.mult, [acc, ef, sg], [sg])
        else:
            _tt(P, "vector", acc[:, 0:nt], acc[:, 0:nt], ef[:, 0:nt], ALU.mult, [acc, ef], [acc])
            _tt(P, "vector", sq[:, 0:nt], acc[:, 0:nt], acc[:, 0:nt], ALU.mult, [acc, sq], [sq])
            b2 = PS.one()
            _mm(P, PS.f32(b2)[:, 0:nt], onesb[:, :], sq[:, 0:nt], True, True, [onesb, sq], [bank[b2]])
            _act(P, rv[:, 0:nt], PS.f32(b2)[:, 0:nt], AF.Ln, [bank[b2], rv], [rv], bias=EPS)
            _act(P, rv[:, 0:nt], rv[:, 0:nt], AF.Exp, [rv], [rv], scale=-0.5)
            if ty == 0:
                _stt(P, "vector", sg[:, 0:nt], acc[:, 0:nt], DH ** -0.5, rv[:, 0:nt], ALU.mult, ALU.mult,
                     [acc, rv, sg], [sg])
            else:
                _tt(P, "vector", sg[:, 0:nt], acc[:, 0:nt], rv[:, 0:nt], ALU.mult, [acc, rv, sg], [sg])
        store(S[name][h, :, t0:t0 + nt], sg, nt)
        return (pd, nt)

    def gate_unit(i, bk):
        gr, GP = graw[i % 2], GPt[i % 2]
        _cp(P, "vector", gr[:], PS.f32(bk)[:, 0:32], [bank[bk], gr], [gr])
        _tt(P, "vector", GP[:, 0:8], gr[:, 0:8], gbr[:, 0:8], ALU.add, [gr, gbr, GP], [GP])
        _tt(P, "vector", gt1[:], gr[:, 8:24], biasrow[:], ALU.add, [gr, biasrow, gt1], [gt1])
        _tt(P, "vector", gt1[:], gt1[:], C("sign"), ALU.mult, [gt1, cst], [gt1])
        _act(P, gt2[:], gt1[:], AF.Abs, [gt1, gt2], [gt2])
        _act(P, gt2[:], gt2[:], AF.Exp, [gt2], [gt2], scale=-1.0)
        _act(P, gt2[:], gt2[:], AF.Ln, [gt2], [gt2], bias=1.0)
        _ts(P, "vector", gt3[:], gt1[:], 0.0, ALU.max, [gt1, gt3], [gt3])
        _tt(P, "vector", gt3[:], gt3[:], gt2[:], ALU.add, [gt3, gt2], [gt3])
        _tt(P, "vector", GP[:, 8:24], gt3[:], coefrow[:], ALU.mult, [gt3, coefrow, GP], [GP])
        _act(P, gt4[:], gr[:, 24:32], AF.Exp, [gr, gt4], [gt4], scale=-1.0)
        _ts(P, "vector", gt4[:], gt4[:], 1.0, ALU.add, [gt4], [gt4])
        P.V(lambda e: e.reciprocal(out=GP[:, 24:32], in_=gt4[:]), [gt4, GP], [GP])
        _ld(P, "sync", "p1g%d" % (i % 2), S["gt"][i * 128:(i + 1) * 128, :], GP[:], r=[GP])

    fm_types = [("qa", 0), ("kaT", 0), ("qb", 1), ("kb", 2), ("vb", 2)]
    tiles_q = [(1024, 512), (1536, 512), (2048, 128)]
    tiles_all = [(0, 512), (512, 512)] + tiles_q
    items = []
    for jb in range(19):
        slot = wr[jb % 3]
        if jb == 10:
            while True:
                it_ = ada_item()
                if it_ is None:
                    break
                items.append(it_)
            P.replay_pipe(items, 3, burst=1)
            items = []
        if jb >= 3:
            if jb < 10:
                P.capture()
                wload(jb)
                items.append(P.end_capture())
            else:
                wload(jb)
        if jb < 10:
            name, mode = fm_types[jb // 2]
            tl = {0: tiles_q, 1: [(512, 512)] + tiles_q, 2: tiles_all}[mode]
            for sbk in range(4):
                h = (jb % 2) * 4 + sbk
                prev = None
                for ti, (t0, nt) in enumerate(tl):
                    P.capture()
                    bk = PS.one()
                    for k in range(16):
                        _mm(P, PS.f32(bk)[:, 0:nt], slot[:, k, sbk * 128:(sbk + 1) * 128], hT[:, k, t0:t0 + nt],
                            k == 0, k == 15, [slot, hT], [PS.banks[bk]])
                    if name in ("qa", "kaT"):
                        sg = next_stg()
                        evac(sg[:, 0:nt], PS.f32(bk)[:, 0:nt], PS.banks[bk], sg,
                             scale=(DH ** -0.5 if name == "kaT" else None))
                        store(S[name][h, :, t0 - T0:t0 - T0 + nt], sg, nt)
                    else:
                        ty = {"qb": 0, "kb": 1, "vb": 2}[name]
                        prev = gdn_unit(name, ty, h, t0, nt, bk, ti == 0, not (name == "qb" and t0 < T0), prev)
                    items.append(P.end_capture())
                    if len(items) % 5 == 0:
                        it_ = ada_item()
                        if it_ is not None:
                            items.append(it_)
        elif jb < 18:
            name = ("ka", "va", "oa", "zb")[(jb - 10) // 2]
            c0 = ((jb - 10) % 2) * 512
            qonly = name in ("oa", "zb")
            for i in range(NSUB):
                if qonly and i < 8:
                    continue
                bk = PS.one()
                for k in range(16):
                    _mm(P, PS.f32(bk)[:, :], hT[:, k, i * 128:(i + 1) * 128], slot[:, k, :], k == 0, k == 15,
                        [slot, hT], [PS.banks[bk]])
                sg = next_stg()
                if name == "ka":
                    evac(sg[:], PS.f32(bk), PS.banks[bk], sg, scale=DH ** -0.5)
                elif name == "va":
                    evac(sg[:], PS.f32(bk), PS.banks[bk], sg)
                elif name == "oa":
                    evac(sg[:], PS.f32(bk), PS.banks[bk], sg, func=AF.Sigmoid)
                else:
                    evac(sg[:], PS.f32(bk), PS.banks[bk], sg, func=AF.Silu)
                r0 = i * 128 - (T0 if qonly else 0)
                store(S[name][r0:r0 + 128, c0:c0 + 512], sg, 512)
        else:
            for i in range(NSUB):
                bk = PS.one()
                for k in range(16):
                    _mm(P, PS.f32(bk)[:, 0:32], hT[:, k, i * 128:(i + 1) * 128], slot[:, k, 0:32], k == 0, k == 15,
                        [slot, hT], [PS.banks[bk]])
                gate_unit(i, bk)
    _ld(P, "sync", "p1pc", pcv_d, pcvt[:, :], r=[pcvt])
    _ld(P, "sync", "p1oc", ocv_d, cvs[:, :], r=[cvs])
    P.barrier()
    P.flush()
    XB.close()
    X.close()


def _shared_inputs(inp):
    w_in = np.asarray(inp["w_in"][0])
    o = np.cumsum([0, 1024, 1024, 1024, 8, 8, 1024, 1024, 1024, 1024, 8, 8, 1024])
    seg = {n: w_in[:, o[i]:o[i + 1]] for i, n in enumerate(
        ["qa", "ka", "va", "ia", "fa", "oa", "qb", "kb", "vb", "ab", "bb", "zb"])}
    w_in_r = np.ascontiguousarray(np.concatenate(
        [seg[n] for n in ("qa", "ka", "qb", "kb", "vb", "ka", "va", "oa", "zb", "ia", "fa", "ab", "bb")], axis=1))
    sh = {
        "consts": _build_consts(),
        "ada_w": np.ascontiguousarray(np.concatenate([inp["ada_w"][0], inp["ada_final_w"]], axis=1)),
        "ada_b": np.ascontiguousarray(np.concatenate([inp["ada_b"][0], inp["ada_final_b"]])[None, :]),
        "w_in": w_in_r,
        "w_out": np.ascontiguousarray(inp["w_out"][0]),
        "w_up": np.ascontiguousarray(inp["w_up"][0]),
        "w_down": np.ascontiguousarray(inp["w_down"][0]),
        "norms": np.ascontiguousarray(np.stack([inp["norm1"][0], inp["norm2"][0], inp["norm_final"]])),
        "hnorm": np.ascontiguousarray(np.stack([inp["mlstm_norm"][0], inp["gdn_norm"][0]])),
        "gvec": np.ascontiguousarray(np.concatenate(
            [inp["mlstm_gate_bias"][0], inp["gdn_A_log"][0], inp["gdn_dt_bias"][0]])[None, :]),
        "conv_w": np.ascontiguousarray(
            np.asarray(inp["gdn_conv_w"][0]).T.reshape(24, 128, 4).transpose(1, 0, 2).reshape(128, 96)),
    }
    return {k: np.asarray(v, np.float32) for k, v in sh.items()}


def _core_inputs(inp, c, sh):
    b, half = c // 2, c % 2
    sl = slice(c * NSEQ, (c + 1) * NSEQ)
    xp = inp["x_prompt"][b]
    m = dict(sh)
    m["x"] = np.ascontiguousarray(np.concatenate(
        [xp[0:T0], xp[half * T1:(half + 1) * T1], inp["x_sample"][sl].reshape(TS, D)], axis=0), np.float32)
    m["c"] = np.ascontiguousarray(np.concatenate([inp["c_prompt"][b:b + 1], inp["c_sample"][sl]], axis=0), np.float32)
    m["flag"] = np.full((128, 1), float(half), np.float32)
    m["sC"] = np.ascontiguousarray(inp["state_mlstm_C"][0, sl], np.float32)
    m["sn_t"] = np.ascontiguousarray(np.asarray(inp["state_mlstm_n"][0, sl]).transpose(0, 2, 1), np.float32)
    m["sm"] = np.ascontiguousarray(inp["state_mlstm_m"][0, sl], np.float32)
    m["sS"] = np.ascontiguousarray(inp["state_gdn_S"][0, sl], np.float32)
    cv = np.asarray(inp["state_gdn_conv"][0, sl])
    m["sconv_t"] = np.ascontiguousarray(
        cv.transpose(2, 0, 1).reshape(24, 128, NSEQ, 3).transpose(1, 0, 2, 3).reshape(128, 24 * NSEQ * 3), np.float32)
    return m


LP = 128
NLEV = {8: 2, 64: 5, 128: 6}


def _v3(ap2d, n):
    return ap2d.rearrange("p (h x) -> p h x", h=NH)


class TLV:
    def __init__(self, t, c0, c1, name=""):
        self.t, self.c0, self.c1 = t, c0, c1
        self.buf = Buf(name)

    def __getitem__(self, k):
        rows, cols = k
        a = 0 if cols.start is None else cols.start
        b = (self.c1 - self.c0) if cols.stop is None else cols.stop
        return self.t[rows, self.c0 + a:self.c0 + b]


class TLP:
    def __init__(self, t, kids):
        self.t, self.kids = t, kids

    def __getitem__(self, k):
        return self.t[k]


class Grp:
    pass


class Scan:
    NG = 2

    def __init__(self, B, P, PS, cst, C, identb, onesb, flag, mixT, S, hnorm_d, gvec_d, convw_d, io):
        self.B, self.P, self.PS, self.cst, self.C = B, P, PS, cst, C
        self.identb, self.onesb, self.flag, self.mixT, self.S, self.io = identb, onesb, flag, mixT, S, io
        nc = B.nc
        X = self.X = Ctx(nc)
        sb = X.sb
        NG = self.NG
        nh = NH // NG
        self.gA = sb("gA", [128, 1024], F32)
        self.gB = sb("gB", [128, 1024], F32)
        _ld(P, "sync", "s2a", self.gA[:], hnorm_d[0:1, :].partition_broadcast(128), w=[self.gA])
        _ld(P, "sync", "s2b", self.gB[:], hnorm_d[1:2, :].partition_broadcast(128), w=[self.gB])
        self.fm = [dict(qaT=sb("qaT%d" % i, [128, 8, 128], BF16), kaT=sb("kaT%d" % i, [128, 8, 128], BF16),
                        q=sb("qpost%d" % i, [128, 8, 128], BF16), k=sb("kpost%d" % i, [128, 8, 128], BF16),
                        v=sb("vpost%d" % i, [128, 8, 128], BF16)) for i in range(2)]
        self.tm = [dict(ka=sb("ka_t%d" % i, [128, 1024], BF16), va=sb("va_t%d" % i, [128, 1024], BF16),
                        oa=sb("oa_t%d" % i, [128, 1024], BF16), zb=sb("zb_t%d" % i, [128, 1024], BF16),
                        GP=sb("GP%d" % i, [128, 32], F32)) for i in range(2)]
        specA_big = (("dgx", F32), ("R", F32), ("sloc", BF16), ("sTsb", BF16), ("nl", F32), ("wlk", BF16), ("dCs", F32),
                     ("h1", F32), ("hnb", BF16), ("gs", BF16), ("Cst", F32), ("Cbf", BF16))
        specB_big = (("dG", F32), ("NZ", F32), ("qg", BF16), ("wT", F32), ("dTi", F32), ("P0", BF16), ("P1", BF16),
                     ("PT0", BF16), ("PT1", BF16), ("Tacc", BF16), ("qkT", BF16), ("Mf", BF16), ("MoT", BF16), ("MoT2", BF16),
                     ("kbg", BF16), ("kdec", BF16), ("vbt", BF16), ("WT", BF16), ("U0", F32), ("u", BF16),
                     ("ob", BF16), ("gz", BF16), ("Sst", F32), ("Sbf", BF16))

        def smallA(n):
            return (("sm1", 2 * n), ("g", n), ("cm", n), ("rows", n), ("gl", n), ("dns", n), ("mx", 2 * n), ("t12", 2 * n),
                    ("fd", 2 * n), ("mt", n), ("en", n), ("dd", n), ("d2", n), ("a12", 2 * n), ("ssq", n), ("sm2", 2 * n),
                    ("dC", n))

        def smallB(n):
            return (("gsm", 2 * n), ("gsmall", 2 * n), ("gLe", n), ("ssq2", n))
        parents = {}
        for kind, spec in (("a", specA_big), ("b", specB_big)):
            for n, dt in spec:
                parents[(kind, n)] = sb("P%s_%s" % (kind, n), [128, 1024], dt)
        nstP = sb("Pa_nst", [128, NH], F32)
        nbfP = sb("Pa_nbf", [128, NH], BF16)
        mprevP = sb("Pa_mprev", [128, NH], F32)
        self.ga, self.gb = [], []
        for gi in range(NG):
            for kind in ("a", "b"):
                g = Grp()
                g.h0, g.nh, g.kind = gi * nh, nh, kind
                base = (0 if kind == "a" else 4) + 2 * gi
                g.PS = PS.sub([base, base + 1])
                t = "%s%d_" % (kind, gi)
                g.W = {}
                for n, dt in (specA_big if kind == "a" else specB_big):
                    g.W[n] = TLV(parents[(kind, n)].t, gi * nh * 128, (gi + 1) * nh * 128, t + n)
                for n, wd in (smallA(nh) if kind == "a" else smallB(nh)):
                    g.W[n] = sb(t + n, [128, wd], F32)
                if kind == "a":
                    g.Cst, g.Cbf = g.W["Cst"], g.W["Cbf"]
                    g.nst = TLV(nstP.t, gi * nh, (gi + 1) * nh, t + "nst")
                    g.nbf = TLV(nbfP.t, gi * nh, (gi + 1) * nh, t + "nbf")
                    g.mprev = TLV(mprevP.t, gi * nh, (gi + 1) * nh, t + "mprev")
                else:
                    g.Sst, g.Sbf = g.W["Sst"], g.W["Sbf"]
                    g.W["dB"] = g.W["dG"]
                    g.W["gre"] = g.W["wT"]
                    g.W["osb"] = g.W["dG"]
                (self.ga if kind == "a" else self.gb).append(g)
        self.alt = dict(Cst=sb("alt_Cst", [128, 1024], F32), Sst=sb("alt_Sst", [128, 1024], F32),
                        nst=sb("alt_nst", [128, NH], F32), mprev=sb("alt_mprev", [128, NH], F32))
        self.gaS, self.gbS = Grp(), Grp()
        for g, kind, spec, small, grps in ((self.gaS, "a", specA_big, smallA, self.ga), (self.gbS, "b", specB_big, smallB, self.gb)):
            g.h0, g.nh, g.kind = 0, NH, kind
            g.PS = PS.sub([0, 1, 2, 3] if kind == "a" else [4, 5, 6, 7])
            g.W = {}
            for n, dt in spec:
                g.W[n] = TLP(parents[(kind, n)].t, [gg.W[n] for gg in grps])
            for n, wd in small(NH):
                g.W[n] = sb("S%s_%s" % (kind, n), [128, wd], F32)
            if kind == "a":
                g.Cst, g.Cbf = g.W["Cst"], g.W["Cbf"]
                g.nst = TLP(nstP.t, [gg.nst for gg in grps])
                g.nbf = TLP(nbfP.t, [gg.nbf for gg in grps])
                g.mprev = TLP(mprevP.t, [gg.mprev for gg in grps])
            else:
                g.Sst, g.Sbf = g.W["Sst"], g.W["Sbf"]
                g.W["dB"] = g.W["dG"]
                g.W["gre"] = g.W["wT"]
                g.W["osb"] = g.W["dG"]

    def mlstm_chunk(self, g, L, c0, q0, full, fm, tm):
        P, PS, C, W, cst = self.P, g.PS, self.C, g.W, self.cst
        bank = PS.banks
        h0, nh = g.h0, g.nh
        cn = str(L)
        GP = tm["GP"]
        lf = GP[0:L, 8 + h0:8 + h0 + nh]
        li = GP[0:L, h0:h0 + nh]
        HL = nh * L
        assert HL <= 512

        wide = nh * 128 > 512

        def m2():
            return PS.pair() if wide else PS.one()

        def f2(b_):
            return PS.f32(b_, 2) if wide else PS.f32(b_)

        def bl2(b_):
            return [bank[b_], bank[b_ + 1]] if wide else [bank[b_]]

        def hb2(b_, h_):
            return bank[b_ + (h_ * 128) // 512]
        identb, onesb = self.identb, self.onesb
        qaT, kaT, ka, va, oa = fm["qaT"], fm["kaT"], tm["ka"], tm["va"], tm["oa"]
        gc = slice(h0 * 128, (h0 + nh) * 128)

        def v3(ap2d):
            return ap2d.rearrange("p (h x) -> p h x", h=nh)

        def bcl(ap, n):
            return ap[:, :, None].to_broadcast([L, nh, n])
        bs = PS.one()
        _mm(P, PS.f32(bs)[0:L, 0:nh], C("uinc" + cn, L), lf, True, True, [cst, GP], [bank[bs]])
        _mm(P, PS.f32(bs)[:, nh:2 * nh], C("ones", L)[:, 0:128], lf, True, True, [cst, GP], [bank[bs]])
        sm1 = W["sm1"]
        _cp(P, "scalar", sm1[:, :], PS.f32(bs)[:, 0:2 * nh], [bank[bs]], [sm1])
        gg = W["g"]
        _tt(P, "vector", gg[0:L, :], li, sm1[0:L, 0:nh], ALU.subtract, [GP, sm1], [gg])
        dgx = W["dgx"]
        _tt(P, "gpsimd", v3(dgx[0:L, 0:HL]), C("ident", L)[:, None, 0:L].to_broadcast([L, nh, L]),
            bcl(gg[0:L, :], L), ALU.mult, [cst, gg], [dgx])
        br = PS.one()
        _mm(P, PS.f32(br)[0:L, 0:HL], C("ones", L)[:, 0:L], dgx[0:L, 0:HL], True, True, [cst, dgx], [bank[br]])
        R = W["R"]
        R3 = v3(R[0:L, 0:HL])
        _tt(P, "vector", R3, v3(PS.f32(br)[0:L, 0:HL]), C("neg" + cn, L)[:, None, :].to_broadcast([L, nh, L]),
            ALU.add, [bank[br], cst], [R])
        cm = W["cm"]
        _red(P, "vector", cm[0:L, :], R3, ALU.max, [R], [cm])
        _tt(P, "vector", R3, R3, bcl(cm[0:L, :], L), ALU.subtract, [R, cm], [R])
        _act(P, R[0:L, 0:HL], R[0:L, 0:HL], AF.Exp, [R], [R])
        if full:
            bq = PS.one()
            for h in range(nh):
                _mm(P, PS.f32(bq)[0:L, h * L:(h + 1) * L], qaT[:, h0 + h, c0:c0 + L], kaT[:, h0 + h, c0:c0 + L],
                    True, True, [qaT, kaT], [bank[bq]])
            sloc = W["sloc"]
            _tt(P, "vector", sloc[0:L, 0:HL], PS.f32(bq)[0:L, 0:HL], R[0:L, 0:HL], ALU.mult, [bank[bq], R], [sloc])
            rows = W["rows"]
            _red(P, "vector", rows[0:L, :], v3(sloc[0:L, 0:HL]), ALU.add, [sloc], [rows])
            bt = PS.one()
            for h in range(nh):
                _tr(P, PS.bf(bt)[0:L, h * L:(h + 1) * L], sloc[0:L, h * L:(h + 1) * L], identb[0:L, 0:L],
                    [sloc, identb], [bank[bt]])
            sTsb = W["sTsb"]
            _cp(P, "scalar", sTsb[0:L, 0:HL], PS.bf(bt)[0:L, 0:HL], [bank[bt]], [sTsb])
            bn = m2()
            for h in range(nh):
                _mm(P, f2(bn)[0:L, h * 128:(h + 1) * 128], sTsb[0:L, h * L:(h + 1) * L],
                    va[0:L, (h0 + h) * 128:(h0 + h + 1) * 128], True, True, [sTsb, va], [hb2(bn, h)])
            nl = W["nl"]
            _cp(P, "scalar", nl[0:L, :], f2(bn)[0:L, :], bl2(bn), [nl])
            gs = W["gs"]
            _tt(P, "gpsimd", gs[0:L, :], oa[0:L, gc], self.gA[0:L, gc], ALU.mult, [oa, self.gA], [gs])
        b2 = PS.one()
        _mm(P, PS.f32(b2)[0:L, 0:nh], C("sel" + cn, L)[:, 0:L], cm[0:L, :], True, True, [cst, cm], [bank[b2]])
        gl = W["gl"]
        _tt(P, "vector", gl[0:L, :], gg[0:L, :], PS.f32(b2)[0:L, 0:nh], ALU.subtract, [gg, bank[b2]], [gl])
        _act(P, gl[0:L, :], gl[0:L, :], AF.Exp, [gl], [gl])
        wlk = W["wlk"]
        _tt(P, "gpsimd", v3(wlk[0:L, :]), v3(ka[0:L, gc]), bcl(gl[0:L, :], 128), ALU.mult, [ka, gl], [wlk])
        bd = m2()
        for h in range(nh):
            _mm(P, f2(bd)[:, h * 128:(h + 1) * 128], wlk[0:L, h * 128:(h + 1) * 128],
                va[0:L, (h0 + h) * 128:(h0 + h + 1) * 128], True, True, [wlk, va], [hb2(bd, h)])
        dCs, dns = W["dCs"], W["dns"]
        _cp(P, "scalar", dCs[:, :], f2(bd)[:, :], bl2(bd), [dCs])
        b3 = PS.one()
        for h in range(nh):
            _mm(P, PS.f32(b3)[:, h:h + 1], wlk[0:L, h * 128:(h + 1) * 128], onesb[0:L, 0:1], True, True,
                [wlk, onesb], [bank[b3]])
        _cp(P, "vector", dns[:, :], PS.f32(b3)[:, 0:nh], [bank[b3]], [dns])
        mprev, Cst, Cbf, nst, nbf = g.mprev, g.Cst, g.Cbf, g.nst, g.nbf
        mx, t12, fd = W["mx"], W["t12"], W["fd"]
        _tt(P, "vector", mx[0:L, 0:nh], mprev[0:L, :], cm[0:L, :], ALU.max, [mprev, cm], [mx])
        _tt(P, "vector", t12[0:L, 0:nh], cm[0:L, :], mx[0:L, 0:nh], ALU.subtract, [cm, mx], [t12])
        _tt(P, "vector", t12[0:L, nh:2 * nh], mprev[0:L, :], mx[0:L, 0:nh], ALU.subtract, [mprev, mx, t12], [t12])
        _act(P, fd[0:L, :], t12[0:L, :], AF.Exp, [t12], [fd])
        _cp(P, "vector", mx[0:L, nh:2 * nh], fd[0:L, 0:nh], [fd, mx], [mx])
        if full:
            bc_ = m2()
            for h in range(nh):
                _mm(P, f2(bc_)[0:L, h * 128:(h + 1) * 128], qaT[:, h0 + h, c0:c0 + L], Cbf[:, h * 128:(h + 1) * 128],
                    True, True, [qaT, Cbf], [hb2(bc_, h)])
            mt, en, dd, d2, a12 = W["mt"], W["en"], W["dd"], W["d2"], W["a12"]
            h1, nl, ssq, hnb = W["h1"], W["nl"], W["ssq"], W["hnb"]
            _tt(P, "vector", mt[0:L, :], sm1[0:L, 0:nh], mx[0:L, 0:nh], ALU.add, [sm1, mx], [mt])
            _act(P, en[0:L, :], mt[0:L, :], AF.Exp, [mt], [en], scale=-1.0)
            _cp(P, "scalar", h1[0:L, :], f2(bc_)[0:L, :], bl2(bc_), [h1])
            b4 = PS.one()
            for h in range(nh):
                _mm(P, PS.f32(b4)[0:L, h:h + 1], qaT[:, h0 + h, c0:c0 + L], nbf[:, h:h + 1], True, True,
                    [qaT, nbf], [bank[b4]])
            _tt(P, "vector", dd[0:L, :], fd[0:L, nh:2 * nh], PS.f32(b4)[0:L, 0:nh], ALU.mult, [fd, bank[b4]], [dd])
            _tt(P, "vector", d2[0:L, :], fd[0:L, 0:nh], W["rows"][0:L, :], ALU.mult, [fd, W["rows"]], [d2])
            _tt(P, "vector", dd[0:L, :], dd[0:L, :], d2[0:L, :], ALU.add, [dd, d2], [dd])
            _act(P, dd[0:L, :], dd[0:L, :], AF.Abs, [dd], [dd])
            _tt(P, "vector", dd[0:L, :], dd[0:L, :], en[0:L, :], ALU.max, [dd, en], [dd])
            P.V(lambda e: e.reciprocal(out=dd[0:L, :], in_=dd[0:L, :]), [dd], [dd])
            _tt(P, "vector", a12[0:L, 0:nh], fd[0:L, nh:2 * nh], dd[0:L, :], ALU.mult, [fd, dd], [a12])
            _tt(P, "vector", a12[0:L, nh:2 * nh], fd[0:L, 0:nh], dd[0:L, :], ALU.mult, [fd, dd, a12], [a12])
            _tt(P, "vector", v3(h1[0:L, :]), v3(h1[0:L, :]), bcl(a12[0:L, 0:nh], 128), ALU.mult, [h1, a12], [h1])
            _tt(P, "gpsimd", v3(nl[0:L, :]), v3(nl[0:L, :]), bcl(a12[0:L, nh:2 * nh], 128), ALU.mult, [nl, a12], [nl])
            _tt(P, "vector", h1[0:L, :], h1[0:L, :], nl[0:L, :], ALU.add, [h1, nl], [h1])
            _act(P, nl[0:L, :], h1[0:L, :], AF.Square, [h1, nl], [nl])
            _red(P, "vector", ssq[0:L, :], v3(nl[0:L, :]), ALU.add, [nl], [ssq])
            _act(P, ssq[0:L, :], ssq[0:L, :], AF.Ln, [ssq], [ssq], bias=EPS, scale=1.0 / DH)
            _act(P, ssq[0:L, :], ssq[0:L, :], AF.Exp, [ssq], [ssq], scale=-0.5)
            _tt(P, "vector", v3(h1[0:L, :]), v3(h1[0:L, :]), bcl(ssq[0:L, :], 128), ALU.mult, [h1, ssq], [h1])
            _tt(P, "gpsimd", hnb[0:L, :], h1[0:L, :], W["gs"][0:L, :], ALU.mult, [h1, W["gs"]], [hnb])
            bh = PS.one()
            for h in range(nh):
                _tr(P, PS.bf(bh)[:, h * L:(h + 1) * L], hnb[0:L, h * 128:(h + 1) * 128], identb[0:L, 0:L],
                    [hnb, identb], [bank[bh]])
            _cp(P, "scalar", self.mixT[:, h0:h0 + nh, q0:q0 + L], PS.bf(bh)[:, 0:HL].rearrange("p (h t) -> p h t", h=nh),
                [bank[bh]], [self.mixT])
        b5 = PS.one()
        _mm(P, PS.f32(b5)[:, 0:2 * nh], C("sel" + cn, L)[:, 0:128], mx[0:L, 0:2 * nh], True, True, [cst, mx], [bank[b5]])
        sm2, dC = W["sm2"], W["dC"]
        _cp(P, "scalar", sm2[:, :], PS.f32(b5)[:, 0:2 * nh], [bank[b5]], [sm2])
        _tt(P, "vector", dC[:, :], mprev[:, :], sm2[:, 0:nh], ALU.subtract, [mprev, sm2], [dC])
        _act(P, dC[:, :], dC[:, :], AF.Exp, [dC], [dC])

        def bc128(ap):
            return ap[:, :, None].to_broadcast([128, nh, 128])
        _tt(P, "vector", v3(Cst[:, :]), v3(Cst[:, :]), bc128(dC[:, :]), ALU.mult, [Cst, dC], [Cst])
        _tt(P, "gpsimd", v3(dCs[:, :]), v3(dCs[:, :]), bc128(sm2[:, nh:2 * nh]), ALU.mult, [dCs, sm2], [dCs])
        _tt(P, "vector", Cst[:, :], Cst[:, :], dCs[:, :], ALU.add, [Cst, dCs], [Cst])
        _tt(P, "vector", nst[:, :], nst[:, :], dC[:, :], ALU.mult, [nst, dC], [nst])
        _tt(P, "vector", dns[:, :], dns[:, :], sm2[:, nh:2 * nh], ALU.mult, [dns, sm2], [dns])
        _tt(P, "vector", nst[:, :], nst[:, :], dns[:, :], ALU.add, [nst, dns], [nst])
        _cp(P, "scalar", Cbf[:, :], Cst[:, :], [Cst], [Cbf])
        _cp(P, "vector", nbf[:, :], nst[:, :], [nst], [nbf])
        _tt(P, "vector", mprev[:, :], sm1[:, nh:2 * nh], sm2[:, 0:nh], ALU.add, [sm1, sm2, mprev], [mprev])

    def gdn_chunk(self, g, L, c0, q0, full, fm, tm):
        P, PS, C, W, cst = self.P, g.PS, self.C, g.W, self.cst
        bank = PS.banks
        h0, nh = g.h0, g.nh
        cn = str(L)
        GP = tm["GP"]
        zb = tm["zb"]
        logg = GP[0:L, 16 + h0:16 + h0 + nh]
        beta = GP[0:L, 24 + h0:24 + h0 + nh]
        HL = nh * L
        assert HL <= 512

        wide = nh * 128 > 512

        def m2():
            return PS.pair() if wide else PS.one()

        def f2(b_):
            return PS.f32(b_, 2) if wide else PS.f32(b_)

        def bl2(b_):
            return [bank[b_], bank[b_ + 1]] if wide else [bank[b_]]

        def hb2(b_, h_):
            return bank[b_ + (h_ * 128) // 512]
        identb = self.identb
        qpost, kpost, vpost = fm["q"], fm["k"], fm["v"]
        Sst, Sbf = g.Sst, g.Sbf
        gc = slice(h0 * 128, (h0 + nh) * 128)

        def v3(ap2d):
            return ap2d.rearrange("p (h x) -> p h x", h=nh)

        def bcl(ap, n):
            return ap[:, :, None].to_broadcast([L, nh, n])
        identL = C("ident", L)[:, None, 0:L].to_broadcast([L, nh, L])
        bs = PS.one()
        _mm(P, PS.f32(bs)[0:L, 0:nh], C("uinc" + cn, L), logg, True, True, [cst, GP], [bank[bs]])
        _mm(P, PS.f32(bs)[:, nh:2 * nh], C("ones", L)[:, 0:128], logg, True, True, [cst, GP], [bank[bs]])
        gsm = W["gsm"]
        _cp(P, "scalar", gsm[:, :], PS.f32(bs)[:, 0:2 * nh], [bank[bs]], [gsm])
        Gt = gsm[0:L, 0:nh]
        dG = W["dG"]
        _tt(P, "gpsimd", v3(dG[0:L, 0:HL]), identL, bcl(Gt, L), ALU.mult, [cst, gsm], [dG])
        bg = PS.one()
        _mm(P, PS.f32(bg)[:, 0:HL], C("ones", L)[:, 0:128], dG[0:L, 0:HL], True, True, [cst, dG], [bank[bg]])
        NZ = W["NZ"]
        _tt(P, "vector", v3(NZ[0:L, 0:HL]), v3(PS.f32(bg)[0:L, 0:HL]), bcl(Gt, L), ALU.subtract, [bank[bg], gsm], [NZ])
        _ts(P, "vector", NZ[0:L, 0:HL], NZ[0:L, 0:HL], 0.0, ALU.min, [NZ], [NZ])
        _act(P, NZ[0:L, 0:HL], NZ[0:L, 0:HL], AF.Exp, [NZ], [NZ])
        if full:
            gre, qg = W["gre"], W["qg"]
            _act(P, gre[:, 0:HL], PS.f32(bg)[:, 0:HL], AF.Exp, [bank[bg]], [gre])
            _tt(P, "vector", v3(qg[:, 0:HL]), qpost[:, h0:h0 + nh, c0:c0 + L], v3(gre[:, 0:HL]), ALU.mult,
                [qpost, gre], [qg])
        dB = W["dB"]
        _tt(P, "gpsimd", v3(dB[0:L, 0:HL]), identL, bcl(beta, L), ALU.mult, [cst, GP, dB], [dB])
        bb_ = PS.one()
        _mm(P, PS.f32(bb_)[0:L, 0:HL], C("ones", L)[:, 0:L], dB[0:L, 0:HL], True, True, [cst, dB], [bank[bb_]])
        wT = W["wT"]
        _tt(P, "gpsimd", v3(wT[0:L, 0:HL]), v3(NZ[0:L, 0:HL]),
            C("ustr" + cn, L)[:, None, :].to_broadcast([L, nh, L]), ALU.mult, [NZ, cst, wT], [wT])
        _tt(P, "vector", wT[0:L, 0:HL], wT[0:L, 0:HL], PS.f32(bb_)[0:L, 0:HL], ALU.mult, [wT, bank[bb_]], [wT])
        if full:
            dTi = W["dTi"]
            _tt(P, "gpsimd", v3(dTi[0:L, 0:HL]), v3(NZ[0:L, 0:HL]),
                C("uinc" + cn, L)[:, None, :].to_broadcast([L, nh, L]), ALU.mult, [NZ, cst], [dTi])
        bk = PS.one()
        for h in range(nh):
            _mm(P, PS.f32(bk)[0:L, h * L:(h + 1) * L], kpost[:, h0 + h, c0:c0 + L], kpost[:, h0 + h, c0:c0 + L],
                True, True, [kpost], [bank[bk]])
        blocked = (L == 128)
        Pc, PTc, Pn_, PTn_ = W["P0"], W["PT0"], W["P1"], W["PT1"]
        Mf = W["Mf"] if blocked else Pc
        _stt(P, "vector", Mf[0:L, 0:HL], PS.f32(bk)[0:L, 0:HL], -1.0, wT[0:L, 0:HL], ALU.mult, ALU.mult,
             [bank[bk], wT], [Mf])
        if full:
            bq = PS.one()
            for h in range(nh):
                _mm(P, PS.f32(bq)[0:L, h * L:(h + 1) * L], kpost[:, h0 + h, c0:c0 + L], qpost[:, h0 + h, c0:c0 + L],
                    True, True, [kpost, qpost], [bank[bq]])
            qkT = W["qkT"]
            _tt(P, "vector", qkT[0:L, 0:HL], PS.f32(bq)[0:L, 0:HL], W["dTi"][0:L, 0:HL], ALU.mult,
                [bank[bq], W["dTi"]], [qkT])
        bt = PS.one()
        for h in range(nh):
            _tr(P, PS.bf(bt)[0:L, h * L:(h + 1) * L], Mf[0:L, h * L:(h + 1) * L], identb[0:L, 0:L],
                [Mf, identb], [bank[bt]])
        if blocked:
            bdm = C("bd32", L)[:, None, :].to_broadcast([L, nh, L])
            MoT, MoT2 = W["MoT"], W["MoT2"]
            _tt(P, "gpsimd", v3(Pc[0:L, 0:HL]), v3(Mf[0:L, 0:HL]), bdm, ALU.mult, [Mf, cst], [Pc])
            _tt(P, "vector", v3(PTc[0:L, 0:HL]), v3(PS.bf(bt)[0:L, 0:HL]), bdm, ALU.mult, [bank[bt], cst], [PTc])
            _tt(P, "vector", v3(MoT[0:L, 0:HL]), v3(PS.bf(bt)[0:L, 0:HL]),
                C("o64", L)[:, None, :].to_broadcast([L, nh, L]), ALU.mult, [bank[bt], cst], [MoT])
            _tt(P, "vector", v3(MoT2[0:L, 0:HL]), v3(PS.bf(bt)[0:L, 0:HL]),
                C("offL128", L)[:, None, :].to_broadcast([L, nh, L]), ALU.mult, [bank[bt], cst], [MoT2])
        else:
            _cp(P, "scalar", PTc[0:L, 0:HL], PS.bf(bt)[0:L, 0:HL], [bank[bt]], [PTc])
        Tacc = W["Tacc"]
        _tt(P, "gpsimd", v3(Tacc[0:L, 0:HL]), v3(Pc[0:L, 0:HL]), identL, ALU.add, [Pc, cst], [Tacc])
        nlev = 4 if blocked else NLEV[L]
        for lev in range(1, nlev + 1):
            b1 = PS.one()
            for h in range(nh):
                sl = slice(h * L, (h + 1) * L)
                _mm(P, PS.f32(b1)[0:L, sl], Pc[0:L, sl], PTc[0:L, sl], True, True, [Pc, PTc], [bank[b1]])
            _cp(P, "scalar", PTn_[0:L, 0:HL], PS.f32(b1)[0:L, 0:HL], [bank[b1]], [PTn_])
            if lev < nlev:
                b2 = PS.one()
                for h in range(nh):
                    sl = slice(h * L, (h + 1) * L)
                    _mm(P, PS.f32(b2)[0:L, sl], PTc[0:L, sl], Pc[0:L, sl], True, True, [Pc, PTc], [bank[b2]])
                _cp(P, "vector", Pn_[0:L, 0:HL], PS.f32(b2)[0:L, 0:HL], [bank[b2]], [Pn_])
            b3 = PS.one()
            for h in range(nh):
                sl = slice(h * L, (h + 1) * L)
                _mm(P, PS.f32(b3)[0:L, sl], PTn_[0:L, sl], Tacc[0:L, sl], True, True, [PTn_, Tacc], [bank[b3]])
            _tt(P, "vector", Tacc[0:L, 0:HL], Tacc[0:L, 0:HL], PS.f32(b3)[0:L, 0:HL], ALU.add, [Tacc, bank[b3]], [Tacc])
            Pc, PTc, Pn_, PTn_ = Pn_, PTn_, Pc, PTc
        if blocked:
            TbT, Xt = Pn_, PTn_
            for Mo in (W["MoT"], W["MoT2"]):
                b7 = PS.one()
                for h in range(nh):
                    sl = slice(h * L, (h + 1) * L)
                    _tr(P, PS.bf(b7)[0:L, sl], Tacc[0:L, sl], identb[0:L, 0:L], [Tacc, identb], [bank[b7]])
                _cp(P, "scalar", TbT[0:L, 0:HL], PS.bf(b7)[0:L, 0:HL], [bank[b7]], [TbT])
                b8 = PS.one()
                for h in range(nh):
                    sl = slice(h * L, (h + 1) * L)
                    _mm(P, PS.f32(b8)[0:L, sl], Mo[0:L, sl], Tacc[0:L, sl], True, True, [Mo, Tacc], [bank[b8]])
                _cp(P, "scalar", Xt[0:L, 0:HL], PS.f32(b8)[0:L, 0:HL], [bank[b8]], [Xt])
                b9 = PS.one()
                for h in range(nh):
                    sl = slice(h * L, (h + 1) * L)
                    _mm(P, PS.f32(b9)[0:L, sl], TbT[0:L, sl], Xt[0:L, sl], True, True, [TbT, Xt], [bank[b9]])
                _tt(P, "vector", Tacc[0:L, 0:HL], Tacc[0:L, 0:HL], PS.f32(b9)[0:L, 0:HL], ALU.add, [Tacc, bank[b9]], [Tacc])
        gsl, gLe = W["gsmall"], W["gLe"]
        _act(P, gsl[0:L, 0:nh], Gt, AF.Exp, [gsm], [gsl])
        _tt(P, "vector", gsl[0:L, 0:nh], gsl[0:L, 0:nh], beta, ALU.mult, [gsl, GP], [gsl])
        _tt(P, "vector", gsl[0:L, nh:2 * nh], gsm[0:L, nh:2 * nh], Gt, ALU.subtract, [gsm, gsl], [gsl])
        _act(P, gsl[0:L, nh:2 * nh], gsl[0:L, nh:2 * nh], AF.Exp, [gsl], [gsl])
        _act(P, gLe[:, :], gsm[:, nh:2 * nh], AF.Exp, [gsm], [gLe])
        kbg, kdec, vbt = W["kbg"], W["kdec"], W["vbt"]
        bkt = PS.one()
        for h in range(nh):
            _tr(P, PS.bf(bkt)[0:L, h * 128:(h + 1) * 128], kpost[:, h0 + h, c0:c0 + L], identb[:, :], [kpost, identb],
                [bank[bkt]])
        _tt(P, "vector", v3(kbg[0:L, :]), v3(PS.bf(bkt)[0:L, 0:nh * 128]), bcl(gsl[0:L, 0:nh], 128), ALU.mult,
            [bank[bkt], gsl], [kbg])
        _tt(P, "vector", v3(kdec[0:L, :]), v3(PS.bf(bkt)[0:L, 0:nh * 128]), bcl(gsl[0:L, nh:2 * nh], 128), ALU.mult,
            [bank[bkt], gsl], [kdec])
        bvt = PS.one()
        for h in range(nh):
            _tr(P, PS.bf(bvt)[0:L, h * 128:(h + 1) * 128], vpost[:, h0 + h, c0:c0 + L], identb[:, :], [vpost, identb],
                [bank[bvt]])
        _tt(P, "vector", v3(vbt[0:L, :]), v3(PS.bf(bvt)[0:L, 0:nh * 128]), bcl(beta, 128), ALU.mult,
            [bank[bvt], GP], [vbt])
        bw = PS.one()
        for h in range(nh):
            _mm(P, PS.f32(bw)[:, h * L:(h + 1) * L], kbg[0:L, h * 128:(h + 1) * 128], Tacc[0:L, h * L:(h + 1) * L],
                True, True, [kbg, Tacc], [bank[bw]])
        WT = W["WT"]
        _cp(P, "scalar", WT[:, 0:HL], PS.f32(bw)[:, 0:HL], [bank[bw]], [WT])
        bu = m2()
        for h in range(nh):
            _mm(P, f2(bu)[0:L, h * 128:(h + 1) * 128], Tacc[0:L, h * L:(h + 1) * L],
                vbt[0:L, h * 128:(h + 1) * 128], True, True, [Tacc, vbt], [hb2(bu, h)])
        U0 = W["U0"]
        _cp(P, "scalar", U0[0:L, :], f2(bu)[0:L, :], bl2(bu), [U0])
        if full:
            gz = W["gz"]
            _tt(P, "gpsimd", gz[0:L, :], zb[0:L, gc], self.gB[0:L, gc], ALU.mult, [zb, self.gB], [gz])
        bpu = m2()
        for h in range(nh):
            _mm(P, f2(bpu)[0:L, h * 128:(h + 1) * 128], WT[:, h * L:(h + 1) * L], Sbf[:, h * 128:(h + 1) * 128],
                True, True, [WT, Sbf], [hb2(bpu, h)])
        u = W["u"]
        _tt(P, "vector", u[0:L, :], U0[0:L, :], f2(bpu)[0:L, :], ALU.subtract, [U0] + bl2(bpu), [u])
        if full:
            bo = m2()
            for h in range(nh):
                o_ = f2(bo)[0:L, h * 128:(h + 1) * 128]
                _mm(P, o_, W["qg"][:, h * L:(h + 1) * L], Sbf[:, h * 128:(h + 1) * 128], True, False,
                    [W["qg"], Sbf], [hb2(bo, h)])
                _mm(P, o_, W["qkT"][0:L, h * L:(h + 1) * L], u[0:L, h * 128:(h + 1) * 128], False, True,
                    [W["qkT"], u], [hb2(bo, h)])
            osb = W["osb"]
            _cp(P, "scalar", osb[0:L, :], f2(bo)[0:L, :], bl2(bo) + [osb], [osb])
        bss = m2()
        for h in range(nh):
            _mm(P, f2(bss)[:, h * 128:(h + 1) * 128], kdec[0:L, h * 128:(h + 1) * 128],
                u[0:L, h * 128:(h + 1) * 128], True, True, [kdec, u], [hb2(bss, h)])
        _tt(P, "vector", v3(Sst[:, :]), v3(Sst[:, :]), gLe[:, :, None].to_broadcast([128, nh, 128]),
            ALU.mult, [Sst, gLe], [Sst])
        _tt(P, "vector", Sst[:, :], Sst[:, :], f2(bss)[:, :], ALU.add, [Sst] + bl2(bss), [Sst])
        _cp(P, "scalar", Sbf[:, :], Sst[:, :], [Sst], [Sbf])
        if full:
            ob, ssq = W["ob"], W["ssq2"]
            _act(P, U0[0:L, :], osb[0:L, :], AF.Square, [osb, U0], [U0])
            _red(P, "vector", ssq[0:L, :], v3(U0[0:L, :]), ALU.add, [U0], [ssq])
            _act(P, ssq[0:L, :], ssq[0:L, :], AF.Ln, [ssq], [ssq], bias=EPS, scale=1.0 / DH)
            _act(P, ssq[0:L, :], ssq[0:L, :], AF.Exp, [ssq], [ssq], scale=-0.5)
            _tt(P, "vector", v3(osb[0:L, :]), v3(osb[0:L, :]), bcl(ssq[0:L, :], 128), ALU.mult, [osb, ssq], [osb])
            _tt(P, "gpsimd", ob[0:L, :], osb[0:L, :], W["gz"][0:L, :], ALU.mult, [osb, W["gz"]], [ob])
            bh = PS.one()
            for h in range(nh):
                _tr(P, PS.bf(bh)[:, h * L:(h + 1) * L], ob[0:L, h * 128:(h + 1) * 128], identb[0:L, 0:L],
                    [ob, identb], [bank[bh]])
            _cp(P, "scalar", self.mixT[:, 8 + h0:8 + h0 + nh, q0:q0 + L],
                PS.bf(bh)[:, 0:HL].rearrange("p (h t) -> p h t", h=nh), [bank[bh]], [self.mixT])

    def _refresh_bf(self):
        P, a, b = self.P, self.gaS, self.gbS
        _cp(P, "scalar", a.Cbf[:, :], a.Cst[:, :], [a.Cst], [a.Cbf])
        _cp(P, "vector", a.nbf[:, :], a.nst[:, :], [a.nst], [a.nbf])
        _cp(P, "scalar", b.Sbf[:, :], b.Sst[:, :], [b.Sst], [b.Sbf])

    def _state_tiles(self):
        return [self.gaS.Cst, self.gaS.nst, self.gaS.mprev, self.gbS.Sst]

    def _use_set(self, i):
        a, b = self.gaS, self.gbS
        if not hasattr(self, "_set0"):
            self._set0 = dict(Cst=a.Cst, Sst=b.Sst, nst=a.nst, mprev=a.mprev)
        st = self._set0 if i == 0 else self.alt
        a.Cst, a.nst, a.mprev, b.Sst = st["Cst"], st["nst"], st["mprev"], st["Sst"]
        a.W["Cst"], b.W["Sst"] = st["Cst"], st["Sst"]

    def load_state(self, j):
        P, io, a, b = self.P, self.io, self.gaS, self.gbS
        pairs = [
            (a.Cst[:, :].rearrange("p (h x) -> p h x", h=NH), io["sC"][j].rearrange("h d e -> d h e")),
            (b.Sst[:, :].rearrange("p (h x) -> p h x", h=NH), io["sS"][j].rearrange("h d e -> d h e")),
            (a.nst[:, :], io["sn"][j]),
            (a.mprev[:, :], io["sm"][j:j + 1, :].partition_broadcast(128)),
        ]
        P.dma("sync", "s2ld", [(lambda e, o=o, i=i: e.dma_start(out=o, in_=i)) for o, i in pairs], w=self._state_tiles())

    def store_state(self, dC, dS, dn, dm):
        P, a, b = self.P, self.gaS, self.gbS
        pairs = [
            (dC.rearrange("h d e -> d h e"), a.Cst[:, :].rearrange("p (h x) -> p h x", h=NH)),
            (dS.rearrange("h d e -> d h e"), b.Sst[:, :].rearrange("p (h x) -> p h x", h=NH)),
            (dn, a.nst[:, :]),
            (dm, a.mprev[0:1, :]),
        ]
        P.dma("sync", "s2st", [(lambda e, o=o, i=i: e.dma_start(out=o, in_=i)) for o, i in pairs], r=self._state_tiles())

    def tm_load(self, L, r0, full, tm):
        P, S = self.P, self.S
        fns = [
            lambda e: e.dma_start(out=tm["ka"][0:L, :], in_=S["ka"][r0:r0 + L, :]),
            lambda e: e.dma_start(out=tm["va"][0:L, :], in_=S["va"][r0:r0 + L, :]),
            lambda e: e.dma_start(out=tm["GP"][0:L, :], in_=S["gt"][r0:r0 + L, :]),
        ]
        wl_ = [tm["ka"], tm["va"], tm["GP"]]
        if full:
            rq = r0 - T0
            fns += [
                lambda e: e.dma_start(out=tm["oa"][0:L, :], in_=S["oa"][rq:rq + L, :]),
                lambda e: e.dma_start(out=tm["zb"][0:L, :], in_=S["zb"][rq:rq + L, :]),
            ]
            wl_ += [tm["oa"], tm["zb"]]
        P.dma("sync", "s2t%d" % self.tm.index(tm), fns, w=wl_)

    def fm_load(self, sc, fm):
        P, S = self.P, self.S
        t0 = sc * 128
        full = sc >= 8
        fns = [
            lambda e: e.dma_start(out=fm["k"][:, :, :], in_=S["kb"][:, :, t0:t0 + 128].rearrange("h d t -> d h t")),
            lambda e: e.dma_start(out=fm["v"][:, :, :], in_=S["vb"][:, :, t0:t0 + 128].rearrange("h d t -> d h t")),
        ]
        wl_ = [fm["k"], fm["v"]]
        if full:
            tq = t0 - T0
            fns += [
                lambda e: e.dma_start(out=fm["q"][:, :, :], in_=S["qb"][:, :, t0:t0 + 128].rearrange("h d t -> d h t")),
                lambda e: e.dma_start(out=fm["qaT"][:, :, :], in_=S["qa"][:, :, tq:tq + 128].rearrange("h d t -> d h t")),
                lambda e: e.dma_start(out=fm["kaT"][:, :, :], in_=S["kaT"][:, :, tq:tq + 128].rearrange("h d t -> d h t")),
            ]
            wl_ += [fm["q"], fm["qaT"], fm["kaT"]]
        P.dma("sync", "s2f%d" % self.fm.index(fm), fns, w=wl_)

    def run(self):
        P, S, io = self.P, self.S, self.io
        for t in self._state_tiles():
            _ms(P, "gpsimd", t[:, :], 0.0, [t])
        self._refresh_bf()
        chunks = []
        for sc in range(NSUB):
            sample = sc == NSUB - 1
            L = LS if sample else LP
            for ch in range(128 // L):
                chunks.append((sc, ch, L, sample))
        self.fm_load(0, self.fm[0])
        self.tm_load(chunks[0][2], 0, False, self.tm[0])
        for idx, (sc, ch, L, sample) in enumerate(chunks):
            full = sc >= 8
            c0 = ch * L
            r0 = sc * 128 + c0
            q0 = r0 - T0
            fm, tm = self.fm[sc % 2], self.tm[idx % 2]
            if idx + 1 < len(chunks):
                nsc, nch, nL, _ = chunks[idx + 1]
                if nsc != sc:
                    self.fm_load(nsc, self.fm[nsc % 2])
                self.tm_load(nL, nsc * 128 + nch * nL, nsc >= 8, self.tm[(idx + 1) % 2])
            if sample:
                if ch == 0:
                    self._use_set(0)
                    self.load_state(0)
                if ch + 1 < 128 // L:
                    self._use_set((ch + 1) % 2)
                    self.load_state(ch + 1)
                self._use_set(ch % 2)
                self._refresh_bf()
            lists = []
            for g in ([self.gaS] if sample else self.ga):
                P.capture()
                self.mlstm_chunk(g, L, c0, q0, full, fm, tm)
                lists.append(P.end_capture())
            for g in ([self.gbS] if sample else self.gb):
                P.capture()
                self.gdn_chunk(g, L, c0, q0, full, fm, tm)
                lists.append(P.end_capture())
            P.replay(lists)
            if sample:
                self.store_state(io["oC"][ch], io["oS"][ch], io["on"][ch], io["om"][ch:ch + 1, :])
            if sc == 7 and ch == 128 // L - 1:
                for t in self._state_tiles():
                    _ts(P, "vector", t[:, :], t[:, :], self.flag[:, 0:1], ALU.mult, [t, self.flag], [t])
                self._refresh_bf()
            if sc == 15 and ch == 128 // L - 1:
                self.store_state(io["pC"], io["pS"], io["pn"], io["pm"])


def _phase3(B, P, PS, cst, C, identb, mixT, x_d, norms_d, modd, wout_d, wup_d, wdn_d, y_d):
    nc = B.nc
    bank = PS.banks
    X = Ctx(nc)
    x1 = X.sb("x1", [128, NSUBQ, D], F32)
    x1s = [TL(x1.t, "x1_%d" % i) for i in range(NSUBQ)]
    wr = [X.sb("p3_w%d" % i, [128, 16, 512], BF16) for i in range(3)]
    mt0 = X.sb("p3_mt0", [128, D], F32)
    mt1 = X.sb("p3_mt1", [128, D], F32)
    st = [X.sb("p3_st%d" % i, [128, 4], F32) for i in range(2)]
    wi = [0]

    def wslot():
        s = wr[wi[0] % 3]
        k = "wr%d" % (wi[0] % 3)
        wi[0] += 1
        return s, k

    def mtile(i):
        return mt0 if i < 8 else mt1

    for i in range(NSUBQ):
        _ld(P, "sync", "p3x%d" % (i % 3), x1[:, i, :], x_d[T0 + i * 128:T0 + (i + 1) * 128, :], w=[x1s[i]])

    XA = Ctx(nc)
    tmpf = XA.sb("p3_tmp", [128, D], F32)
    hb = XA.sb("p3_hb", [128, D], BF16)
    sgA = [XA.sb("p3_sgA%d" % i, [128, 512], F32) for i in range(2)]
    _mod_tiles(P, modd, 4096, mt0, mt1, "p3G")
    ne = 0
    for cb in range(4):
        slot, key = wslot()
        _ld(P, "gpsimd", key, slot[:], wout_d[:, cb * 512:(cb + 1) * 512].rearrange("(k p) c -> p k c", p=128), w=[slot])
        for i in range(NSUBQ):
            bk = PS.one()
            for k in range(16):
                _mm(P, PS.f32(bk)[:, :], mixT[:, k, i * 128:(i + 1) * 128], slot[:, k, :], k == 0, k == 15,
                    [mixT, slot], [bank[bk]])
            sg = sgA[ne % 2]
            ne += 1
            _tt(P, "vector", sg[:], PS.f32(bk)[:, :], mtile(i)[:, cb * 512:(cb + 1) * 512], ALU.mult,
                [bank[bk], mtile(i)], [sg])
            _tt(P, "vector", x1[:, i, cb * 512:(cb + 1) * 512], x1[:, i, cb * 512:(cb + 1) * 512], sg[:], ALU.add,
                [x1s[i], sg], [x1s[i]])

    def norm_tiles(col_sc, col_sh, nrow, first):
        _ld(P, "sync", "p3n", tmpf[:], norms_d[nrow:nrow + 1, :].partition_broadcast(128), w=[tmpf])
        if first:
            _ld(P, "sync", "p3m0", mt0[:], modd[0:1, col_sc:col_sc + D].partition_broadcast(128), w=[mt0])
            _ld(P, "sync", "p3m1", mt1[:], modd[0:1, col_sh:col_sh + D].partition_broadcast(128), w=[mt1])
        else:
            for t, col, key in ((mt0, col_sc, "p3m0"), (mt1, col_sh, "p3m1")):
                fns = []
                for b in range(NSEQ):
                    fns.append(lambda e, b=b, t=t, col=col: e.dma_start(
                        out=t[8 * b:8 * b + 8, :], in_=modd[1 + b:2 + b, col:col + D].partition_broadcast(8)))
                P.dma("sync", key, fns, w=[t])
        _stt(P, "vector", mt0[:], mt0[:], 1.0, tmpf[:], ALU.add, ALU.mult, [mt0, tmpf], [mt0])

    def norm_sub(i, out_ap, out_tl, junk):
        s_ = st[i % 2]
        _act(P, junk[:], x1[:, i, :], AF.Square, [x1s[i]], [junk, s_], accum=s_[:, 0:1])
        _act(P, s_[:, 1:2], s_[:, 0:1], AF.Ln, [s_], [s_], bias=EPS, scale=1.0 / D)
        _act(P, s_[:, 2:3], s_[:, 1:2], AF.Exp, [s_], [s_], scale=-0.5)
        _stt(P, "vector", tmpf[:], x1[:, i, :], s_[:, 2:3], mt0[:], ALU.mult, ALU.mult, [x1s[i], s_, mt0], [tmpf])
        _tt(P, "vector", out_ap, tmpf[:], mt1[:], ALU.add, [tmpf, mt1], [out_tl])

    h2T = mixT
    for i in range(NSUBQ):
        if i == 0:
            norm_tiles(8192, 6144, 1, True)
        if i == 8:
            norm_tiles(8192, 6144, 1, False)
        norm_sub(i, hb[:], hb, hb)
        for half in range(2):
            bk = PS.one()
            for kk in range(8):
                k = half * 8 + kk
                _tr(P, PS.bf(bk)[:, kk * 128:(kk + 1) * 128], hb[:, k * 128:(k + 1) * 128], identb[:],
                    [hb, identb], [bank[bk]])
            _cp(P, "scalar" if half else "vector", h2T[:, half * 8:(half + 1) * 8, i * 128:(i + 1) * 128],
                PS.bf(bk).rearrange("p (k t) -> p k t", k=8), [bank[bk]], [h2T])
    P.barrier()
    P.flush()
    XA.close()

    XB = Ctx(nc)
    actT = XB.sb("actT", [128, 8, NQ], BF16)
    sgB = [XB.sb("p3_sgB%d" % i, [128, 512], F32) for i in range(2)]
    rl = [XB.sb("p3_rl%d" % i, [128, 512], F32) for i in range(2)]
    _mod_tiles(P, modd, 10240, mt0, mt1, "p3G")
    ttiles = [(0, 512), (512, 512), (1024, 128)]
    ne = 0
    nr = 0
    for fb in range(8):
        for half in range(2):
            slot, key = wslot()
            c0 = fb * 1024 + half * 512
            _ld(P, "gpsimd", key, slot[:], wup_d[:, c0:c0 + 512].rearrange("(k p) c -> p k c", p=128), w=[slot])
            for sbk in range(4):
                for (t0, nt) in ttiles:
                    bk = PS.one()
                    for k in range(16):
                        _mm(P, PS.f32(bk)[:, 0:nt], slot[:, k, sbk * 128:(sbk + 1) * 128], h2T[:, k, t0:t0 + nt],
                            k == 0, k == 15, [slot, h2T], [bank[bk]])
                    r_ = rl[nr % 2]
                    nr += 1
                    _act(P, r_[:, 0:nt], PS.f32(bk)[:, 0:nt], AF.Relu, [bank[bk]], [r_])
                    _tt(P, "vector", actT[:, half * 4 + sbk, t0:t0 + nt], r_[:, 0:nt], r_[:, 0:nt], ALU.mult,
                        [r_], [actT])
        sd = []
        for half in range(2):
            slot, key = wslot()
            r0 = fb * 1024 + half * 512
            sv = slot[:, :, :].rearrange("p k c -> p (k c)").rearrange("p (s c) -> p s c", s=4)
            _ld(P, "gpsimd", key, sv, wdn_d[r0:r0 + 512, :].rearrange("(s p) c -> p s c", p=128), w=[slot])
            sd.append((slot, sv))
        for i in range(NSUBQ):
            for cb in range(4):
                bk = PS.one()
                for s8 in range(8):
                    slot, sv = sd[s8 // 4]
                    _mm(P, PS.f32(bk)[:, :], actT[:, s8, i * 128:(i + 1) * 128], sv[:, s8 % 4, cb * 512:(cb + 1) * 512],
                        s8 == 0, s8 == 7, [actT, slot], [bank[bk]])
                sg = sgB[ne % 2]
                ne += 1
                _tt(P, "vector", sg[:], PS.f32(bk)[:, :], mtile(i)[:, cb * 512:(cb + 1) * 512], ALU.mult,
                    [bank[bk], mtile(i)], [sg])
                _tt(P, "vector", x1[:, i, cb * 512:(cb + 1) * 512], x1[:, i, cb * 512:(cb + 1) * 512], sg[:], ALU.add,
                    [x1s[i], sg], [x1s[i]])
    P.barrier()
    P.flush()
    XB.close()

    XC = Ctx(nc)
    tmpf = XC.sb("p3c_tmp", [128, D], F32)
    yo = [XC.sb("p3c_y%d" % i, [128, D], F32) for i in range(2)]
    junk = XC.sb("p3c_junk", [128, D], BF16)
    for i in range(NSUBQ):
        if i == 0:
            norm_tiles(14336, 12288, 2, True)
        if i == 8:
            norm_tiles(14336, 12288, 2, False)
        y_ = yo[i % 2]
        norm_sub(i, y_[:], y_, junk)
        _ld(P, "sync", "p3y%d" % (i % 2), y_d[i * 128:(i + 1) * 128, :], y_[:], r=[y_])
    P.barrier()
    P.flush()
    XC.close()
    X.close()


def kernel(**inputs):
    inp = {k: np.asarray(v) for k, v in inputs.items()}
    B = build(debug=False)
    sh = _shared_inputs(inp)
    in_maps = []
    for c in range(8):
        m = _core_inputs(inp, c, sh)
        in_maps.append({k: v for k, v in m.items() if k in B.ins})
    res = run_bass_kernel_spmd(B.nc, in_maps, core_ids=list(range(8)))
    r = [{k: np.asarray(v) for k, v in rr.items()} for rr in res.results]

    y_prompt = np.empty((4, 2048, D), np.float32)
    y_sample = np.empty((128, LS, D), np.float32)
    pC = np.empty((1, 4, NH, DH, DH), np.float32)
    pn = np.empty((1, 4, NH, DH), np.float32)
    pm = np.empty((1, 4, NH), np.float32)
    pS = np.empty((1, 4, NH, DH, DH), np.float32)
    pconv = np.empty((1, 4, 3, 3072), np.float32)
    sC = np.empty((1, 128, NH, DH, DH), np.float32)
    sn = np.empty((1, 128, NH, DH), np.float32)
    sm = np.empty((1, 128, NH), np.float32)
    sS = np.empty((1, 128, NH, DH, DH), np.float32)
    sconv = np.empty((1, 128, 3, 3072), np.float32)
    for c in range(8):
        b, half = c // 2, c % 2
        sl = slice(c * NSEQ, (c + 1) * NSEQ)
        o = r[c]
        y_prompt[b, half * T1:(half + 1) * T1] = o["y"][:T1]
        y_sample[sl] = o["y"][T1:].reshape(NSEQ, LS, D)
        if half == 1:
            pC[0, b] = o["pC"]
            pn[0, b] = o["pn_t"].T
            pm[0, b] = o["pm"][0]
            pS[0, b] = o["pS"]
            pconv[0, b] = o["pconv_t"].reshape(128, 24, 3).transpose(2, 1, 0).reshape(3, 3072)
        sC[0, sl] = o["oC"]
        sn[0, sl] = o["on_t"].transpose(0, 2, 1)
        sm[0, sl] = o["om"]
        sS[0, sl] = o["oS"]
        sconv[0, sl] = o["oconv_t"].reshape(128, 24, NSEQ, 3).transpose(2, 3, 1, 0).reshape(NSEQ, 3, 3072)
    return (y_prompt, y_sample, pC, pn, pm, pS, pconv, sC, sn, sm, sS, sconv)
```

```python
import numpy as np
import concourse.bass as bass
import concourse.mybir as mybir
from concourse.bass_utils import run_bass_kernel_spmd

F32 = mybir.dt.float32
BF16 = mybir.dt.bfloat16
AF = mybir.ActivationFunctionType
ALU = mybir.AluOpType
AX = mybir.AxisListType

D = 2048
KD = 16
NH = 8
DH = 128
T0 = 1024
T1 = 1024
TS = 128
NSEQ = 16
LS = 8
NTOK = T0 + T1 + TS
NQ = T1 + TS
NSUB = NTOK // 128
NSUBQ = NQ // 128
DFF = 8192
EPS = 1e-6
NEG = -1.0e30
WIN_COLS = 5120 + 4096 + 32

ENGS = ("tensor", "vector", "scalar", "gpsimd", "sync")


class Buf:
    __slots__ = ("name", "w", "r")

    def __init__(self, name=""):
        self.name = name
        self.w = None
        self.r = {}


class TL:
    def __init__(self, t, name=""):
        self.t = t
        self.buf = Buf(name)

    def __getitem__(self, k):
        return self.t[k]


class Prog:
    def __init__(self, nc):
        self.nc = nc
        self.sems = {}
        self.cnt = {}
        self.ops = {e: [] for e in ENGS}
        self.seen = {e: {} for e in ENGS}
        for e in ENGS:
            self.sems[e] = nc.alloc_semaphore(name="s_" + e)
            self.cnt[e] = 0
        self.dkeys = []
        self.nins = 0

    def key(self, name):
        k = "d_" + name
        if k not in self.sems:
            self.sems[k] = self.nc.alloc_semaphore(name="s" + k)
            self.cnt[k] = 0
            self.dkeys.append(k)
        return k

    def _deps(self, eng, reads, writes):
        deps = {}

        def add(kv):
            if kv is None:
                return
            k, v = kv
            if k == eng and eng == "tensor":
                return
            if deps.get(k, 0) < v:
                deps[k] = v
        for b in reads:
            add(b.buf.w)
        for b in writes:
            add(b.buf.w)
            for kv in b.buf.r.items():
                add(kv)
        out = []
        seen = self.seen[eng]
        for k, v in deps.items():
            if seen.get(k, 0) >= v:
                continue
            seen[k] = v
            out.append((k, v))
        return out

    def capture(self):
        self.cap = []
        return self.cap

    def end_capture(self):
        c = self.cap
        self.cap = None
        return c

    def replay(self, lists):
        lists = [l for l in lists if l]
        if not lists:
            return
        n = max(len(l) for l in lists)
        pos = [0] * len(lists)
        for i in range(1, n + 1):
            for j, l in enumerate(lists):
                tgt = (i * len(l) + n - 1) // n
                while pos[j] < tgt:
                    it = l[pos[j]]
                    pos[j] += 1
                    if it[0] == "op":
                        self.op(*it[1:])
                    else:
                        self.dma(*it[1:])

    def _emit_item(self, it):
        if it[0] == "op":
            self.op(*it[1:])
        else:
            self.dma(*it[1:])

    def replay_pipe(self, items, depth, burst=2):
        active = []
        nxt = 0

        def prefix(it):
            n = 0
            while n < len(it) and it[n][0] == "op" and it[n][1] == "tensor":
                n += 1
            return n

        def rw(ops):
            rs, ws = set(), set()
            for o in ops:
                for b_ in self._fl(o[-2]):
                    rs.add(id(b_.buf))
                for b_ in self._fl(o[-1]):
                    ws.add(id(b_.buf))
            return rs, ws

        def conflict(it):
            r1, w1 = rw(it)
            for a_ in active:
                r2, w2 = rw(a_[0][a_[1]:])
                if (w1 & (r2 | w2)) or (r1 & w2):
                    return True
            return False
        while nxt < len(items) or active:
            while nxt < len(items) and len(active) < depth and (
                    not active or (active[-1][1] - active[-1][2]) >= max(1, (len(active[-1][0]) - active[-1][2]) // depth)):
                it = items[nxt]
                if active and conflict(it):
                    break
                nxt += 1
                npre = prefix(it)
                for i in range(npre):
                    self._emit_item(it[i])
                if npre < len(it):
                    active.append([it, npre, npre])
            for a in list(active):
                for _ in range(burst):
                    if a[1] < len(a[0]):
                        self._emit_item(a[0][a[1]])
                        a[1] += 1
                if a[1] >= len(a[0]):
                    active.remove(a)

    @staticmethod
    def _fl(lst):
        out = []
        for b in lst:
            if hasattr(b, "kids"):
                out.extend(b.kids)
            else:
                out.append(b)
        return out

    def op(self, eng, fn, r=(), w=()):
        if getattr(self, "cap", None) is not None:
            self.cap.append(("op", eng, fn, tuple(r), tuple(w)))
            return
        r, w = self._fl(r), self._fl(w)
        waits = self._deps(eng, r, w)
        self.cnt[eng] += 1
        v = self.cnt[eng]
        self.ops[eng].append((waits, fn, eng, 1))
        for b in r:
            if b.buf.r.get(eng, 0) < v:
                b.buf.r[eng] = v
        for b in w:
            b.buf.w = (eng, v)
            b.buf.r = {}
        self.nins += 1

    def V(self, fn, r=(), w=()):
        self.op("vector", fn, r, w)

    def A(self, fn, r=(), w=()):
        self.op("scalar", fn, r, w)

    def G(self, fn, r=(), w=()):
        self.op("gpsimd", fn, r, w)

    def T(self, fn, r=(), w=()):
        self.op("tensor", fn, r, w)

    def dma(self, eng, key, fns, r=(), w=()):
        if not isinstance(fns, (list, tuple)):
            fns = [fns]
        if getattr(self, "cap", None) is not None:
            self.cap.append(("dma", eng, key, fns, tuple(r), tuple(w)))
            return
        r, w = self._fl(r), self._fl(w)
        key = self.key(key) if not key.startswith("d_") else key
        waits = self._deps(eng, r, w)
        for i, fn in enumerate(fns):
            self.cnt[key] += 16
            self.ops[eng].append((waits if i == 0 else [], fn, key, 16))
        v = self.cnt[key]
        for b in r:
            if b.buf.r.get(key, 0) < v:
                b.buf.r[key] = v
        for b in w:
            b.buf.w = (key, v)
            b.buf.r = {}
        self.nins += len(fns)

    def barrier(self):
        for e in ENGS:
            waits = []
            for k, v in self.cnt.items():
                if k == e or v == 0:
                    continue
                if self.seen[e].get(k, 0) >= v:
                    continue
                self.seen[e][k] = v
                waits.append((k, v))
            self.ops[e].append((waits, None, None, 0))

    def flush(self, final=False):
        nc = self.nc
        sems = self.sems
        ops = self.ops

        def run(e, name):
            for waits, fn, key, inc in ops[name]:
                for k, v in waits:
                    e.wait_ge(sems[k], v)
                if fn is not None:
                    fn(e).then_inc(sems[key], inc)

        with nc.Block() as block:
            @block.tensor
            def _(e):
                run(e, "tensor")

            @block.vector
            def _(e):
                run(e, "vector")

            @block.scalar
            def _(e):
                run(e, "scalar")

            @block.gpsimd
            def _(e):
                run(e, "gpsimd")

            @block.sync
            def _(e):
                run(e, "sync")
        self.ops = {e: [] for e in ENGS}


def _const_layout():
    off = {}
    c = 0
    for name, n in (("ident", 128), ("ones", 128), ("neg128", 128), ("uinc128", 128), ("ustr128", 128),
                    ("sel128", 128), ("bd32", 128), ("o64", 128), ("offL128", 128), ("neg8", 8), ("uinc8", 8), ("ustr8", 8), ("sel8", 128),
                    ("sign", 16)):
        off[name] = (c, n)
        c += n
    return off, c


CO, NCONST = _const_layout()


def _build_consts():
    a = np.zeros((128, NCONST), np.float32)

    def put(name, m):
        o, n = CO[name]
        a[: m.shape[0], o:o + m.shape[1]] = m
    put("ident", np.eye(128, dtype=np.float32))
    put("ones", np.ones((128, 128), np.float32))
    for L, sfx in ((128, "128"), (8, "8")):
        t = np.arange(L)
        neg = np.where(t[None, :] <= t[:, None], 0.0, NEG).astype(np.float32)
        uinc = (t[None, :] >= t[:, None]).astype(np.float32)
        ustr = (t[None, :] > t[:, None]).astype(np.float32)
        sel = np.zeros((L, 128), np.float32)
        sel[L - 1, :] = 1.0
        put("neg" + sfx, neg)
        put("uinc" + sfx, uinc)
        put("ustr" + sfx, ustr)
        put("sel" + sfx, sel)
    bd32 = np.zeros((128, 128), np.float32)
    bd64 = np.zeros((128, 128), np.float32)
    for i in range(0, 128, 32):
        bd32[i:i + 32, i:i + 32] = 1.0
    for i in range(0, 128, 64):
        bd64[i:i + 64, i:i + 64] = 1.0
    put("bd32", bd32)
    put("o64", bd64 - bd32)
    ofl = np.zeros((128, 128), np.float32)
    ofl[64:, :64] = 1.0
    put("offL128", ofl)
    sg = np.ones((128, 16), np.float32)
    sg[:, 0:8] = -1.0
    put("sign", sg)
    return a


class Builder:
    def __init__(self, debug=False, phases=(0, 1, 2, 3)):
        self.debug = debug
        self.phases = phases
        nc = bass.Bass("TRN2", target_bir_lowering=False)
        self.nc = nc
        self.P = Prog(nc)
        self.ins = {}
        self.outs = {}
        self.psum = TLBank(nc)

    def din(self, name, shape, dt=F32):
        t = self.nc.dram_tensor(name, list(shape), dt, kind="ExternalInput").ap()
        self.ins[name] = t
        return t

    def dout(self, name, shape, dt=F32):
        t = self.nc.dram_tensor(name, list(shape), dt, kind="ExternalOutput").ap()
        self.outs[name] = t
        return t

    def dscr(self, name, shape, dt):
        kind = "ExternalOutput" if self.debug else "Internal"
        t = self.nc.dram_tensor(name, list(shape), dt, kind=kind).ap()
        if self.debug:
            self.outs[name] = t
        return t


class TLBank:
    def __init__(self, nc):
        self.t = nc.alloc_psum_tensor("psum_all", [128, 4096], F32)
        self.banks = [TL(None, "bank%d" % i) for i in range(8)]
        self.ptr = 0

        self.ids = list(range(8))

    def sub(self, ids):
        o = TLBank.__new__(TLBank)
        o.t, o.banks, o.ptr, o.ids = self.t, self.banks, 0, list(ids)
        return o

    def one(self):
        i = self.ids[self.ptr]
        self.ptr = (self.ptr + 1) % len(self.ids)
        return i

    def pair(self):
        if self.ptr % 2:
            self.ptr = (self.ptr + 1) % len(self.ids)
        i = self.ids[self.ptr]
        self.ptr = (self.ptr + 2) % len(self.ids)
        return i

    def f32(self, i, n=1):
        return self.t[:, i * 512:(i + n) * 512]

    def bf(self, i, n=1):
        return self.t[:, i * 512:(i + n) * 512].bitcast(BF16)


def _mm(P, out, lhsT, rhs, start, stop, r, w):
    P.T(lambda e: e.matmul(out, lhsT=lhsT, rhs=rhs, start=start, stop=stop), r, w)


def _tr(P, out, in_, ident, r, w):
    P.T(lambda e: e.transpose(out=out, in_=in_, identity=ident), r, w)


def _act(P, out, in_, func, r, w, bias=None, scale=None, accum=None):
    kw = {}
    if bias is not None:
        kw["bias"] = bias
    if scale is not None:
        kw["scale"] = scale
    if accum is not None:
        kw["accum_out"] = accum
    P.A(lambda e: e.activation(out=out, in_=in_, func=func, **kw), r, w)


def _tt(P, eng, out, in0, in1, op, r, w):
    P.op(eng, lambda e: e.tensor_tensor(out=out, in0=in0, in1=in1, op=op), r, w)


def _ts(P, eng, out, in0, s1, op0, r, w, s2=None, op1=None):
    if op1 is None:
        P.op(eng, lambda e: e.tensor_single_scalar(out=out, in_=in0, scalar=s1, op=op0), r, w)
    else:
        P.op(eng, lambda e: e.tensor_scalar(out=out, in0=in0, scalar1=s1, scalar2=s2, op0=op0, op1=op1), r, w)


def _stt(P, eng, out, in0, scalar, in1, op0, op1, r, w):
    P.op(eng, lambda e: e.scalar_tensor_tensor(out=out, in0=in0, scalar=scalar, in1=in1, op0=op0, op1=op1), r, w)


def _red(P, eng, out, in_, op, r, w):
    P.op(eng, lambda e: e.tensor_reduce(out=out, in_=in_, axis=AX.X, op=op), r, w)


def _cp(P, eng, out, in_, r, w):
    if eng == "scalar":
        P.A(lambda e: e.activation(out=out, in_=in_, func=AF.Copy), r, w)
    else:
        P.op(eng, lambda e: e.tensor_copy(out=out, in_=in_), r, w)


def _ms(P, eng, ap, val, w):
    P.op(eng, lambda e: e.memset(ap, val), (), w)


def _ld(P, eng, key, out, in_, r=(), w=()):
    P.dma(eng, key, lambda e: e.dma_start(out=out, in_=in_), r, w)


class Ctx:
    def __init__(self, nc):
        self.nc = nc
        self.guards = []

    def sb(self, name, shape, dt):
        g = self.nc.sbuf_tensor(name, list(shape), dt)
        t = g.__enter__()
        self.guards.append(g)
        return TL(t, name)

    def close(self):
        for g in reversed(self.guards):
            g.__exit__(None, None, None)
        self.guards = []


def build(debug=False, phases=(0, 1, 2, 3)):
    B = Builder(debug, phases)
    nc, P, PS = B.nc, B.P, B.psum
    bankb = PS.banks

    x_d = B.din("x", [NTOK, D])
    c_d = B.din("c", [17, D])
    flag_d = B.din("flag", [128, 1])
    consts_d = B.din("consts", [128, NCONST])
    adaw_d = B.din("ada_w", [D, 16384])
    adab_d = B.din("ada_b", [1, 16384])
    win_d = B.din("w_in", [D, WIN_COLS])
    wout_d = B.din("w_out", [D, D])
    wup_d = B.din("w_up", [D, DFF])
    wdn_d = B.din("w_down", [DFF, D])
    norms_d = B.din("norms", [3, D])
    hnorm_d = B.din("hnorm", [2, 1024])
    gvec_d = B.din("gvec", [1, 32])
    convw_d = B.din("conv_w", [128, 24 * 4])
    sC_d = B.din("sC", [NSEQ, NH, DH, DH])
    sn_d = B.din("sn_t", [NSEQ, DH, NH])
    sm_d = B.din("sm", [NSEQ, NH])
    sS_d = B.din("sS", [NSEQ, NH, DH, DH])
    scv_d = B.din("sconv_t", [128, 24 * NSEQ * 3])

    y_d = B.dout("y", [NQ, D])
    pC_d = B.dout("pC", [NH, DH, DH])
    pn_d = B.dout("pn_t", [DH, NH])
    pm_d = B.dout("pm", [1, NH])
    pS_d = B.dout("pS", [NH, DH, DH])
    pcv_d = B.dout("pconv_t", [128, 24 * 3])
    oC_d = B.dout("oC", [NSEQ, NH, DH, DH])
    on_d = B.dout("on_t", [NSEQ, DH, NH])
    om_d = B.dout("om", [NSEQ, NH])
    oS_d = B.dout("oS", [NSEQ, NH, DH, DH])
    ocv_d = B.dout("oconv_t", [128, 24 * NSEQ * 3])

    modd = B.dscr("modd", [17, 16384], F32)
    s_qa = B.dscr("s_qa", [NH, DH, NQ], BF16)
    s_kaT = B.dscr("s_kaT", [NH, DH, NQ], BF16)
    s_qb = B.dscr("s_qb", [NH, DH, NTOK], BF16)
    s_kb = B.dscr("s_kb", [NH, DH, NTOK], BF16)
    s_vb = B.dscr("s_vb", [NH, DH, NTOK], BF16)
    s_ka = B.dscr("s_ka", [NTOK, 1024], BF16)
    s_va = B.dscr("s_va", [NTOK, 1024], BF16)
    s_oa = B.dscr("s_oa", [NQ, 1024], BF16)
    s_zb = B.dscr("s_zb", [NQ, 1024], BF16)
    s_gt = B.dscr("s_gt", [NTOK, 32], F32)

    G = Ctx(nc)
    cst = G.sb("consts_sb", [128, NCONST], F32)
    identb = G.sb("identb", [128, 128], BF16)
    onesb = G.sb("onesb", [128, 128], BF16)
    flag = G.sb("flag_sb", [128, 1], F32)
    _ld(P, "sync", "g0", cst[:], consts_d, w=[cst])
    _ld(P, "sync", "g1", flag[:], flag_d, w=[flag])

    def C(name, rows=128):
        o, n = CO[name]
        return cst[0:rows, o:o + n]
    _cp(P, "vector", identb[:], C("ident"), [cst], [identb])
    _cp(P, "vector", onesb[:], C("ones"), [cst], [onesb])

    cT = G.sb("cT_sb", [128, 16 * 17], BF16)
    if 0 in phases:
        _phase0(B, P, PS, cst, C, c_d, adaw_d, adab_d, modd, cT)
        P.barrier()
        P.flush()
    if 1 in phases:
        _phase1(B, P, PS, cst, C, identb, onesb, flag, x_d, norms_d, modd, win_d,
                dict(qa=s_qa, kaT=s_kaT, qb=s_qb, kb=s_kb, vb=s_vb, ka=s_ka, va=s_va, oa=s_oa, zb=s_zb, gt=s_gt),
                gvec_d, convw_d, scv_d, pcv_d, ocv_d, cT, adaw_d, adab_d)
    S = dict(qa=s_qa, kaT=s_kaT, qb=s_qb, kb=s_kb, vb=s_vb, ka=s_ka, va=s_va, oa=s_oa, zb=s_zb, gt=s_gt)
    G2 = Ctx(nc)
    mixT = G2.sb("mixT", [128, 16, NQ], BF16)
    if debug:
        mixd = B.dout("mix_dbg", [128, 16 * NQ], BF16)
    if 2 in phases:
        io = dict(sC=sC_d, sn=sn_d, sm=sm_d, sS=sS_d, scv=scv_d, pC=pC_d, pn=pn_d, pm=pm_d, pS=pS_d, pcv=pcv_d,
                  oC=oC_d, on=on_d, om=om_d, oS=oS_d, ocv=ocv_d)
        sc = Scan(B, P, PS, cst, C, identb, onesb, flag, mixT, S, hnorm_d, gvec_d, convw_d, io)
        sc.run()
        if debug:
            _ld(P, "sync", "dbgm", mixd, mixT[:, :, :].rearrange("p k t -> p (k t)"), r=[mixT])
        P.barrier()
        P.flush()
        sc.X.close()
    if 3 in phases:
        _phase3(B, P, PS, cst, C, identb, mixT, x_d, norms_d, modd, wout_d, wup_d, wdn_d, y_d)
    P.barrier()
    P.flush()
    return B


NADA0 = 8


def _ada_block(P, PS, jb, slot, key, bt, bkey, m, mkey, cT, adaw_d, adab_d, modd):
    cols = slice(jb * 512, (jb + 1) * 512)
    _ld(P, "gpsimd", key, slot[:], adaw_d[:, cols].rearrange("(k p) c -> p k c", p=128), w=[slot])
    _ld(P, "sync", bkey, bt[:], adab_d[0:1, cols].partition_broadcast(17), w=[bt])
    bk = PS.one()
    for k in range(16):
        _mm(P, PS.f32(bk)[0:17, :], cT[:, k * 17:(k + 1) * 17], slot[:, k, :], k == 0, k == 15,
            [cT, slot], [PS.banks[bk]])
    _tt(P, "vector", m[:], PS.f32(bk)[0:17, :], bt[:], ALU.add, [PS.banks[bk], bt], [m])
    _ld(P, "sync", mkey, modd[:, cols], m[:], r=[m])


def _phase0(B, P, PS, cst, C, c_d, adaw_d, adab_d, modd, cT):
    nc = B.nc
    X = Ctx(nc)
    c_sb = X.sb("p0_c", [17, D], F32)
    wr = [X.sb("p0_w%d" % i, [128, 16, 512], BF16) for i in range(3)]
    bias = [X.sb("p0_b%d" % i, [17, 512], F32) for i in range(2)]
    mo = [X.sb("p0_m%d" % i, [17, 512], F32) for i in range(2)]
    _ld(P, "sync", "p0c", c_sb[:], c_d, w=[c_sb])
    _act(P, c_sb[:], c_sb[:], AF.Silu, [c_sb], [c_sb])
    bk = PS.one()
    for k in range(16):
        _tr(P, PS.f32(bk)[:, k * 17:(k + 1) * 17], c_sb[:, k * 128:(k + 1) * 128], C("ident", 17)[:, 0:17],
            [c_sb, cst], [PS.banks[bk]])
    _cp(P, "vector", cT[:], PS.f32(bk)[:, 0:272], [PS.banks[bk]], [cT])
    items = []
    for jb in range(NADA0):
        P.capture()
        _ada_block(P, PS, jb, wr[jb % 3], "wr%d" % (jb % 3), bias[jb % 2], "p0b%d" % (jb % 2), mo[jb % 2],
                   "p0m%d" % (jb % 2), cT, adaw_d, adab_d, modd)
        items.append(P.end_capture())
    P.replay_pipe(items, 3, burst=1)
    X.close()


def _mod_tiles(P, modd, col0, tp, ts, key):
    _ld(P, "sync", key + "p", tp[:], modd[0:1, col0:col0 + D].partition_broadcast(128), w=[tp])
    fns = []
    for b in range(NSEQ):
        fns.append(lambda e, b=b: e.dma_start(out=ts[8 * b:8 * b + 8, :],
                                              in_=modd[1 + b:2 + b, col0:col0 + D].partition_broadcast(8)))
    P.dma("sync", key + "s", fns, w=[ts])


def _phase1(B, P, PS, cst, C, identb, onesb, flag, x_d, norms_d, modd, win_d, S, gvec_d, convw_d, scv_d, pcv_d, ocv_d,
            cT, adaw_d, adab_d):
    nc = B.nc
    bank = PS.banks
    X = Ctx(nc)
    hT = X.sb("hT", [128, 16, NTOK], BF16)
    hTs = [TL(hT.t, "hT_%d" % i) for i in range(NSUB)]
    hT = TLP(hT.t, hTs)
    wr = [X.sb("p1_w%d" % i, [128, 16, 512], BF16) for i in range(3)]

    XA = Ctx(nc)
    xt = [XA.sb("p1_x%d" % i, [128, D], F32) for i in range(2)]
    tmpf = XA.sb("p1_tmp", [128, D], F32)
    hb = [XA.sb("p1_hb%d" % i, [128, D], BF16) for i in range(2)]
    Ap = XA.sb("p1_Ap", [128, D], F32)
    Bp = XA.sb("p1_Bp", [128, D], F32)
    As = XA.sb("p1_As", [128, D], F32)
    Bs = XA.sb("p1_Bs", [128, D], F32)
    st = [XA.sb("p1_st%d" % i, [128, 4], F32) for i in range(2)]
    _ld(P, "sync", "p1n", tmpf[:], norms_d[0:1, :].partition_broadcast(128), w=[tmpf])
    _mod_tiles(P, modd, 2048, Ap, As, "p1A")
    _mod_tiles(P, modd, 0, Bp, Bs, "p1B")
    for A_ in (Ap, As):
        _stt(P, "vector", A_[:], A_[:], 1.0, tmpf[:], ALU.add, ALU.mult, [A_, tmpf], [A_])
    wq = []

    def wload(jb):
        slot = wr[jb % 3]
        ncol = 512 if jb < 18 else 32
        _ld(P, "gpsimd", "wr%d" % (jb % 3), slot[:, :, 0:ncol],
            win_d[:, jb * 512:jb * 512 + ncol].rearrange("(k p) c -> p k c", p=128), w=[slot])
    for jb in range(3):
        wload(jb)
    nitems = []
    for i in range(NSUB):
        P.capture()
        x_ = xt[i % 2]
        h_ = hb[i % 2]
        s_ = st[i % 2]
        _ld(P, "sync", "p1x%d" % (i % 2), x_[:], x_d[i * 128:(i + 1) * 128, :], w=[x_])
        _act(P, h_[:], x_[:], AF.Square, [x_], [h_, s_], accum=s_[:, 0:1])
        _act(P, s_[:, 1:2], s_[:, 0:1], AF.Ln, [s_], [s_], bias=EPS, scale=1.0 / D)
        _act(P, s_[:, 2:3], s_[:, 1:2], AF.Exp, [s_], [s_], scale=-0.5)
        A_, B_ = (Ap, Bp) if i < 16 else (As, Bs)
        _stt(P, "vector", tmpf[:], x_[:], s_[:, 2:3], A_[:], ALU.mult, ALU.mult, [x_, s_, A_], [tmpf])
        _tt(P, "vector", h_[:], tmpf[:], B_[:], ALU.add, [tmpf, B_], [h_])
        for half in range(2):
            bk = PS.one()
            for kk in range(8):
                k = half * 8 + kk
                _tr(P, PS.bf(bk)[:, kk * 128:(kk + 1) * 128], h_[:, k * 128:(k + 1) * 128], identb[:],
                    [h_, identb], [PS.banks[bk]])
            _cp(P, "scalar" if half else "vector", hT[:, half * 8:(half + 1) * 8, i * 128:(i + 1) * 128],
                PS.bf(bk).rearrange("p (k t) -> p k t", k=8), [PS.banks[bk]], [hTs[i]])
        nitems.append(P.end_capture())
    P.replay_pipe(nitems, 3, burst=1)
    P.barrier()
    P.flush()
    XA.close()

    XB = Ctx(nc)
    stg = [XB.sb("p1_stg%d" % i, [128, 512], BF16) for i in range(4)]
    NU = 4
    stg = stg + [XB.sb("p1_stg%d" % i, [128, 512], BF16) for i in range(4, 8)]
    pds = [XB.sb("p1_pd%d" % i, [128, 3 + 512], F32) for i in range(NU)]
    accs = [XB.sb("p1_acc%d" % i, [128, 512], F32) for i in range(NU)]
    efs = [XB.sb("p1_ef%d" % i, [128, 512], F32) for i in range(NU)]
    sqs = [XB.sb("p1_sq%d" % i, [128, 512], BF16) for i in range(NU)]
    rvs = accs
    cvs = XB.sb("p1_cvs", [128, 24 * NSEQ * 3], F32)
    pcvt = XB.sb("p1_pcvt", [128, 72], F32)
    convw = XB.sb("p1_convw", [128, 96], F32)
    gbr = XB.sb("p1_gbr", [128, 32], F32)
    biasrow = XB.sb("p1_biasrow", [128, 16], F32)
    coefrow = XB.sb("p1_coefrow", [128, 16], F32)
    graw = [XB.sb("p1_graw%d" % i, [128, 32], F32) for i in range(2)]
    GPt = [XB.sb("p1_GP%d" % i, [128, 32], F32) for i in range(2)]
    gt1 = XB.sb("p1_gt1", [128, 16], F32)
    gt2 = XB.sb("p1_gt2", [128, 16], F32)
    gt3 = XB.sb("p1_gt3", [128, 16], F32)
    gt4 = XB.sb("p1_gt4", [128, 8], F32)
    aw = [XB.sb("p1_aw%d" % i, [128, 16, 512], BF16) for i in range(2)]
    abias = [XB.sb("p1_ab%d" % i, [17, 512], F32) for i in range(2)]
    amo = [XB.sb("p1_am%d" % i, [17, 512], F32) for i in range(2)]
    ada_next = [NADA0]

    def ada_item():
        jb_ = ada_next[0]
        if jb_ >= 32:
            return None
        ada_next[0] += 1
        P.capture()
        _ada_block(P, PS, jb_, aw[jb_ % 2], "aw%d" % (jb_ % 2), abias[jb_ % 2], "p1ab%d" % (jb_ % 2), amo[jb_ % 2],
                   "p1am%d" % (jb_ % 2), cT, adaw_d, adab_d, modd)
        return P.end_capture()
    _ld(P, "sync", "p1cv", cvs[:], scv_d, w=[cvs])
    _ld(P, "sync", "p1cw", convw[:], convw_d, w=[convw])
    _ld(P, "sync", "p1gb", gbr[:], gvec_d.partition_broadcast(128), w=[gbr])
    _cp(P, "vector", biasrow[:, 0:8], gbr[:, 8:16], [gbr], [biasrow])
    _cp(P, "vector", biasrow[:, 8:16], gbr[:, 24:32], [gbr], [biasrow])
    _ms(P, "vector", coefrow[:], -1.0, [coefrow])
    _act(P, coefrow[:, 8:16], gbr[:, 16:24], AF.Exp, [gbr, coefrow], [coefrow])
    _ts(P, "vector", coefrow[:, 8:16], coefrow[:, 8:16], -1.0, ALU.mult, [coefrow], [coefrow])

    ev = [0]

    def evac(out, in_, bank_, dst, func=AF.Copy, scale=None):
        if func == AF.Copy and scale is None and (ev[0] % 2 == 0):
            _cp(P, "vector", out, in_, [bank_], [dst])
        else:
            _act(P, out, in_, func, [bank_], [dst], scale=scale)
        ev[0] += 1

    sidx = [0]

    def store(ap_dst, sg, nt):
        _ld(P, "sync", "p1s%d" % ((sidx[0] - 1) % 8), ap_dst, sg[:, 0:nt], r=[sg])

    def next_stg():
        sg = stg[sidx[0] % 8]
        sidx[0] += 1
        return sg

    ucnt = [0]

    def gdn_unit(name, ty, h, t0, nt, bk, first, do_store, prev):
        blk = 8 * ty + h
        u = ucnt[0]
        ucnt[0] += 1
        pd, acc, ef, sq = pds[u % NU], accs[u % NU], efs[u % NU], sqs[u % NU]
        rv = ef
        sample = t0 == T0 + T1
        cw = convw[:, blk * 4:(blk + 1) * 4]
        if sample:
            pv = pd[:, 0:NSEQ * 11].rearrange("p (s l) -> p s l", s=NSEQ)
            cv3 = cvs[:, blk * 48:(blk + 1) * 48].rearrange("p (s j) -> p s j", s=NSEQ)
            _cp(P, "scalar", pv[:, :, 0:3], cv3, [cvs, pd], [pd])
            _cp(P, "scalar", pv[:, :, 3:11], PS.f32(bk)[:, 0:nt].rearrange("p (s l) -> p s l", s=NSEQ),
                [bank[bk], pd], [pd])
            _cp(P, "scalar", cv3, pv[:, :, 8:11], [pd, cvs], [cvs])
            a_ = acc[:, 0:nt].rearrange("p (s l) -> p s l", s=NSEQ)

            def tap(j):
                return pv[:, :, j:j + LS]
        else:
            if first:
                _ms(P, "vector", pd[:, 0:3], 0.0, [pd])
            else:
                ppd, pnt = prev
                if t0 == T0:
                    _ts(P, "vector", pd[:, 0:3], ppd[:, pnt:pnt + 3], flag[:, 0:1], ALU.mult, [ppd, flag, pd], [pd])
                else:
                    _cp(P, "scalar", pd[:, 0:3], ppd[:, pnt:pnt + 3], [ppd, pd], [pd])
            _cp(P, "scalar", pd[:, 3:3 + nt], PS.f32(bk)[:, 0:nt], [bank[bk], pd], [pd])
            if t0 + nt == T0 + T1:
                _cp(P, "scalar", pcvt[:, blk * 3:(blk + 1) * 3], pd[:, nt:nt + 3], [pd, pcvt], [pcvt])
            a_ = acc[:, 0:nt]

            def tap(j):
                return pd[:, j:j + nt]
        if not do_store:
            return (pd, nt)
        _ts(P, "vector", a_, tap(0), cw[:, 0:1], ALU.mult, [pd, convw, acc], [acc])
        for j in range(1, 4):
            _stt(P, "vector", a_, tap(j), cw[:, j:j + 1], a_, ALU.mult, ALU.add, [pd, convw, acc], [acc])
        _act(P, ef[:, 0:nt], acc[:, 0:nt], AF.Exp, [acc, ef], [ef], scale=-1.0)
        _act(P, ef[:, 0:nt], ef[:, 0:nt], AF.Ln, [ef], [ef], bias=1.0)
        _act(P, ef[:, 0:nt], ef[:, 0:nt], AF.Exp, [ef], [ef], scale=-1.0)
        sg = next_stg()
        if ty == 2:
            _tt(P, "vector", sg[:, 0:nt], acc[:, 0:nt], ef[:, 0:nt], ALU.mult, [acc, ef, sg], [sg])
        else:
            _tt(P, "vector", acc[:, 0:nt], acc[:, 0:nt], ef[:, 0:nt], ALU.mult, [acc, ef], [acc])
            _tt(P, "vector", sq[:, 0:nt], acc[:, 0:nt], acc[:, 0:nt], ALU.mult, [acc, sq], [sq])
            b2 = PS.one()
            _mm(P, PS.f32(b2)[:, 0:nt], onesb[:, :], sq[:, 0:nt], True, True, [onesb, sq], [bank[b2]])
            _act(P, rv[:, 0:nt], PS.f32(b2)[:, 0:nt], AF.Ln, [bank[b2], rv], [rv], bias=EPS)
            _act(P, rv[:, 0:nt], rv[:, 0:nt], AF.Exp, [rv], [rv], scale=-0.5)
            if ty == 0:
                _stt(P, "vector", sg[:, 0:nt], acc[:, 0:nt], DH ** -0.5, rv[:, 0:nt], ALU.mult, ALU.mult,
                     [acc, rv, sg], [sg])
            else:
                _tt(P, "vector", sg[:, 0:nt], acc[:, 0:nt], rv[:, 0:nt], ALU.mult, [acc, rv, sg], [sg])
        store(S[name][h, :, t0:t0 + nt], sg, nt)
        return (pd, nt)

    def gate_unit(i, bk):
        gr, GP = graw[i % 2], GPt[i % 2]
        _cp(P, "vector", gr[:], PS.f32(bk)[:, 0:32], [bank[bk], gr], [gr])
        _tt(P, "vector", GP[:, 0:8], gr[:, 0:8], gbr[:, 0:8], ALU.add, [gr, gbr, GP], [GP])
        _tt(P, "vector", gt1[:], gr[:, 8:24], biasrow[:], ALU.add, [gr, biasrow, gt1], [gt1])
        _tt(P, "vector", gt1[:], gt1[:], C("sign"), ALU.mult, [gt1, cst], [gt1])
        _act(P, gt2[:], gt1[:], AF.Abs, [gt1, gt2], [gt2])
        _act(P, gt2[:], gt2[:], AF.Exp, [gt2], [gt2], scale=-1.0)
        _act(P, gt2[:], gt2[:], AF.Ln, [gt2], [gt2], bias=1.0)
        _ts(P, "vector", gt3[:], gt1[:], 0.0, ALU.max, [gt1, gt3], [gt3])
        _tt(P, "vector", gt3[:], gt3[:], gt2[:], ALU.add, [gt3, gt2], [gt3])
        _tt(P, "vector", GP[:, 8:24], gt3[:], coefrow[:], ALU.mult, [gt3, coefrow, GP], [GP])
        _act(P, gt4[:], gr[:, 24:32], AF.Exp, [gr, gt4], [gt4], scale=-1.0)
        _ts(P, "vector", gt4[:], gt4[:], 1.0, ALU.add, [gt4], [gt4])
        P.V(lambda e: e.reciprocal(out=GP[:, 24:32], in_=gt4[:]), [gt4, GP], [GP])
        _ld(P, "sync", "p1g%d" % (i % 2), S["gt"][i * 128:(i + 1) * 128, :], GP[:], r=[GP])

    fm_types = [("qa", 0), ("kaT", 0), ("qb", 1), ("kb", 2), ("vb", 2)]
    tiles_q = [(1024, 512), (1536, 512), (2048, 128)]
    tiles_all = [(0, 512), (512, 512)] + tiles_q
    items = []
    for jb in range(19):
        slot = wr[jb % 3]
        if jb == 10:
            while True:
                it_ = ada_item()
                if it_ is None:
                    break
                items.append(it_)
            P.replay_pipe(items, 6, burst=1)
            items = []
        if jb >= 3:
            if jb < 10:
                P.capture()
                wload(jb)
                items.append(P.end_capture())
            else:
                wload(jb)
        if jb < 10:
            name, mode = fm_types[jb // 2]
            tl = {0: tiles_q, 1: [(512, 512)] + tiles_q, 2: tiles_all}[mode]
            for sbk in range(4):
                h = (jb % 2) * 4 + sbk
                prev = None
                for ti, (t0, nt) in enumerate(tl):
                    P.capture()
                    bk = PS.one()
                    for k in range(16):
                        _mm(P, PS.f32(bk)[:, 0:nt], slot[:, k, sbk * 128:(sbk + 1) * 128], hT[:, k, t0:t0 + nt],
                            k == 0, k == 15, [slot, hT], [PS.banks[bk]])
                    if name in ("qa", "kaT"):
                        sg = next_stg()
                        evac(sg[:, 0:nt], PS.f32(bk)[:, 0:nt], PS.banks[bk], sg,
                             scale=(DH ** -0.5 if name == "kaT" else None))
                        store(S[name][h, :, t0 - T0:t0 - T0 + nt], sg, nt)
                    else:
                        ty = {"qb": 0, "kb": 1, "vb": 2}[name]
                        prev = gdn_unit(name, ty, h, t0, nt, bk, ti == 0, not (name == "qb" and t0 < T0), prev)
                    items.append(P.end_capture())
                    if len(items) % 5 == 0:
                        it_ = ada_item()
                        if it_ is not None:
                            items.append(it_)
        elif jb < 18:
            name = ("ka", "va", "oa", "zb")[(jb - 10) // 2]
            c0 = ((jb - 10) % 2) * 512
            qonly = name in ("oa", "zb")
            for i in range(NSUB):
                if qonly and i < 8:
                    continue
                bk = PS.one()
                for k in range(16):
                    _mm(P, PS.f32(bk)[:, :], hT[:, k, i * 128:(i + 1) * 128], slot[:, k, :], k == 0, k == 15,
                        [slot, hT], [PS.banks[bk]])
                sg = next_stg()
                if name == "ka":
                    evac(sg[:], PS.f32(bk), PS.banks[bk], sg, scale=DH ** -0.5)
                elif name == "va":
                    evac(sg[:], PS.f32(bk), PS.banks[bk], sg)
                elif name == "oa":
                    evac(sg[:], PS.f32(bk), PS.banks[bk], sg, func=AF.Sigmoid)
                else:
                    evac(sg[:], PS.f32(bk), PS.banks[bk], sg, func=AF.Silu)
                r0 = i * 128 - (T0 if qonly else 0)
                store(S[name][r0:r0 + 128, c0:c0 + 512], sg, 512)
        else:
            for i in range(NSUB):
                bk = PS.one()
                for k in range(16):
                    _mm(P, PS.f32(bk)[:, 0:32], hT[:, k, i * 128:(i + 1) * 128], slot[:, k, 0:32], k == 0, k == 15,
                        [slot, hT], [PS.banks[bk]])
                gate_unit(i, bk)
    _ld(P, "sync", "p1pc", pcv_d, pcvt[:, :], r=[pcvt])
    _ld(P, "sync", "p1oc", ocv_d, cvs[:, :], r=[cvs])
    P.barrier()
    P.flush()
    XB.close()
    X.close()


def _shared_inputs(inp):
    w_in = np.asarray(inp["w_in"][0])
    o = np.cumsum([0, 1024, 1024, 1024, 8, 8, 1024, 1024, 1024, 1024, 8, 8, 1024])
    seg = {n: w_in[:, o[i]:o[i + 1]] for i, n in enumerate(
        ["qa", "ka", "va", "ia", "fa", "oa", "qb", "kb", "vb", "ab", "bb", "zb"])}
    w_in_r = np.ascontiguousarray(np.concatenate(
        [seg[n] for n in ("qa", "ka", "qb", "kb", "vb", "ka", "va", "oa", "zb", "ia", "fa", "ab", "bb")], axis=1))
    sh = {
        "consts": _build_consts(),
        "ada_w": np.ascontiguousarray(np.concatenate([inp["ada_w"][0], inp["ada_final_w"]], axis=1)),
        "ada_b": np.ascontiguousarray(np.concatenate([inp["ada_b"][0], inp["ada_final_b"]])[None, :]),
        "w_in": w_in_r,
        "w_out": np.ascontiguousarray(inp["w_out"][0]),
        "w_up": np.ascontiguousarray(inp["w_up"][0]),
        "w_down": np.ascontiguousarray(inp["w_down"][0]),
        "norms": np.ascontiguousarray(np.stack([inp["norm1"][0], inp["norm2"][0], inp["norm_final"]])),
        "hnorm": np.ascontiguousarray(np.stack([inp["mlstm_norm"][0], inp["gdn_norm"][0]])),
        "gvec": np.ascontiguousarray(np.concatenate(
            [inp["mlstm_gate_bias"][0], inp["gdn_A_log"][0], inp["gdn_dt_bias"][0]])[None, :]),
        "conv_w": np.ascontiguousarray(
            np.asarray(inp["gdn_conv_w"][0]).T.reshape(24, 128, 4).transpose(1, 0, 2).reshape(128, 96)),
    }
    return {k: np.asarray(v, np.float32) for k, v in sh.items()}


def _core_inputs(inp, c, sh):
    b, half = c // 2, c % 2
    sl = slice(c * NSEQ, (c + 1) * NSEQ)
    xp = inp["x_prompt"][b]
    m = dict(sh)
    m["x"] = np.ascontiguousarray(np.concatenate(
        [xp[0:T0], xp[half * T1:(half + 1) * T1], inp["x_sample"][sl].reshape(TS, D)], axis=0), np.float32)
    m["c"] = np.ascontiguousarray(np.concatenate([inp["c_prompt"][b:b + 1], inp["c_sample"][sl]], axis=0), np.float32)
    m["flag"] = np.full((128, 1), float(half), np.float32)
    m["sC"] = np.ascontiguousarray(inp["state_mlstm_C"][0, sl], np.float32)
    m["sn_t"] = np.ascontiguousarray(np.asarray(inp["state_mlstm_n"][0, sl]).transpose(0, 2, 1), np.float32)
    m["sm"] = np.ascontiguousarray(inp["state_mlstm_m"][0, sl], np.float32)
    m["sS"] = np.ascontiguousarray(inp["state_gdn_S"][0, sl], np.float32)
    cv = np.asarray(inp["state_gdn_conv"][0, sl])
    m["sconv_t"] = np.ascontiguousarray(
        cv.transpose(2, 0, 1).reshape(24, 128, NSEQ, 3).transpose(1, 0, 2, 3).reshape(128, 24 * NSEQ * 3), np.float32)
    return m


LP = 128
NLEV = {8: 2, 64: 5, 128: 6}


def _v3(ap2d, n):
    return ap2d.rearrange("p (h x) -> p h x", h=NH)


class TLV:
    def __init__(self, t, c0, c1, name=""):
        self.t, self.c0, self.c1 = t, c0, c1
        self.buf = Buf(name)

    def __getitem__(self, k):
        rows, cols = k
        a = 0 if cols.start is None else cols.start
        b = (self.c1 - self.c0) if cols.stop is None else cols.stop
        return self.t[rows, self.c0 + a:self.c0 + b]


class TLP:
    def __init__(self, t, kids):
        self.t, self.kids = t, kids

    def __getitem__(self, k):
        return self.t[k]


class Grp:
    pass


class Scan:
    NG = 2

    def __init__(self, B, P, PS, cst, C, identb, onesb, flag, mixT, S, hnorm_d, gvec_d, convw_d, io):
        self.B, self.P, self.PS, self.cst, self.C = B, P, PS, cst, C
        self.identb, self.onesb, self.flag, self.mixT, self.S, self.io = identb, onesb, flag, mixT, S, io
        nc = B.nc
        X = self.X = Ctx(nc)
        sb = X.sb
        NG = self.NG
        nh = NH // NG
        self.gA = sb("gA", [128, 1024], F32)
        self.gB = sb("gB", [128, 1024], F32)
        _ld(P, "sync", "s2a", self.gA[:], hnorm_d[0:1, :].partition_broadcast(128), w=[self.gA])
        _ld(P, "sync", "s2b", self.gB[:], hnorm_d[1:2, :].partition_broadcast(128), w=[self.gB])
        self.fm = [dict(qaT=sb("qaT%d" % i, [128, 8, 128], BF16), kaT=sb("kaT%d" % i, [128, 8, 128], BF16),
                        q=sb("qpost%d" % i, [128, 8, 128], BF16), k=sb("kpost%d" % i, [128, 8, 128], BF16),
                        v=sb("vpost%d" % i, [128, 8, 128], BF16)) for i in range(2)]
        self.tm = [dict(ka=sb("ka_t%d" % i, [128, 1024], BF16), va=sb("va_t%d" % i, [128, 1024], BF16),
                        oa=sb("oa_t%d" % i, [128, 1024], BF16), zb=sb("zb_t%d" % i, [128, 1024], BF16),
                        GP=sb("GP%d" % i, [128, 32], F32)) for i in range(2)]
        specA_big = (("dgx", F32), ("R", F32), ("sloc", BF16), ("sTsb", BF16), ("nl", F32), ("wlk", BF16), ("dCs", F32),
                     ("h1", F32), ("hnb", BF16), ("gs", BF16), ("Cst", F32), ("Cbf", BF16))
        specB_big = (("dG", F32), ("NZ", F32), ("qg", BF16), ("wT", F32), ("dTi", F32), ("P0", BF16), ("P1", BF16),
                     ("PT0", BF16), ("PT1", BF16), ("Tacc", BF16), ("qkT", BF16), ("Mf", BF16), ("MoT", BF16), ("MoT2", BF16),
                     ("kbg", BF16), ("kdec", BF16), ("vbt", BF16), ("WT", BF16), ("U0", F32), ("u", BF16),
                     ("ob", BF16), ("gz", BF16), ("Sst", F32), ("Sbf", BF16))

        def smallA(n):
            return (("sm1", 2 * n), ("g", n), ("cm", n), ("rows", n), ("gl", n), ("dns", n), ("mx", 2 * n), ("t12", 2 * n),
                    ("fd", 2 * n), ("mt", n), ("en", n), ("dd", n), ("d2", n), ("a12", 2 * n), ("ssq", n), ("sm2", 2 * n),
                    ("dC", n))

        def smallB(n):
            return (("gsm", 2 * n), ("gsmall", 2 * n), ("gLe", n), ("ssq2", n))
        parents = {}
        for kind, spec in (("a", specA_big), ("b", specB_big)):
            for n, dt in spec:
                parents[(kind, n)] = sb("P%s_%s" % (kind, n), [128, 1024], dt)
        nstP = sb("Pa_nst", [128, NH], F32)
        nbfP = sb("Pa_nbf", [128, NH], BF16)
        mprevP = sb("Pa_mprev", [128, NH], F32)
        self.ga, self.gb = [], []
        for gi in range(NG):
            for kind in ("a", "b"):
                g = Grp()
                g.h0, g.nh, g.kind = gi * nh, nh, kind
                base = (0 if kind == "a" else 4) + 2 * gi
                g.PS = PS.sub([base, base + 1])
                t = "%s%d_" % (kind, gi)
                g.W = {}
                for n, dt in (specA_big if kind == "a" else specB_big):
                    g.W[n] = TLV(parents[(kind, n)].t, gi * nh * 128, (gi + 1) * nh * 128, t + n)
                for n, wd in (smallA(nh) if kind == "a" else smallB(nh)):
                    g.W[n] = sb(t + n, [128, wd], F32)
                if kind == "a":
                    g.Cst, g.Cbf = g.W["Cst"], g.W["Cbf"]
                    g.nst = TLV(nstP.t, gi * nh, (gi + 1) * nh, t + "nst")
                    g.nbf = TLV(nbfP.t, gi * nh, (gi + 1) * nh, t + "nbf")
                    g.mprev = TLV(mprevP.t, gi * nh, (gi + 1) * nh, t + "mprev")
                else:
                    g.Sst, g.Sbf = g.W["Sst"], g.W["Sbf"]
                    g.W["dB"] = g.W["dG"]
                    g.W["gre"] = g.W["wT"]
                    g.W["osb"] = g.W["dG"]
                (self.ga if kind == "a" else self.gb).append(g)
        self.alt = dict(Cst=sb("alt_Cst", [128, 1024], F32), Sst=sb("alt_Sst", [128, 1024], F32),
                        nst=sb("alt_nst", [128, NH], F32), mprev=sb("alt_mprev", [128, NH], F32))
        self.gaS, self.gbS = Grp(), Grp()
        for g, kind, spec, small, grps in ((self.gaS, "a", specA_big, smallA, self.ga), (self.gbS, "b", specB_big, smallB, self.gb)):
            g.h0, g.nh, g.kind = 0, NH, kind
            g.PS = PS.sub([0, 1, 2, 3] if kind == "a" else [4, 5, 6, 7])
            g.W = {}
            for n, dt in spec:
                g.W[n] = TLP(parents[(kind, n)].t, [gg.W[n] for gg in grps])
            for n, wd in small(NH):
                g.W[n] = sb("S%s_%s" % (kind, n), [128, wd], F32)
            if kind == "a":
                g.Cst, g.Cbf = g.W["Cst"], g.W["Cbf"]
                g.nst = TLP(nstP.t, [gg.nst for gg in grps])
                g.nbf = TLP(nbfP.t, [gg.nbf for gg in grps])
                g.mprev = TLP(mprevP.t, [gg.mprev for gg in grps])
            else:
                g.Sst, g.Sbf = g.W["Sst"], g.W["Sbf"]
                g.W["dB"] = g.W["dG"]
                g.W["gre"] = g.W["wT"]
                g.W["osb"] = g.W["dG"]

    def mlstm_chunk(self, g, L, c0, q0, full, fm, tm):
        P, PS, C, W, cst = self.P, g.PS, self.C, g.W, self.cst
        bank = PS.banks
        h0, nh = g.h0, g.nh
        cn = str(L)
        GP = tm["GP"]
        lf = GP[0:L, 8 + h0:8 + h0 + nh]
        li = GP[0:L, h0:h0 + nh]
        HL = nh * L
        assert HL <= 512

        wide = nh * 128 > 512

        def m2():
            return PS.pair() if wide else PS.one()

        def f2(b_):
            return PS.f32(b_, 2) if wide else PS.f32(b_)

        def bl2(b_):
            return [bank[b_], bank[b_ + 1]] if wide else [bank[b_]]

        def hb2(b_, h_):
            return bank[b_ + (h_ * 128) // 512]
        identb, onesb = self.identb, self.onesb
        qaT, kaT, ka, va, oa = fm["qaT"], fm["kaT"], tm["ka"], tm["va"], tm["oa"]
        gc = slice(h0 * 128, (h0 + nh) * 128)

        def v3(ap2d):
            return ap2d.rearrange("p (h x) -> p h x", h=nh)

        def bcl(ap, n):
            return ap[:, :, None].to_broadcast([L, nh, n])
        bs = PS.one()
        _mm(P, PS.f32(bs)[0:L, 0:nh], C("uinc" + cn, L), lf, True, True, [cst, GP], [bank[bs]])
        _mm(P, PS.f32(bs)[:, nh:2 * nh], C("ones", L)[:, 0:128], lf, True, True, [cst, GP], [bank[bs]])
        sm1 = W["sm1"]
        _cp(P, "scalar", sm1[:, :], PS.f32(bs)[:, 0:2 * nh], [bank[bs]], [sm1])
        gg = W["g"]
        _tt(P, "vector", gg[0:L, :], li, sm1[0:L, 0:nh], ALU.subtract, [GP, sm1], [gg])
        dgx = W["dgx"]
        _tt(P, "gpsimd", v3(dgx[0:L, 0:HL]), C("ident", L)[:, None, 0:L].to_broadcast([L, nh, L]),
            bcl(gg[0:L, :], L), ALU.mult, [cst, gg], [dgx])
        br = PS.one()
        _mm(P, PS.f32(br)[0:L, 0:HL], C("ones", L)[:, 0:L], dgx[0:L, 0:HL], True, True, [cst, dgx], [bank[br]])
        R = W["R"]
        R3 = v3(R[0:L, 0:HL])
        _tt(P, "vector", R3, v3(PS.f32(br)[0:L, 0:HL]), C("neg" + cn, L)[:, None, :].to_broadcast([L, nh, L]),
            ALU.add, [bank[br], cst], [R])
        cm = W["cm"]
        _red(P, "vector", cm[0:L, :], R3, ALU.max, [R], [cm])
        _tt(P, "vector", R3, R3, bcl(cm[0:L, :], L), ALU.subtract, [R, cm], [R])
        _act(P, R[0:L, 0:HL], R[0:L, 0:HL], AF.Exp, [R], [R])
        if full:
            bq = PS.one()
            for h in range(nh):
                _mm(P, PS.f32(bq)[0:L, h * L:(h + 1) * L], qaT[:, h0 + h, c0:c0 + L], kaT[:, h0 + h, c0:c0 + L],
                    True, True, [qaT, kaT], [bank[bq]])
            sloc = W["sloc"]
            _tt(P, "vector", sloc[0:L, 0:HL], PS.f32(bq)[0:L, 0:HL], R[0:L, 0:HL], ALU.mult, [bank[bq], R], [sloc])
            rows = W["rows"]
            _red(P, "vector", rows[0:L, :], v3(sloc[0:L, 0:HL]), ALU.add, [sloc], [rows])
            bt = PS.one()
            for h in range(nh):
                _tr(P, PS.bf(bt)[0:L, h * L:(h + 1) * L], sloc[0:L, h * L:(h + 1) * L], identb[0:L, 0:L],
                    [sloc, identb], [bank[bt]])
            sTsb = W["sTsb"]
            _cp(P, "scalar", sTsb[0:L, 0:HL], PS.bf(bt)[0:L, 0:HL], [bank[bt]], [sTsb])
            bn = m2()
            for h in range(nh):
                _mm(P, f2(bn)[0:L, h * 128:(h + 1) * 128], sTsb[0:L, h * L:(h + 1) * L],
                    va[0:L, (h0 + h) * 128:(h0 + h + 1) * 128], True, True, [sTsb, va], [hb2(bn, h)])
            nl = W["nl"]
            _cp(P, "scalar", nl[0:L, :], f2(bn)[0:L, :], bl2(bn), [nl])
            gs = W["gs"]
            _tt(P, "gpsimd", gs[0:L, :], oa[0:L, gc], self.gA[0:L, gc], ALU.mult, [oa, self.gA], [gs])
        b2 = PS.one()
        _mm(P, PS.f32(b2)[0:L, 0:nh], C("sel" + cn, L)[:, 0:L], cm[0:L, :], True, True, [cst, cm], [bank[b2]])
        gl = W["gl"]
        _tt(P, "vector", gl[0:L, :], gg[0:L, :], PS.f32(b2)[0:L, 0:nh], ALU.subtract, [gg, bank[b2]], [gl])
        _act(P, gl[0:L, :], gl[0:L, :], AF.Exp, [gl], [gl])
        wlk = W["wlk"]
        _tt(P, "gpsimd", v3(wlk[0:L, :]), v3(ka[0:L, gc]), bcl(gl[0:L, :], 128), ALU.mult, [ka, gl], [wlk])
        bd = m2()
        for h in range(nh):
            _mm(P, f2(bd)[:, h * 128:(h + 1) * 128], wlk[0:L, h * 128:(h + 1) * 128],
                va[0:L, (h0 + h) * 128:(h0 + h + 1) * 128], True, True, [wlk, va], [hb2(bd, h)])
        dCs, dns = W["dCs"], W["dns"]
        _cp(P, "scalar", dCs[:, :], f2(bd)[:, :], bl2(bd), [dCs])
        b3 = PS.one()
        for h in range(nh):
            _mm(P, PS.f32(b3)[:, h:h + 1], wlk[0:L, h * 128:(h + 1) * 128], onesb[0:L, 0:1], True, True,
                [wlk, onesb], [bank[b3]])
        _cp(P, "vector", dns[:, :], PS.f32(b3)[:, 0:nh], [bank[b3]], [dns])
        mprev, Cst, Cbf, nst, nbf = g.mprev, g.Cst, g.Cbf, g.nst, g.nbf
        mx, t12, fd = W["mx"], W["t12"], W["fd"]
        _tt(P, "vector", mx[0:L, 0:nh], mprev[0:L, :], cm[0:L, :], ALU.max, [mprev, cm], [mx])
        _tt(P, "vector", t12[0:L, 0:nh], cm[0:L, :], mx[0:L, 0:nh], ALU.subtract, [cm, mx], [t12])
        _tt(P, "vector", t12[0:L, nh:2 * nh], mprev[0:L, :], mx[0:L, 0:nh], ALU.subtract, [mprev, mx, t12], [t12])
        _act(P, fd[0:L, :], t12[0:L, :], AF.Exp, [t12], [fd])
        _cp(P, "vector", mx[0:L, nh:2 * nh], fd[0:L, 0:nh], [fd, mx], [mx])
        if full:
            bc_ = m2()
            for h in range(nh):
                _mm(P, f2(bc_)[0:L, h * 128:(h + 1) * 128], qaT[:, h0 + h, c0:c0 + L], Cbf[:, h * 128:(h + 1) * 128],
                    True, True, [qaT, Cbf], [hb2(bc_, h)])
            mt, en, dd, d2, a12 = W["mt"], W["en"], W["dd"], W["d2"], W["a12"]
            h1, nl, ssq, hnb = W["h1"], W["nl"], W["ssq"], W["hnb"]
            _tt(P, "vector", mt[0:L, :], sm1[0:L, 0:nh], mx[0:L, 0:nh], ALU.add, [sm1, mx], [mt])
            _act(P, en[0:L, :], mt[0:L, :], AF.Exp, [mt], [en], scale=-1.0)
            _cp(P, "scalar", h1[0:L, :], f2(bc_)[0:L, :], bl2(bc_), [h1])
            b4 = PS.one()
            for h in range(nh):
                _mm(P, PS.f32(b4)[0:L, h:h + 1], qaT[:, h0 + h, c0:c0 + L], nbf[:, h:h + 1], True, True,
                    [qaT, nbf], [bank[b4]])
            _tt(P, "vector", dd[0:L, :], fd[0:L, nh:2 * nh], PS.f32(b4)[0:L, 0:nh], ALU.mult, [fd, bank[b4]], [dd])
            _tt(P, "vector", d2[0:L, :], fd[0:L, 0:nh], W["rows"][0:L, :], ALU.mult, [fd, W["rows"]], [d2])
            _tt(P, "vector", dd[0:L, :], dd[0:L, :], d2[0:L, :], ALU.add, [dd, d2], [dd])
            _act(P, dd[0:L, :], dd[0:L, :], AF.Abs, [dd], [dd])
            _tt(P, "vector", dd[0:L, :], dd[0:L, :], en[0:L, :], ALU.max, [dd, en], [dd])
            P.V(lambda e: e.reciprocal(out=dd[0:L, :], in_=dd[0:L, :]), [dd], [dd])
            _tt(P, "vector", a12[0:L, 0:nh], fd[0:L, nh:2 * nh], dd[0:L, :], ALU.mult, [fd, dd], [a12])
            _tt(P, "vector", a12[0:L, nh:2 * nh], fd[0:L, 0:nh], dd[0:L, :], ALU.mult, [fd, dd, a12], [a12])
            _tt(P, "vector", v3(h1[0:L, :]), v3(h1[0:L, :]), bcl(a12[0:L, 0:nh], 128), ALU.mult, [h1, a12], [h1])
            _tt(P, "gpsimd", v3(nl[0:L, :]), v3(nl[0:L, :]), bcl(a12[0:L, nh:2 * nh], 128), ALU.mult, [nl, a12], [nl])
            _tt(P, "vector", h1[0:L, :], h1[0:L, :], nl[0:L, :], ALU.add, [h1, nl], [h1])
            _act(P, nl[0:L, :], h1[0:L, :], AF.Square, [h1, nl], [nl])
            _red(P, "vector", ssq[0:L, :], v3(nl[0:L, :]), ALU.add, [nl], [ssq])
            _act(P, ssq[0:L, :], ssq[0:L, :], AF.Ln, [ssq], [ssq], bias=EPS, scale=1.0 / DH)
            _act(P, ssq[0:L, :], ssq[0:L, :], AF.Exp, [ssq], [ssq], scale=-0.5)
            _tt(P, "vector", v3(h1[0:L, :]), v3(h1[0:L, :]), bcl(ssq[0:L, :], 128), ALU.mult, [h1, ssq], [h1])
            _tt(P, "gpsimd", hnb[0:L, :], h1[0:L, :], W["gs"][0:L, :], ALU.mult, [h1, W["gs"]], [hnb])
            bh = PS.one()
            for h in range(nh):
                _tr(P, PS.bf(bh)[:, h * L:(h + 1) * L], hnb[0:L, h * 128:(h + 1) * 128], identb[0:L, 0:L],
                    [hnb, identb], [bank[bh]])
            _cp(P, "scalar", self.mixT[:, h0:h0 + nh, q0:q0 + L], PS.bf(bh)[:, 0:HL].rearrange("p (h t) -> p h t", h=nh),
                [bank[bh]], [self.mixT])
        b5 = PS.one()
        _mm(P, PS.f32(b5)[:, 0:2 * nh], C("sel" + cn, L)[:, 0:128], mx[0:L, 0:2 * nh], True, True, [cst, mx], [bank[b5]])
        sm2, dC = W["sm2"], W["dC"]
        _cp(P, "scalar", sm2[:, :], PS.f32(b5)[:, 0:2 * nh], [bank[b5]], [sm2])
        _tt(P, "vector", dC[:, :], mprev[:, :], sm2[:, 0:nh], ALU.subtract, [mprev, sm2], [dC])
        _act(P, dC[:, :], dC[:, :], AF.Exp, [dC], [dC])

        def bc128(ap):
            return ap[:, :, None].to_broadcast([128, nh, 128])
        _tt(P, "vector", v3(Cst[:, :]), v3(Cst[:, :]), bc128(dC[:, :]), ALU.mult, [Cst, dC], [Cst])
        _tt(P, "gpsimd", v3(dCs[:, :]), v3(dCs[:, :]), bc128(sm2[:, nh:2 * nh]), ALU.mult, [dCs, sm2], [dCs])
        _tt(P, "vector", Cst[:, :], Cst[:, :], dCs[:, :], ALU.add, [Cst, dCs], [Cst])
        _tt(P, "vector", nst[:, :], nst[:, :], dC[:, :], ALU.mult, [nst, dC], [nst])
        _tt(P, "vector", dns[:, :], dns[:, :], sm2[:, nh:2 * nh], ALU.mult, [dns, sm2], [dns])
        _tt(P, "vector", nst[:, :], nst[:, :], dns[:, :], ALU.add, [nst, dns], [nst])
        _cp(P, "scalar", Cbf[:, :], Cst[:, :], [Cst], [Cbf])
        _cp(P, "vector", nbf[:, :], nst[:, :], [nst], [nbf])
        _tt(P, "vector", mprev[:, :], sm1[:, nh:2 * nh], sm2[:, 0:nh], ALU.add, [sm1, sm2, mprev], [mprev])

    def gdn_chunk(self, g, L, c0, q0, full, fm, tm):
        P, PS, C, W, cst = self.P, g.PS, self.C, g.W, self.cst
        bank = PS.banks
        h0, nh = g.h0, g.nh
        cn = str(L)
        GP = tm["GP"]
        zb = tm["zb"]
        logg = GP[0:L, 16 + h0:16 + h0 + nh]
        beta = GP[0:L, 24 + h0:24 + h0 + nh]
        HL = nh * L
        assert HL <= 512

        wide = nh * 128 > 512

        def m2():
            return PS.pair() if wide else PS.one()

        def f2(b_):
            return PS.f32(b_, 2) if wide else PS.f32(b_)

        def bl2(b_):
            return [bank[b_], bank[b_ + 1]] if wide else [bank[b_]]

        def hb2(b_, h_):
            return bank[b_ + (h_ * 128) // 512]
        identb = self.identb
        qpost, kpost, vpost = fm["q"], fm["k"], fm["v"]
        Sst, Sbf = g.Sst, g.Sbf
        gc = slice(h0 * 128, (h0 + nh) * 128)

        def v3(ap2d):
            return ap2d.rearrange("p (h x) -> p h x", h=nh)

        def bcl(ap, n):
            return ap[:, :, None].to_broadcast([L, nh, n])
        identL = C("ident", L)[:, None, 0:L].to_broadcast([L, nh, L])
        bs = PS.one()
        _mm(P, PS.f32(bs)[0:L, 0:nh], C("uinc" + cn, L), logg, True, True, [cst, GP], [bank[bs]])
        _mm(P, PS.f32(bs)[:, nh:2 * nh], C("ones", L)[:, 0:128], logg, True, True, [cst, GP], [bank[bs]])
        gsm = W["gsm"]
        _cp(P, "scalar", gsm[:, :], PS.f32(bs)[:, 0:2 * nh], [bank[bs]], [gsm])
        Gt = gsm[0:L, 0:nh]
        dG = W["dG"]
        _tt(P, "gpsimd", v3(dG[0:L, 0:HL]), identL, bcl(Gt, L), ALU.mult, [cst, gsm], [dG])
        bg = PS.one()
        _mm(P, PS.f32(bg)[:, 0:HL], C("ones", L)[:, 0:128], dG[0:L, 0:HL], True, True, [cst, dG], [bank[bg]])
        NZ = W["NZ"]
        _tt(P, "vector", v3(NZ[0:L, 0:HL]), v3(PS.f32(bg)[0:L, 0:HL]), bcl(Gt, L), ALU.subtract, [bank[bg], gsm], [NZ])
        _ts(P, "vector", NZ[0:L, 0:HL], NZ[0:L, 0:HL], 0.0, ALU.min, [NZ], [NZ])
        _act(P, NZ[0:L, 0:HL], NZ[0:L, 0:HL], AF.Exp, [NZ], [NZ])
        if full:
            gre, qg = W["gre"], W["qg"]
            _act(P, gre[:, 0:HL], PS.f32(bg)[:, 0:HL], AF.Exp, [bank[bg]], [gre])
            _tt(P, "vector", v3(qg[:, 0:HL]), qpost[:, h0:h0 + nh, c0:c0 + L], v3(gre[:, 0:HL]), ALU.mult,
                [qpost, gre], [qg])
        dB = W["dB"]
        _tt(P, "gpsimd", v3(dB[0:L, 0:HL]), identL, bcl(beta, L), ALU.mult, [cst, GP, dB], [dB])
        bb_ = PS.one()
        _mm(P, PS.f32(bb_)[0:L, 0:HL], C("ones", L)[:, 0:L], dB[0:L, 0:HL], True, True, [cst, dB], [bank[bb_]])
        wT = W["wT"]
        _tt(P, "gpsimd", v3(wT[0:L, 0:HL]), v3(NZ[0:L, 0:HL]),
            C("ustr" + cn, L)[:, None, :].to_broadcast([L, nh, L]), ALU.mult, [NZ, cst, wT], [wT])
        _tt(P, "vector", wT[0:L, 0:HL], wT[0:L, 0:HL], PS.f32(bb_)[0:L, 0:HL], ALU.mult, [wT, bank[bb_]], [wT])
        if full:
            dTi = W["dTi"]
            _tt(P, "gpsimd", v3(dTi[0:L, 0:HL]), v3(NZ[0:L, 0:HL]),
                C("uinc" + cn, L)[:, None, :].to_broadcast([L, nh, L]), ALU.mult, [NZ, cst], [dTi])
        bk = PS.one()
        for h in range(nh):
            _mm(P, PS.f32(bk)[0:L, h * L:(h + 1) * L], kpost[:, h0 + h, c0:c0 + L], kpost[:, h0 + h, c0:c0 + L],
                True, True, [kpost], [bank[bk]])
        blocked = (L == 128)
        Pc, PTc, Pn_, PTn_ = W["P0"], W["PT0"], W["P1"], W["PT1"]
        Mf = W["Mf"] if blocked else Pc
        _stt(P, "vector", Mf[0:L, 0:HL], PS.f32(bk)[0:L, 0:HL], -1.0, wT[0:L, 0:HL], ALU.mult, ALU.mult,
             [bank[bk], wT], [Mf])
        if full:
            bq = PS.one()
            for h in range(nh):
                _mm(P, PS.f32(bq)[0:L, h * L:(h + 1) * L], kpost[:, h0 + h, c0:c0 + L], qpost[:, h0 + h, c0:c0 + L],
                    True, True, [kpost, qpost], [bank[bq]])
            qkT = W["qkT"]
            _tt(P, "vector", qkT[0:L, 0:HL], PS.f32(bq)[0:L, 0:HL], W["dTi"][0:L, 0:HL], ALU.mult,
                [bank[bq], W["dTi"]], [qkT])
        bt = PS.one()
        for h in range(nh):
            _tr(P, PS.bf(bt)[0:L, h * L:(h + 1) * L], Mf[0:L, h * L:(h + 1) * L], identb[0:L, 0:L],
                [Mf, identb], [bank[bt]])
        if blocked:
            bdm = C("bd32", L)[:, None, :].to_broadcast([L, nh, L])
            MoT, MoT2 = W["MoT"], W["MoT2"]
            _tt(P, "gpsimd", v3(Pc[0:L, 0:HL]), v3(Mf[0:L, 0:HL]), bdm, ALU.mult, [Mf, cst], [Pc])
            _tt(P, "vector", v3(PTc[0:L, 0:HL]), v3(PS.bf(bt)[0:L, 0:HL]), bdm, ALU.mult, [bank[bt], cst], [PTc])
            _tt(P, "vector", v3(MoT[0:L, 0:HL]), v3(PS.bf(bt)[0:L, 0:HL]),
                C("o64", L)[:, None, :].to_broadcast([L, nh, L]), ALU.mult, [bank[bt], cst], [MoT])
            _tt(P, "vector", v3(MoT2[0:L, 0:HL]), v3(PS.bf(bt)[0:L, 0:HL]),
                C("offL128", L)[:, None, :].to_broadcast([L, nh, L]), ALU.mult, [bank[bt], cst], [MoT2])
        else:
            _cp(P, "scalar", PTc[0:L, 0:HL], PS.bf(bt)[0:L, 0:HL], [bank[bt]], [PTc])
        Tacc = W["Tacc"]
        _tt(P, "gpsimd", v3(Tacc[0:L, 0:HL]), v3(Pc[0:L, 0:HL]), identL, ALU.add, [Pc, cst], [Tacc])
        nlev = 4 if blocked else NLEV[L]
        for lev in range(1, nlev + 1):
            b1 = PS.one()
            for h in range(nh):
                sl = slice(h * L, (h + 1) * L)
                _mm(P, PS.f32(b1)[0:L, sl], Pc[0:L, sl], PTc[0:L, sl], True, True, [Pc, PTc], [bank[b1]])
            _cp(P, "scalar", PTn_[0:L, 0:HL], PS.f32(b1)[0:L, 0:HL], [bank[b1]], [PTn_])
            if lev < nlev:
                b2 = PS.one()
                for h in range(nh):
                    sl = slice(h * L, (h + 1) * L)
                    _mm(P, PS.f32(b2)[0:L, sl], PTc[0:L, sl], Pc[0:L, sl], True, True, [Pc, PTc], [bank[b2]])
                _cp(P, "vector", Pn_[0:L, 0:HL], PS.f32(b2)[0:L, 0:HL], [bank[b2]], [Pn_])
            b3 = PS.one()
            for h in range(nh):
                sl = slice(h * L, (h + 1) * L)
                _mm(P, PS.f32(b3)[0:L, sl], PTn_[0:L, sl], Tacc[0:L, sl], True, True, [PTn_, Tacc], [bank[b3]])
            _tt(P, "vector", Tacc[0:L, 0:HL], Tacc[0:L, 0:HL], PS.f32(b3)[0:L, 0:HL], ALU.add, [Tacc, bank[b3]], [Tacc])
            Pc, PTc, Pn_, PTn_ = Pn_, PTn_, Pc, PTc
        if blocked:
            TbT, Xt = Pn_, PTn_
            for Mo in (W["MoT"], W["MoT2"]):
                b7 = PS.one()
                for h in range(nh):
                    sl = slice(h * L, (h + 1) * L)
                    _tr(P, PS.bf(b7)[0:L, sl], Tacc[0:L, sl], identb[0:L, 0:L], [Tacc, identb], [bank[b7]])
                _cp(P, "scalar", TbT[0:L, 0:HL], PS.bf(b7)[0:L, 0:HL], [bank[b7]], [TbT])
                b8 = PS.one()
                for h in range(nh):
                    sl = slice(h * L, (h + 1) * L)
                    _mm(P, PS.f32(b8)[0:L, sl], Mo[0:L, sl], Tacc[0:L, sl], True, True, [Mo, Tacc], [bank[b8]])
                _cp(P, "scalar", Xt[0:L, 0:HL], PS.f32(b8)[0:L, 0:HL], [bank[b8]], [Xt])
                b9 = PS.one()
                for h in range(nh):
                    sl = slice(h * L, (h + 1) * L)
                    _mm(P, PS.f32(b9)[0:L, sl], TbT[0:L, sl], Xt[0:L, sl], True, True, [TbT, Xt], [bank[b9]])
                _tt(P, "vector", Tacc[0:L, 0:HL], Tacc[0:L, 0:HL], PS.f32(b9)[0:L, 0:HL], ALU.add, [Tacc, bank[b9]], [Tacc])
        gsl, gLe = W["gsmall"], W["gLe"]
        _act(P, gsl[0:L, 0:nh], Gt, AF.Exp, [gsm], [gsl])
        _tt(P, "vector", gsl[0:L, 0:nh], gsl[0:L, 0:nh], beta, ALU.mult, [gsl, GP], [gsl])
        _tt(P, "vector", gsl[0:L, nh:2 * nh], gsm[0:L, nh:2 * nh], Gt, ALU.subtract, [gsm, gsl], [gsl])
        _act(P, gsl[0:L, nh:2 * nh], gsl[0:L, nh:2 * nh], AF.Exp, [gsl], [gsl])
        _act(P, gLe[:, :], gsm[:, nh:2 * nh], AF.Exp, [gsm], [gLe])
        kbg, kdec, vbt = W["kbg"], W["kdec"], W["vbt"]
        bkt = PS.one()
        for h in range(nh):
            _tr(P, PS.bf(bkt)[0:L, h * 128:(h + 1) * 128], kpost[:, h0 + h, c0:c0 + L], identb[:, :], [kpost, identb],
                [bank[bkt]])
        _tt(P, "vector", v3(kbg[0:L, :]), v3(PS.bf(bkt)[0:L, 0:nh * 128]), bcl(gsl[0:L, 0:nh], 128), ALU.mult,
            [bank[bkt], gsl], [kbg])
        _tt(P, "vector", v3(kdec[0:L, :]), v3(PS.bf(bkt)[0:L, 0:nh * 128]), bcl(gsl[0:L, nh:2 * nh], 128), ALU.mult,
            [bank[bkt], gsl], [kdec])
        bvt = PS.one()
        for h in range(nh):
            _tr(P, PS.bf(bvt)[0:L, h * 128:(h + 1) * 128], vpost[:, h0 + h, c0:c0 + L], identb[:, :], [vpost, identb],
                [bank[bvt]])
        _tt(P, "vector", v3(vbt[0:L, :]), v3(PS.bf(bvt)[0:L, 0:nh * 128]), bcl(beta, 128), ALU.mult,
            [bank[bvt], GP], [vbt])
        bw = PS.one()
        for h in range(nh):
            _mm(P, PS.f32(bw)[:, h * L:(h + 1) * L], kbg[0:L, h * 128:(h + 1) * 128], Tacc[0:L, h * L:(h + 1) * L],
                True, True, [kbg, Tacc], [bank[bw]])
        WT = W["WT"]
        _cp(P, "scalar", WT[:, 0:HL], PS.f32(bw)[:, 0:HL], [bank[bw]], [WT])
        bu = m2()
        for h in range(nh):
            _mm(P, f2(bu)[0:L, h * 128:(h + 1) * 128], Tacc[0:L, h * L:(h + 1) * L],
                vbt[0:L, h * 128:(h + 1) * 128], True, True, [Tacc, vbt], [hb2(bu, h)])
        U0 = W["U0"]
        _cp(P, "scalar", U0[0:L, :], f2(bu)[0:L, :], bl2(bu), [U0])
        if full:
            gz = W["gz"]
            _tt(P, "gpsimd", gz[0:L, :], zb[0:L, gc], self.gB[0:L, gc], ALU.mult, [zb, self.gB], [gz])
        bpu = m2()
        for h in range(nh):
            _mm(P, f2(bpu)[0:L, h * 128:(h + 1) * 128], WT[:, h * L:(h + 1) * L], Sbf[:, h * 128:(h + 1) * 128],
                True, True, [WT, Sbf], [hb2(bpu, h)])
        u = W["u"]
        _tt(P, "vector", u[0:L, :], U0[0:L, :], f2(bpu)[0:L, :], ALU.subtract, [U0] + bl2(bpu), [u])
        if full:
            bo = m2()
            for h in range(nh):
                o_ = f2(bo)[0:L, h * 128:(h + 1) * 128]
                _mm(P, o_, W["qg"][:, h * L:(h + 1) * L], Sbf[:, h * 128:(h + 1) * 128], True, False,
                    [W["qg"], Sbf], [hb2(bo, h)])
                _mm(P, o_, W["qkT"][0:L, h * L:(h + 1) * L], u[0:L, h * 128:(h + 1) * 128], False, True,
                    [W["qkT"], u], [hb2(bo, h)])
            osb = W["osb"]
            _cp(P, "scalar", osb[0:L, :], f2(bo)[0:L, :], bl2(bo) + [osb], [osb])
        bss = m2()
        for h in range(nh):
            _mm(P, f2(bss)[:, h * 128:(h + 1) * 128], kdec[0:L, h * 128:(h + 1) * 128],
                u[0:L, h * 128:(h + 1) * 128], True, True, [kdec, u], [hb2(bss, h)])
        _tt(P, "vector", v3(Sst[:, :]), v3(Sst[:, :]), gLe[:, :, None].to_broadcast([128, nh, 128]),
            ALU.mult, [Sst, gLe], [Sst])
        _tt(P, "vector", Sst[:, :], Sst[:, :], f2(bss)[:, :], ALU.add, [Sst] + bl2(bss), [Sst])
        _cp(P, "scalar", Sbf[:, :], Sst[:, :], [Sst], [Sbf])
        if full:
            ob, ssq = W["ob"], W["ssq2"]
            _act(P, U0[0:L, :], osb[0:L, :], AF.Square, [osb, U0], [U0])
            _red(P, "vector", ssq[0:L, :], v3(U0[0:L, :]), ALU.add, [U0], [ssq])
            _act(P, ssq[0:L, :], ssq[0:L, :], AF.Ln, [ssq], [ssq], bias=EPS, scale=1.0 / DH)
            _act(P, ssq[0:L, :], ssq[0:L, :], AF.Exp, [ssq], [ssq], scale=-0.5)
            _tt(P, "vector", v3(osb[0:L, :]), v3(osb[0:L, :]), bcl(ssq[0:L, :], 128), ALU.mult, [osb, ssq], [osb])
            _tt(P, "gpsimd", ob[0:L, :], osb[0:L, :], W["gz"][0:L, :], ALU.mult, [osb, W["gz"]], [ob])
            bh = PS.one()
            for h in range(nh):
                _tr(P, PS.bf(bh)[:, h * L:(h + 1) * L], ob[0:L, h * 128:(h + 1) * 128], identb[0:L, 0:L],
                    [ob, identb], [bank[bh]])
            _cp(P, "scalar", self.mixT[:, 8 + h0:8 + h0 + nh, q0:q0 + L],
                PS.bf(bh)[:, 0:HL].rearrange("p (h t) -> p h t", h=nh), [bank[bh]], [self.mixT])

    def _refresh_bf(self):
        P, a, b = self.P, self.gaS, self.gbS
        _cp(P, "scalar", a.Cbf[:, :], a.Cst[:, :], [a.Cst], [a.Cbf])
        _cp(P, "vector", a.nbf[:, :], a.nst[:, :], [a.nst], [a.nbf])
        _cp(P, "scalar", b.Sbf[:, :], b.Sst[:, :], [b.Sst], [b.Sbf])

    def _state_tiles(self):
        return [self.gaS.Cst, self.gaS.nst, self.gaS.mprev, self.gbS.Sst]

    def _use_set(self, i):
        a, b = self.gaS, self.gbS
        if not hasattr(self, "_set0"):
            self._set0 = dict(Cst=a.Cst, Sst=b.Sst, nst=a.nst, mprev=a.mprev)
        st = self._set0 if i == 0 else self.alt
        a.Cst, a.nst, a.mprev, b.Sst = st["Cst"], st["nst"], st["mprev"], st["Sst"]
        a.W["Cst"], b.W["Sst"] = st["Cst"], st["Sst"]

    def load_state(self, j):
        P, io, a, b = self.P, self.io, self.gaS, self.gbS
        pairs = [
            (a.Cst[:, :].rearrange("p (h x) -> p h x", h=NH), io["sC"][j].rearrange("h d e -> d h e")),
            (b.Sst[:, :].rearrange("p (h x) -> p h x", h=NH), io["sS"][j].rearrange("h d e -> d h e")),
            (a.nst[:, :], io["sn"][j]),
            (a.mprev[:, :], io["sm"][j:j + 1, :].partition_broadcast(128)),
        ]
        P.dma("sync", "s2ld", [(lambda e, o=o, i=i: e.dma_start(out=o, in_=i)) for o, i in pairs], w=self._state_tiles())

    def store_state(self, dC, dS, dn, dm):
        P, a, b = self.P, self.gaS, self.gbS
        pairs = [
            (dC.rearrange("h d e -> d h e"), a.Cst[:, :].rearrange("p (h x) -> p h x", h=NH)),
            (dS.rearrange("h d e -> d h e"), b.Sst[:, :].rearrange("p (h x) -> p h x", h=NH)),
            (dn, a.nst[:, :]),
            (dm, a.mprev[0:1, :]),
        ]
        P.dma("sync", "s2st", [(lambda e, o=o, i=i: e.dma_start(out=o, in_=i)) for o, i in pairs], r=self._state_tiles())

    def tm_load(self, L, r0, full, tm):
        P, S = self.P, self.S
        fns = [
            lambda e: e.dma_start(out=tm["ka"][0:L, :], in_=S["ka"][r0:r0 + L, :]),
            lambda e: e.dma_start(out=tm["va"][0:L, :], in_=S["va"][r0:r0 + L, :]),
            lambda e: e.dma_start(out=tm["GP"][0:L, :], in_=S["gt"][r0:r0 + L, :]),
        ]
        wl_ = [tm["ka"], tm["va"], tm["GP"]]
        if full:
            rq = r0 - T0
            fns += [
                lambda e: e.dma_start(out=tm["oa"][0:L, :], in_=S["oa"][rq:rq + L, :]),
                lambda e: e.dma_start(out=tm["zb"][0:L, :], in_=S["zb"][rq:rq + L, :]),
            ]
            wl_ += [tm["oa"], tm["zb"]]
        P.dma("sync", "s2t%d" % self.tm.index(tm), fns, w=wl_)

    def fm_load(self, sc, fm):
        P, S = self.P, self.S
        t0 = sc * 128
        full = sc >= 8
        fns = [
            lambda e: e.dma_start(out=fm["k"][:, :, :], in_=S["kb"][:, :, t0:t0 + 128].rearrange("h d t -> d h t")),
            lambda e: e.dma_start(out=fm["v"][:, :, :], in_=S["vb"][:, :, t0:t0 + 128].rearrange("h d t -> d h t")),
        ]
        wl_ = [fm["k"], fm["v"]]
        if full:
            tq = t0 - T0
            fns += [
                lambda e: e.dma_start(out=fm["q"][:, :, :], in_=S["qb"][:, :, t0:t0 + 128].rearrange("h d t -> d h t")),
                lambda e: e.dma_start(out=fm["qaT"][:, :, :], in_=S["qa"][:, :, tq:tq + 128].rearrange("h d t -> d h t")),
                lambda e: e.dma_start(out=fm["kaT"][:, :, :], in_=S["kaT"][:, :, tq:tq + 128].rearrange("h d t -> d h t")),
            ]
            wl_ += [fm["q"], fm["qaT"], fm["kaT"]]
        P.dma("sync", "s2f%d" % self.fm.index(fm), fns, w=wl_)

    def run(self):
        P, S, io = self.P, self.S, self.io
        for t in self._state_tiles():
            _ms(P, "gpsimd", t[:, :], 0.0, [t])
        self._refresh_bf()
        chunks = []
        for sc in range(NSUB):
            sample = sc == NSUB - 1
            L = LS if sample else LP
            for ch in range(128 // L):
                chunks.append((sc, ch, L, sample))
        self.fm_load(0, self.fm[0])
        self.tm_load(chunks[0][2], 0, False, self.tm[0])
        for idx, (sc, ch, L, sample) in enumerate(chunks):
            full = sc >= 8
            c0 = ch * L
            r0 = sc * 128 + c0
            q0 = r0 - T0
            fm, tm = self.fm[sc % 2], self.tm[idx % 2]
            if idx + 1 < len(chunks):
                nsc, nch, nL, _ = chunks[idx + 1]
                if nsc != sc:
                    self.fm_load(nsc, self.fm[nsc % 2])
                self.tm_load(nL, nsc * 128 + nch * nL, nsc >= 8, self.tm[(idx + 1) % 2])
            if sample:
                if ch == 0:
                    self._use_set(0)
                    self.load_state(0)
                if ch + 1 < 128 // L:
                    self._use_set((ch + 1) % 2)
                    self.load_state(ch + 1)
                self._use_set(ch % 2)
                self._refresh_bf()
            lists = []
            for g in ([self.gaS] if sample else self.ga):
                P.capture()
                self.mlstm_chunk(g, L, c0, q0, full, fm, tm)
                lists.append(P.end_capture())
            for g in ([self.gbS] if sample else self.gb):
                P.capture()
                self.gdn_chunk(g, L, c0, q0, full, fm, tm)
                lists.append(P.end_capture())
            P.replay(lists)
            if sample:
                self.store_state(io["oC"][ch], io["oS"][ch], io["on"][ch], io["om"][ch:ch + 1, :])
            if sc == 7 and ch == 128 // L - 1:
                for t in self._state_tiles():
                    _ts(P, "vector", t[:, :], t[:, :], self.flag[:, 0:1], ALU.mult, [t, self.flag], [t])
                self._refresh_bf()
            if sc == 15 and ch == 128 // L - 1:
                self.store_state(io["pC"], io["pS"], io["pn"], io["pm"])


def _phase3(B, P, PS, cst, C, identb, mixT, x_d, norms_d, modd, wout_d, wup_d, wdn_d, y_d):
    nc = B.nc
    bank = PS.banks
    X = Ctx(nc)
    x1 = X.sb("x1", [128, NSUBQ, D], F32)
    x1s = [TL(x1.t, "x1_%d" % i) for i in range(NSUBQ)]
    wr = [X.sb("p3_w%d" % i, [128, 16, 512], BF16) for i in range(3)]
    mt0 = X.sb("p3_mt0", [128, D], F32)
    mt1 = X.sb("p3_mt1", [128, D], F32)
    st = [X.sb("p3_st%d" % i, [128, 4], F32) for i in range(2)]
    wi = [0]

    def wslot():
        s = wr[wi[0] % 3]
        k = "wr%d" % (wi[0] % 3)
        wi[0] += 1
        return s, k

    def mtile(i):
        return mt0 if i < 8 else mt1

    for i in range(NSUBQ):
        _ld(P, "sync", "p3x%d" % (i % 3), x1[:, i, :], x_d[T0 + i * 128:T0 + (i + 1) * 128, :], w=[x1s[i]])

    XA = Ctx(nc)
    tmpf = XA.sb("p3_tmp", [128, D], F32)
    hb = XA.sb("p3_hb", [128, D], BF16)
    sgA = [XA.sb("p3_sgA%d" % i, [128, 512], F32) for i in range(2)]
    _mod_tiles(P, modd, 4096, mt0, mt1, "p3G")
    ne = 0
    for cb in range(4):
        slot, key = wslot()
        _ld(P, "gpsimd", key, slot[:], wout_d[:, cb * 512:(cb + 1) * 512].rearrange("(k p) c -> p k c", p=128), w=[slot])
        for i in range(NSUBQ):
            bk = PS.one()
            for k in range(16):
                _mm(P, PS.f32(bk)[:, :], mixT[:, k, i * 128:(i + 1) * 128], slot[:, k, :], k == 0, k == 15,
                    [mixT, slot], [bank[bk]])
            sg = sgA[ne % 2]
            ne += 1
            _tt(P, "vector", sg[:], PS.f32(bk)[:, :], mtile(i)[:, cb * 512:(cb + 1) * 512], ALU.mult,
                [bank[bk], mtile(i)], [sg])
            _tt(P, "vector", x1[:, i, cb * 512:(cb + 1) * 512], x1[:, i, cb * 512:(cb + 1) * 512], sg[:], ALU.add,
                [x1s[i], sg], [x1s[i]])

    def norm_tiles(col_sc, col_sh, nrow, first):
        _ld(P, "sync", "p3n", tmpf[:], norms_d[nrow:nrow + 1, :].partition_broadcast(128), w=[tmpf])
        if first:
            _ld(P, "sync", "p3m0", mt0[:], modd[0:1, col_sc:col_sc + D].partition_broadcast(128), w=[mt0])
            _ld(P, "sync", "p3m1", mt1[:], modd[0:1, col_sh:col_sh + D].partition_broadcast(128), w=[mt1])
        else:
            for t, col, key in ((mt0, col_sc, "p3m0"), (mt1, col_sh, "p3m1")):
                fns = []
                for b in range(NSEQ):
                    fns.append(lambda e, b=b, t=t, col=col: e.dma_start(
                        out=t[8 * b:8 * b + 8, :], in_=modd[1 + b:2 + b, col:col + D].partition_broadcast(8)))
                P.dma("sync", key, fns, w=[t])
        _stt(P, "vector", mt0[:], mt0[:], 1.0, tmpf[:], ALU.add, ALU.mult, [mt0, tmpf], [mt0])

    def norm_sub(i, out_ap, out_tl, junk):
        s_ = st[i % 2]
        _act(P, junk[:], x1[:, i, :], AF.Square, [x1s[i]], [junk, s_], accum=s_[:, 0:1])
        _act(P, s_[:, 1:2], s_[:, 0:1], AF.Ln, [s_], [s_], bias=EPS, scale=1.0 / D)
        _act(P, s_[:, 2:3], s_[:, 1:2], AF.Exp, [s_], [s_], scale=-0.5)
        _stt(P, "vector", tmpf[:], x1[:, i, :], s_[:, 2:3], mt0[:], ALU.mult, ALU.mult, [x1s[i], s_, mt0], [tmpf])
        _tt(P, "vector", out_ap, tmpf[:], mt1[:], ALU.add, [tmpf, mt1], [out_tl])

    h2T = mixT
    nitems = []
    for i in range(NSUBQ):
        P.capture()
        if i == 0:
            norm_tiles(8192, 6144, 1, True)
        if i == 8:
            norm_tiles(8192, 6144, 1, False)
        norm_sub(i, hb[:], hb, hb)
        for half in range(2):
            bk = PS.one()
            for kk in range(8):
                k = half * 8 + kk
                _tr(P, PS.bf(bk)[:, kk * 128:(kk + 1) * 128], hb[:, k * 128:(k + 1) * 128], identb[:],
                    [hb, identb], [bank[bk]])
            _cp(P, "scalar" if half else "vector", h2T[:, half * 8:(half + 1) * 8, i * 128:(i + 1) * 128],
                PS.bf(bk).rearrange("p (k t) -> p k t", k=8), [bank[bk]], [h2T])
        nitems.append(P.end_capture())
    P.replay_pipe(nitems, 3, burst=1)
    P.barrier()
    P.flush()
    XA.close()

    XB = Ctx(nc)
    actT = XB.sb("actT", [128, 8, NQ], BF16)
    sgB = [XB.sb("p3_sgB%d" % i, [128, 512], F32) for i in range(2)]
    rl = [XB.sb("p3_rl%d" % i, [128, 512], F32) for i in range(2)]
    _mod_tiles(P, modd, 10240, mt0, mt1, "p3G")
    ttiles = [(0, 512), (512, 512), (1024, 128)]
    ne = 0
    nr = 0
    for fb in range(8):
        for half in range(2):
            slot, key = wslot()
            c0 = fb * 1024 + half * 512
            _ld(P, "gpsimd", key, slot[:], wup_d[:, c0:c0 + 512].rearrange("(k p) c -> p k c", p=128), w=[slot])
            for sbk in range(4):
                for (t0, nt) in ttiles:
                    bk = PS.one()
                    for k in range(16):
                        _mm(P, PS.f32(bk)[:, 0:nt], slot[:, k, sbk * 128:(sbk + 1) * 128], h2T[:, k, t0:t0 + nt],
                            k == 0, k == 15, [slot, h2T], [bank[bk]])
                    r_ = rl[nr % 2]
                    nr += 1
                    _act(P, r_[:, 0:nt], PS.f32(bk)[:, 0:nt], AF.Relu, [bank[bk]], [r_])
                    _tt(P, "vector", actT[:, half * 4 + sbk, t0:t0 + nt], r_[:, 0:nt], r_[:, 0:nt], ALU.mult,
                        [r_], [actT])
        sd = []
        for half in range(2):
            slot, key = wslot()
            r0 = fb * 1024 + half * 512
            sv = slot[:, :, :].rearrange("p k c -> p (k c)").rearrange("p (s c) -> p s c", s=4)
            _ld(P, "gpsimd", key, sv, wdn_d[r0:r0 + 512, :].rearrange("(s p) c -> p s c", p=128), w=[slot])
            sd.append((slot, sv))
        for i in range(NSUBQ):
            for cb in range(4):
                bk = PS.one()
                for s8 in range(8):
                    slot, sv = sd[s8 // 4]
                    _mm(P, PS.f32(bk)[:, :], actT[:, s8, i * 128:(i + 1) * 128], sv[:, s8 % 4, cb * 512:(cb + 1) * 512],
                        s8 == 0, s8 == 7, [actT, slot], [bank[bk]])
                sg = sgB[ne % 2]
                ne += 1
                _tt(P, "vector", sg[:], PS.f32(bk)[:, :], mtile(i)[:, cb * 512:(cb + 1) * 512], ALU.mult,
                    [bank[bk], mtile(i)], [sg])
                _tt(P, "vector", x1[:, i, cb * 512:(cb + 1) * 512], x1[:, i, cb * 512:(cb + 1) * 512], sg[:], ALU.add,
                    [x1s[i], sg], [x1s[i]])
    P.barrier()
    P.flush()
    XB.close()

    XC = Ctx(nc)
    tmpf = XC.sb("p3c_tmp", [128, D], F32)
    yo = [XC.sb("p3c_y%d" % i, [128, D], F32) for i in range(2)]
    junk = XC.sb("p3c_junk", [128, D], BF16)
    nitems = []
    for i in range(NSUBQ):
        P.capture()
        if i == 0:
            norm_tiles(14336, 12288, 2, True)
        if i == 8:
            norm_tiles(14336, 12288, 2, False)
        y_ = yo[i % 2]
        norm_sub(i, y_[:], y_, junk)
        _ld(P, "sync", "p3y%d" % (i % 2), y_d[i * 128:(i + 1) * 128, :], y_[:], r=[y_])
        nitems.append(P.end_capture())
    P.replay_pipe(nitems, 3, burst=1)
    P.barrier()
    P.flush()
    XC.close()
    X.close()


def kernel(**inputs):
    inp = {k: np.asarray(v) for k, v in inputs.items()}
    B = build(debug=False)
    sh = _shared_inputs(inp)
    in_maps = []
    for c in range(8):
        m = _core_inputs(inp, c, sh)
        in_maps.append({k: v for k, v in m.items() if k in B.ins})
    res = run_bass_kernel_spmd(B.nc, in_maps, core_ids=list(range(8)))
    r = [{k: np.asarray(v) for k, v in rr.items()} for rr in res.results]

    y_prompt = np.empty((4, 2048, D), np.float32)
    y_sample = np.empty((128, LS, D), np.float32)
    pC = np.empty((1, 4, NH, DH, DH), np.float32)
    pn = np.empty((1, 4, NH, DH), np.float32)
    pm = np.empty((1, 4, NH), np.float32)
    pS = np.empty((1, 4, NH, DH, DH), np.float32)
    pconv = np.empty((1, 4, 3, 3072), np.float32)
    sC = np.empty((1, 128, NH, DH, DH), np.float32)
    sn = np.empty((1, 128, NH, DH), np.float32)
    sm = np.empty((1, 128, NH), np.float32)
    sS = np.empty((1, 128, NH, DH, DH), np.float32)
    sconv = np.empty((1, 128, 3, 3072), np.float32)
    for c in range(8):
        b, half = c // 2, c % 2
        sl = slice(c * NSEQ, (c + 1) * NSEQ)
        o = r[c]
        y_prompt[b, half * T1:(half + 1) * T1] = o["y"][:T1]
        y_sample[sl] = o["y"][T1:].reshape(NSEQ, LS, D)
        if half == 1:
            pC[0, b] = o["pC"]
            pn[0, b] = o["pn_t"].T
            pm[0, b] = o["pm"][0]
            pS[0, b] = o["pS"]
            pconv[0, b] = o["pconv_t"].reshape(128, 24, 3).transpose(2, 1, 0).reshape(3, 3072)
        sC[0, sl] = o["oC"]
        sn[0, sl] = o["on_t"].transpose(0, 2, 1)
        sm[0, sl] = o["om"]
        sS[0, sl] = o["oS"]
        sconv[0, sl] = o["oconv_t"].reshape(128, 24, NSEQ, 3).transpose(2, 3, 1, 0).reshape(NSEQ, 3, 3072)
    return (y_prompt, y_sample, pC, pn, pm, pS, pconv, sC, sn, sm, sS, sconv)
```

```python
import numpy as np
import concourse.bass as bass
import concourse.mybir as mybir
from concourse.bass_utils import run_bass_kernel_spmd

F32 = mybir.dt.float32
BF16 = mybir.dt.bfloat16
AF = mybir.ActivationFunctionType
ALU = mybir.AluOpType
AX = mybir.AxisListType

D = 2048
KD = 16
NH = 8
DH = 128
T0 = 1024
T1 = 1024
TS = 128
NSEQ = 16
LS = 8
NTOK = T0 + T1 + TS
NQ = T1 + TS
NSUB = NTOK // 128
NSUBQ = NQ // 128
DFF = 8192
EPS = 1e-6
NEG = -1.0e30
WIN_COLS = 5120 + 4096 + 32

ENGS = ("tensor", "vector", "scalar", "gpsimd", "sync")


class Buf:
    __slots__ = ("name", "w", "r")

    def __init__(self, name=""):
        self.name = name
        self.w = None
        self.r = {}


class TL:
    def __init__(self, t, name=""):
        self.t = t
        self.buf = Buf(name)

    def __getitem__(self, k):
        return self.t[k]


class Prog:
    def __init__(self, nc):
        self.nc = nc
        self.sems = {}
        self.cnt = {}
        self.ops = {e: [] for e in ENGS}
        self.seen = {e: {} for e in ENGS}
        for e in ENGS:
            self.sems[e] = nc.alloc_semaphore(name="s_" + e)
            self.cnt[e] = 0
        self.dkeys = []
        self.nins = 0

    def key(self, name):
        k = "d_" + name
        if k not in self.sems:
            self.sems[k] = self.nc.alloc_semaphore(name="s" + k)
            self.cnt[k] = 0
            self.dkeys.append(k)
        return k

    def _deps(self, eng, reads, writes):
        deps = {}

        def add(kv):
            if kv is None:
                return
            k, v = kv
            if k == eng and eng == "tensor":
                return
            if deps.get(k, 0) < v:
                deps[k] = v
        for b in reads:
            add(b.buf.w)
        for b in writes:
            add(b.buf.w)
            for kv in b.buf.r.items():
                add(kv)
        out = []
        seen = self.seen[eng]
        for k, v in deps.items():
            if seen.get(k, 0) >= v:
                continue
            seen[k] = v
            out.append((k, v))
        return out

    def capture(self):
        self.cap = []
        return self.cap

    def end_capture(self):
        c = self.cap
        self.cap = None
        return c

    def replay(self, lists):
        lists = [l for l in lists if l]
        if not lists:
            return
        n = max(len(l) for l in lists)
        pos = [0] * len(lists)
        for i in range(1, n + 1):
            for j, l in enumerate(lists):
                tgt = (i * len(l) + n - 1) // n
                while pos[j] < tgt:
                    it = l[pos[j]]
                    pos[j] += 1
                    if it[0] == "op":
                        self.op(*it[1:])
                    else:
                        self.dma(*it[1:])

    def _emit_item(self, it):
        if it[0] == "op":
            self.op(*it[1:])
        else:
            self.dma(*it[1:])

    def replay_pipe(self, items, depth, burst=2):
        active = []
        nxt = 0

        def prefix(it):
            n = 0
            while n < len(it) and it[n][0] == "op" and it[n][1] == "tensor":
                n += 1
            return n

        def rw(ops):
            rs, ws = set(), set()
            for o in ops:
                for b_ in self._fl(o[-2]):
                    rs.add(id(b_.buf))
                for b_ in self._fl(o[-1]):
                    ws.add(id(b_.buf))
            return rs, ws

        def conflict(it):
            r1, w1 = rw(it)
            for a_ in active:
                r2, w2 = rw(a_[0][a_[1]:])
                if (w1 & (r2 | w2)) or (r1 & w2):
                    return True
            return False
        while nxt < len(items) or active:
            while nxt < len(items) and len(active) < depth and (
                    not active or (active[-1][1] - active[-1][2]) >= max(1, (len(active[-1][0]) - active[-1][2]) // depth)):
                it = items[nxt]
                if active and conflict(it):
                    break
                nxt += 1
                npre = prefix(it)
                for i in range(npre):
                    self._emit_item(it[i])
                if npre < len(it):
                    active.append([it, npre, npre])
            for a in list(active):
                for _ in range(burst):
                    if a[1] < len(a[0]):
                        self._emit_item(a[0][a[1]])
                        a[1] += 1
                if a[1] >= len(a[0]):
                    active.remove(a)

    @staticmethod
    def _fl(lst):
        out = []
        for b in lst:
            if hasattr(b, "kids"):
                out.extend(b.kids)
            else:
                out.append(b)
        return out

    def op(self, eng, fn, r=(), w=()):
        if getattr(self, "cap", None) is not None:
            self.cap.append(("op", eng, fn, tuple(r), tuple(w)))
            return
        r, w = self._fl(r), self._fl(w)
        waits = self._deps(eng, r, w)
        self.cnt[eng] += 1
        v = self.cnt[eng]
        self.ops[eng].append((waits, fn, eng, 1))
        for b in r:
            if b.buf.r.get(eng, 0) < v:
                b.buf.r[eng] = v
        for b in w:
            b.buf.w = (eng, v)
            b.buf.r = {}
        self.nins += 1

    def V(self, fn, r=(), w=()):
        self.op("vector", fn, r, w)

    def A(self, fn, r=(), w=()):
        self.op("scalar", fn, r, w)

    def G(self, fn, r=(), w=()):
        self.op("gpsimd", fn, r, w)

    def T(self, fn, r=(), w=()):
        self.op("tensor", fn, r, w)

    def dma(self, eng, key, fns, r=(), w=()):
        if not isinstance(fns, (list, tuple)):
            fns = [fns]
        if getattr(self, "cap", None) is not None:
            self.cap.append(("dma", eng, key, fns, tuple(r), tuple(w)))
            return
        r, w = self._fl(r), self._fl(w)
        key = self.key(key) if not key.startswith("d_") else key
        waits = self._deps(eng, r, w)
        for i, fn in enumerate(fns):
            self.cnt[key] += 16
            self.ops[eng].append((waits if i == 0 else [], fn, key, 16))
        v = self.cnt[key]
        for b in r:
            if b.buf.r.get(key, 0) < v:
                b.buf.r[key] = v
        for b in w:
            b.buf.w = (key, v)
            b.buf.r = {}
        self.nins += len(fns)

    def barrier(self):
        for e in ENGS:
            waits = []
            for k, v in self.cnt.items():
                if k == e or v == 0:
                    continue
                if self.seen[e].get(k, 0) >= v:
                    continue
                self.seen[e][k] = v
                waits.append((k, v))
            self.ops[e].append((waits, None, None, 0))

    def flush(self, final=False):
        nc = self.nc
        sems = self.sems
        ops = self.ops

        def run(e, name):
            for waits, fn, key, inc in ops[name]:
                for k, v in waits:
                    e.wait_ge(sems[k], v)
                if fn is not None:
                    fn(e).then_inc(sems[key], inc)

        with nc.Block() as block:
            @block.tensor
            def _(e):
                run(e, "tensor")

            @block.vector
            def _(e):
                run(e, "vector")

            @block.scalar
            def _(e):
                run(e, "scalar")

            @block.gpsimd
            def _(e):
                run(e, "gpsimd")

            @block.sync
            def _(e):
                run(e, "sync")
        self.ops = {e: [] for e in ENGS}


def _const_layout():
    off = {}
    c = 0
    for name, n in (("ident", 128), ("ones", 128), ("neg128", 128), ("uinc128", 128), ("ustr128", 128),
                    ("sel128", 128), ("bd32", 128), ("o64", 128), ("offL128", 128), ("neg8", 8), ("uinc8", 8), ("ustr8", 8), ("sel8", 128),
                    ("sign", 16)):
        off[name] = (c, n)
        c += n
    return off, c


CO, NCONST = _const_layout()


def _build_consts():
    a = np.zeros((128, NCONST), np.float32)

    def put(name, m):
        o, n = CO[name]
        a[: m.shape[0], o:o + m.shape[1]] = m
    put("ident", np.eye(128, dtype=np.float32))
    put("ones", np.ones((128, 128), np.float32))
    for L, sfx in ((128, "128"), (8, "8")):
        t = np.arange(L)
        neg = np.where(t[None, :] <= t[:, None], 0.0, NEG).astype(np.float32)
        uinc = (t[None, :] >= t[:, None]).astype(np.float32)
        ustr = (t[None, :] > t[:, None]).astype(np.float32)
        sel = np.zeros((L, 128), np.float32)
        sel[L - 1, :] = 1.0
        put("neg" + sfx, neg)
        put("uinc" + sfx, uinc)
        put("ustr" + sfx, ustr)
        put("sel" + sfx, sel)
    bd32 = np.zeros((128, 128), np.float32)
    bd64 = np.zeros((128, 128), np.float32)
    for i in range(0, 128, 32):
        bd32[i:i + 32, i:i + 32] = 1.0
    for i in range(0, 128, 64):
        bd64[i:i + 64, i:i + 64] = 1.0
    put("bd32", bd32)
    put("o64", bd64 - bd32)
    ofl = np.zeros((128, 128), np.float32)
    ofl[64:, :64] = 1.0
    put("offL128", ofl)
    sg = np.ones((128, 16), np.float32)
    sg[:, 0:8] = -1.0
    put("sign", sg)
    return a


class Builder:
    def __init__(self, debug=False, phases=(0, 1, 2, 3)):
        self.debug = debug
        self.phases = phases
        nc = bass.Bass("TRN2", target_bir_lowering=False)
        self.nc = nc
        self.P = Prog(nc)
        self.ins = {}
        self.outs = {}
        self.psum = TLBank(nc)

    def din(self, name, shape, dt=F32):
        t = self.nc.dram_tensor(name, list(shape), dt, kind="ExternalInput").ap()
        self.ins[name] = t
        return t

    def dout(self, name, shape, dt=F32):
        t = self.nc.dram_tensor(name, list(shape), dt, kind="ExternalOutput").ap()
        self.outs[name] = t
        return t

    def dscr(self, name, shape, dt):
        kind = "ExternalOutput" if self.debug else "Internal"
        t = self.nc.dram_tensor(name, list(shape), dt, kind=kind).ap()
        if self.debug:
            self.outs[name] = t
        return t


class TLBank:
    def __init__(self, nc):
        self.t = nc.alloc_psum_tensor("psum_all", [128, 4096], F32)
        self.banks = [TL(None, "bank%d" % i) for i in range(8)]
        self.ptr = 0

        self.ids = list(range(8))

    def sub(self, ids):
        o = TLBank.__new__(TLBank)
        o.t, o.banks, o.ptr, o.ids = self.t, self.banks, 0, list(ids)
        return o

    def one(self):
        i = self.ids[self.ptr]
        self.ptr = (self.ptr + 1) % len(self.ids)
        return i

    def pair(self):
        if self.ptr % 2:
            self.ptr = (self.ptr + 1) % len(self.ids)
        i = self.ids[self.ptr]
        self.ptr = (self.ptr + 2) % len(self.ids)
        return i

    def f32(self, i, n=1):
        return self.t[:, i * 512:(i + n) * 512]

    def bf(self, i, n=1):
        return self.t[:, i * 512:(i + n) * 512].bitcast(BF16)


def _mm(P, out, lhsT, rhs, start, stop, r, w):
    P.T(lambda e: e.matmul(out, lhsT=lhsT, rhs=rhs, start=start, stop=stop), r, w)


def _tr(P, out, in_, ident, r, w):
    P.T(lambda e: e.transpose(out=out, in_=in_, identity=ident), r, w)


def _act(P, out, in_, func, r, w, bias=None, scale=None, accum=None):
    kw = {}
    if bias is not None:
        kw["bias"] = bias
    if scale is not None:
        kw["scale"] = scale
    if accum is not None:
        kw["accum_out"] = accum
    P.A(lambda e: e.activation(out=out, in_=in_, func=func, **kw), r, w)


def _tt(P, eng, out, in0, in1, op, r, w):
    P.op(eng, lambda e: e.tensor_tensor(out=out, in0=in0, in1=in1, op=op), r, w)


def _ts(P, eng, out, in0, s1, op0, r, w, s2=None, op1=None):
    if op1 is None:
        P.op(eng, lambda e: e.tensor_single_scalar(out=out, in_=in0, scalar=s1, op=op0), r, w)
    else:
        P.op(eng, lambda e: e.tensor_scalar(out=out, in0=in0, scalar1=s1, scalar2=s2, op0=op0, op1=op1), r, w)


def _stt(P, eng, out, in0, scalar, in1, op0, op1, r, w):
    P.op(eng, lambda e: e.scalar_tensor_tensor(out=out, in0=in0, scalar=scalar, in1=in1, op0=op0, op1=op1), r, w)


def _red(P, eng, out, in_, op, r, w):
    P.op(eng, lambda e: e.tensor_reduce(out=out, in_=in_, axis=AX.X, op=op), r, w)


def _cp(P, eng, out, in_, r, w):
    if eng == "scalar":
        P.A(lambda e: e.activation(out=out, in_=in_, func=AF.Copy), r, w)
    else:
        P.op(eng, lambda e: e.tensor_copy(out=out, in_=in_), r, w)


def _ms(P, eng, ap, val, w):
    P.op(eng, lambda e: e.memset(ap, val), (), w)


def _ld(P, eng, key, out, in_, r=(), w=()):
    P.dma(eng, key, lambda e: e.dma_start(out=out, in_=in_), r, w)


class Ctx:
    def __init__(self, nc):
        self.nc = nc
        self.guards = []

    def sb(self, name, shape, dt):
        g = self.nc.sbuf_tensor(name, list(shape), dt)
        t = g.__enter__()
        self.guards.append(g)
        return TL(t, name)

    def close(self):
        for g in reversed(self.guards):
            g.__exit__(None, None, None)
        self.guards = []


def build(debug=False, phases=(0, 1, 2, 3)):
    B = Builder(debug, phases)
    nc, P, PS = B.nc, B.P, B.psum
    bankb = PS.banks

    x_d = B.din("x", [NTOK, D])
    c_d = B.din("c", [17, D])
    flag_d = B.din("flag", [128, 1])
    consts_d = B.din("consts", [128, NCONST])
    adaw_d = B.din("ada_w", [D, 16384])
    adab_d = B.din("ada_b", [1, 16384])
    win_d = B.din("w_in", [D, WIN_COLS])
    wout_d = B.din("w_out", [D, D])
    wup_d = B.din("w_up", [D, DFF])
    wdn_d = B.din("w_down", [DFF, D])
    norms_d = B.din("norms", [3, D])
    hnorm_d = B.din("hnorm", [2, 1024])
    gvec_d = B.din("gvec", [1, 32])
    convw_d = B.din("conv_w", [128, 24 * 4])
    sC_d = B.din("sC", [NSEQ, NH, DH, DH])
    sn_d = B.din("sn_t", [NSEQ, DH, NH])
    sm_d = B.din("sm", [NSEQ, NH])
    sS_d = B.din("sS", [NSEQ, NH, DH, DH])
    scv_d = B.din("sconv_t", [128, 24 * NSEQ * 3])

    y_d = B.dout("y", [NQ, D])
    pC_d = B.dout("pC", [NH, DH, DH])
    pn_d = B.dout("pn_t", [DH, NH])
    pm_d = B.dout("pm", [1, NH])
    pS_d = B.dout("pS", [NH, DH, DH])
    pcv_d = B.dout("pconv_t", [128, 24 * 3])
    oC_d = B.dout("oC", [NSEQ, NH, DH, DH])
    on_d = B.dout("on_t", [NSEQ, DH, NH])
    om_d = B.dout("om", [NSEQ, NH])
    oS_d = B.dout("oS", [NSEQ, NH, DH, DH])
    ocv_d = B.dout("oconv_t", [128, 24 * NSEQ * 3])

    modd = B.dscr("modd", [17, 16384], F32)
    s_qa = B.dscr("s_qa", [NH, DH, NQ], BF16)
    s_kaT = B.dscr("s_kaT", [NH, DH, NQ], BF16)
    s_qb = B.dscr("s_qb", [NH, DH, NTOK], BF16)
    s_kb = B.dscr("s_kb", [NH, DH, NTOK], BF16)
    s_vb = B.dscr("s_vb", [NH, DH, NTOK], BF16)
    s_ka = B.dscr("s_ka", [NTOK, 1024], BF16)
    s_va = B.dscr("s_va", [NTOK, 1024], BF16)
    s_oa = B.dscr("s_oa", [NQ, 1024], BF16)
    s_zb = B.dscr("s_zb", [NQ, 1024], BF16)
    s_gt = B.dscr("s_gt", [NTOK, 32], F32)

    G = Ctx(nc)
    cst = G.sb("consts_sb", [128, NCONST], F32)
    identb = G.sb("identb", [128, 128], BF16)
    onesb = G.sb("onesb", [128, 128], BF16)
    flag = G.sb("flag_sb", [128, 1], F32)
    _ld(P, "sync", "g0", cst[:], consts_d, w=[cst])
    _ld(P, "sync", "g1", flag[:], flag_d, w=[flag])

    def C(name, rows=128):
        o, n = CO[name]
        return cst[0:rows, o:o + n]
    _cp(P, "vector", identb[:], C("ident"), [cst], [identb])
    _cp(P, "vector", onesb[:], C("ones"), [cst], [onesb])

    cT = G.sb("cT_sb", [128, 16 * 17], BF16)
    if 0 in phases:
        _phase0(B, P, PS, cst, C, c_d, adaw_d, adab_d, modd, cT)
        P.barrier()
        P.flush()
    if 1 in phases:
        _phase1(B, P, PS, cst, C, identb, onesb, flag, x_d, norms_d, modd, win_d,
                dict(qa=s_qa, kaT=s_kaT, qb=s_qb, kb=s_kb, vb=s_vb, ka=s_ka, va=s_va, oa=s_oa, zb=s_zb, gt=s_gt),
                gvec_d, convw_d, scv_d, pcv_d, ocv_d, cT, adaw_d, adab_d)
    S = dict(qa=s_qa, kaT=s_kaT, qb=s_qb, kb=s_kb, vb=s_vb, ka=s_ka, va=s_va, oa=s_oa, zb=s_zb, gt=s_gt)
    G2 = Ctx(nc)
    mixT = G2.sb("mixT", [128, 16, NQ], BF16)
    if debug:
        mixd = B.dout("mix_dbg", [128, 16 * NQ], BF16)
    if 2 in phases:
        io = dict(sC=sC_d, sn=sn_d, sm=sm_d, sS=sS_d, scv=scv_d, pC=pC_d, pn=pn_d, pm=pm_d, pS=pS_d, pcv=pcv_d,
                  oC=oC_d, on=on_d, om=om_d, oS=oS_d, ocv=ocv_d)
        sc = Scan(B, P, PS, cst, C, identb, onesb, flag, mixT, S, hnorm_d, gvec_d, convw_d, io)
        sc.run()
        if debug:
            _ld(P, "sync", "dbgm", mixd, mixT[:, :, :].rearrange("p k t -> p (k t)"), r=[mixT])
        P.barrier()
        P.flush()
        sc.X.close()
    if 3 in phases:
        _phase3(B, P, PS, cst, C, identb, mixT, x_d, norms_d, modd, wout_d, wup_d, wdn_d, y_d)
    P.barrier()
    P.flush()
    return B


NADA0 = 8


def _ada_block(P, PS, jb, slot, key, bt, bkey, m, mkey, cT, adaw_d, adab_d, modd):
    cols = slice(jb * 512, (jb + 1) * 512)
    _ld(P, "gpsimd", key, slot[:], adaw_d[:, cols].rearrange("(k p) c -> p k c", p=128), w=[slot])
    _ld(P, "sync", bkey, bt[:], adab_d[0:1, cols].partition_broadcast(17), w=[bt])
    bk = PS.one()
    for k in range(16):
        _mm(P, PS.f32(bk)[0:17, :], cT[:, k * 17:(k + 1) * 17], slot[:, k, :], k == 0, k == 15,
            [cT, slot], [PS.banks[bk]])
    _tt(P, "vector", m[:], PS.f32(bk)[0:17, :], bt[:], ALU.add, [PS.banks[bk], bt], [m])
    _ld(P, "sync", mkey, modd[:, cols], m[:], r=[m])


def _phase0(B, P, PS, cst, C, c_d, adaw_d, adab_d, modd, cT):
    nc = B.nc
    X = Ctx(nc)
    c_sb = X.sb("p0_c", [17, D], F32)
    wr = [X.sb("p0_w%d" % i, [128, 16, 512], BF16) for i in range(3)]
    bias = [X.sb("p0_b%d" % i, [17, 512], F32) for i in range(2)]
    mo = [X.sb("p0_m%d" % i, [17, 512], F32) for i in range(2)]
    _ld(P, "sync", "p0c", c_sb[:], c_d, w=[c_sb])
    _act(P, c_sb[:], c_sb[:], AF.Silu, [c_sb], [c_sb])
    bk = PS.one()
    for k in range(16):
        _tr(P, PS.f32(bk)[:, k * 17:(k + 1) * 17], c_sb[:, k * 128:(k + 1) * 128], C("ident", 17)[:, 0:17],
            [c_sb, cst], [PS.banks[bk]])
    _cp(P, "vector", cT[:], PS.f32(bk)[:, 0:272], [PS.banks[bk]], [cT])
    items = []
    for jb in range(NADA0):
        P.capture()
        _ada_block(P, PS, jb, wr[jb % 3], "wr%d" % (jb % 3), bias[jb % 2], "p0b%d" % (jb % 2), mo[jb % 2],
                   "p0m%d" % (jb % 2), cT, adaw_d, adab_d, modd)
        items.append(P.end_capture())
    P.replay_pipe(items, 3, burst=1)
    X.close()


def _mod_tiles(P, modd, col0, tp, ts, key):
    _ld(P, "sync", key + "p", tp[:], modd[0:1, col0:col0 + D].partition_broadcast(128), w=[tp])
    fns = []
    for b in range(NSEQ):
        fns.append(lambda e, b=b: e.dma_start(out=ts[8 * b:8 * b + 8, :],
                                              in_=modd[1 + b:2 + b, col0:col0 + D].partition_broadcast(8)))
    P.dma("sync", key + "s", fns, w=[ts])


def _phase1(B, P, PS, cst, C, identb, onesb, flag, x_d, norms_d, modd, win_d, S, gvec_d, convw_d, scv_d, pcv_d, ocv_d,
            cT, adaw_d, adab_d):
    nc = B.nc
    bank = PS.banks
    X = Ctx(nc)
    hT = X.sb("hT", [128, 16, NTOK], BF16)
    hTs = [TL(hT.t, "hT_%d" % i) for i in range(NSUB)]
    hT = TLP(hT.t, hTs)
    wr = [X.sb("p1_w%d" % i, [128, 16, 512], BF16) for i in range(3)]

    XA = Ctx(nc)
    xt = [XA.sb("p1_x%d" % i, [128, D], F32) for i in range(2)]
    tmpf = XA.sb("p1_tmp", [128, D], F32)
    hb = [XA.sb("p1_hb%d" % i, [128, D], BF16) for i in range(2)]
    Ap = XA.sb("p1_Ap", [128, D], F32)
    Bp = XA.sb("p1_Bp", [128, D], F32)
    As = XA.sb("p1_As", [128, D], F32)
    Bs = XA.sb("p1_Bs", [128, D], F32)
    st = [XA.sb("p1_st%d" % i, [128, 4], F32) for i in range(2)]
    _ld(P, "sync", "p1n", tmpf[:], norms_d[0:1, :].partition_broadcast(128), w=[tmpf])
    _mod_tiles(P, modd, 2048, Ap, As, "p1A")
    _mod_tiles(P, modd, 0, Bp, Bs, "p1B")
    for A_ in (Ap, As):
        _stt(P, "vector", A_[:], A_[:], 1.0, tmpf[:], ALU.add, ALU.mult, [A_, tmpf], [A_])
    wq = []

    def wload(jb):
        slot = wr[jb % 3]
        ncol = 512 if jb < 18 else 32
        _ld(P, "gpsimd", "wr%d" % (jb % 3), slot[:, :, 0:ncol],
            win_d[:, jb * 512:jb * 512 + ncol].rearrange("(k p) c -> p k c", p=128), w=[slot])
    for jb in range(3):
        wload(jb)
    nitems = []
    for i in range(NSUB):
        P.capture()
        x_ = xt[i % 2]
        h_ = hb[i % 2]
        s_ = st[i % 2]
        _ld(P, "sync", "p1x%d" % (i % 2), x_[:], x_d[i * 128:(i + 1) * 128, :], w=[x_])
        _act(P, h_[:], x_[:], AF.Square, [x_], [h_, s_], accum=s_[:, 0:1])
        _act(P, s_[:, 1:2], s_[:, 0:1], AF.Ln, [s_], [s_], bias=EPS, scale=1.0 / D)
        _act(P, s_[:, 2:3], s_[:, 1:2], AF.Exp, [s_], [s_], scale=-0.5)
        A_, B_ = (Ap, Bp) if i < 16 else (As, Bs)
        _stt(P, "vector", tmpf[:], x_[:], s_[:, 2:3], A_[:], ALU.mult, ALU.mult, [x_, s_, A_], [tmpf])
        _tt(P, "vector", h_[:], tmpf[:], B_[:], ALU.add, [tmpf, B_], [h_])
        for half in range(2):
            bk = PS.one()
            for kk in range(8):
                k = half * 8 + kk
                _tr(P, PS.bf(bk)[:, kk * 128:(kk + 1) * 128], h_[:, k * 128:(k + 1) * 128], identb[:],
                    [h_, identb], [PS.banks[bk]])
            _cp(P, "scalar" if half else "vector", hT[:, half * 8:(half + 1) * 8, i * 128:(i + 1) * 128],
                PS.bf(bk).rearrange("p (k t) -> p k t", k=8), [PS.banks[bk]], [hTs[i]])
        nitems.append(P.end_capture())
    P.replay_pipe(nitems, 3, burst=1)
    P.barrier()
    P.flush()
    XA.close()

    XB = Ctx(nc)
    stg = [XB.sb("p1_stg%d" % i, [128, 512], BF16) for i in range(4)]
    NU = 4
    stg = stg + [XB.sb("p1_stg%d" % i, [128, 512], BF16) for i in range(4, 8)]
    pds = [XB.sb("p1_pd%d" % i, [128, 3 + 512], F32) for i in range(NU)]
    accs = [XB.sb("p1_acc%d" % i, [128, 512], F32) for i in range(NU)]
    efs = [XB.sb("p1_ef%d" % i, [128, 512], F32) for i in range(NU)]
    sqs = [XB.sb("p1_sq%d" % i, [128, 512], BF16) for i in range(NU)]
    rvs = accs
    cvs = XB.sb("p1_cvs", [128, 24 * NSEQ * 3], F32)
    pcvt = XB.sb("p1_pcvt", [128, 72], F32)
    convw = XB.sb("p1_convw", [128, 96], F32)
    gbr = XB.sb("p1_gbr", [128, 32], F32)
    biasrow = XB.sb("p1_biasrow", [128, 16], F32)
    coefrow = XB.sb("p1_coefrow", [128, 16], F32)
    graw = [XB.sb("p1_graw%d" % i, [128, 32], F32) for i in range(2)]
    GPt = [XB.sb("p1_GP%d" % i, [128, 32], F32) for i in range(2)]
    gt1 = XB.sb("p1_gt1", [128, 16], F32)
    gt2 = XB.sb("p1_gt2", [128, 16], F32)
    gt3 = XB.sb("p1_gt3", [128, 16], F32)
    gt4 = XB.sb("p1_gt4", [128, 8], F32)
    aw = [XB.sb("p1_aw%d" % i, [128, 16, 512], BF16) for i in range(2)]
    abias = [XB.sb("p1_ab%d" % i, [17, 512], F32) for i in range(2)]
    amo = [XB.sb("p1_am%d" % i, [17, 512], F32) for i in range(2)]
    ada_next = [NADA0]

    def ada_item():
        jb_ = ada_next[0]
        if jb_ >= 32:
            return None
        ada_next[0] += 1
        P.capture()
        _ada_block(P, PS, jb_, aw[jb_ % 2], "aw%d" % (jb_ % 2), abias[jb_ % 2], "p1ab%d" % (jb_ % 2), amo[jb_ % 2],
                   "p1am%d" % (jb_ % 2), cT, adaw_d, adab_d, modd)
        return P.end_capture()
    _ld(P, "sync", "p1cv", cvs[:], scv_d, w=[cvs])
    _ld(P, "sync", "p1cw", convw[:], convw_d, w=[convw])
    _ld(P, "sync", "p1gb", gbr[:], gvec_d.partition_broadcast(128), w=[gbr])
    _cp(P, "vector", biasrow[:, 0:8], gbr[:, 8:16], [gbr], [biasrow])
    _cp(P, "vector", biasrow[:, 8:16], gbr[:, 24:32], [gbr], [biasrow])
    _ms(P, "vector", coefrow[:], -1.0, [coefrow])
    _act(P, coefrow[:, 8:16], gbr[:, 16:24], AF.Exp, [gbr, coefrow], [coefrow])
    _ts(P, "vector", coefrow[:, 8:16], coefrow[:, 8:16], -1.0, ALU.mult, [coefrow], [coefrow])

    ev = [0]

    def evac(out, in_, bank_, dst, func=AF.Copy, scale=None):
        if func == AF.Copy and scale is None and (ev[0] % 2 == 0):
            _cp(P, "vector", out, in_, [bank_], [dst])
        else:
            _act(P, out, in_, func, [bank_], [dst], scale=scale)
        ev[0] += 1

    sidx = [0]

    def store(ap_dst, sg, nt):
        _ld(P, "sync", "p1s%d" % ((sidx[0] - 1) % 8), ap_dst, sg[:, 0:nt], r=[sg])

    def next_stg():
        sg = stg[sidx[0] % 8]
        sidx[0] += 1
        return sg

    ucnt = [0]

    def gdn_unit(name, ty, h, t0, nt, bk, first, do_store, prev):
        blk = 8 * ty + h
        u = ucnt[0]
        ucnt[0] += 1
        pd, acc, ef, sq = pds[u % NU], accs[u % NU], efs[u % NU], sqs[u % NU]
        rv = ef
        sample = t0 == T0 + T1
        cw = convw[:, blk * 4:(blk + 1) * 4]
        if sample:
            pv = pd[:, 0:NSEQ * 11].rearrange("p (s l) -> p s l", s=NSEQ)
            cv3 = cvs[:, blk * 48:(blk + 1) * 48].rearrange("p (s j) -> p s j", s=NSEQ)
            _cp(P, "scalar", pv[:, :, 0:3], cv3, [cvs, pd], [pd])
            _cp(P, "scalar", pv[:, :, 3:11], PS.f32(bk)[:, 0:nt].rearrange("p (s l) -> p s l", s=NSEQ),
                [bank[bk], pd], [pd])
            _cp(P, "scalar", cv3, pv[:, :, 8:11], [pd, cvs], [cvs])
            a_ = acc[:, 0:nt].rearrange("p (s l) -> p s l", s=NSEQ)

            def tap(j):
                return pv[:, :, j:j + LS]
        else:
            if first:
                _ms(P, "vector", pd[:, 0:3], 0.0, [pd])
            else:
                ppd, pnt = prev
                if t0 == T0:
                    _ts(P, "vector", pd[:, 0:3], ppd[:, pnt:pnt + 3], flag[:, 0:1], ALU.mult, [ppd, flag, pd], [pd])
                else:
                    _cp(P, "scalar", pd[:, 0:3], ppd[:, pnt:pnt + 3], [ppd, pd], [pd])
            _cp(P, "scalar", pd[:, 3:3 + nt], PS.f32(bk)[:, 0:nt], [bank[bk], pd], [pd])
            if t0 + nt == T0 + T1:
                _cp(P, "scalar", pcvt[:, blk * 3:(blk + 1) * 3], pd[:, nt:nt + 3], [pd, pcvt], [pcvt])
            a_ = acc[:, 0:nt]

            def tap(j):
                return pd[:, j:j + nt]
        if not do_store:
            return (pd, nt)
        _ts(P, "vector", a_, tap(0), cw[:, 0:1], ALU.mult, [pd, convw, acc], [acc])
        for j in range(1, 4):
            _stt(P, "vector", a_, tap(j), cw[:, j:j + 1], a_, ALU.mult, ALU.add, [pd, convw, acc], [acc])
        _act(P, ef[:, 0:nt], acc[:, 0:nt], AF.Exp, [acc, ef], [ef], scale=-1.0)
        _act(P, ef[:, 0:nt], ef[:, 0:nt], AF.Ln, [ef], [ef], bias=1.0)
        _act(P, ef[:, 0:nt], ef[:, 0:nt], AF.Exp, [ef], [ef], scale=-1.0)
        sg = next_stg()
        if ty == 2:
            _tt(P, "vector", sg[:, 0:nt], acc[:, 0:nt], ef[:, 0:nt], ALU.mult, [acc, ef, sg], [sg])
        else:
            _tt(P, "vector", acc[:, 0:nt], acc[:, 0:nt], ef[:, 0:nt], ALU.mult, [acc, ef], [acc])
            _tt(P, "vector", sq[:, 0:nt], acc[:, 0:nt], acc[:, 0:nt], ALU.mult, [acc, sq], [sq])
            b2 = PS.one()
            _mm(P, PS.f32(b2)[:, 0:nt], onesb[:, :], sq[:, 0:nt], True, True, [onesb, sq], [bank[b2]])
            _act(P, rv[:, 0:nt], PS.f32(b2)[:, 0:nt], AF.Ln, [bank[b2], rv], [rv], bias=EPS)
            _act(P, rv[:, 0:nt], rv[:, 0:nt], AF.Exp, [rv], [rv], scale=-0.5)
            if ty == 0:
                _stt(P, "vector", sg[:, 0:nt], acc[:, 0:nt], DH ** -0.5, rv[:, 0:nt], ALU.mult, ALU.mult,
                     [acc, rv, sg], [sg])
            else:
                _tt(P, "vector", sg[:, 0:nt], acc[:, 0:nt], rv[:, 0:nt], ALU.mult, [acc, rv, sg], [sg])
        store(S[name][h, :, t0:t0 + nt], sg, nt)
        return (pd, nt)

    def gate_unit(i, bk):
        gr, GP = graw[i % 2], GPt[i % 2]
        _cp(P, "vector", gr[:], PS.f32(bk)[:, 0:32], [bank[bk], gr], [gr])
        _tt(P, "vector", GP[:, 0:8], gr[:, 0:8], gbr[:, 0:8], ALU.add, [gr, gbr, GP], [GP])
        _tt(P, "vector", gt1[:], gr[:, 8:24], biasrow[:], ALU.add, [gr, biasrow, gt1], [gt1])
        _tt(P, "vector", gt1[:], gt1[:], C("sign"), ALU.mult, [gt1, cst], [gt1])
        _act(P, gt2[:], gt1[:], AF.Abs, [gt1, gt2], [gt2])
        _act(P, gt2[:], gt2[:], AF.Exp, [gt2], [gt2], scale=-1.0)
        _act(P, gt2[:], gt2[:], AF.Ln, [gt2], [gt2], bias=1.0)
        _ts(P, "vector", gt3[:], gt1[:], 0.0, ALU.max, [gt1, gt3], [gt3])
        _tt(P, "vector", gt3[:], gt3[:], gt2[:], ALU.add, [gt3, gt2], [gt3])
        _tt(P, "vector", GP[:, 8:24], gt3[:], coefrow[:], ALU.mult, [gt3, coefrow, GP], [GP])
        _act(P, gt4[:], gr[:, 24:32], AF.Exp, [gr, gt4], [gt4], scale=-1.0)
        _ts(P, "vector", gt4[:], gt4[:], 1.0, ALU.add, [gt4], [gt4])
        P.V(lambda e: e.reciprocal(out=GP[:, 24:32], in_=gt4[:]), [gt4, GP], [GP])
        _ld(P, "sync", "p1g%d" % (i % 2), S["gt"][i * 128:(i + 1) * 128, :], GP[:], r=[GP])

    fm_types = [("qa", 0), ("kaT", 0), ("qb", 1), ("kb", 2), ("vb", 2)]
    tiles_q = [(1024, 512), (1536, 512), (2048, 128)]
    tiles_all = [(0, 512), (512, 512)] + tiles_q
    items = []
    since = [0]
    for jb in range(19):
        slot = wr[jb % 3]
        if jb == 10:
            while True:
                it_ = ada_item()
                if it_ is None:
                    break
                items.append(it_)
            P.replay_pipe(items, 6, burst=1)
            items = []
        if jb >= 3:
            if jb < 10:
                P.capture()
                wload(jb)
                items.append(P.end_capture())
            else:
                wload(jb)
        if jb < 10:
            name, mode = fm_types[jb // 2]
            tl = {0: tiles_q, 1: [(512, 512)] + tiles_q, 2: tiles_all}[mode]
            for sbk in range(4):
                h = (jb % 2) * 4 + sbk
                prev = None
                for ti, (t0, nt) in enumerate(tl):
                    P.capture()
                    bk = PS.one()
                    for k in range(16):
                        _mm(P, PS.f32(bk)[:, 0:nt], slot[:, k, sbk * 128:(sbk + 1) * 128], hT[:, k, t0:t0 + nt],
                            k == 0, k == 15, [slot, hT], [PS.banks[bk]])
                    if name in ("qa", "kaT"):
                        sg = next_stg()
                        evac(sg[:, 0:nt], PS.f32(bk)[:, 0:nt], PS.banks[bk], sg,
                             scale=(DH ** -0.5 if name == "kaT" else None))
                        store(S[name][h, :, t0 - T0:t0 - T0 + nt], sg, nt)
                    else:
                        ty = {"qb": 0, "kb": 1, "vb": 2}[name]
                        prev = gdn_unit(name, ty, h, t0, nt, bk, ti == 0, not (name == "qb" and t0 < T0), prev)
                    items.append(P.end_capture())
                    since[0] += 1
                    if since[0] >= (11 if jb < 4 else 5):
                        since[0] = 0
                        it_ = ada_item()
                        if it_ is not None:
                            items.append(it_)
        elif jb < 18:
            name = ("ka", "va", "oa", "zb")[(jb - 10) // 2]
            c0 = ((jb - 10) % 2) * 512
            qonly = name in ("oa", "zb")
            for i in range(NSUB):
                if qonly and i < 8:
                    continue
                bk = PS.one()
                for k in range(16):
                    _mm(P, PS.f32(bk)[:, :], hT[:, k, i * 128:(i + 1) * 128], slot[:, k, :], k == 0, k == 15,
                        [slot, hT], [PS.banks[bk]])
                sg = next_stg()
                if name == "ka":
                    evac(sg[:], PS.f32(bk), PS.banks[bk], sg, scale=DH ** -0.5)
                elif name == "va":
                    evac(sg[:], PS.f32(bk), PS.banks[bk], sg)
                elif name == "oa":
                    evac(sg[:], PS.f32(bk), PS.banks[bk], sg, func=AF.Sigmoid)
                else:
                    evac(sg[:], PS.f32(bk), PS.banks[bk], sg, func=AF.Silu)
                r0 = i * 128 - (T0 if qonly else 0)
                store(S[name][r0:r0 + 128, c0:c0 + 512], sg, 512)
        else:
            for i in range(NSUB):
                bk = PS.one()
                for k in range(16):
                    _mm(P, PS.f32(bk)[:, 0:32], hT[:, k, i * 128:(i + 1) * 128], slot[:, k, 0:32], k == 0, k == 15,
                        [slot, hT], [PS.banks[bk]])
                gate_unit(i, bk)
    _ld(P, "sync", "p1pc", pcv_d, pcvt[:, :], r=[pcvt])
    _ld(P, "sync", "p1oc", ocv_d, cvs[:, :], r=[cvs])
    P.barrier()
    P.flush()
    XB.close()
    X.close()


def _shared_inputs(inp):
    w_in = np.asarray(inp["w_in"][0])
    o = np.cumsum([0, 1024, 1024, 1024, 8, 8, 1024, 1024, 1024, 1024, 8, 8, 1024])
    seg = {n: w_in[:, o[i]:o[i + 1]] for i, n in enumerate(
        ["qa", "ka", "va", "ia", "fa", "oa", "qb", "kb", "vb", "ab", "bb", "zb"])}
    w_in_r = np.ascontiguousarray(np.concatenate(
        [seg[n] for n in ("qa", "ka", "qb", "kb", "vb", "ka", "va", "oa", "zb", "ia", "fa", "ab", "bb")], axis=1))
    sh = {
        "consts": _build_consts(),
        "ada_w": np.ascontiguousarray(np.concatenate([inp["ada_w"][0], inp["ada_final_w"]], axis=1)),
        "ada_b": np.ascontiguousarray(np.concatenate([inp["ada_b"][0], inp["ada_final_b"]])[None, :]),
        "w_in": w_in_r,
        "w_out": np.ascontiguousarray(inp["w_out"][0]),
        "w_up": np.ascontiguousarray(inp["w_up"][0]),
        "w_down": np.ascontiguousarray(inp["w_down"][0]),
        "norms": np.ascontiguousarray(np.stack([inp["norm1"][0], inp["norm2"][0], inp["norm_final"]])),
        "hnorm": np.ascontiguousarray(np.stack([inp["mlstm_norm"][0], inp["gdn_norm"][0]])),
        "gvec": np.ascontiguousarray(np.concatenate(
            [inp["mlstm_gate_bias"][0], inp["gdn_A_log"][0], inp["gdn_dt_bias"][0]])[None, :]),
        "conv_w": np.ascontiguousarray(
            np.asarray(inp["gdn_conv_w"][0]).T.reshape(24, 128, 4).transpose(1, 0, 2).reshape(128, 96)),
    }
    return {k: np.asarray(v, np.float32) for k, v in sh.items()}


def _core_inputs(inp, c, sh):
    b, half = c // 2, c % 2
    sl = slice(c * NSEQ, (c + 1) * NSEQ)
    xp = inp["x_prompt"][b]
    m = dict(sh)
    m["x"] = np.ascontiguousarray(np.concatenate(
        [xp[0:T0], xp[half * T1:(half + 1) * T1], inp["x_sample"][sl].reshape(TS, D)], axis=0), np.float32)
    m["c"] = np.ascontiguousarray(np.concatenate([inp["c_prompt"][b:b + 1], inp["c_sample"][sl]], axis=0), np.float32)
    m["flag"] = np.full((128, 1), float(half), np.float32)
    m["sC"] = np.ascontiguousarray(inp["state_mlstm_C"][0, sl], np.float32)
    m["sn_t"] = np.ascontiguousarray(np.asarray(inp["state_mlstm_n"][0, sl]).transpose(0, 2, 1), np.float32)
    m["sm"] = np.ascontiguousarray(inp["state_mlstm_m"][0, sl], np.float32)
    m["sS"] = np.ascontiguousarray(inp["state_gdn_S"][0, sl], np.float32)
    cv = np.asarray(inp["state_gdn_conv"][0, sl])
    m["sconv_t"] = np.ascontiguousarray(
        cv.transpose(2, 0, 1).reshape(24, 128, NSEQ, 3).transpose(1, 0, 2, 3).reshape(128, 24 * NSEQ * 3), np.float32)
    return m


LP = 128
NLEV = {8: 2, 64: 5, 128: 6}


def _v3(ap2d, n):
    return ap2d.rearrange("p (h x) -> p h x", h=NH)


class TLV:
    def __init__(self, t, c0, c1, name=""):
        self.t, self.c0, self.c1 = t, c0, c1
        self.buf = Buf(name)

    def __getitem__(self, k):
        rows, cols = k
        a = 0 if cols.start is None else cols.start
        b = (self.c1 - self.c0) if cols.stop is None else cols.stop
        return self.t[rows, self.c0 + a:self.c0 + b]


class TLP:
    def __init__(self, t, kids):
        self.t, self.kids = t, kids

    def __getitem__(self, k):
        return self.t[k]


class Grp:
    pass


class Scan:
    NG = 2

    def __init__(self, B, P, PS, cst, C, identb, onesb, flag, mixT, S, hnorm_d, gvec_d, convw_d, io):
        self.B, self.P, self.PS, self.cst, self.C = B, P, PS, cst, C
        self.identb, self.onesb, self.flag, self.mixT, self.S, self.io = identb, onesb, flag, mixT, S, io
        nc = B.nc
        X = self.X = Ctx(nc)
        sb = X.sb
        NG = self.NG
        nh = NH // NG
        self.gA = sb("gA", [128, 1024], F32)
        self.gB = sb("gB", [128, 1024], F32)
        _ld(P, "sync", "s2a", self.gA[:], hnorm_d[0:1, :].partition_broadcast(128), w=[self.gA])
        _ld(P, "sync", "s2b", self.gB[:], hnorm_d[1:2, :].partition_broadcast(128), w=[self.gB])
        self.fm = [dict(qaT=sb("qaT%d" % i, [128, 8, 128], BF16), kaT=sb("kaT%d" % i, [128, 8, 128], BF16),
                        q=sb("qpost%d" % i, [128, 8, 128], BF16), k=sb("kpost%d" % i, [128, 8, 128], BF16),
                        v=sb("vpost%d" % i, [128, 8, 128], BF16)) for i in range(2)]
        self.tm = [dict(ka=sb("ka_t%d" % i, [128, 1024], BF16), va=sb("va_t%d" % i, [128, 1024], BF16),
                        oa=sb("oa_t%d" % i, [128, 1024], BF16), zb=sb("zb_t%d" % i, [128, 1024], BF16),
                        GP=sb("GP%d" % i, [128, 32], F32)) for i in range(2)]
        specA_big = (("dgx", F32), ("R", F32), ("sloc", BF16), ("sTsb", BF16), ("nl", F32), ("wlk", BF16), ("dCs", F32),
                     ("h1", F32), ("hnb", BF16), ("gs", BF16), ("Cst", F32), ("Cbf", BF16))
        specB_big = (("dG", F32), ("NZ", F32), ("qg", BF16), ("wT", F32), ("dTi", F32), ("P0", BF16), ("P1", BF16),
                     ("PT0", BF16), ("PT1", BF16), ("Tacc", BF16), ("qkT", BF16), ("Mf", BF16), ("MoT", BF16), ("MoT2", BF16),
                     ("kbg", BF16), ("kdec", BF16), ("vbt", BF16), ("WT", BF16), ("U0", F32), ("u", BF16),
                     ("ob", BF16), ("gz", BF16), ("Sst", F32), ("Sbf", BF16))

        def smallA(n):
            return (("sm1", 2 * n), ("g", n), ("cm", n), ("rows", n), ("gl", n), ("dns", n), ("mx", 2 * n), ("t12", 2 * n),
                    ("fd", 2 * n), ("mt", n), ("en", n), ("dd", n), ("d2", n), ("a12", 2 * n), ("ssq", n), ("sm2", 2 * n),
                    ("dC", n))

        def smallB(n):
            return (("gsm", 2 * n), ("gsmall", 2 * n), ("gLe", n), ("ssq2", n))
        parents = {}
        for kind, spec in (("a", specA_big), ("b", specB_big)):
            for n, dt in spec:
                parents[(kind, n)] = sb("P%s_%s" % (kind, n), [128, 1024], dt)
        nstP = sb("Pa_nst", [128, NH], F32)
        nbfP = sb("Pa_nbf", [128, NH], BF16)
        mprevP = sb("Pa_mprev", [128, NH], F32)
        self.ga, self.gb = [], []
        for gi in range(NG):
            for kind in ("a", "b"):
                g = Grp()
                g.h0, g.nh, g.kind = gi * nh, nh, kind
                base = (0 if kind == "a" else 4) + 2 * gi
                g.PS = PS.sub([base, base + 1])
                t = "%s%d_" % (kind, gi)
                g.W = {}
                for n, dt in (specA_big if kind == "a" else specB_big):
                    g.W[n] = TLV(parents[(kind, n)].t, gi * nh * 128, (gi + 1) * nh * 128, t + n)
                for n, wd in (smallA(nh) if kind == "a" else smallB(nh)):
                    g.W[n] = sb(t + n, [128, wd], F32)
                if kind == "a":
                    g.Cst, g.Cbf = g.W["Cst"], g.W["Cbf"]
                    g.nst = TLV(nstP.t, gi * nh, (gi + 1) * nh, t + "nst")
                    g.nbf = TLV(nbfP.t, gi * nh, (gi + 1) * nh, t + "nbf")
                    g.mprev = TLV(mprevP.t, gi * nh, (gi + 1) * nh, t + "mprev")
                else:
                    g.Sst, g.Sbf = g.W["Sst"], g.W["Sbf"]
                    g.W["dB"] = g.W["dG"]
                    g.W["gre"] = g.W["wT"]
                    g.W["osb"] = g.W["dG"]
                (self.ga if kind == "a" else self.gb).append(g)
        self.alt = dict(Cst=sb("alt_Cst", [128, 1024], F32), Sst=sb("alt_Sst", [128, 1024], F32),
                        nst=sb("alt_nst", [128, NH], F32), mprev=sb("alt_mprev", [128, NH], F32))
        self.gaS, self.gbS = Grp(), Grp()
        for g, kind, spec, small, grps in ((self.gaS, "a", specA_big, smallA, self.ga), (self.gbS, "b", specB_big, smallB, self.gb)):
            g.h0, g.nh, g.kind = 0, NH, kind
            g.PS = PS.sub([0, 1, 2, 3] if kind == "a" else [4, 5, 6, 7])
            g.W = {}
            for n, dt in spec:
                g.W[n] = TLP(parents[(kind, n)].t, [gg.W[n] for gg in grps])
            for n, wd in small(NH):
                g.W[n] = sb("S%s_%s" % (kind, n), [128, wd], F32)
            if kind == "a":
                g.Cst, g.Cbf = g.W["Cst"], g.W["Cbf"]
                g.nst = TLP(nstP.t, [gg.nst for gg in grps])
                g.nbf = TLP(nbfP.t, [gg.nbf for gg in grps])
                g.mprev = TLP(mprevP.t, [gg.mprev for gg in grps])
            else:
                g.Sst, g.Sbf = g.W["Sst"], g.W["Sbf"]
                g.W["dB"] = g.W["dG"]
                g.W["gre"] = g.W["wT"]
                g.W["osb"] = g.W["dG"]

    def mlstm_chunk(self, g, L, c0, q0, full, fm, tm):
        P, PS, C, W, cst = self.P, g.PS, self.C, g.W, self.cst
        bank = PS.banks
        h0, nh = g.h0, g.nh
        cn = str(L)
        GP = tm["GP"]
        lf = GP[0:L, 8 + h0:8 + h0 + nh]
        li = GP[0:L, h0:h0 + nh]
        HL = nh * L
        assert HL <= 512

        wide = nh * 128 > 512

        def m2():
            return PS.pair() if wide else PS.one()

        def f2(b_):
            return PS.f32(b_, 2) if wide else PS.f32(b_)

        def bl2(b_):
            return [bank[b_], bank[b_ + 1]] if wide else [bank[b_]]

        def hb2(b_, h_):
            return bank[b_ + (h_ * 128) // 512]
        identb, onesb = self.identb, self.onesb
        qaT, kaT, ka, va, oa = fm["qaT"], fm["kaT"], tm["ka"], tm["va"], tm["oa"]
        gc = slice(h0 * 128, (h0 + nh) * 128)

        def v3(ap2d):
            return ap2d.rearrange("p (h x) -> p h x", h=nh)

        def bcl(ap, n):
            return ap[:, :, None].to_broadcast([L, nh, n])
        bs = PS.one()
        _mm(P, PS.f32(bs)[0:L, 0:nh], C("uinc" + cn, L), lf, True, True, [cst, GP], [bank[bs]])
        _mm(P, PS.f32(bs)[:, nh:2 * nh], C("ones", L)[:, 0:128], lf, True, True, [cst, GP], [bank[bs]])
        sm1 = W["sm1"]
        _cp(P, "scalar", sm1[:, :], PS.f32(bs)[:, 0:2 * nh], [bank[bs]], [sm1])
        gg = W["g"]
        _tt(P, "vector", gg[0:L, :], li, sm1[0:L, 0:nh], ALU.subtract, [GP, sm1], [gg])
        dgx = W["dgx"]
        _tt(P, "gpsimd", v3(dgx[0:L, 0:HL]), C("ident", L)[:, None, 0:L].to_broadcast([L, nh, L]),
            bcl(gg[0:L, :], L), ALU.mult, [cst, gg], [dgx])
        br = PS.one()
        _mm(P, PS.f32(br)[0:L, 0:HL], C("ones", L)[:, 0:L], dgx[0:L, 0:HL], True, True, [cst, dgx], [bank[br]])
        R = W["R"]
        R3 = v3(R[0:L, 0:HL])
        _tt(P, "vector", R3, v3(PS.f32(br)[0:L, 0:HL]), C("neg" + cn, L)[:, None, :].to_broadcast([L, nh, L]),
            ALU.add, [bank[br], cst], [R])
        cm = W["cm"]
        _red(P, "vector", cm[0:L, :], R3, ALU.max, [R], [cm])
        _tt(P, "vector", R3, R3, bcl(cm[0:L, :], L), ALU.subtract, [R, cm], [R])
        _act(P, R[0:L, 0:HL], R[0:L, 0:HL], AF.Exp, [R], [R])
        if full:
            bq = PS.one()
            for h in range(nh):
                _mm(P, PS.f32(bq)[0:L, h * L:(h + 1) * L], qaT[:, h0 + h, c0:c0 + L], kaT[:, h0 + h, c0:c0 + L],
                    True, True, [qaT, kaT], [bank[bq]])
            sloc = W["sloc"]
            _tt(P, "vector", sloc[0:L, 0:HL], PS.f32(bq)[0:L, 0:HL], R[0:L, 0:HL], ALU.mult, [bank[bq], R], [sloc])
            rows = W["rows"]
            _red(P, "vector", rows[0:L, :], v3(sloc[0:L, 0:HL]), ALU.add, [sloc], [rows])
            bt = PS.one()
            for h in range(nh):
                _tr(P, PS.bf(bt)[0:L, h * L:(h + 1) * L], sloc[0:L, h * L:(h + 1) * L], identb[0:L, 0:L],
                    [sloc, identb], [bank[bt]])
            sTsb = W["sTsb"]
            _cp(P, "scalar", sTsb[0:L, 0:HL], PS.bf(bt)[0:L, 0:HL], [bank[bt]], [sTsb])
            bn = m2()
            for h in range(nh):
                _mm(P, f2(bn)[0:L, h * 128:(h + 1) * 128], sTsb[0:L, h * L:(h + 1) * L],
                    va[0:L, (h0 + h) * 128:(h0 + h + 1) * 128], True, True, [sTsb, va], [hb2(bn, h)])
            nl = W["nl"]
            _cp(P, "scalar", nl[0:L, :], f2(bn)[0:L, :], bl2(bn), [nl])
            gs = W["gs"]
            _tt(P, "gpsimd", gs[0:L, :], oa[0:L, gc], self.gA[0:L, gc], ALU.mult, [oa, self.gA], [gs])
        b2 = PS.one()
        _mm(P, PS.f32(b2)[0:L, 0:nh], C("sel" + cn, L)[:, 0:L], cm[0:L, :], True, True, [cst, cm], [bank[b2]])
        gl = W["gl"]
        _tt(P, "vector", gl[0:L, :], gg[0:L, :], PS.f32(b2)[0:L, 0:nh], ALU.subtract, [gg, bank[b2]], [gl])
        _act(P, gl[0:L, :], gl[0:L, :], AF.Exp, [gl], [gl])
        wlk = W["wlk"]
        _tt(P, "gpsimd", v3(wlk[0:L, :]), v3(ka[0:L, gc]), bcl(gl[0:L, :], 128), ALU.mult, [ka, gl], [wlk])
        bd = m2()
        for h in range(nh):
            _mm(P, f2(bd)[:, h * 128:(h + 1) * 128], wlk[0:L, h * 128:(h + 1) * 128],
                va[0:L, (h0 + h) * 128:(h0 + h + 1) * 128], True, True, [wlk, va], [hb2(bd, h)])
        dCs, dns = W["dCs"], W["dns"]
        _cp(P, "scalar", dCs[:, :], f2(bd)[:, :], bl2(bd), [dCs])
        b3 = PS.one()
        for h in range(nh):
            _mm(P, PS.f32(b3)[:, h:h + 1], wlk[0:L, h * 128:(h + 1) * 128], onesb[0:L, 0:1], True, True,
                [wlk, onesb], [bank[b3]])
        _cp(P, "vector", dns[:, :], PS.f32(b3)[:, 0:nh], [bank[b3]], [dns])
        mprev, Cst, Cbf, nst, nbf = g.mprev, g.Cst, g.Cbf, g.nst, g.nbf
        mx, t12, fd = W["mx"], W["t12"], W["fd"]
        _tt(P, "vector", mx[0:L, 0:nh], mprev[0:L, :], cm[0:L, :], ALU.max, [mprev, cm], [mx])
        _tt(P, "vector", t12[0:L, 0:nh], cm[0:L, :], mx[0:L, 0:nh], ALU.subtract, [cm, mx], [t12])
        _tt(P, "vector", t12[0:L, nh:2 * nh], mprev[0:L, :], mx[0:L, 0:nh], ALU.subtract, [mprev, mx, t12], [t12])
        _act(P, fd[0:L, :], t12[0:L, :], AF.Exp, [t12], [fd])
        _cp(P, "vector", mx[0:L, nh:2 * nh], fd[0:L, 0:nh], [fd, mx], [mx])
        if full:
            bc_ = m2()
            for h in range(nh):
                _mm(P, f2(bc_)[0:L, h * 128:(h + 1) * 128], qaT[:, h0 + h, c0:c0 + L], Cbf[:, h * 128:(h + 1) * 128],
                    True, True, [qaT, Cbf], [hb2(bc_, h)])
            mt, en, dd, d2, a12 = W["mt"], W["en"], W["dd"], W["d2"], W["a12"]
            h1, nl, ssq, hnb = W["h1"], W["nl"], W["ssq"], W["hnb"]
            _tt(P, "vector", mt[0:L, :], sm1[0:L, 0:nh], mx[0:L, 0:nh], ALU.add, [sm1, mx], [mt])
            _act(P, en[0:L, :], mt[0:L, :], AF.Exp, [mt], [en], scale=-1.0)
            _cp(P, "scalar", h1[0:L, :], f2(bc_)[0:L, :], bl2(bc_), [h1])
            b4 = PS.one()
            for h in range(nh):
                _mm(P, PS.f32(b4)[0:L, h:h + 1], qaT[:, h0 + h, c0:c0 + L], nbf[:, h:h + 1], True, True,
                    [qaT, nbf], [bank[b4]])
            _tt(P, "vector", dd[0:L, :], fd[0:L, nh:2 * nh], PS.f32(b4)[0:L, 0:nh], ALU.mult, [fd, bank[b4]], [dd])
            _tt(P, "vector", d2[0:L, :], fd[0:L, 0:nh], W["rows"][0:L, :], ALU.mult, [fd, W["rows"]], [d2])
            _tt(P, "vector", dd[0:L, :], dd[0:L, :], d2[0:L, :], ALU.add, [dd, d2], [dd])
            _act(P, dd[0:L, :], dd[0:L, :], AF.Abs, [dd], [dd])
            _tt(P, "vector", dd[0:L, :], dd[0:L, :], en[0:L, :], ALU.max, [dd, en], [dd])
            P.V(lambda e: e.reciprocal(out=dd[0:L, :], in_=dd[0:L, :]), [dd], [dd])
            _tt(P, "vector", a12[0:L, 0:nh], fd[0:L, nh:2 * nh], dd[0:L, :], ALU.mult, [fd, dd], [a12])
            _tt(P, "vector", a12[0:L, nh:2 * nh], fd[0:L, 0:nh], dd[0:L, :], ALU.mult, [fd, dd, a12], [a12])
            _tt(P, "vector", v3(h1[0:L, :]), v3(h1[0:L, :]), bcl(a12[0:L, 0:nh], 128), ALU.mult, [h1, a12], [h1])
            _tt(P, "gpsimd", v3(nl[0:L, :]), v3(nl[0:L, :]), bcl(a12[0:L, nh:2 * nh], 128), ALU.mult, [nl, a12], [nl])
            _tt(P, "vector", h1[0:L, :], h1[0:L, :], nl[0:L, :], ALU.add, [h1, nl], [h1])
            _act(P, nl[0:L, :], h1[0:L, :], AF.Square, [h1, nl], [nl])
            _red(P, "vector", ssq[0:L, :], v3(nl[0:L, :]), ALU.add, [nl], [ssq])
            _act(P, ssq[0:L, :], ssq[0:L, :], AF.Ln, [ssq], [ssq], bias=EPS, scale=1.0 / DH)
            _act(P, ssq[0:L, :], ssq[0:L, :], AF.Exp, [ssq], [ssq], scale=-0.5)
            _tt(P, "vector", v3(h1[0:L, :]), v3(h1[0:L, :]), bcl(ssq[0:L, :], 128), ALU.mult, [h1, ssq], [h1])
            _tt(P, "gpsimd", hnb[0:L, :], h1[0:L, :], W["gs"][0:L, :], ALU.mult, [h1, W["gs"]], [hnb])
            bh = PS.one()
            for h in range(nh):
                _tr(P, PS.bf(bh)[:, h * L:(h + 1) * L], hnb[0:L, h * 128:(h + 1) * 128], identb[0:L, 0:L],
                    [hnb, identb], [bank[bh]])
            _cp(P, "scalar", self.mixT[:, h0:h0 + nh, q0:q0 + L], PS.bf(bh)[:, 0:HL].rearrange("p (h t) -> p h t", h=nh),
                [bank[bh]], [self.mixT])
        b5 = PS.one()
        _mm(P, PS.f32(b5)[:, 0:2 * nh], C("sel" + cn, L)[:, 0:128], mx[0:L, 0:2 * nh], True, True, [cst, mx], [bank[b5]])
        sm2, dC = W["sm2"], W["dC"]
        _cp(P, "scalar", sm2[:, :], PS.f32(b5)[:, 0:2 * nh], [bank[b5]], [sm2])
        _tt(P, "vector", dC[:, :], mprev[:, :], sm2[:, 0:nh], ALU.subtract, [mprev, sm2], [dC])
        _act(P, dC[:, :], dC[:, :], AF.Exp, [dC], [dC])

        def bc128(ap):
            return ap[:, :, None].to_broadcast([128, nh, 128])
        _tt(P, "vector", v3(Cst[:, :]), v3(Cst[:, :]), bc128(dC[:, :]), ALU.mult, [Cst, dC], [Cst])
        _tt(P, "gpsimd", v3(dCs[:, :]), v3(dCs[:, :]), bc128(sm2[:, nh:2 * nh]), ALU.mult, [dCs, sm2], [dCs])
        _tt(P, "vector", Cst[:, :], Cst[:, :], dCs[:, :], ALU.add, [Cst, dCs], [Cst])
        _tt(P, "vector", nst[:, :], nst[:, :], dC[:, :], ALU.mult, [nst, dC], [nst])
        _tt(P, "vector", dns[:, :], dns[:, :], sm2[:, nh:2 * nh], ALU.mult, [dns, sm2], [dns])
        _tt(P, "vector", nst[:, :], nst[:, :], dns[:, :], ALU.add, [nst, dns], [nst])
        _cp(P, "scalar", Cbf[:, :], Cst[:, :], [Cst], [Cbf])
        _cp(P, "vector", nbf[:, :], nst[:, :], [nst], [nbf])
        _tt(P, "vector", mprev[:, :], sm1[:, nh:2 * nh], sm2[:, 0:nh], ALU.add, [sm1, sm2, mprev], [mprev])

    def gdn_chunk(self, g, L, c0, q0, full, fm, tm):
        P, PS, C, W, cst = self.P, g.PS, self.C, g.W, self.cst
        bank = PS.banks
        h0, nh = g.h0, g.nh
        cn = str(L)
        GP = tm["GP"]
        zb = tm["zb"]
        logg = GP[0:L, 16 + h0:16 + h0 + nh]
        beta = GP[0:L, 24 + h0:24 + h0 + nh]
        HL = nh * L
        assert HL <= 512

        wide = nh * 128 > 512

        def m2():
            return PS.pair() if wide else PS.one()

        def f2(b_):
            return PS.f32(b_, 2) if wide else PS.f32(b_)

        def bl2(b_):
            return [bank[b_], bank[b_ + 1]] if wide else [bank[b_]]

        def hb2(b_, h_):
            return bank[b_ + (h_ * 128) // 512]
        identb = self.identb
        qpost, kpost, vpost = fm["q"], fm["k"], fm["v"]
        Sst, Sbf = g.Sst, g.Sbf
        gc = slice(h0 * 128, (h0 + nh) * 128)

        def v3(ap2d):
            return ap2d.rearrange("p (h x) -> p h x", h=nh)

        def bcl(ap, n):
            return ap[:, :, None].to_broadcast([L, nh, n])
        identL = C("ident", L)[:, None, 0:L].to_broadcast([L, nh, L])
        bs = PS.one()
        _mm(P, PS.f32(bs)[0:L, 0:nh], C("uinc" + cn, L), logg, True, True, [cst, GP], [bank[bs]])
        _mm(P, PS.f32(bs)[:, nh:2 * nh], C("ones", L)[:, 0:128], logg, True, True, [cst, GP], [bank[bs]])
        gsm = W["gsm"]
        _cp(P, "scalar", gsm[:, :], PS.f32(bs)[:, 0:2 * nh], [bank[bs]], [gsm])
        Gt = gsm[0:L, 0:nh]
        dG = W["dG"]
        _tt(P, "gpsimd", v3(dG[0:L, 0:HL]), identL, bcl(Gt, L), ALU.mult, [cst, gsm], [dG])
        bg = PS.one()
        _mm(P, PS.f32(bg)[:, 0:HL], C("ones", L)[:, 0:128], dG[0:L, 0:HL], True, True, [cst, dG], [bank[bg]])
        NZ = W["NZ"]
        _tt(P, "vector", v3(NZ[0:L, 0:HL]), v3(PS.f32(bg)[0:L, 0:HL]), bcl(Gt, L), ALU.subtract, [bank[bg], gsm], [NZ])
        _ts(P, "vector", NZ[0:L, 0:HL], NZ[0:L, 0:HL], 0.0, ALU.min, [NZ], [NZ])
        _act(P, NZ[0:L, 0:HL], NZ[0:L, 0:HL], AF.Exp, [NZ], [NZ])
        if full:
            gre, qg = W["gre"], W["qg"]
            _act(P, gre[:, 0:HL], PS.f32(bg)[:, 0:HL], AF.Exp, [bank[bg]], [gre])
            _tt(P, "vector", v3(qg[:, 0:HL]), qpost[:, h0:h0 + nh, c0:c0 + L], v3(gre[:, 0:HL]), ALU.mult,
                [qpost, gre], [qg])
        dB = W["dB"]
        _tt(P, "gpsimd", v3(dB[0:L, 0:HL]), identL, bcl(beta, L), ALU.mult, [cst, GP, dB], [dB])
        bb_ = PS.one()
        _mm(P, PS.f32(bb_)[0:L, 0:HL], C("ones", L)[:, 0:L], dB[0:L, 0:HL], True, True, [cst, dB], [bank[bb_]])
        wT = W["wT"]
        _tt(P, "gpsimd", v3(wT[0:L, 0:HL]), v3(NZ[0:L, 0:HL]),
            C("ustr" + cn, L)[:, None, :].to_broadcast([L, nh, L]), ALU.mult, [NZ, cst, wT], [wT])
        _tt(P, "vector", wT[0:L, 0:HL], wT[0:L, 0:HL], PS.f32(bb_)[0:L, 0:HL], ALU.mult, [wT, bank[bb_]], [wT])
        if full:
            dTi = W["dTi"]
            _tt(P, "gpsimd", v3(dTi[0:L, 0:HL]), v3(NZ[0:L, 0:HL]),
                C("uinc" + cn, L)[:, None, :].to_broadcast([L, nh, L]), ALU.mult, [NZ, cst], [dTi])
        bk = PS.one()
        for h in range(nh):
            _mm(P, PS.f32(bk)[0:L, h * L:(h + 1) * L], kpost[:, h0 + h, c0:c0 + L], kpost[:, h0 + h, c0:c0 + L],
                True, True, [kpost], [bank[bk]])
        blocked = (L == 128)
        Pc, PTc, Pn_, PTn_ = W["P0"], W["PT0"], W["P1"], W["PT1"]
        Mf = W["Mf"] if blocked else Pc
        _stt(P, "vector", Mf[0:L, 0:HL], PS.f32(bk)[0:L, 0:HL], -1.0, wT[0:L, 0:HL], ALU.mult, ALU.mult,
             [bank[bk], wT], [Mf])
        if full:
            bq = PS.one()
            for h in range(nh):
                _mm(P, PS.f32(bq)[0:L, h * L:(h + 1) * L], kpost[:, h0 + h, c0:c0 + L], qpost[:, h0 + h, c0:c0 + L],
                    True, True, [kpost, qpost], [bank[bq]])
            qkT = W["qkT"]
            _tt(P, "vector", qkT[0:L, 0:HL], PS.f32(bq)[0:L, 0:HL], W["dTi"][0:L, 0:HL], ALU.mult,
                [bank[bq], W["dTi"]], [qkT])
        bt = PS.one()
        for h in range(nh):
            _tr(P, PS.bf(bt)[0:L, h * L:(h + 1) * L], Mf[0:L, h * L:(h + 1) * L], identb[0:L, 0:L],
                [Mf, identb], [bank[bt]])
        if blocked:
            bdm = C("bd32", L)[:, None, :].to_broadcast([L, nh, L])
            MoT, MoT2 = W["MoT"], W["MoT2"]
            _tt(P, "gpsimd", v3(Pc[0:L, 0:HL]), v3(Mf[0:L, 0:HL]), bdm, ALU.mult, [Mf, cst], [Pc])
            _tt(P, "vector", v3(PTc[0:L, 0:HL]), v3(PS.bf(bt)[0:L, 0:HL]), bdm, ALU.mult, [bank[bt], cst], [PTc])
            _tt(P, "vector", v3(MoT[0:L, 0:HL]), v3(PS.bf(bt)[0:L, 0:HL]),
                C("o64", L)[:, None, :].to_broadcast([L, nh, L]), ALU.mult, [bank[bt], cst], [MoT])
            _tt(P, "vector", v3(MoT2[0:L, 0:HL]), v3(PS.bf(bt)[0:L, 0:HL]),
                C("offL128", L)[:, None, :].to_broadcast([L, nh, L]), ALU.mult, [bank[bt], cst], [MoT2])
        else:
            _cp(P, "scalar", PTc[0:L, 0:HL], PS.bf(bt)[0:L, 0:HL], [bank[bt]], [PTc])
        Tacc = W["Tacc"]
        _tt(P, "gpsimd", v3(Tacc[0:L, 0:HL]), v3(Pc[0:L, 0:HL]), identL, ALU.add, [Pc, cst], [Tacc])
        nlev = 4 if blocked else NLEV[L]
        for lev in range(1, nlev + 1):
            b1 = PS.one()
            for h in range(nh):
                sl = slice(h * L, (h + 1) * L)
                _mm(P, PS.f32(b1)[0:L, sl], Pc[0:L, sl], PTc[0:L, sl], True, True, [Pc, PTc], [bank[b1]])
            _cp(P, "scalar", PTn_[0:L, 0:HL], PS.f32(b1)[0:L, 0:HL], [bank[b1]], [PTn_])
            if lev < nlev:
                b2 = PS.one()
                for h in range(nh):
                    sl = slice(h * L, (h + 1) * L)
                    _mm(P, PS.f32(b2)[0:L, sl], PTc[0:L, sl], Pc[0:L, sl], True, True, [Pc, PTc], [bank[b2]])
                _cp(P, "vector", Pn_[0:L, 0:HL], PS.f32(b2)[0:L, 0:HL], [bank[b2]], [Pn_])
            b3 = PS.one()
            for h in range(nh):
                sl = slice(h * L, (h + 1) * L)
                _mm(P, PS.f32(b3)[0:L, sl], PTn_[0:L, sl], Tacc[0:L, sl], True, True, [PTn_, Tacc], [bank[b3]])
            _tt(P, "vector", Tacc[0:L, 0:HL], Tacc[0:L, 0:HL], PS.f32(b3)[0:L, 0:HL], ALU.add, [Tacc, bank[b3]], [Tacc])
            Pc, PTc, Pn_, PTn_ = Pn_, PTn_, Pc, PTc
        if blocked:
            TbT, Xt = Pn_, PTn_
            for Mo in (W["MoT"], W["MoT2"]):
                b7 = PS.one()
                for h in range(nh):
                    sl = slice(h * L, (h + 1) * L)
                    _tr(P, PS.bf(b7)[0:L, sl], Tacc[0:L, sl], identb[0:L, 0:L], [Tacc, identb], [bank[b7]])
                _cp(P, "scalar", TbT[0:L, 0:HL], PS.bf(b7)[0:L, 0:HL], [bank[b7]], [TbT])
                b8 = PS.one()
                for h in range(nh):
                    sl = slice(h * L, (h + 1) * L)
                    _mm(P, PS.f32(b8)[0:L, sl], Mo[0:L, sl], Tacc[0:L, sl], True, True, [Mo, Tacc], [bank[b8]])
                _cp(P, "scalar", Xt[0:L, 0:HL], PS.f32(b8)[0:L, 0:HL], [bank[b8]], [Xt])
                b9 = PS.one()
                for h in range(nh):
                    sl = slice(h * L, (h + 1) * L)
                    _mm(P, PS.f32(b9)[0:L, sl], TbT[0:L, sl], Xt[0:L, sl], True, True, [TbT, Xt], [bank[b9]])
                _tt(P, "vector", Tacc[0:L, 0:HL], Tacc[0:L, 0:HL], PS.f32(b9)[0:L, 0:HL], ALU.add, [Tacc, bank[b9]], [Tacc])
        gsl, gLe = W["gsmall"], W["gLe"]
        _act(P, gsl[0:L, 0:nh], Gt, AF.Exp, [gsm], [gsl])
        _tt(P, "vector", gsl[0:L, 0:nh], gsl[0:L, 0:nh], beta, ALU.mult, [gsl, GP], [gsl])
        _tt(P, "vector", gsl[0:L, nh:2 * nh], gsm[0:L, nh:2 * nh], Gt, ALU.subtract, [gsm, gsl], [gsl])
        _act(P, gsl[0:L, nh:2 * nh], gsl[0:L, nh:2 * nh], AF.Exp, [gsl], [gsl])
        _act(P, gLe[:, :], gsm[:, nh:2 * nh], AF.Exp, [gsm], [gLe])
        kbg, kdec, vbt = W["kbg"], W["kdec"], W["vbt"]
        bkt = PS.one()
        for h in range(nh):
            _tr(P, PS.bf(bkt)[0:L, h * 128:(h + 1) * 128], kpost[:, h0 + h, c0:c0 + L], identb[:, :], [kpost, identb],
                [bank[bkt]])
        _tt(P, "vector", v3(kbg[0:L, :]), v3(PS.bf(bkt)[0:L, 0:nh * 128]), bcl(gsl[0:L, 0:nh], 128), ALU.mult,
            [bank[bkt], gsl], [kbg])
        _tt(P, "vector", v3(kdec[0:L, :]), v3(PS.bf(bkt)[0:L, 0:nh * 128]), bcl(gsl[0:L, nh:2 * nh], 128), ALU.mult,
            [bank[bkt], gsl], [kdec])
        bvt = PS.one()
        for h in range(nh):
            _tr(P, PS.bf(bvt)[0:L, h * 128:(h + 1) * 128], vpost[:, h0 + h, c0:c0 + L], identb[:, :], [vpost, identb],
                [bank[bvt]])
        _tt(P, "vector", v3(vbt[0:L, :]), v3(PS.bf(bvt)[0:L, 0:nh * 128]), bcl(beta, 128), ALU.mult,
            [bank[bvt], GP], [vbt])
        bw = PS.one()
        for h in range(nh):
            _mm(P, PS.f32(bw)[:, h * L:(h + 1) * L], kbg[0:L, h * 128:(h + 1) * 128], Tacc[0:L, h * L:(h + 1) * L],
                True, True, [kbg, Tacc], [bank[bw]])
        WT = W["WT"]
        _cp(P, "scalar", WT[:, 0:HL], PS.f32(bw)[:, 0:HL], [bank[bw]], [WT])
        bu = m2()
        for h in range(nh):
            _mm(P, f2(bu)[0:L, h * 128:(h + 1) * 128], Tacc[0:L, h * L:(h + 1) * L],
                vbt[0:L, h * 128:(h + 1) * 128], True, True, [Tacc, vbt], [hb2(bu, h)])
        U0 = W["U0"]
        _cp(P, "scalar", U0[0:L, :], f2(bu)[0:L, :], bl2(bu), [U0])
        if full:
            gz = W["gz"]
            _tt(P, "gpsimd", gz[0:L, :], zb[0:L, gc], self.gB[0:L, gc], ALU.mult, [zb, self.gB], [gz])
        bpu = m2()
        for h in range(nh):
            _mm(P, f2(bpu)[0:L, h * 128:(h + 1) * 128], WT[:, h * L:(h + 1) * L], Sbf[:, h * 128:(h + 1) * 128],
                True, True, [WT, Sbf], [hb2(bpu, h)])
        u = W["u"]
        _tt(P, "vector", u[0:L, :], U0[0:L, :], f2(bpu)[0:L, :], ALU.subtract, [U0] + bl2(bpu), [u])
        if full:
            bo = m2()
            for h in range(nh):
                o_ = f2(bo)[0:L, h * 128:(h + 1) * 128]
                _mm(P, o_, W["qg"][:, h * L:(h + 1) * L], Sbf[:, h * 128:(h + 1) * 128], True, False,
                    [W["qg"], Sbf], [hb2(bo, h)])
                _mm(P, o_, W["qkT"][0:L, h * L:(h + 1) * L], u[0:L, h * 128:(h + 1) * 128], False, True,
                    [W["qkT"], u], [hb2(bo, h)])
            osb = W["osb"]
            _cp(P, "scalar", osb[0:L, :], f2(bo)[0:L, :], bl2(bo) + [osb], [osb])
        bss = m2()
        for h in range(nh):
            _mm(P, f2(bss)[:, h * 128:(h + 1) * 128], kdec[0:L, h * 128:(h + 1) * 128],
                u[0:L, h * 128:(h + 1) * 128], True, True, [kdec, u], [hb2(bss, h)])
        _tt(P, "vector", v3(Sst[:, :]), v3(Sst[:, :]), gLe[:, :, None].to_broadcast([128, nh, 128]),
            ALU.mult, [Sst, gLe], [Sst])
        _tt(P, "vector", Sst[:, :], Sst[:, :], f2(bss)[:, :], ALU.add, [Sst] + bl2(bss), [Sst])
        _cp(P, "scalar", Sbf[:, :], Sst[:, :], [Sst], [Sbf])
        if full:
            ob, ssq = W["ob"], W["ssq2"]
            _act(P, U0[0:L, :], osb[0:L, :], AF.Square, [osb, U0], [U0])
            _red(P, "vector", ssq[0:L, :], v3(U0[0:L, :]), ALU.add, [U0], [ssq])
            _act(P, ssq[0:L, :], ssq[0:L, :], AF.Ln, [ssq], [ssq], bias=EPS, scale=1.0 / DH)
            _act(P, ssq[0:L, :], ssq[0:L, :], AF.Exp, [ssq], [ssq], scale=-0.5)
            _tt(P, "vector", v3(osb[0:L, :]), v3(osb[0:L, :]), bcl(ssq[0:L, :], 128), ALU.mult, [osb, ssq], [osb])
            _tt(P, "gpsimd", ob[0:L, :], osb[0:L, :], W["gz"][0:L, :], ALU.mult, [osb, W["gz"]], [ob])
            bh = PS.one()
            for h in range(nh):
                _tr(P, PS.bf(bh)[:, h * L:(h + 1) * L], ob[0:L, h * 128:(h + 1) * 128], identb[0:L, 0:L],
                    [ob, identb], [bank[bh]])
            _cp(P, "scalar", self.mixT[:, 8 + h0:8 + h0 + nh, q0:q0 + L],
                PS.bf(bh)[:, 0:HL].rearrange("p (h t) -> p h t", h=nh), [bank[bh]], [self.mixT])

    def _refresh_bf(self):
        P, a, b = self.P, self.gaS, self.gbS
        _cp(P, "scalar", a.Cbf[:, :], a.Cst[:, :], [a.Cst], [a.Cbf])
        _cp(P, "vector", a.nbf[:, :], a.nst[:, :], [a.nst], [a.nbf])
        _cp(P, "scalar", b.Sbf[:, :], b.Sst[:, :], [b.Sst], [b.Sbf])

    def _state_tiles(self):
        return [self.gaS.Cst, self.gaS.nst, self.gaS.mprev, self.gbS.Sst]

    def _use_set(self, i):
        a, b = self.gaS, self.gbS
        if not hasattr(self, "_set0"):
            self._set0 = dict(Cst=a.Cst, Sst=b.Sst, nst=a.nst, mprev=a.mprev)
        st = self._set0 if i == 0 else self.alt
        a.Cst, a.nst, a.mprev, b.Sst = st["Cst"], st["nst"], st["mprev"], st["Sst"]
        a.W["Cst"], b.W["Sst"] = st["Cst"], st["Sst"]

    def load_state(self, j):
        P, io, a, b = self.P, self.io, self.gaS, self.gbS
        pairs = [
            (a.Cst[:, :].rearrange("p (h x) -> p h x", h=NH), io["sC"][j].rearrange("h d e -> d h e")),
            (b.Sst[:, :].rearrange("p (h x) -> p h x", h=NH), io["sS"][j].rearrange("h d e -> d h e")),
            (a.nst[:, :], io["sn"][j]),
            (a.mprev[:, :], io["sm"][j:j + 1, :].partition_broadcast(128)),
        ]
        P.dma("sync", "s2ld", [(lambda e, o=o, i=i: e.dma_start(out=o, in_=i)) for o, i in pairs], w=self._state_tiles())

    def store_state(self, dC, dS, dn, dm):
        P, a, b = self.P, self.gaS, self.gbS
        pairs = [
            (dC.rearrange("h d e -> d h e"), a.Cst[:, :].rearrange("p (h x) -> p h x", h=NH)),
            (dS.rearrange("h d e -> d h e"), b.Sst[:, :].rearrange("p (h x) -> p h x", h=NH)),
            (dn, a.nst[:, :]),
            (dm, a.mprev[0:1, :]),
        ]
        P.dma("sync", "s2st", [(lambda e, o=o, i=i: e.dma_start(out=o, in_=i)) for o, i in pairs], r=self._state_tiles())

    def tm_load(self, L, r0, full, tm):
        P, S = self.P, self.S
        fns = [
            lambda e: e.dma_start(out=tm["ka"][0:L, :], in_=S["ka"][r0:r0 + L, :]),
            lambda e: e.dma_start(out=tm["va"][0:L, :], in_=S["va"][r0:r0 + L, :]),
            lambda e: e.dma_start(out=tm["GP"][0:L, :], in_=S["gt"][r0:r0 + L, :]),
        ]
        wl_ = [tm["ka"], tm["va"], tm["GP"]]
        if full:
            rq = r0 - T0
            fns += [
                lambda e: e.dma_start(out=tm["oa"][0:L, :], in_=S["oa"][rq:rq + L, :]),
                lambda e: e.dma_start(out=tm["zb"][0:L, :], in_=S["zb"][rq:rq + L, :]),
            ]
            wl_ += [tm["oa"], tm["zb"]]
        P.dma("sync", "s2t%d" % self.tm.index(tm), fns, w=wl_)

    def fm_load(self, sc, fm):
        P, S = self.P, self.S
        t0 = sc * 128
        full = sc >= 8
        fns = [
            lambda e: e.dma_start(out=fm["k"][:, :, :], in_=S["kb"][:, :, t0:t0 + 128].rearrange("h d t -> d h t")),
            lambda e: e.dma_start(out=fm["v"][:, :, :], in_=S["vb"][:, :, t0:t0 + 128].rearrange("h d t -> d h t")),
        ]
        wl_ = [fm["k"], fm["v"]]
        if full:
            tq = t0 - T0
            fns += [
                lambda e: e.dma_start(out=fm["q"][:, :, :], in_=S["qb"][:, :, t0:t0 + 128].rearrange("h d t -> d h t")),
                lambda e: e.dma_start(out=fm["qaT"][:, :, :], in_=S["qa"][:, :, tq:tq + 128].rearrange("h d t -> d h t")),
                lambda e: e.dma_start(out=fm["kaT"][:, :, :], in_=S["kaT"][:, :, tq:tq + 128].rearrange("h d t -> d h t")),
            ]
            wl_ += [fm["q"], fm["qaT"], fm["kaT"]]
        P.dma("sync", "s2f%d" % self.fm.index(fm), fns, w=wl_)

    def run(self):
        P, S, io = self.P, self.S, self.io
        for t in self._state_tiles():
            _ms(P, "gpsimd", t[:, :], 0.0, [t])
        self._refresh_bf()
        chunks = []
        for sc in range(NSUB):
            sample = sc == NSUB - 1
            L = LS if sample else LP
            for ch in range(128 // L):
                chunks.append((sc, ch, L, sample))
        self.fm_load(0, self.fm[0])
        self.tm_load(chunks[0][2], 0, False, self.tm[0])
        for idx, (sc, ch, L, sample) in enumerate(chunks):
            full = sc >= 8
            c0 = ch * L
            r0 = sc * 128 + c0
            q0 = r0 - T0
            fm, tm = self.fm[sc % 2], self.tm[idx % 2]
            if idx + 1 < len(chunks):
                nsc, nch, nL, _ = chunks[idx + 1]
                if nsc != sc:
                    self.fm_load(nsc, self.fm[nsc % 2])
                self.tm_load(nL, nsc * 128 + nch * nL, nsc >= 8, self.tm[(idx + 1) % 2])
            if sample:
                if ch == 0:
                    self._use_set(0)
                    self.load_state(0)
                if ch + 1 < 128 // L:
                    self._use_set((ch + 1) % 2)
                    self.load_state(ch + 1)
                self._use_set(ch % 2)
                self._refresh_bf()
            lists = []
            for g in ([self.gaS] if sample else self.ga):
                P.capture()
                self.mlstm_chunk(g, L, c0, q0, full, fm, tm)
                lists.append(P.end_capture())
            for g in ([self.gbS] if sample else self.gb):
                P.capture()
                self.gdn_chunk(g, L, c0, q0, full, fm, tm)
                lists.append(P.end_capture())
            P.replay(lists)
            if sample:
                self.store_state(io["oC"][ch], io["oS"][ch], io["on"][ch], io["om"][ch:ch + 1, :])
            if sc == 7 and ch == 128 // L - 1:
                for t in self._state_tiles():
                    _ts(P, "vector", t[:, :], t[:, :], self.flag[:, 0:1], ALU.mult, [t, self.flag], [t])
                self._refresh_bf()
            if sc == 15 and ch == 128 // L - 1:
                self.store_state(io["pC"], io["pS"], io["pn"], io["pm"])


def _phase3(B, P, PS, cst, C, identb, mixT, x_d, norms_d, modd, wout_d, wup_d, wdn_d, y_d):
    nc = B.nc
    bank = PS.banks
    X = Ctx(nc)
    x1 = X.sb("x1", [128, NSUBQ, D], F32)
    x1s = [TL(x1.t, "x1_%d" % i) for i in range(NSUBQ)]
    wr = [X.sb("p3_w%d" % i, [128, 16, 512], BF16) for i in range(3)]
    mt0 = X.sb("p3_mt0", [128, D], F32)
    mt1 = X.sb("p3_mt1", [128, D], F32)
    st = [X.sb("p3_st%d" % i, [128, 4], F32) for i in range(2)]
    wi = [0]

    def wslot():
        s = wr[wi[0] % 3]
        k = "wr%d" % (wi[0] % 3)
        wi[0] += 1
        return s, k

    def mtile(i):
        return mt0 if i < 8 else mt1

    for i in range(NSUBQ):
        _ld(P, "sync", "p3x%d" % (i % 3), x1[:, i, :], x_d[T0 + i * 128:T0 + (i + 1) * 128, :], w=[x1s[i]])

    XA = Ctx(nc)
    tmpf = XA.sb("p3_tmp", [128, D], F32)
    hb = XA.sb("p3_hb", [128, D], BF16)
    sgA = [XA.sb("p3_sgA%d" % i, [128, 512], F32) for i in range(2)]
    _mod_tiles(P, modd, 4096, mt0, mt1, "p3G")
    ne = 0
    for cb in range(4):
        slot, key = wslot()
        _ld(P, "gpsimd", key, slot[:], wout_d[:, cb * 512:(cb + 1) * 512].rearrange("(k p) c -> p k c", p=128), w=[slot])
        for i in range(NSUBQ):
            bk = PS.one()
            for k in range(16):
                _mm(P, PS.f32(bk)[:, :], mixT[:, k, i * 128:(i + 1) * 128], slot[:, k, :], k == 0, k == 15,
                    [mixT, slot], [bank[bk]])
            sg = sgA[ne % 2]
            ne += 1
            _tt(P, "vector", sg[:], PS.f32(bk)[:, :], mtile(i)[:, cb * 512:(cb + 1) * 512], ALU.mult,
                [bank[bk], mtile(i)], [sg])
            _tt(P, "vector", x1[:, i, cb * 512:(cb + 1) * 512], x1[:, i, cb * 512:(cb + 1) * 512], sg[:], ALU.add,
                [x1s[i], sg], [x1s[i]])

    def norm_tiles(col_sc, col_sh, nrow, first):
        _ld(P, "sync", "p3n", tmpf[:], norms_d[nrow:nrow + 1, :].partition_broadcast(128), w=[tmpf])
        if first:
            _ld(P, "sync", "p3m0", mt0[:], modd[0:1, col_sc:col_sc + D].partition_broadcast(128), w=[mt0])
            _ld(P, "sync", "p3m1", mt1[:], modd[0:1, col_sh:col_sh + D].partition_broadcast(128), w=[mt1])
        else:
            for t, col, key in ((mt0, col_sc, "p3m0"), (mt1, col_sh, "p3m1")):
                fns = []
                for b in range(NSEQ):
                    fns.append(lambda e, b=b, t=t, col=col: e.dma_start(
                        out=t[8 * b:8 * b + 8, :], in_=modd[1 + b:2 + b, col:col + D].partition_broadcast(8)))
                P.dma("sync", key, fns, w=[t])
        _stt(P, "vector", mt0[:], mt0[:], 1.0, tmpf[:], ALU.add, ALU.mult, [mt0, tmpf], [mt0])

    def norm_sub(i, out_ap, out_tl, junk):
        s_ = st[i % 2]
        _act(P, junk[:], x1[:, i, :], AF.Square, [x1s[i]], [junk, s_], accum=s_[:, 0:1])
        _act(P, s_[:, 1:2], s_[:, 0:1], AF.Ln, [s_], [s_], bias=EPS, scale=1.0 / D)
        _act(P, s_[:, 2:3], s_[:, 1:2], AF.Exp, [s_], [s_], scale=-0.5)
        _stt(P, "vector", tmpf[:], x1[:, i, :], s_[:, 2:3], mt0[:], ALU.mult, ALU.mult, [x1s[i], s_, mt0], [tmpf])
        _tt(P, "vector", out_ap, tmpf[:], mt1[:], ALU.add, [tmpf, mt1], [out_tl])

    h2T = mixT
    nitems = []
    for i in range(NSUBQ):
        P.capture()
        if i == 0:
            norm_tiles(8192, 6144, 1, True)
        if i == 8:
            norm_tiles(8192, 6144, 1, False)
        norm_sub(i, hb[:], hb, hb)
        for half in range(2):
            bk = PS.one()
            for kk in range(8):
                k = half * 8 + kk
                _tr(P, PS.bf(bk)[:, kk * 128:(kk + 1) * 128], hb[:, k * 128:(k + 1) * 128], identb[:],
                    [hb, identb], [bank[bk]])
            _cp(P, "scalar" if half else "vector", h2T[:, half * 8:(half + 1) * 8, i * 128:(i + 1) * 128],
                PS.bf(bk).rearrange("p (k t) -> p k t", k=8), [bank[bk]], [h2T])
        nitems.append(P.end_capture())
    P.replay_pipe(nitems, 3, burst=1)
    P.barrier()
    P.flush()
    XA.close()

    XB = Ctx(nc)
    actT = XB.sb("actT", [128, 8, NQ], BF16)
    sgB = [XB.sb("p3_sgB%d" % i, [128, 512], F32) for i in range(2)]
    rl = [XB.sb("p3_rl%d" % i, [128, 512], F32) for i in range(2)]
    _mod_tiles(P, modd, 10240, mt0, mt1, "p3G")
    ttiles = [(0, 512), (512, 512), (1024, 128)]
    ne = 0
    nr = 0
    for fb in range(8):
        for half in range(2):
            slot, key = wslot()
            c0 = fb * 1024 + half * 512
            _ld(P, "gpsimd", key, slot[:], wup_d[:, c0:c0 + 512].rearrange("(k p) c -> p k c", p=128), w=[slot])
            for sbk in range(4):
                for (t0, nt) in ttiles:
                    bk = PS.one()
                    for k in range(16):
                        _mm(P, PS.f32(bk)[:, 0:nt], slot[:, k, sbk * 128:(sbk + 1) * 128], h2T[:, k, t0:t0 + nt],
                            k == 0, k == 15, [slot, h2T], [bank[bk]])
                    r_ = rl[nr % 2]
                    nr += 1
                    _act(P, r_[:, 0:nt], PS.f32(bk)[:, 0:nt], AF.Relu, [bank[bk]], [r_])
                    _tt(P, "vector", actT[:, half * 4 + sbk, t0:t0 + nt], r_[:, 0:nt], r_[:, 0:nt], ALU.mult,
                        [r_], [actT])
        sd = []
        for half in range(2):
            slot, key = wslot()
            r0 = fb * 1024 + half * 512
            sv = slot[:, :, :].rearrange("p k c -> p (k c)").rearrange("p (s c) -> p s c", s=4)
            _ld(P, "gpsimd", key, sv, wdn_d[r0:r0 + 512, :].rearrange("(s p) c -> p s c", p=128), w=[slot])
            sd.append((slot, sv))
        for i in range(NSUBQ):
            for cb in range(4):
                bk = PS.one()
                for s8 in range(8):
                    slot, sv = sd[s8 // 4]
                    _mm(P, PS.f32(bk)[:, :], actT[:, s8, i * 128:(i + 1) * 128], sv[:, s8 % 4, cb * 512:(cb + 1) * 512],
                        s8 == 0, s8 == 7, [actT, slot], [bank[bk]])
                sg = sgB[ne % 2]
                ne += 1
                _tt(P, "vector", sg[:], PS.f32(bk)[:, :], mtile(i)[:, cb * 512:(cb + 1) * 512], ALU.mult,
                    [bank[bk], mtile(i)], [sg])
                _tt(P, "vector", x1[:, i, cb * 512:(cb + 1) * 512], x1[:, i, cb * 512:(cb + 1) * 512], sg[:], ALU.add,
                    [x1s[i], sg], [x1s[i]])
    P.barrier()
    P.flush()
    XB.close()

    XC = Ctx(nc)
    tmpf = XC.sb("p3c_tmp", [128, D], F32)
    yo = [XC.sb("p3c_y%d" % i, [128, D], F32) for i in range(2)]
    junk = XC.sb("p3c_junk", [128, D], BF16)
    nitems = []
    for i in range(NSUBQ):
        P.capture()
        if i == 0:
            norm_tiles(14336, 12288, 2, True)
        if i == 8:
            norm_tiles(14336, 12288, 2, False)
        y_ = yo[i % 2]
        norm_sub(i, y_[:], y_, junk)
        _ld(P, "sync", "p3y%d" % (i % 2), y_d[i * 128:(i + 1) * 128, :], y_[:], r=[y_])
        nitems.append(P.end_capture())
    P.replay_pipe(nitems, 3, burst=1)
    P.barrier()
    P.flush()
    XC.close()
    X.close()


def kernel(**inputs):
    inp = {k: np.asarray(v) for k, v in inputs.items()}
    B = build(debug=False)
    sh = _shared_inputs(inp)
    in_maps = []
    for c in range(8):
        m = _core_inputs(inp, c, sh)
        in_maps.append({k: v for k, v in m.items() if k in B.ins})
    res = run_bass_kernel_spmd(B.nc, in_maps, core_ids=list(range(8)))
    r = [{k: np.asarray(v) for k, v in rr.items()} for rr in res.results]

    y_prompt = np.empty((4, 2048, D), np.float32)
    y_sample = np.empty((128, LS, D), np.float32)
    pC = np.empty((1, 4, NH, DH, DH), np.float32)
    pn = np.empty((1, 4, NH, DH), np.float32)
    pm = np.empty((1, 4, NH), np.float32)
    pS = np.empty((1, 4, NH, DH, DH), np.float32)
    pconv = np.empty((1, 4, 3, 3072), np.float32)
    sC = np.empty((1, 128, NH, DH, DH), np.float32)
    sn = np.empty((1, 128, NH, DH), np.float32)
    sm = np.empty((1, 128, NH), np.float32)
    sS = np.empty((1, 128, NH, DH, DH), np.float32)
    sconv = np.empty((1, 128, 3, 3072), np.float32)
    for c in range(8):
        b, half = c // 2, c % 2
        sl = slice(c * NSEQ, (c + 1) * NSEQ)
        o = r[c]
        y_prompt[b, half * T1:(half + 1) * T1] = o["y"][:T1]
        y_sample[sl] = o["y"][T1:].reshape(NSEQ, LS, D)
        if half == 1:
            pC[0, b] = o["pC"]
            pn[0, b] = o["pn_t"].T
            pm[0, b] = o["pm"][0]
            pS[0, b] = o["pS"]
            pconv[0, b] = o["pconv_t"].reshape(128, 24, 3).transpose(2, 1, 0).reshape(3, 3072)
        sC[0, sl] = o["oC"]
        sn[0, sl] = o["on_t"].transpose(0, 2, 1)
        sm[0, sl] = o["om"]
        sS[0, sl] = o["oS"]
        sconv[0, sl] = o["oconv_t"].reshape(128, 24, NSEQ, 3).transpose(2, 3, 1, 0).reshape(NSEQ, 3, 3072)
    return (y_prompt, y_sample, pC, pn, pm, pS, pconv, sC, sn, sm, sS, sconv)
```

```python
import numpy as np
import concourse.bass as bass
import concourse.mybir as mybir
from concourse.bass_utils import run_bass_kernel_spmd

F32 = mybir.dt.float32
BF16 = mybir.dt.bfloat16
AF = mybir.ActivationFunctionType
ALU = mybir.AluOpType
AX = mybir.AxisListType

D = 2048
KD = 16
NH = 8
DH = 128
T0 = 1024
T1 = 1024
TS = 128
NSEQ = 16
LS = 8
NTOK = T0 + T1 + TS
NQ = T1 + TS
NSUB = NTOK // 128
NSUBQ = NQ // 128
DFF = 8192
EPS = 1e-6
NEG = -1.0e30
WIN_COLS = 5120 + 4096 + 32

ENGS = ("tensor", "vector", "scalar", "gpsimd", "sync")


class Buf:
    __slots__ = ("name", "w", "r")

    def __init__(self, name=""):
        self.name = name
        self.w = None
        self.r = {}


class TL:
    def __init__(self, t, name=""):
        self.t = t
        self.buf = Buf(name)

    def __getitem__(self, k):
        return self.t[k]


class Prog:
    def __init__(self, nc):
        self.nc = nc
        self.sems = {}
        self.cnt = {}
        self.ops = {e: [] for e in ENGS}
        self.seen = {e: {} for e in ENGS}
        for e in ENGS:
            self.sems[e] = nc.alloc_semaphore(name="s_" + e)
            self.cnt[e] = 0
        self.dkeys = []
        self.nins = 0

    def key(self, name):
        k = "d_" + name
        if k not in self.sems:
            self.sems[k] = self.nc.alloc_semaphore(name="s" + k)
            self.cnt[k] = 0
            self.dkeys.append(k)
        return k

    def _deps(self, eng, reads, writes):
        deps = {}

        def add(kv):
            if kv is None:
                return
            k, v = kv
            if k == eng and eng == "tensor":
                return
            if deps.get(k, 0) < v:
                deps[k] = v
        for b in reads:
            add(b.buf.w)
        for b in writes:
            add(b.buf.w)
            for kv in b.buf.r.items():
                add(kv)
        out = []
        seen = self.seen[eng]
        for k, v in deps.items():
            if seen.get(k, 0) >= v:
                continue
            seen[k] = v
            out.append((k, v))
        return out

    def capture(self):
        self.cap = []
        return self.cap

    def end_capture(self):
        c = self.cap
        self.cap = None
        return c

    def replay(self, lists):
        lists = [l for l in lists if l]
        if not lists:
            return
        n = max(len(l) for l in lists)
        pos = [0] * len(lists)
        for i in range(1, n + 1):
            for j, l in enumerate(lists):
                tgt = (i * len(l) + n - 1) // n
                while pos[j] < tgt:
                    it = l[pos[j]]
                    pos[j] += 1
                    if it[0] == "op":
                        self.op(*it[1:])
                    else:
                        self.dma(*it[1:])

    def _emit_item(self, it):
        if it[0] == "op":
            self.op(*it[1:])
        else:
            self.dma(*it[1:])

    def replay_pipe(self, items, depth, burst=2):
        active = []
        nxt = 0

        def prefix(it):
            n = 0
            while n < len(it) and it[n][0] == "op" and it[n][1] == "tensor":
                n += 1
            return n

        def rw(ops):
            rs, ws = set(), set()
            for o in ops:
                for b_ in self._fl(o[-2]):
                    rs.add(id(b_.buf))
                for b_ in self._fl(o[-1]):
                    ws.add(id(b_.buf))
            return rs, ws

        def conflict(it):
            r1, w1 = rw(it)
            for a_ in active:
                r2, w2 = rw(a_[0][a_[1]:])
                if (w1 & (r2 | w2)) or (r1 & w2):
                    return True
            return False
        while nxt < len(items) or active:
            while nxt < len(items) and len(active) < depth and (
                    not active or (active[-1][1] - active[-1][2]) >= max(1, (len(active[-1][0]) - active[-1][2]) // depth)):
                it = items[nxt]
                if active and conflict(it):
                    break
                nxt += 1
                npre = prefix(it)
                for i in range(npre):
                    self._emit_item(it[i])
                if npre < len(it):
                    active.append([it, npre, npre])
            for a in list(active):
                for _ in range(burst):
                    if a[1] < len(a[0]):
                        self._emit_item(a[0][a[1]])
                        a[1] += 1
                if a[1] >= len(a[0]):
                    active.remove(a)

    @staticmethod
    def _fl(lst):
        out = []
        for b in lst:
            if hasattr(b, "kids"):
                out.extend(b.kids)
            else:
                out.append(b)
        return out

    def op(self, eng, fn, r=(), w=()):
        if getattr(self, "cap", None) is not None:
            self.cap.append(("op", eng, fn, tuple(r), tuple(w)))
            return
        r, w = self._fl(r), self._fl(w)
        waits = self._deps(eng, r, w)
        self.cnt[eng] += 1
        v = self.cnt[eng]
        self.ops[eng].append((waits, fn, eng, 1))
        for b in r:
            if b.buf.r.get(eng, 0) < v:
                b.buf.r[eng] = v
        for b in w:
            b.buf.w = (eng, v)
            b.buf.r = {}
        self.nins += 1

    def V(self, fn, r=(), w=()):
        self.op("vector", fn, r, w)

    def A(self, fn, r=(), w=()):
        self.op("scalar", fn, r, w)

    def G(self, fn, r=(), w=()):
        self.op("gpsimd", fn, r, w)

    def T(self, fn, r=(), w=()):
        self.op("tensor", fn, r, w)

    def dma(self, eng, key, fns, r=(), w=()):
        if not isinstance(fns, (list, tuple)):
            fns = [fns]
        if getattr(self, "cap", None) is not None:
            self.cap.append(("dma", eng, key, fns, tuple(r), tuple(w)))
            return
        r, w = self._fl(r), self._fl(w)
        key = self.key(key) if not key.startswith("d_") else key
        waits = self._deps(eng, r, w)
        for i, fn in enumerate(fns):
            self.cnt[key] += 16
            self.ops[eng].append((waits if i == 0 else [], fn, key, 16))
        v = self.cnt[key]
        for b in r:
            if b.buf.r.get(key, 0) < v:
                b.buf.r[key] = v
        for b in w:
            b.buf.w = (key, v)
            b.buf.r = {}
        self.nins += len(fns)

    def barrier(self):
        for e in ENGS:
            waits = []
            for k, v in self.cnt.items():
                if k == e or v == 0:
                    continue
                if self.seen[e].get(k, 0) >= v:
                    continue
                self.seen[e][k] = v
                waits.append((k, v))
            self.ops[e].append((waits, None, None, 0))

    def flush(self, final=False):
        nc = self.nc
        sems = self.sems
        ops = self.ops

        def run(e, name):
            for waits, fn, key, inc in ops[name]:
                for k, v in waits:
                    e.wait_ge(sems[k], v)
                if fn is not None:
                    fn(e).then_inc(sems[key], inc)

        with nc.Block() as block:
            @block.tensor
            def _(e):
                run(e, "tensor")

            @block.vector
            def _(e):
                run(e, "vector")

            @block.scalar
            def _(e):
                run(e, "scalar")

            @block.gpsimd
            def _(e):
                run(e, "gpsimd")

            @block.sync
            def _(e):
                run(e, "sync")
        self.ops = {e: [] for e in ENGS}


def _const_layout():
    off = {}
    c = 0
    for name, n in (("ident", 128), ("ones", 128), ("neg128", 128), ("uinc128", 128), ("ustr128", 128),
                    ("sel128", 128), ("bd32", 128), ("o64", 128), ("offL128", 128), ("neg8", 8), ("uinc8", 8), ("ustr8", 8), ("sel8", 128),
                    ("sign", 16)):
        off[name] = (c, n)
        c += n
    return off, c


CO, NCONST = _const_layout()


def _build_consts():
    a = np.zeros((128, NCONST), np.float32)

    def put(name, m):
        o, n = CO[name]
        a[: m.shape[0], o:o + m.shape[1]] = m
    put("ident", np.eye(128, dtype=np.float32))
    put("ones", np.ones((128, 128), np.float32))
    for L, sfx in ((128, "128"), (8, "8")):
        t = np.arange(L)
        neg = np.where(t[None, :] <= t[:, None], 0.0, NEG).astype(np.float32)
        uinc = (t[None, :] >= t[:, None]).astype(np.float32)
        ustr = (t[None, :] > t[:, None]).astype(np.float32)
        sel = np.zeros((L, 128), np.float32)
        sel[L - 1, :] = 1.0
        put("neg" + sfx, neg)
        put("uinc" + sfx, uinc)
        put("ustr" + sfx, ustr)
        put("sel" + sfx, sel)
    bd32 = np.zeros((128, 128), np.float32)
    bd64 = np.zeros((128, 128), np.float32)
    for i in range(0, 128, 32):
        bd32[i:i + 32, i:i + 32] = 1.0
    for i in range(0, 128, 64):
        bd64[i:i + 64, i:i + 64] = 1.0
    put("bd32", bd32)
    put("o64", bd64 - bd32)
    ofl = np.zeros((128, 128), np.float32)
    ofl[64:, :64] = 1.0
    put("offL128", ofl)
    sg = np.ones((128, 16), np.float32)
    sg[:, 0:8] = -1.0
    put("sign", sg)
    return a


class Builder:
    def __init__(self, debug=False, phases=(0, 1, 2, 3)):
        self.debug = debug
        self.phases = phases
        nc = bass.Bass("TRN2", target_bir_lowering=False)
        self.nc = nc
        self.P = Prog(nc)
        self.ins = {}
        self.outs = {}
        self.psum = TLBank(nc)

    def din(self, name, shape, dt=F32):
        t = self.nc.dram_tensor(name, list(shape), dt, kind="ExternalInput").ap()
        self.ins[name] = t
        return t

    def dout(self, name, shape, dt=F32):
        t = self.nc.dram_tensor(name, list(shape), dt, kind="ExternalOutput").ap()
        self.outs[name] = t
        return t

    def dscr(self, name, shape, dt):
        kind = "ExternalOutput" if self.debug else "Internal"
        t = self.nc.dram_tensor(name, list(shape), dt, kind=kind).ap()
        if self.debug:
            self.outs[name] = t
        return t


class TLBank:
    def __init__(self, nc):
        self.t = nc.alloc_psum_tensor("psum_all", [128, 4096], F32)
        self.banks = [TL(None, "bank%d" % i) for i in range(8)]
        self.ptr = 0

        self.ids = list(range(8))

    def sub(self, ids):
        o = TLBank.__new__(TLBank)
        o.t, o.banks, o.ptr, o.ids = self.t, self.banks, 0, list(ids)
        return o

    def one(self):
        i = self.ids[self.ptr]
        self.ptr = (self.ptr + 1) % len(self.ids)
        return i

    def pair(self):
        if self.ptr % 2:
            self.ptr = (self.ptr + 1) % len(self.ids)
        i = self.ids[self.ptr]
        self.ptr = (self.ptr + 2) % len(self.ids)
        return i

    def f32(self, i, n=1):
        return self.t[:, i * 512:(i + n) * 512]

    def bf(self, i, n=1):
        return self.t[:, i * 512:(i + n) * 512].bitcast(BF16)


def _mm(P, out, lhsT, rhs, start, stop, r, w):
    P.T(lambda e: e.matmul(out, lhsT=lhsT, rhs=rhs, start=start, stop=stop), r, w)


def _tr(P, out, in_, ident, r, w):
    P.T(lambda e: e.transpose(out=out, in_=in_, identity=ident), r, w)


def _act(P, out, in_, func, r, w, bias=None, scale=None, accum=None):
    kw = {}
    if bias is not None:
        kw["bias"] = bias
    if scale is not None:
        kw["scale"] = scale
    if accum is not None:
        kw["accum_out"] = accum
    P.A(lambda e: e.activation(out=out, in_=in_, func=func, **kw), r, w)


def _tt(P, eng, out, in0, in1, op, r, w):
    P.op(eng, lambda e: e.tensor_tensor(out=out, in0=in0, in1=in1, op=op), r, w)


def _ts(P, eng, out, in0, s1, op0, r, w, s2=None, op1=None):
    if op1 is None:
        P.op(eng, lambda e: e.tensor_single_scalar(out=out, in_=in0, scalar=s1, op=op0), r, w)
    else:
        P.op(eng, lambda e: e.tensor_scalar(out=out, in0=in0, scalar1=s1, scalar2=s2, op0=op0, op1=op1), r, w)


def _stt(P, eng, out, in0, scalar, in1, op0, op1, r, w):
    P.op(eng, lambda e: e.scalar_tensor_tensor(out=out, in0=in0, scalar=scalar, in1=in1, op0=op0, op1=op1), r, w)


def _red(P, eng, out, in_, op, r, w):
    P.op(eng, lambda e: e.tensor_reduce(out=out, in_=in_, axis=AX.X, op=op), r, w)


def _cp(P, eng, out, in_, r, w):
    if eng == "scalar":
        P.A(lambda e: e.activation(out=out, in_=in_, func=AF.Copy), r, w)
    else:
        P.op(eng, lambda e: e.tensor_copy(out=out, in_=in_), r, w)


def _ms(P, eng, ap, val, w):
    P.op(eng, lambda e: e.memset(ap, val), (), w)


def _ld(P, eng, key, out, in_, r=(), w=()):
    P.dma(eng, key, lambda e: e.dma_start(out=out, in_=in_), r, w)


class Ctx:
    def __init__(self, nc):
        self.nc = nc
        self.guards = []

    def sb(self, name, shape, dt):
        g = self.nc.sbuf_tensor(name, list(shape), dt)
        t = g.__enter__()
        self.guards.append(g)
        return TL(t, name)

    def close(self):
        for g in reversed(self.guards):
            g.__exit__(None, None, None)
        self.guards = []


def build(debug=False, phases=(0, 1, 2, 3)):
    B = Builder(debug, phases)
    nc, P, PS = B.nc, B.P, B.psum
    bankb = PS.banks

    x_d = B.din("x", [NTOK, D])
    c_d = B.din("c", [17, D])
    flag_d = B.din("flag", [128, 1])
    consts_d = B.din("consts", [128, NCONST])
    adaw_d = B.din("ada_w", [D, 16384])
    adab_d = B.din("ada_b", [1, 16384])
    win_d = B.din("w_in", [D, WIN_COLS])
    wout_d = B.din("w_out", [D, D])
    wup_d = B.din("w_up", [D, DFF])
    wdn_d = B.din("w_down", [DFF, D])
    norms_d = B.din("norms", [3, D])
    hnorm_d = B.din("hnorm", [2, 1024])
    gvec_d = B.din("gvec", [1, 32])
    convw_d = B.din("conv_w", [128, 24 * 4])
    sC_d = B.din("sC", [NSEQ, NH, DH, DH])
    sn_d = B.din("sn_t", [NSEQ, DH, NH])
    sm_d = B.din("sm", [NSEQ, NH])
    sS_d = B.din("sS", [NSEQ, NH, DH, DH])
    scv_d = B.din("sconv_t", [128, 24 * NSEQ * 3])

    y_d = B.dout("y", [NQ, D])
    pC_d = B.dout("pC", [NH, DH, DH])
    pn_d = B.dout("pn_t", [DH, NH])
    pm_d = B.dout("pm", [1, NH])
    pS_d = B.dout("pS", [NH, DH, DH])
    pcv_d = B.dout("pconv_t", [128, 24 * 3])
    oC_d = B.dout("oC", [NSEQ, NH, DH, DH])
    on_d = B.dout("on_t", [NSEQ, DH, NH])
    om_d = B.dout("om", [NSEQ, NH])
    oS_d = B.dout("oS", [NSEQ, NH, DH, DH])
    ocv_d = B.dout("oconv_t", [128, 24 * NSEQ * 3])

    modd = B.dscr("modd", [17, 16384], F32)
    s_qa = B.dscr("s_qa", [NH, DH, NQ], BF16)
    s_kaT = B.dscr("s_kaT", [NH, DH, NQ], BF16)
    s_qb = B.dscr("s_qb", [NH, DH, NTOK], BF16)
    s_kb = B.dscr("s_kb", [NH, DH, NTOK], BF16)
    s_vb = B.dscr("s_vb", [NH, DH, NTOK], BF16)
    s_ka = B.dscr("s_ka", [NTOK, 1024], BF16)
    s_va = B.dscr("s_va", [NTOK, 1024], BF16)
    s_oa = B.dscr("s_oa", [NQ, 1024], BF16)
    s_zb = B.dscr("s_zb", [NQ, 1024], BF16)
    s_gt = B.dscr("s_gt", [NTOK, 32], F32)

    G = Ctx(nc)
    cst = G.sb("consts_sb", [128, NCONST], F32)
    identb = G.sb("identb", [128, 128], BF16)
    onesb = G.sb("onesb", [128, 128], BF16)
    flag = G.sb("flag_sb", [128, 1], F32)
    _ld(P, "sync", "g0", cst[:], consts_d, w=[cst])
    _ld(P, "sync", "g1", flag[:], flag_d, w=[flag])

    def C(name, rows=128):
        o, n = CO[name]
        return cst[0:rows, o:o + n]
    _cp(P, "vector", identb[:], C("ident"), [cst], [identb])
    _cp(P, "vector", onesb[:], C("ones"), [cst], [onesb])

    cT = G.sb("cT_sb", [128, 16 * 17], BF16)
    if 0 in phases:
        _phase0(B, P, PS, cst, C, c_d, adaw_d, adab_d, modd, cT)
        P.barrier()
        P.flush()
    if 1 in phases:
        _phase1(B, P, PS, cst, C, identb, onesb, flag, x_d, norms_d, modd, win_d,
                dict(qa=s_qa, kaT=s_kaT, qb=s_qb, kb=s_kb, vb=s_vb, ka=s_ka, va=s_va, oa=s_oa, zb=s_zb, gt=s_gt),
                gvec_d, convw_d, scv_d, pcv_d, ocv_d, cT, adaw_d, adab_d)
    S = dict(qa=s_qa, kaT=s_kaT, qb=s_qb, kb=s_kb, vb=s_vb, ka=s_ka, va=s_va, oa=s_oa, zb=s_zb, gt=s_gt)
    G2 = Ctx(nc)
    mixT = G2.sb("mixT", [128, 16, NQ], BF16)
    if debug:
        mixd = B.dout("mix_dbg", [128, 16 * NQ], BF16)
    if 2 in phases:
        io = dict(sC=sC_d, sn=sn_d, sm=sm_d, sS=sS_d, scv=scv_d, pC=pC_d, pn=pn_d, pm=pm_d, pS=pS_d, pcv=pcv_d,
                  oC=oC_d, on=on_d, om=om_d, oS=oS_d, ocv=ocv_d)
        sc = Scan(B, P, PS, cst, C, identb, onesb, flag, mixT, S, hnorm_d, gvec_d, convw_d, io)
        sc.run()
        if debug:
            _ld(P, "sync", "dbgm", mixd, mixT[:, :, :].rearrange("p k t -> p (k t)"), r=[mixT])
        P.barrier()
        P.flush()
        sc.X.close()
    if 3 in phases:
        _phase3(B, P, PS, cst, C, identb, mixT, x_d, norms_d, modd, wout_d, wup_d, wdn_d, y_d)
    P.barrier()
    P.flush()
    return B


NADA0 = 8


def _ada_block(P, PS, jb, slot, key, bt, bkey, m, mkey, cT, adaw_d, adab_d, modd):
    cols = slice(jb * 512, (jb + 1) * 512)
    _ld(P, "gpsimd", key, slot[:], adaw_d[:, cols].rearrange("(k p) c -> p k c", p=128), w=[slot])
    _ld(P, "sync", bkey, bt[:], adab_d[0:1, cols].partition_broadcast(17), w=[bt])
    bk = PS.one()
    for k in range(16):
        _mm(P, PS.f32(bk)[0:17, :], cT[:, k * 17:(k + 1) * 17], slot[:, k, :], k == 0, k == 15,
            [cT, slot], [PS.banks[bk]])
    _tt(P, "vector", m[:], PS.f32(bk)[0:17, :], bt[:], ALU.add, [PS.banks[bk], bt], [m])
    _ld(P, "sync", mkey, modd[:, cols], m[:], r=[m])


def _phase0(B, P, PS, cst, C, c_d, adaw_d, adab_d, modd, cT):
    nc = B.nc
    X = Ctx(nc)
    c_sb = X.sb("p0_c", [17, D], F32)
    wr = [X.sb("p0_w%d" % i, [128, 16, 512], BF16) for i in range(3)]
    bias = [X.sb("p0_b%d" % i, [17, 512], F32) for i in range(2)]
    mo = [X.sb("p0_m%d" % i, [17, 512], F32) for i in range(2)]
    _ld(P, "sync", "p0c", c_sb[:], c_d, w=[c_sb])
    _act(P, c_sb[:], c_sb[:], AF.Silu, [c_sb], [c_sb])
    bk = PS.one()
    for k in range(16):
        _tr(P, PS.f32(bk)[:, k * 17:(k + 1) * 17], c_sb[:, k * 128:(k + 1) * 128], C("ident", 17)[:, 0:17],
            [c_sb, cst], [PS.banks[bk]])
    _cp(P, "vector", cT[:], PS.f32(bk)[:, 0:272], [PS.banks[bk]], [cT])
    items = []
    for jb in range(NADA0):
        P.capture()
        _ada_block(P, PS, jb, wr[jb % 3], "wr%d" % (jb % 3), bias[jb % 2], "p0b%d" % (jb % 2), mo[jb % 2],
                   "p0m%d" % (jb % 2), cT, adaw_d, adab_d, modd)
        items.append(P.end_capture())
    P.replay_pipe(items, 3, burst=1)
    X.close()


def _mod_tiles(P, modd, col0, tp, ts, key):
    _ld(P, "sync", key + "p", tp[:], modd[0:1, col0:col0 + D].partition_broadcast(128), w=[tp])
    fns = []
    for b in range(NSEQ):
        fns.append(lambda e, b=b: e.dma_start(out=ts[8 * b:8 * b + 8, :],
                                              in_=modd[1 + b:2 + b, col0:col0 + D].partition_broadcast(8)))
    P.dma("sync", key + "s", fns, w=[ts])


def _phase1(B, P, PS, cst, C, identb, onesb, flag, x_d, norms_d, modd, win_d, S, gvec_d, convw_d, scv_d, pcv_d, ocv_d,
            cT, adaw_d, adab_d):
    nc = B.nc
    bank = PS.banks
    X = Ctx(nc)
    hT = X.sb("hT", [128, 16, NTOK], BF16)
    hTs = [TL(hT.t, "hT_%d" % i) for i in range(NSUB)]
    hT = TLP(hT.t, hTs)
    wr = [X.sb("p1_w%d" % i, [128, 16, 512], BF16) for i in range(3)]

    XA = Ctx(nc)
    xt = [XA.sb("p1_x%d" % i, [128, D], F32) for i in range(2)]
    tmpf = XA.sb("p1_tmp", [128, D], F32)
    hb = [XA.sb("p1_hb%d" % i, [128, D], BF16) for i in range(2)]
    Ap = XA.sb("p1_Ap", [128, D], F32)
    Bp = XA.sb("p1_Bp", [128, D], F32)
    As = XA.sb("p1_As", [128, D], F32)
    Bs = XA.sb("p1_Bs", [128, D], F32)
    st = [XA.sb("p1_st%d" % i, [128, 4], F32) for i in range(2)]
    _ld(P, "sync", "p1n", tmpf[:], norms_d[0:1, :].partition_broadcast(128), w=[tmpf])
    _mod_tiles(P, modd, 2048, Ap, As, "p1A")
    _mod_tiles(P, modd, 0, Bp, Bs, "p1B")
    for A_ in (Ap, As):
        _stt(P, "vector", A_[:], A_[:], 1.0, tmpf[:], ALU.add, ALU.mult, [A_, tmpf], [A_])
    wq = []

    def wload(jb):
        slot = wr[jb % 3]
        ncol = 512 if jb < 18 else 32
        _ld(P, "gpsimd", "wr%d" % (jb % 3), slot[:, :, 0:ncol],
            win_d[:, jb * 512:jb * 512 + ncol].rearrange("(k p) c -> p k c", p=128), w=[slot])
    for jb in range(3):
        wload(jb)
    nitems = []
    for i in range(NSUB):
        P.capture()
        x_ = xt[i % 2]
        h_ = hb[i % 2]
        s_ = st[i % 2]
        _ld(P, "sync", "p1x%d" % (i % 2), x_[:], x_d[i * 128:(i + 1) * 128, :], w=[x_])
        _act(P, h_[:], x_[:], AF.Square, [x_], [h_, s_], accum=s_[:, 0:1])
        _act(P, s_[:, 1:2], s_[:, 0:1], AF.Ln, [s_], [s_], bias=EPS, scale=1.0 / D)
        _act(P, s_[:, 2:3], s_[:, 1:2], AF.Exp, [s_], [s_], scale=-0.5)
        A_, B_ = (Ap, Bp) if i < 16 else (As, Bs)
        _stt(P, "vector", tmpf[:], x_[:], s_[:, 2:3], A_[:], ALU.mult, ALU.mult, [x_, s_, A_], [tmpf])
        _tt(P, "vector", h_[:], tmpf[:], B_[:], ALU.add, [tmpf, B_], [h_])
        for half in range(2):
            bk = PS.one()
            for kk in range(8):
                k = half * 8 + kk
                _tr(P, PS.bf(bk)[:, kk * 128:(kk + 1) * 128], h_[:, k * 128:(k + 1) * 128], identb[:],
                    [h_, identb], [PS.banks[bk]])
            _cp(P, "scalar" if half else "vector", hT[:, half * 8:(half + 1) * 8, i * 128:(i + 1) * 128],
                PS.bf(bk).rearrange("p (k t) -> p k t", k=8), [PS.banks[bk]], [hTs[i]])
        nitems.append(P.end_capture())
    P.replay_pipe(nitems, 3, burst=1)
    P.barrier()
    P.flush()
    XA.close()

    XB = Ctx(nc)
    stg = [XB.sb("p1_stg%d" % i, [128, 512], BF16) for i in range(4)]
    NU = 4
    stg = stg + [XB.sb("p1_stg%d" % i, [128, 512], BF16) for i in range(4, 8)]
    pds = [XB.sb("p1_pd%d" % i, [128, 3 + 512], F32) for i in range(NU)]
    accs = [XB.sb("p1_acc%d" % i, [128, 512], F32) for i in range(NU)]
    efs = [XB.sb("p1_ef%d" % i, [128, 512], F32) for i in range(NU)]
    sqs = [XB.sb("p1_sq%d" % i, [128, 512], BF16) for i in range(NU)]
    rvs = accs
    cvs = XB.sb("p1_cvs", [128, 24 * NSEQ * 3], F32)
    pcvt = XB.sb("p1_pcvt", [128, 72], F32)
    convw = XB.sb("p1_convw", [128, 96], F32)
    gbr = XB.sb("p1_gbr", [128, 32], F32)
    biasrow = XB.sb("p1_biasrow", [128, 16], F32)
    coefrow = XB.sb("p1_coefrow", [128, 16], F32)
    graw = [XB.sb("p1_graw%d" % i, [128, 32], F32) for i in range(2)]
    GPt = [XB.sb("p1_GP%d" % i, [128, 32], F32) for i in range(2)]
    gt1 = XB.sb("p1_gt1", [128, 16], F32)
    gt2 = XB.sb("p1_gt2", [128, 16], F32)
    gt3 = XB.sb("p1_gt3", [128, 16], F32)
    gt4 = XB.sb("p1_gt4", [128, 8], F32)
    aw = [XB.sb("p1_aw%d" % i, [128, 16, 512], BF16) for i in range(2)]
    abias = [XB.sb("p1_ab%d" % i, [17, 512], F32) for i in range(2)]
    amo = [XB.sb("p1_am%d" % i, [17, 512], F32) for i in range(2)]
    ada_next = [NADA0]

    def ada_item():
        jb_ = ada_next[0]
        if jb_ >= 32:
            return None
        ada_next[0] += 1
        P.capture()
        _ada_block(P, PS, jb_, aw[jb_ % 2], "aw%d" % (jb_ % 2), abias[jb_ % 2], "p1ab%d" % (jb_ % 2), amo[jb_ % 2],
                   "p1am%d" % (jb_ % 2), cT, adaw_d, adab_d, modd)
        return P.end_capture()
    _ld(P, "sync", "p1cv", cvs[:], scv_d, w=[cvs])
    _ld(P, "sync", "p1cw", convw[:], convw_d, w=[convw])
    _ld(P, "sync", "p1gb", gbr[:], gvec_d.partition_broadcast(128), w=[gbr])
    _cp(P, "vector", biasrow[:, 0:8], gbr[:, 8:16], [gbr], [biasrow])
    _cp(P, "vector", biasrow[:, 8:16], gbr[:, 24:32], [gbr], [biasrow])
    _ms(P, "vector", coefrow[:], -1.0, [coefrow])
    _act(P, coefrow[:, 8:16], gbr[:, 16:24], AF.Exp, [gbr, coefrow], [coefrow])
    _ts(P, "vector", coefrow[:, 8:16], coefrow[:, 8:16], -1.0, ALU.mult, [coefrow], [coefrow])

    ev = [0]

    def evac(out, in_, bank_, dst, func=AF.Copy, scale=None):
        if func == AF.Copy and scale is None and (ev[0] % 2 == 0):
            _cp(P, "vector", out, in_, [bank_], [dst])
        else:
            _act(P, out, in_, func, [bank_], [dst], scale=scale)
        ev[0] += 1

    sidx = [0]

    def store(ap_dst, sg, nt):
        _ld(P, "sync", "p1s%d" % ((sidx[0] - 1) % 8), ap_dst, sg[:, 0:nt], r=[sg])

    def next_stg():
        sg = stg[sidx[0] % 8]
        sidx[0] += 1
        return sg

    ucnt = [0]

    def gdn_unit(name, ty, h, t0, nt, bk, first, do_store, prev):
        blk = 8 * ty + h
        u = ucnt[0]
        ucnt[0] += 1
        pd, acc, ef, sq = pds[u % NU], accs[u % NU], efs[u % NU], sqs[u % NU]
        rv = ef
        sample = t0 == T0 + T1
        cw = convw[:, blk * 4:(blk + 1) * 4]
        if sample:
            pv = pd[:, 0:NSEQ * 11].rearrange("p (s l) -> p s l", s=NSEQ)
            cv3 = cvs[:, blk * 48:(blk + 1) * 48].rearrange("p (s j) -> p s j", s=NSEQ)
            _cp(P, "scalar", pv[:, :, 0:3], cv3, [cvs, pd], [pd])
            _cp(P, "scalar", pv[:, :, 3:11], PS.f32(bk)[:, 0:nt].rearrange("p (s l) -> p s l", s=NSEQ),
                [bank[bk], pd], [pd])
            _cp(P, "scalar", cv3, pv[:, :, 8:11], [pd, cvs], [cvs])
            a_ = acc[:, 0:nt].rearrange("p (s l) -> p s l", s=NSEQ)

            def tap(j):
                return pv[:, :, j:j + LS]
        else:
            if first:
                _ms(P, "vector", pd[:, 0:3], 0.0, [pd])
            else:
                ppd, pnt = prev
                if t0 == T0:
                    _ts(P, "vector", pd[:, 0:3], ppd[:, pnt:pnt + 3], flag[:, 0:1], ALU.mult, [ppd, flag, pd], [pd])
                else:
                    _cp(P, "scalar", pd[:, 0:3], ppd[:, pnt:pnt + 3], [ppd, pd], [pd])
            _cp(P, "scalar", pd[:, 3:3 + nt], PS.f32(bk)[:, 0:nt], [bank[bk], pd], [pd])
            if t0 + nt == T0 + T1:
                _cp(P, "scalar", pcvt[:, blk * 3:(blk + 1) * 3], pd[:, nt:nt + 3], [pd, pcvt], [pcvt])
            a_ = acc[:, 0:nt]

            def tap(j):
                return pd[:, j:j + nt]
        if not do_store:
            return (pd, nt)
        _ts(P, "vector", a_, tap(0), cw[:, 0:1], ALU.mult, [pd, convw, acc], [acc])
        for j in range(1, 4):
            _stt(P, "vector", a_, tap(j), cw[:, j:j + 1], a_, ALU.mult, ALU.add, [pd, convw, acc], [acc])
        _act(P, ef[:, 0:nt], acc[:, 0:nt], AF.Exp, [acc, ef], [ef], scale=-1.0)
        _act(P, ef[:, 0:nt], ef[:, 0:nt], AF.Ln, [ef], [ef], bias=1.0)
        _act(P, ef[:, 0:nt], ef[:, 0:nt], AF.Exp, [ef], [ef], scale=-1.0)
        sg = next_stg()
        if ty == 2:
            _tt(P, "vector", sg[:, 0:nt], acc[:, 0:nt], ef[:, 0:nt], ALU.mult, [acc, ef, sg], [sg])
        else:
            _tt(P, "vector", acc[:, 0:nt], acc[:, 0:nt], ef[:, 0:nt], ALU.mult, [acc, ef], [acc])
            _tt(P, "vector", sq[:, 0:nt], acc[:, 0:nt], acc[:, 0:nt], ALU.mult, [acc, sq], [sq])
            b2 = PS.one()
            _mm(P, PS.f32(b2)[:, 0:nt], onesb[:, :], sq[:, 0:nt], True, True, [onesb, sq], [bank[b2]])
            _act(P, rv[:, 0:nt], PS.f32(b2)[:, 0:nt], AF.Ln, [bank[b2], rv], [rv], bias=EPS)
            _act(P, rv[:, 0:nt], rv[:, 0:nt], AF.Exp, [rv], [rv], scale=-0.5)
            if ty == 0:
                _stt(P, "vector", sg[:, 0:nt], acc[:, 0:nt], DH ** -0.5, rv[:, 0:nt], ALU.mult, ALU.mult,
                     [acc, rv, sg], [sg])
            else:
                _tt(P, "vector", sg[:, 0:nt], acc[:, 0:nt], rv[:, 0:nt], ALU.mult, [acc, rv, sg], [sg])
        store(S[name][h, :, t0:t0 + nt], sg, nt)
        return (pd, nt)

    def gate_unit(i, bk):
        gr, GP = graw[i % 2], GPt[i % 2]
        _cp(P, "vector", gr[:], PS.f32(bk)[:, 0:32], [bank[bk], gr], [gr])
        _tt(P, "vector", GP[:, 0:8], gr[:, 0:8], gbr[:, 0:8], ALU.add, [gr, gbr, GP], [GP])
        _tt(P, "vector", gt1[:], gr[:, 8:24], biasrow[:], ALU.add, [gr, biasrow, gt1], [gt1])
        _tt(P, "vector", gt1[:], gt1[:], C("sign"), ALU.mult, [gt1, cst], [gt1])
        _act(P, gt2[:], gt1[:], AF.Abs, [gt1, gt2], [gt2])
        _act(P, gt2[:], gt2[:], AF.Exp, [gt2], [gt2], scale=-1.0)
        _act(P, gt2[:], gt2[:], AF.Ln, [gt2], [gt2], bias=1.0)
        _ts(P, "vector", gt3[:], gt1[:], 0.0, ALU.max, [gt1, gt3], [gt3])
        _tt(P, "vector", gt3[:], gt3[:], gt2[:], ALU.add, [gt3, gt2], [gt3])
        _tt(P, "vector", GP[:, 8:24], gt3[:], coefrow[:], ALU.mult, [gt3, coefrow, GP], [GP])
        _act(P, gt4[:], gr[:, 24:32], AF.Exp, [gr, gt4], [gt4], scale=-1.0)
        _ts(P, "vector", gt4[:], gt4[:], 1.0, ALU.add, [gt4], [gt4])
        P.V(lambda e: e.reciprocal(out=GP[:, 24:32], in_=gt4[:]), [gt4, GP], [GP])
        _ld(P, "sync", "p1g%d" % (i % 2), S["gt"][i * 128:(i + 1) * 128, :], GP[:], r=[GP])

    fm_types = [("qa", 0), ("kaT", 0), ("qb", 1), ("kb", 2), ("vb", 2)]
    tiles_q = [(1024, 512), (1536, 512), (2048, 128)]
    tiles_all = [(0, 512), (512, 512)] + tiles_q
    items = []
    since = [0]
    for jb in range(19):
        slot = wr[jb % 3]
        if jb == 10:
            while True:
                it_ = ada_item()
                if it_ is None:
                    break
                items.append(it_)
            P.replay_pipe(items, 6, burst=1)
            items = []
        if jb >= 3:
            if jb < 10:
                P.capture()
                wload(jb)
                items.append(P.end_capture())
            else:
                wload(jb)
        if jb < 10:
            name, mode = fm_types[jb // 2]
            tl = {0: tiles_q, 1: [(512, 512)] + tiles_q, 2: tiles_all}[mode]
            for sbk in range(4):
                h = (jb % 2) * 4 + sbk
                prev = None
                for ti, (t0, nt) in enumerate(tl):
                    P.capture()
                    bk = PS.one()
                    for k in range(16):
                        _mm(P, PS.f32(bk)[:, 0:nt], slot[:, k, sbk * 128:(sbk + 1) * 128], hT[:, k, t0:t0 + nt],
                            k == 0, k == 15, [slot, hT], [PS.banks[bk]])
                    if name in ("qa", "kaT"):
                        sg = next_stg()
                        evac(sg[:, 0:nt], PS.f32(bk)[:, 0:nt], PS.banks[bk], sg,
                             scale=(DH ** -0.5 if name == "kaT" else None))
                        store(S[name][h, :, t0 - T0:t0 - T0 + nt], sg, nt)
                    else:
                        ty = {"qb": 0, "kb": 1, "vb": 2}[name]
                        prev = gdn_unit(name, ty, h, t0, nt, bk, ti == 0, not (name == "qb" and t0 < T0), prev)
                    items.append(P.end_capture())
                    since[0] += 1
                    if since[0] >= (11 if jb < 4 else 5):
                        since[0] = 0
                        it_ = ada_item()
                        if it_ is not None:
                            items.append(it_)
        elif jb < 18:
            name = ("ka", "va", "oa", "zb")[(jb - 10) // 2]
            c0 = ((jb - 10) % 2) * 512
            qonly = name in ("oa", "zb")
            for i in range(NSUB):
                if qonly and i < 8:
                    continue
                bk = PS.one()
                for k in range(16):
                    _mm(P, PS.f32(bk)[:, :], hT[:, k, i * 128:(i + 1) * 128], slot[:, k, :], k == 0, k == 15,
                        [slot, hT], [PS.banks[bk]])
                sg = next_stg()
                if name == "ka":
                    evac(sg[:], PS.f32(bk), PS.banks[bk], sg, scale=DH ** -0.5)
                elif name == "va":
                    evac(sg[:], PS.f32(bk), PS.banks[bk], sg)
                elif name == "oa":
                    evac(sg[:], PS.f32(bk), PS.banks[bk], sg, func=AF.Sigmoid)
                else:
                    evac(sg[:], PS.f32(bk), PS.banks[bk], sg, func=AF.Silu)
                r0 = i * 128 - (T0 if qonly else 0)
                store(S[name][r0:r0 + 128, c0:c0 + 512], sg, 512)
        else:
            for i in range(NSUB):
                bk = PS.one()
                for k in range(16):
                    _mm(P, PS.f32(bk)[:, 0:32], hT[:, k, i * 128:(i + 1) * 128], slot[:, k, 0:32], k == 0, k == 15,
                        [slot, hT], [PS.banks[bk]])
                gate_unit(i, bk)
    _ld(P, "sync", "p1pc", pcv_d, pcvt[:, :], r=[pcvt])
    _ld(P, "sync", "p1oc", ocv_d, cvs[:, :], r=[cvs])
    P.barrier()
    P.flush()
    XB.close()
    X.close()


def _shared_inputs(inp):
    w_in = np.asarray(inp["w_in"][0])
    o = np.cumsum([0, 1024, 1024, 1024, 8, 8, 1024, 1024, 1024, 1024, 8, 8, 1024])
    seg = {n: w_in[:, o[i]:o[i + 1]] for i, n in enumerate(
        ["qa", "ka", "va", "ia", "fa", "oa", "qb", "kb", "vb", "ab", "bb", "zb"])}
    w_in_r = np.ascontiguousarray(np.concatenate(
        [seg[n] for n in ("qa", "ka", "qb", "kb", "vb", "ka", "va", "oa", "zb", "ia", "fa", "ab", "bb")], axis=1))
    sh = {
        "consts": _build_consts(),
        "ada_w": np.ascontiguousarray(np.concatenate([inp["ada_w"][0], inp["ada_final_w"]], axis=1)),
        "ada_b": np.ascontiguousarray(np.concatenate([inp["ada_b"][0], inp["ada_final_b"]])[None, :]),
        "w_in": w_in_r,
        "w_out": np.ascontiguousarray(inp["w_out"][0]),
        "w_up": np.ascontiguousarray(inp["w_up"][0]),
        "w_down": np.ascontiguousarray(inp["w_down"][0]),
        "norms": np.ascontiguousarray(np.stack([inp["norm1"][0], inp["norm2"][0], inp["norm_final"]])),
        "hnorm": np.ascontiguousarray(np.stack([inp["mlstm_norm"][0], inp["gdn_norm"][0]])),
        "gvec": np.ascontiguousarray(np.concatenate(
            [inp["mlstm_gate_bias"][0], inp["gdn_A_log"][0], inp["gdn_dt_bias"][0]])[None, :]),
        "conv_w": np.ascontiguousarray(
            np.asarray(inp["gdn_conv_w"][0]).T.reshape(24, 128, 4).transpose(1, 0, 2).reshape(128, 96)),
    }
    return {k: np.asarray(v, np.float32) for k, v in sh.items()}


def _core_inputs(inp, c, sh):
    b, half = c // 2, c % 2
    sl = slice(c * NSEQ, (c + 1) * NSEQ)
    xp = inp["x_prompt"][b]
    m = dict(sh)
    m["x"] = np.ascontiguousarray(np.concatenate(
        [xp[0:T0], xp[half * T1:(half + 1) * T1], inp["x_sample"][sl].reshape(TS, D)], axis=0), np.float32)
    m["c"] = np.ascontiguousarray(np.concatenate([inp["c_prompt"][b:b + 1], inp["c_sample"][sl]], axis=0), np.float32)
    m["flag"] = np.full((128, 1), float(half), np.float32)
    m["sC"] = np.ascontiguousarray(inp["state_mlstm_C"][0, sl], np.float32)
    m["sn_t"] = np.ascontiguousarray(np.asarray(inp["state_mlstm_n"][0, sl]).transpose(0, 2, 1), np.float32)
    m["sm"] = np.ascontiguousarray(inp["state_mlstm_m"][0, sl], np.float32)
    m["sS"] = np.ascontiguousarray(inp["state_gdn_S"][0, sl], np.float32)
    cv = np.asarray(inp["state_gdn_conv"][0, sl])
    m["sconv_t"] = np.ascontiguousarray(
        cv.transpose(2, 0, 1).reshape(24, 128, NSEQ, 3).transpose(1, 0, 2, 3).reshape(128, 24 * NSEQ * 3), np.float32)
    return m


LP = 128
NLEV = {8: 2, 64: 5, 128: 6}


def _v3(ap2d, n):
    return ap2d.rearrange("p (h x) -> p h x", h=NH)


class TLV:
    def __init__(self, t, c0, c1, name=""):
        self.t, self.c0, self.c1 = t, c0, c1
        self.buf = Buf(name)

    def __getitem__(self, k):
        rows, cols = k
        a = 0 if cols.start is None else cols.start
        b = (self.c1 - self.c0) if cols.stop is None else cols.stop
        return self.t[rows, self.c0 + a:self.c0 + b]


class TLP:
    def __init__(self, t, kids):
        self.t, self.kids = t, kids

    def __getitem__(self, k):
        return self.t[k]


class Grp:
    pass


class Scan:
    NG = 2

    def __init__(self, B, P, PS, cst, C, identb, onesb, flag, mixT, S, hnorm_d, gvec_d, convw_d, io):
        self.B, self.P, self.PS, self.cst, self.C = B, P, PS, cst, C
        self.identb, self.onesb, self.flag, self.mixT, self.S, self.io = identb, onesb, flag, mixT, S, io
        nc = B.nc
        X = self.X = Ctx(nc)
        sb = X.sb
        NG = self.NG
        nh = NH // NG
        self.gA = sb("gA", [128, 1024], F32)
        self.gB = sb("gB", [128, 1024], F32)
        _ld(P, "sync", "s2a", self.gA[:], hnorm_d[0:1, :].partition_broadcast(128), w=[self.gA])
        _ld(P, "sync", "s2b", self.gB[:], hnorm_d[1:2, :].partition_broadcast(128), w=[self.gB])
        self.fm = [dict(qaT=sb("qaT%d" % i, [128, 8, 128], BF16), kaT=sb("kaT%d" % i, [128, 8, 128], BF16),
                        q=sb("qpost%d" % i, [128, 8, 128], BF16), k=sb("kpost%d" % i, [128, 8, 128], BF16),
                        v=sb("vpost%d" % i, [128, 8, 128], BF16)) for i in range(2)]
        self.tm = [dict(ka=sb("ka_t%d" % i, [128, 1024], BF16), va=sb("va_t%d" % i, [128, 1024], BF16),
                        oa=sb("oa_t%d" % i, [128, 1024], BF16), zb=sb("zb_t%d" % i, [128, 1024], BF16),
                        GP=sb("GP%d" % i, [128, 32], F32)) for i in range(2)]
        specA_big = (("dgx", F32), ("R", F32), ("sloc", BF16), ("sTsb", BF16), ("nl", F32), ("wlk", BF16), ("dCs", F32),
                     ("h1", F32), ("hnb", BF16), ("gs", BF16), ("Cst", F32), ("Cbf", BF16))
        specB_big = (("dG", F32), ("NZ", F32), ("qg", BF16), ("wT", F32), ("dTi", F32), ("P0", BF16), ("P1", BF16),
                     ("PT0", BF16), ("PT1", BF16), ("Tacc", BF16), ("qkT", BF16), ("Mf", BF16), ("MoT", BF16), ("MoT2", BF16),
                     ("kbg", BF16), ("kdec", BF16), ("vbt", BF16), ("WT", BF16), ("U0", F32), ("u", BF16),
                     ("ob", BF16), ("gz", BF16), ("Sst", F32), ("Sbf", BF16))

        def smallA(n):
            return (("sm1", 2 * n), ("g", n), ("cm", n), ("rows", n), ("gl", n), ("dns", n), ("mx", 2 * n), ("t12", 2 * n),
                    ("fd", 2 * n), ("mt", n), ("en", n), ("dd", n), ("d2", n), ("a12", 2 * n), ("ssq", n), ("sm2", 2 * n),
                    ("dC", n))

        def smallB(n):
            return (("gsm", 2 * n), ("gsmall", 2 * n), ("gLe", n), ("ssq2", n))
        parents = {}
        for kind, spec in (("a", specA_big), ("b", specB_big)):
            for n, dt in spec:
                parents[(kind, n)] = sb("P%s_%s" % (kind, n), [128, 1024], dt)
        nstP = sb("Pa_nst", [128, NH], F32)
        nbfP = sb("Pa_nbf", [128, NH], BF16)
        mprevP = sb("Pa_mprev", [128, NH], F32)
        self.ga, self.gb = [], []
        for gi in range(NG):
            for kind in ("a", "b"):
                g = Grp()
                g.h0, g.nh, g.kind = gi * nh, nh, kind
                base = (0 if kind == "a" else 4) + 2 * gi
                g.PS = PS.sub([base, base + 1])
                t = "%s%d_" % (kind, gi)
                g.W = {}
                for n, dt in (specA_big if kind == "a" else specB_big):
                    g.W[n] = TLV(parents[(kind, n)].t, gi * nh * 128, (gi + 1) * nh * 128, t + n)
                for n, wd in (smallA(nh) if kind == "a" else smallB(nh)):
                    g.W[n] = sb(t + n, [128, wd], F32)
                if kind == "a":
                    g.Cst, g.Cbf = g.W["Cst"], g.W["Cbf"]
                    g.nst = TLV(nstP.t, gi * nh, (gi + 1) * nh, t + "nst")
                    g.nbf = TLV(nbfP.t, gi * nh, (gi + 1) * nh, t + "nbf")
                    g.mprev = TLV(mprevP.t, gi * nh, (gi + 1) * nh, t + "mprev")
                else:
                    g.Sst, g.Sbf = g.W["Sst"], g.W["Sbf"]
                    g.W["dB"] = g.W["dG"]
                    g.W["gre"] = g.W["wT"]
                    g.W["osb"] = g.W["dG"]
                (self.ga if kind == "a" else self.gb).append(g)
        self.alt = dict(Cst=sb("alt_Cst", [128, 1024], F32), Sst=sb("alt_Sst", [128, 1024], F32),
                        nst=sb("alt_nst", [128, NH], F32), mprev=sb("alt_mprev", [128, NH], F32))
        self.gaS, self.gbS = Grp(), Grp()
        for g, kind, spec, small, grps in ((self.gaS, "a", specA_big, smallA, self.ga), (self.gbS, "b", specB_big, smallB, self.gb)):
            g.h0, g.nh, g.kind = 0, NH, kind
            g.PS = PS.sub([0, 1, 2, 3] if kind == "a" else [4, 5, 6, 7])
            g.W = {}
            for n, dt in spec:
                g.W[n] = TLP(parents[(kind, n)].t, [gg.W[n] for gg in grps])
            for n, wd in small(NH):
                g.W[n] = sb("S%s_%s" % (kind, n), [128, wd], F32)
            if kind == "a":
                g.Cst, g.Cbf = g.W["Cst"], g.W["Cbf"]
                g.nst = TLP(nstP.t, [gg.nst for gg in grps])
                g.nbf = TLP(nbfP.t, [gg.nbf for gg in grps])
                g.mprev = TLP(mprevP.t, [gg.mprev for gg in grps])
            else:
                g.Sst, g.Sbf = g.W["Sst"], g.W["Sbf"]
                g.W["dB"] = g.W["dG"]
                g.W["gre"] = g.W["wT"]
                g.W["osb"] = g.W["dG"]

    def mlstm_chunk(self, g, L, c0, q0, full, fm, tm):
        P, PS, C, W, cst = self.P, g.PS, self.C, g.W, self.cst
        bank = PS.banks
        h0, nh = g.h0, g.nh
        cn = str(L)
        GP = tm["GP"]
        lf = GP[0:L, 8 + h0:8 + h0 + nh]
        li = GP[0:L, h0:h0 + nh]
        HL = nh * L
        assert HL <= 512

        wide = nh * 128 > 512

        def m2():
            return PS.pair() if wide else PS.one()

        def f2(b_):
            return PS.f32(b_, 2) if wide else PS.f32(b_)

        def bl2(b_):
            return [bank[b_], bank[b_ + 1]] if wide else [bank[b_]]

        def hb2(b_, h_):
            return bank[b_ + (h_ * 128) // 512]
        identb, onesb = self.identb, self.onesb
        qaT, kaT, ka, va, oa = fm["qaT"], fm["kaT"], tm["ka"], tm["va"], tm["oa"]
        gc = slice(h0 * 128, (h0 + nh) * 128)

        def v3(ap2d):
            return ap2d.rearrange("p (h x) -> p h x", h=nh)

        def bcl(ap, n):
            return ap[:, :, None].to_broadcast([L, nh, n])
        bs = PS.one()
        _mm(P, PS.f32(bs)[0:L, 0:nh], C("uinc" + cn, L), lf, True, True, [cst, GP], [bank[bs]])
        _mm(P, PS.f32(bs)[:, nh:2 * nh], C("ones", L)[:, 0:128], lf, True, True, [cst, GP], [bank[bs]])
        sm1 = W["sm1"]
        _cp(P, "scalar", sm1[:, :], PS.f32(bs)[:, 0:2 * nh], [bank[bs]], [sm1])
        gg = W["g"]
        _tt(P, "vector", gg[0:L, :], li, sm1[0:L, 0:nh], ALU.subtract, [GP, sm1], [gg])
        dgx = W["dgx"]
        _tt(P, "gpsimd", v3(dgx[0:L, 0:HL]), C("ident", L)[:, None, 0:L].to_broadcast([L, nh, L]),
            bcl(gg[0:L, :], L), ALU.mult, [cst, gg], [dgx])
        br = PS.one()
        _mm(P, PS.f32(br)[0:L, 0:HL], C("ones", L)[:, 0:L], dgx[0:L, 0:HL], True, True, [cst, dgx], [bank[br]])
        R = W["R"]
        R3 = v3(R[0:L, 0:HL])
        _tt(P, "vector", R3, v3(PS.f32(br)[0:L, 0:HL]), C("neg" + cn, L)[:, None, :].to_broadcast([L, nh, L]),
            ALU.add, [bank[br], cst], [R])
        cm = W["cm"]
        _red(P, "vector", cm[0:L, :], R3, ALU.max, [R], [cm])
        _tt(P, "vector", R3, R3, bcl(cm[0:L, :], L), ALU.subtract, [R, cm], [R])
        _act(P, R[0:L, 0:HL], R[0:L, 0:HL], AF.Exp, [R], [R])
        if full:
            bq = PS.one()
            for h in range(nh):
                _mm(P, PS.f32(bq)[0:L, h * L:(h + 1) * L], qaT[:, h0 + h, c0:c0 + L], kaT[:, h0 + h, c0:c0 + L],
                    True, True, [qaT, kaT], [bank[bq]])
            sloc = W["sloc"]
            _tt(P, "vector", sloc[0:L, 0:HL], PS.f32(bq)[0:L, 0:HL], R[0:L, 0:HL], ALU.mult, [bank[bq], R], [sloc])
            rows = W["rows"]
            _red(P, "vector", rows[0:L, :], v3(sloc[0:L, 0:HL]), ALU.add, [sloc], [rows])
            bt = PS.one()
            for h in range(nh):
                _tr(P, PS.bf(bt)[0:L, h * L:(h + 1) * L], sloc[0:L, h * L:(h + 1) * L], identb[0:L, 0:L],
                    [sloc, identb], [bank[bt]])
            sTsb = W["sTsb"]
            _cp(P, "scalar", sTsb[0:L, 0:HL], PS.bf(bt)[0:L, 0:HL], [bank[bt]], [sTsb])
            bn = m2()
            for h in range(nh):
                _mm(P, f2(bn)[0:L, h * 128:(h + 1) * 128], sTsb[0:L, h * L:(h + 1) * L],
                    va[0:L, (h0 + h) * 128:(h0 + h + 1) * 128], True, True, [sTsb, va], [hb2(bn, h)])
            nl = W["nl"]
            _cp(P, "scalar", nl[0:L, :], f2(bn)[0:L, :], bl2(bn), [nl])
            gs = W["gs"]
            _tt(P, "gpsimd", gs[0:L, :], oa[0:L, gc], self.gA[0:L, gc], ALU.mult, [oa, self.gA], [gs])
        b2 = PS.one()
        _mm(P, PS.f32(b2)[0:L, 0:nh], C("sel" + cn, L)[:, 0:L], cm[0:L, :], True, True, [cst, cm], [bank[b2]])
        gl = W["gl"]
        _tt(P, "vector", gl[0:L, :], gg[0:L, :], PS.f32(b2)[0:L, 0:nh], ALU.subtract, [gg, bank[b2]], [gl])
        _act(P, gl[0:L, :], gl[0:L, :], AF.Exp, [gl], [gl])
        wlk = W["wlk"]
        _tt(P, "gpsimd", v3(wlk[0:L, :]), v3(ka[0:L, gc]), bcl(gl[0:L, :], 128), ALU.mult, [ka, gl], [wlk])
        bd = m2()
        for h in range(nh):
            _mm(P, f2(bd)[:, h * 128:(h + 1) * 128], wlk[0:L, h * 128:(h + 1) * 128],
                va[0:L, (h0 + h) * 128:(h0 + h + 1) * 128], True, True, [wlk, va], [hb2(bd, h)])
        dCs, dns = W["dCs"], W["dns"]
        _cp(P, "scalar", dCs[:, :], f2(bd)[:, :], bl2(bd), [dCs])
        b3 = PS.one()
        for h in range(nh):
            _mm(P, PS.f32(b3)[:, h:h + 1], wlk[0:L, h * 128:(h + 1) * 128], onesb[0:L, 0:1], True, True,
                [wlk, onesb], [bank[b3]])
        _cp(P, "vector", dns[:, :], PS.f32(b3)[:, 0:nh], [bank[b3]], [dns])
        mprev, Cst, Cbf, nst, nbf = g.mprev, g.Cst, g.Cbf, g.nst, g.nbf
        mx, t12, fd = W["mx"], W["t12"], W["fd"]
        _tt(P, "vector", mx[0:L, 0:nh], mprev[0:L, :], cm[0:L, :], ALU.max, [mprev, cm], [mx])
        _tt(P, "vector", t12[0:L, 0:nh], cm[0:L, :], mx[0:L, 0:nh], ALU.subtract, [cm, mx], [t12])
        _tt(P, "vector", t12[0:L, nh:2 * nh], mprev[0:L, :], mx[0:L, 0:nh], ALU.subtract, [mprev, mx, t12], [t12])
        _act(P, fd[0:L, :], t12[0:L, :], AF.Exp, [t12], [fd])
        _cp(P, "vector", mx[0:L, nh:2 * nh], fd[0:L, 0:nh], [fd, mx], [mx])
        if full:
            bc_ = m2()
            for h in range(nh):
                _mm(P, f2(bc_)[0:L, h * 128:(h + 1) * 128], qaT[:, h0 + h, c0:c0 + L], Cbf[:, h * 128:(h + 1) * 128],
                    True, True, [qaT, Cbf], [hb2(bc_, h)])
            mt, en, dd, d2, a12 = W["mt"], W["en"], W["dd"], W["d2"], W["a12"]
            h1, nl, ssq, hnb = W["h1"], W["nl"], W["ssq"], W["hnb"]
            _tt(P, "vector", mt[0:L, :], sm1[0:L, 0:nh], mx[0:L, 0:nh], ALU.add, [sm1, mx], [mt])
            _act(P, en[0:L, :], mt[0:L, :], AF.Exp, [mt], [en], scale=-1.0)
            _cp(P, "scalar", h1[0:L, :], f2(bc_)[0:L, :], bl2(bc_), [h1])
            b4 = PS.one()
            for h in range(nh):
                _mm(P, PS.f32(b4)[0:L, h:h + 1], qaT[:, h0 + h, c0:c0 + L], nbf[:, h:h + 1], True, True,
                    [qaT, nbf], [bank[b4]])
            _tt(P, "vector", dd[0:L, :], fd[0:L, nh:2 * nh], PS.f32(b4)[0:L, 0:nh], ALU.mult, [fd, bank[b4]], [dd])
            _tt(P, "vector", d2[0:L, :], fd[0:L, 0:nh], W["rows"][0:L, :], ALU.mult, [fd, W["rows"]], [d2])
            _tt(P, "vector", dd[0:L, :], dd[0:L, :], d2[0:L, :], ALU.add, [dd, d2], [dd])
            _act(P, dd[0:L, :], dd[0:L, :], AF.Abs, [dd], [dd])
            _tt(P, "vector", dd[0:L, :], dd[0:L, :], en[0:L, :], ALU.max, [dd, en], [dd])
            P.V(lambda e: e.reciprocal(out=dd[0:L, :], in_=dd[0:L, :]), [dd], [dd])
            _tt(P, "vector", a12[0:L, 0:nh], fd[0:L, nh:2 * nh], dd[0:L, :], ALU.mult, [fd, dd], [a12])
            _tt(P, "vector", a12[0:L, nh:2 * nh], fd[0:L, 0:nh], dd[0:L, :], ALU.mult, [fd, dd, a12], [a12])
            _tt(P, "vector", v3(h1[0:L, :]), v3(h1[0:L, :]), bcl(a12[0:L, 0:nh], 128), ALU.mult, [h1, a12], [h1])
            _tt(P, "gpsimd", v3(nl[0:L, :]), v3(nl[0:L, :]), bcl(a12[0:L, nh:2 * nh], 128), ALU.mult, [nl, a12], [nl])
            _tt(P, "vector", h1[0:L, :], h1[0:L, :], nl[0:L, :], ALU.add, [h1, nl], [h1])
            _act(P, nl[0:L, :], h1[0:L, :], AF.Square, [h1, nl], [nl])
            _red(P, "vector", ssq[0:L, :], v3(nl[0:L, :]), ALU.add, [nl], [ssq])
            _act(P, ssq[0:L, :], ssq[0:L, :], AF.Ln, [ssq], [ssq], bias=EPS, scale=1.0 / DH)
            _act(P, ssq[0:L, :], ssq[0:L, :], AF.Exp, [ssq], [ssq], scale=-0.5)
            _tt(P, "vector", v3(h1[0:L, :]), v3(h1[0:L, :]), bcl(ssq[0:L, :], 128), ALU.mult, [h1, ssq], [h1])
            _tt(P, "gpsimd", hnb[0:L, :], h1[0:L, :], W["gs"][0:L, :], ALU.mult, [h1, W["gs"]], [hnb])
            bh = PS.one()
            for h in range(nh):
                _tr(P, PS.bf(bh)[:, h * L:(h + 1) * L], hnb[0:L, h * 128:(h + 1) * 128], identb[0:L, 0:L],
                    [hnb, identb], [bank[bh]])
            _cp(P, "scalar", self.mixT[:, h0:h0 + nh, q0:q0 + L], PS.bf(bh)[:, 0:HL].rearrange("p (h t) -> p h t", h=nh),
                [bank[bh]], [self.mixT])
        b5 = PS.one()
        _mm(P, PS.f32(b5)[:, 0:2 * nh], C("sel" + cn, L)[:, 0:128], mx[0:L, 0:2 * nh], True, True, [cst, mx], [bank[b5]])
        sm2, dC = W["sm2"], W["dC"]
        _cp(P, "scalar", sm2[:, :], PS.f32(b5)[:, 0:2 * nh], [bank[b5]], [sm2])
        _tt(P, "vector", dC[:, :], mprev[:, :], sm2[:, 0:nh], ALU.subtract, [mprev, sm2], [dC])
        _act(P, dC[:, :], dC[:, :], AF.Exp, [dC], [dC])

        def bc128(ap):
            return ap[:, :, None].to_broadcast([128, nh, 128])
        _tt(P, "vector", v3(Cst[:, :]), v3(Cst[:, :]), bc128(dC[:, :]), ALU.mult, [Cst, dC], [Cst])
        _tt(P, "gpsimd", v3(dCs[:, :]), v3(dCs[:, :]), bc128(sm2[:, nh:2 * nh]), ALU.mult, [dCs, sm2], [dCs])
        _tt(P, "vector", Cst[:, :], Cst[:, :], dCs[:, :], ALU.add, [Cst, dCs], [Cst])
        _tt(P, "vector", nst[:, :], nst[:, :], dC[:, :], ALU.mult, [nst, dC], [nst])
        _tt(P, "vector", dns[:, :], dns[:, :], sm2[:, nh:2 * nh], ALU.mult, [dns, sm2], [dns])
        _tt(P, "vector", nst[:, :], nst[:, :], dns[:, :], ALU.add, [nst, dns], [nst])
        _cp(P, "scalar", Cbf[:, :], Cst[:, :], [Cst], [Cbf])
        _cp(P, "vector", nbf[:, :], nst[:, :], [nst], [nbf])
        _tt(P, "vector", mprev[:, :], sm1[:, nh:2 * nh], sm2[:, 0:nh], ALU.add, [sm1, sm2, mprev], [mprev])

    def gdn_chunk(self, g, L, c0, q0, full, fm, tm):
        P, PS, C, W, cst = self.P, g.PS, self.C, g.W, self.cst
        bank = PS.banks
        h0, nh = g.h0, g.nh
        cn = str(L)
        GP = tm["GP"]
        zb = tm["zb"]
        logg = GP[0:L, 16 + h0:16 + h0 + nh]
        beta = GP[0:L, 24 + h0:24 + h0 + nh]
        HL = nh * L
        assert HL <= 512

        wide = nh * 128 > 512

        def m2():
            return PS.pair() if wide else PS.one()

        def f2(b_):
            return PS.f32(b_, 2) if wide else PS.f32(b_)

        def bl2(b_):
            return [bank[b_], bank[b_ + 1]] if wide else [bank[b_]]

        def hb2(b_, h_):
            return bank[b_ + (h_ * 128) // 512]
        identb = self.identb
        qpost, kpost, vpost = fm["q"], fm["k"], fm["v"]
        Sst, Sbf = g.Sst, g.Sbf
        gc = slice(h0 * 128, (h0 + nh) * 128)

        def v3(ap2d):
            return ap2d.rearrange("p (h x) -> p h x", h=nh)

        def bcl(ap, n):
            return ap[:, :, None].to_broadcast([L, nh, n])
        identL = C("ident", L)[:, None, 0:L].to_broadcast([L, nh, L])
        bs = PS.one()
        _mm(P, PS.f32(bs)[0:L, 0:nh], C("uinc" + cn, L), logg, True, True, [cst, GP], [bank[bs]])
        _mm(P, PS.f32(bs)[:, nh:2 * nh], C("ones", L)[:, 0:128], logg, True, True, [cst, GP], [bank[bs]])
        gsm = W["gsm"]
        _cp(P, "scalar", gsm[:, :], PS.f32(bs)[:, 0:2 * nh], [bank[bs]], [gsm])
        Gt = gsm[0:L, 0:nh]
        dG = W["dG"]
        _tt(P, "gpsimd", v3(dG[0:L, 0:HL]), identL, bcl(Gt, L), ALU.mult, [cst, gsm], [dG])
        bg = PS.one()
        _mm(P, PS.f32(bg)[:, 0:HL], C("ones", L)[:, 0:128], dG[0:L, 0:HL], True, True, [cst, dG], [bank[bg]])
        NZ = W["NZ"]
        _tt(P, "vector", v3(NZ[0:L, 0:HL]), v3(PS.f32(bg)[0:L, 0:HL]), bcl(Gt, L), ALU.subtract, [bank[bg], gsm], [NZ])
        _ts(P, "vector", NZ[0:L, 0:HL], NZ[0:L, 0:HL], 0.0, ALU.min, [NZ], [NZ])
        _act(P, NZ[0:L, 0:HL], NZ[0:L, 0:HL], AF.Exp, [NZ], [NZ])
        if full:
            gre, qg = W["gre"], W["qg"]
            _act(P, gre[:, 0:HL], PS.f32(bg)[:, 0:HL], AF.Exp, [bank[bg]], [gre])
            _tt(P, "vector", v3(qg[:, 0:HL]), qpost[:, h0:h0 + nh, c0:c0 + L], v3(gre[:, 0:HL]), ALU.mult,
                [qpost, gre], [qg])
        dB = W["dB"]
        _tt(P, "gpsimd", v3(dB[0:L, 0:HL]), identL, bcl(beta, L), ALU.mult, [cst, GP, dB], [dB])
        bb_ = PS.one()
        _mm(P, PS.f32(bb_)[0:L, 0:HL], C("ones", L)[:, 0:L], dB[0:L, 0:HL], True, True, [cst, dB], [bank[bb_]])
        wT = W["wT"]
        _tt(P, "gpsimd", v3(wT[0:L, 0:HL]), v3(NZ[0:L, 0:HL]),
            C("ustr" + cn, L)[:, None, :].to_broadcast([L, nh, L]), ALU.mult, [NZ, cst, wT], [wT])
        _tt(P, "vector", wT[0:L, 0:HL], wT[0:L, 0:HL], PS.f32(bb_)[0:L, 0:HL], ALU.mult, [wT, bank[bb_]], [wT])
        if full:
            dTi = W["dTi"]
            _tt(P, "gpsimd", v3(dTi[0:L, 0:HL]), v3(NZ[0:L, 0:HL]),
                C("uinc" + cn, L)[:, None, :].to_broadcast([L, nh, L]), ALU.mult, [NZ, cst], [dTi])
        bk = PS.one()
        for h in range(nh):
            _mm(P, PS.f32(bk)[0:L, h * L:(h + 1) * L], kpost[:, h0 + h, c0:c0 + L], kpost[:, h0 + h, c0:c0 + L],
                True, True, [kpost], [bank[bk]])
        blocked = (L == 128)
        Pc, PTc, Pn_, PTn_ = W["P0"], W["PT0"], W["P1"], W["PT1"]
        Mf = W["Mf"] if blocked else Pc
        _stt(P, "vector", Mf[0:L, 0:HL], PS.f32(bk)[0:L, 0:HL], -1.0, wT[0:L, 0:HL], ALU.mult, ALU.mult,
             [bank[bk], wT], [Mf])
        if full:
            bq = PS.one()
            for h in range(nh):
                _mm(P, PS.f32(bq)[0:L, h * L:(h + 1) * L], kpost[:, h0 + h, c0:c0 + L], qpost[:, h0 + h, c0:c0 + L],
                    True, True, [kpost, qpost], [bank[bq]])
            qkT = W["qkT"]
            _tt(P, "vector", qkT[0:L, 0:HL], PS.f32(bq)[0:L, 0:HL], W["dTi"][0:L, 0:HL], ALU.mult,
                [bank[bq], W["dTi"]], [qkT])
        bt = PS.one()
        for h in range(nh):
            _tr(P, PS.bf(bt)[0:L, h * L:(h + 1) * L], Mf[0:L, h * L:(h + 1) * L], identb[0:L, 0:L],
                [Mf, identb], [bank[bt]])
        if blocked:
            bdm = C("bd32", L)[:, None, :].to_broadcast([L, nh, L])
            MoT, MoT2 = W["MoT"], W["MoT2"]
            _tt(P, "gpsimd", v3(Pc[0:L, 0:HL]), v3(Mf[0:L, 0:HL]), bdm, ALU.mult, [Mf, cst], [Pc])
            _tt(P, "vector", v3(PTc[0:L, 0:HL]), v3(PS.bf(bt)[0:L, 0:HL]), bdm, ALU.mult, [bank[bt], cst], [PTc])
            _tt(P, "vector", v3(MoT[0:L, 0:HL]), v3(PS.bf(bt)[0:L, 0:HL]),
                C("o64", L)[:, None, :].to_broadcast([L, nh, L]), ALU.mult, [bank[bt], cst], [MoT])
            _tt(P, "vector", v3(MoT2[0:L, 0:HL]), v3(PS.bf(bt)[0:L, 0:HL]),
                C("offL128", L)[:, None, :].to_broadcast([L, nh, L]), ALU.mult, [bank[bt], cst], [MoT2])
        else:
            _cp(P, "scalar", PTc[0:L, 0:HL], PS.bf(bt)[0:L, 0:HL], [bank[bt]], [PTc])
        Tacc = W["Tacc"]
        _tt(P, "gpsimd", v3(Tacc[0:L, 0:HL]), v3(Pc[0:L, 0:HL]), identL, ALU.add, [Pc, cst], [Tacc])
        nlev = 4 if blocked else NLEV[L]
        for lev in range(1, nlev + 1):
            b1 = PS.one()
            for h in range(nh):
                sl = slice(h * L, (h + 1) * L)
                _mm(P, PS.f32(b1)[0:L, sl], Pc[0:L, sl], PTc[0:L, sl], True, True, [Pc, PTc], [bank[b1]])
            _cp(P, "scalar", PTn_[0:L, 0:HL], PS.f32(b1)[0:L, 0:HL], [bank[b1]], [PTn_])
            if lev < nlev:
                b2 = PS.one()
                for h in range(nh):
                    sl = slice(h * L, (h + 1) * L)
                    _mm(P, PS.f32(b2)[0:L, sl], PTc[0:L, sl], Pc[0:L, sl], True, True, [Pc, PTc], [bank[b2]])
                _cp(P, "vector", Pn_[0:L, 0:HL], PS.f32(b2)[0:L, 0:HL], [bank[b2]], [Pn_])
            b3 = PS.one()
            for h in range(nh):
                sl = slice(h * L, (h + 1) * L)
                _mm(P, PS.f32(b3)[0:L, sl], PTn_[0:L, sl], Tacc[0:L, sl], True, True, [PTn_, Tacc], [bank[b3]])
            _tt(P, "vector", Tacc[0:L, 0:HL], Tacc[0:L, 0:HL], PS.f32(b3)[0:L, 0:HL], ALU.add, [Tacc, bank[b3]], [Tacc])
            Pc, PTc, Pn_, PTn_ = Pn_, PTn_, Pc, PTc
        if blocked:
            TbT, Xt = Pn_, PTn_
            for Mo in (W["MoT"], W["MoT2"]):
                b7 = PS.one()
                for h in range(nh):
                    sl = slice(h * L, (h + 1) * L)
                    _tr(P, PS.bf(b7)[0:L, sl], Tacc[0:L, sl], identb[0:L, 0:L], [Tacc, identb], [bank[b7]])
                _cp(P, "scalar", TbT[0:L, 0:HL], PS.bf(b7)[0:L, 0:HL], [bank[b7]], [TbT])
                b8 = PS.one()
                for h in range(nh):
                    sl = slice(h * L, (h + 1) * L)
                    _mm(P, PS.f32(b8)[0:L, sl], Mo[0:L, sl], Tacc[0:L, sl], True, True, [Mo, Tacc], [bank[b8]])
                _cp(P, "scalar", Xt[0:L, 0:HL], PS.f32(b8)[0:L, 0:HL], [bank[b8]], [Xt])
                b9 = PS.one()
                for h in range(nh):
                    sl = slice(h * L, (h + 1) * L)
                    _mm(P, PS.f32(b9)[0:L, sl], TbT[0:L, sl], Xt[0:L, sl], True, True, [TbT, Xt], [bank[b9]])
                _tt(P, "vector", Tacc[0:L, 0:HL], Tacc[0:L, 0:HL], PS.f32(b9)[0:L, 0:HL], ALU.add, [Tacc, bank[b9]], [Tacc])
        gsl, gLe = W["gsmall"], W["gLe"]
        _act(P, gsl[0:L, 0:nh], Gt, AF.Exp, [gsm], [gsl])
        _tt(P, "vector", gsl[0:L, 0:nh], gsl[0:L, 0:nh], beta, ALU.mult, [gsl, GP], [gsl])
        _tt(P, "vector", gsl[0:L, nh:2 * nh], gsm[0:L, nh:2 * nh], Gt, ALU.subtract, [gsm, gsl], [gsl])
        _act(P, gsl[0:L, nh:2 * nh], gsl[0:L, nh:2 * nh], AF.Exp, [gsl], [gsl])
        _act(P, gLe[:, :], gsm[:, nh:2 * nh], AF.Exp, [gsm], [gLe])
        kbg, kdec, vbt = W["kbg"], W["kdec"], W["vbt"]
        bkt = PS.one()
        for h in range(nh):
            _tr(P, PS.bf(bkt)[0:L, h * 128:(h + 1) * 128], kpost[:, h0 + h, c0:c0 + L], identb[:, :], [kpost, identb],
                [bank[bkt]])
        _tt(P, "vector", v3(kbg[0:L, :]), v3(PS.bf(bkt)[0:L, 0:nh * 128]), bcl(gsl[0:L, 0:nh], 128), ALU.mult,
            [bank[bkt], gsl], [kbg])
        _tt(P, "vector", v3(kdec[0:L, :]), v3(PS.bf(bkt)[0:L, 0:nh * 128]), bcl(gsl[0:L, nh:2 * nh], 128), ALU.mult,
            [bank[bkt], gsl], [kdec])
        bvt = PS.one()
        for h in range(nh):
            _tr(P, PS.bf(bvt)[0:L, h * 128:(h + 1) * 128], vpost[:, h0 + h, c0:c0 + L], identb[:, :], [vpost, identb],
                [bank[bvt]])
        _tt(P, "vector", v3(vbt[0:L, :]), v3(PS.bf(bvt)[0:L, 0:nh * 128]), bcl(beta, 128), ALU.mult,
            [bank[bvt], GP], [vbt])
        bw = PS.one()
        for h in range(nh):
            _mm(P, PS.f32(bw)[:, h * L:(h + 1) * L], kbg[0:L, h * 128:(h + 1) * 128], Tacc[0:L, h * L:(h + 1) * L],
                True, True, [kbg, Tacc], [bank[bw]])
        WT = W["WT"]
        _cp(P, "scalar", WT[:, 0:HL], PS.f32(bw)[:, 0:HL], [bank[bw]], [WT])
        bu = m2()
        for h in range(nh):
            _mm(P, f2(bu)[0:L, h * 128:(h + 1) * 128], Tacc[0:L, h * L:(h + 1) * L],
                vbt[0:L, h * 128:(h + 1) * 128], True, True, [Tacc, vbt], [hb2(bu, h)])
        U0 = W["U0"]
        _cp(P, "scalar", U0[0:L, :], f2(bu)[0:L, :], bl2(bu), [U0])
        if full:
            gz = W["gz"]
            _tt(P, "gpsimd", gz[0:L, :], zb[0:L, gc], self.gB[0:L, gc], ALU.mult, [zb, self.gB], [gz])
        bpu = m2()
        for h in range(nh):
            _mm(P, f2(bpu)[0:L, h * 128:(h + 1) * 128], WT[:, h * L:(h + 1) * L], Sbf[:, h * 128:(h + 1) * 128],
                True, True, [WT, Sbf], [hb2(bpu, h)])
        u = W["u"]
        _tt(P, "vector", u[0:L, :], U0[0:L, :], f2(bpu)[0:L, :], ALU.subtract, [U0] + bl2(bpu), [u])
        if full:
            bo = m2()
            for h in range(nh):
                o_ = f2(bo)[0:L, h * 128:(h + 1) * 128]
                _mm(P, o_, W["qg"][:, h * L:(h + 1) * L], Sbf[:, h * 128:(h + 1) * 128], True, False,
                    [W["qg"], Sbf], [hb2(bo, h)])
                _mm(P, o_, W["qkT"][0:L, h * L:(h + 1) * L], u[0:L, h * 128:(h + 1) * 128], False, True,
                    [W["qkT"], u], [hb2(bo, h)])
            osb = W["osb"]
            _cp(P, "scalar", osb[0:L, :], f2(bo)[0:L, :], bl2(bo) + [osb], [osb])
        bss = m2()
        for h in range(nh):
            _mm(P, f2(bss)[:, h * 128:(h + 1) * 128], kdec[0:L, h * 128:(h + 1) * 128],
                u[0:L, h * 128:(h + 1) * 128], True, True, [kdec, u], [hb2(bss, h)])
        _tt(P, "vector", v3(Sst[:, :]), v3(Sst[:, :]), gLe[:, :, None].to_broadcast([128, nh, 128]),
            ALU.mult, [Sst, gLe], [Sst])
        _tt(P, "vector", Sst[:, :], Sst[:, :], f2(bss)[:, :], ALU.add, [Sst] + bl2(bss), [Sst])
        _cp(P, "scalar", Sbf[:, :], Sst[:, :], [Sst], [Sbf])
        if full:
            ob, ssq = W["ob"], W["ssq2"]
            _act(P, U0[0:L, :], osb[0:L, :], AF.Square, [osb, U0], [U0])
            _red(P, "vector", ssq[0:L, :], v3(U0[0:L, :]), ALU.add, [U0], [ssq])
            _act(P, ssq[0:L, :], ssq[0:L, :], AF.Ln, [ssq], [ssq], bias=EPS, scale=1.0 / DH)
            _act(P, ssq[0:L, :], ssq[0:L, :], AF.Exp, [ssq], [ssq], scale=-0.5)
            _tt(P, "vector", v3(osb[0:L, :]), v3(osb[0:L, :]), bcl(ssq[0:L, :], 128), ALU.mult, [osb, ssq], [osb])
            _tt(P, "gpsimd", ob[0:L, :], osb[0:L, :], W["gz"][0:L, :], ALU.mult, [osb, W["gz"]], [ob])
            bh = PS.one()
            for h in range(nh):
                _tr(P, PS.bf(bh)[:, h * L:(h + 1) * L], ob[0:L, h * 128:(h + 1) * 128], identb[0:L, 0:L],
                    [ob, identb], [bank[bh]])
            _cp(P, "scalar", self.mixT[:, 8 + h0:8 + h0 + nh, q0:q0 + L],
                PS.bf(bh)[:, 0:HL].rearrange("p (h t) -> p h t", h=nh), [bank[bh]], [self.mixT])

    def _refresh_bf(self):
        P, a, b = self.P, self.gaS, self.gbS
        _cp(P, "scalar", a.Cbf[:, :], a.Cst[:, :], [a.Cst], [a.Cbf])
        _cp(P, "vector", a.nbf[:, :], a.nst[:, :], [a.nst], [a.nbf])
        _cp(P, "scalar", b.Sbf[:, :], b.Sst[:, :], [b.Sst], [b.Sbf])

    def _state_tiles(self):
        return [self.gaS.Cst, self.gaS.nst, self.gaS.mprev, self.gbS.Sst]

    def _use_set(self, i):
        a, b = self.gaS, self.gbS
        if not hasattr(self, "_set0"):
            self._set0 = dict(Cst=a.Cst, Sst=b.Sst, nst=a.nst, mprev=a.mprev)
        st = self._set0 if i == 0 else self.alt
        a.Cst, a.nst, a.mprev, b.Sst = st["Cst"], st["nst"], st["mprev"], st["Sst"]
        a.W["Cst"], b.W["Sst"] = st["Cst"], st["Sst"]

    def load_state(self, j):
        P, io, a, b = self.P, self.io, self.gaS, self.gbS
        pairs = [
            (a.Cst[:, :].rearrange("p (h x) -> p h x", h=NH), io["sC"][j].rearrange("h d e -> d h e")),
            (b.Sst[:, :].rearrange("p (h x) -> p h x", h=NH), io["sS"][j].rearrange("h d e -> d h e")),
            (a.nst[:, :], io["sn"][j]),
            (a.mprev[:, :], io["sm"][j:j + 1, :].partition_broadcast(128)),
        ]
        P.dma("sync", "s2ld%d" % (j % 2), [(lambda e, o=o, i=i: e.dma_start(out=o, in_=i)) for o, i in pairs],
              w=self._state_tiles())

    def store_state(self, dC, dS, dn, dm):
        P, a, b = self.P, self.gaS, self.gbS
        pairs = [
            (dC.rearrange("h d e -> d h e"), a.Cst[:, :].rearrange("p (h x) -> p h x", h=NH)),
            (dS.rearrange("h d e -> d h e"), b.Sst[:, :].rearrange("p (h x) -> p h x", h=NH)),
            (dn, a.nst[:, :]),
            (dm, a.mprev[0:1, :]),
        ]
        self._nst = getattr(self, "_nst", 0) + 1
        P.dma("sync", "s2st%d" % (self._nst % 2), [(lambda e, o=o, i=i: e.dma_start(out=o, in_=i)) for o, i in pairs],
              r=self._state_tiles())

    def tm_load(self, L, r0, full, tm):
        P, S = self.P, self.S
        fns = [
            lambda e: e.dma_start(out=tm["ka"][0:L, :], in_=S["ka"][r0:r0 + L, :]),
            lambda e: e.dma_start(out=tm["va"][0:L, :], in_=S["va"][r0:r0 + L, :]),
            lambda e: e.dma_start(out=tm["GP"][0:L, :], in_=S["gt"][r0:r0 + L, :]),
        ]
        wl_ = [tm["ka"], tm["va"], tm["GP"]]
        if full:
            rq = r0 - T0
            fns += [
                lambda e: e.dma_start(out=tm["oa"][0:L, :], in_=S["oa"][rq:rq + L, :]),
                lambda e: e.dma_start(out=tm["zb"][0:L, :], in_=S["zb"][rq:rq + L, :]),
            ]
            wl_ += [tm["oa"], tm["zb"]]
        P.dma("sync", "s2t%d" % self.tm.index(tm), fns, w=wl_)

    def fm_load(self, sc, fm):
        P, S = self.P, self.S
        t0 = sc * 128
        full = sc >= 8
        fns = [
            lambda e: e.dma_start(out=fm["k"][:, :, :], in_=S["kb"][:, :, t0:t0 + 128].rearrange("h d t -> d h t")),
            lambda e: e.dma_start(out=fm["v"][:, :, :], in_=S["vb"][:, :, t0:t0 + 128].rearrange("h d t -> d h t")),
        ]
        wl_ = [fm["k"], fm["v"]]
        if full:
            tq = t0 - T0
            fns += [
                lambda e: e.dma_start(out=fm["q"][:, :, :], in_=S["qb"][:, :, t0:t0 + 128].rearrange("h d t -> d h t")),
                lambda e: e.dma_start(out=fm["qaT"][:, :, :], in_=S["qa"][:, :, tq:tq + 128].rearrange("h d t -> d h t")),
                lambda e: e.dma_start(out=fm["kaT"][:, :, :], in_=S["kaT"][:, :, tq:tq + 128].rearrange("h d t -> d h t")),
            ]
            wl_ += [fm["q"], fm["qaT"], fm["kaT"]]
        P.dma("sync", "s2f%d" % self.fm.index(fm), fns, w=wl_)

    def run(self):
        P, S, io = self.P, self.S, self.io
        for t in self._state_tiles():
            _ms(P, "gpsimd", t[:, :], 0.0, [t])
        self._refresh_bf()
        chunks = []
        for sc in range(NSUB):
            sample = sc == NSUB - 1
            L = LS if sample else LP
            for ch in range(128 // L):
                chunks.append((sc, ch, L, sample))
        self.fm_load(0, self.fm[0])
        self.tm_load(chunks[0][2], 0, False, self.tm[0])
        for idx, (sc, ch, L, sample) in enumerate(chunks):
            full = sc >= 8
            c0 = ch * L
            r0 = sc * 128 + c0
            q0 = r0 - T0
            fm, tm = self.fm[sc % 2], self.tm[idx % 2]
            if idx + 1 < len(chunks):
                nsc, nch, nL, _ = chunks[idx + 1]
                if nsc != sc:
                    self.fm_load(nsc, self.fm[nsc % 2])
                self.tm_load(nL, nsc * 128 + nch * nL, nsc >= 8, self.tm[(idx + 1) % 2])
            if sample:
                if ch == 0:
                    self._use_set(0)
                    self.load_state(0)
                if ch + 1 < 128 // L:
                    self._use_set((ch + 1) % 2)
                    self.load_state(ch + 1)
                self._use_set(ch % 2)
                self._refresh_bf()
            lists = []
            for g in ([self.gaS] if sample else self.ga):
                P.capture()
                self.mlstm_chunk(g, L, c0, q0, full, fm, tm)
                lists.append(P.end_capture())
            for g in ([self.gbS] if sample else self.gb):
                P.capture()
                self.gdn_chunk(g, L, c0, q0, full, fm, tm)
                lists.append(P.end_capture())
            P.replay(lists)
            if sample:
                self.store_state(io["oC"][ch], io["oS"][ch], io["on"][ch], io["om"][ch:ch + 1, :])
            if sc == 7 and ch == 128 // L - 1:
                for t in self._state_tiles():
                    _ts(P, "vector", t[:, :], t[:, :], self.flag[:, 0:1], ALU.mult, [t, self.flag], [t])
                self._refresh_bf()
            if sc == 15 and ch == 128 // L - 1:
                self.store_state(io["pC"], io["pS"], io["pn"], io["pm"])


def _phase3(B, P, PS, cst, C, identb, mixT, x_d, norms_d, modd, wout_d, wup_d, wdn_d, y_d):
    nc = B.nc
    bank = PS.banks
    X = Ctx(nc)
    x1 = X.sb("x1", [128, NSUBQ, D], F32)
    x1s = [TL(x1.t, "x1_%d" % i) for i in range(NSUBQ)]
    wr = [X.sb("p3_w%d" % i, [128, 16, 512], BF16) for i in range(3)]
    mt0 = X.sb("p3_mt0", [128, D], F32)
    mt1 = X.sb("p3_mt1", [128, D], F32)
    st = [X.sb("p3_st%d" % i, [128, 4], F32) for i in range(2)]
    wi = [0]

    def wslot():
        s = wr[wi[0] % 3]
        k = "wr%d" % (wi[0] % 3)
        wi[0] += 1
        return s, k

    def mtile(i):
        return mt0 if i < 8 else mt1

    for i in range(NSUBQ):
        _ld(P, "sync", "p3x%d" % i, x1[:, i, :], x_d[T0 + i * 128:T0 + (i + 1) * 128, :], w=[x1s[i]])

    XA = Ctx(nc)
    tmpf = XA.sb("p3_tmp", [128, D], F32)
    hb = XA.sb("p3_hb", [128, D], BF16)
    sgA = [XA.sb("p3_sgA%d" % i, [128, 512], F32) for i in range(2)]
    _mod_tiles(P, modd, 4096, mt0, mt1, "p3G")
    ne = 0
    for cb in range(4):
        slot, key = wslot()
        _ld(P, "gpsimd", key, slot[:], wout_d[:, cb * 512:(cb + 1) * 512].rearrange("(k p) c -> p k c", p=128), w=[slot])
        for i in range(NSUBQ):
            bk = PS.one()
            for k in range(16):
                _mm(P, PS.f32(bk)[:, :], mixT[:, k, i * 128:(i + 1) * 128], slot[:, k, :], k == 0, k == 15,
                    [mixT, slot], [bank[bk]])
            sg = sgA[ne % 2]
            ne += 1
            _tt(P, "vector", sg[:], PS.f32(bk)[:, :], mtile(i)[:, cb * 512:(cb + 1) * 512], ALU.mult,
                [bank[bk], mtile(i)], [sg])
            _tt(P, "vector", x1[:, i, cb * 512:(cb + 1) * 512], x1[:, i, cb * 512:(cb + 1) * 512], sg[:], ALU.add,
                [x1s[i], sg], [x1s[i]])

    def norm_tiles(col_sc, col_sh, nrow, first):
        _ld(P, "sync", "p3n", tmpf[:], norms_d[nrow:nrow + 1, :].partition_broadcast(128), w=[tmpf])
        if first:
            _ld(P, "sync", "p3m0", mt0[:], modd[0:1, col_sc:col_sc + D].partition_broadcast(128), w=[mt0])
            _ld(P, "sync", "p3m1", mt1[:], modd[0:1, col_sh:col_sh + D].partition_broadcast(128), w=[mt1])
        else:
            for t, col, key in ((mt0, col_sc, "p3m0"), (mt1, col_sh, "p3m1")):
                fns = []
                for b in range(NSEQ):
                    fns.append(lambda e, b=b, t=t, col=col: e.dma_start(
                        out=t[8 * b:8 * b + 8, :], in_=modd[1 + b:2 + b, col:col + D].partition_broadcast(8)))
                P.dma("sync", key, fns, w=[t])
        _stt(P, "vector", mt0[:], mt0[:], 1.0, tmpf[:], ALU.add, ALU.mult, [mt0, tmpf], [mt0])

    def norm_sub(i, out_ap, out_tl, junk):
        s_ = st[i % 2]
        _act(P, junk[:], x1[:, i, :], AF.Square, [x1s[i]], [junk, s_], accum=s_[:, 0:1])
        _act(P, s_[:, 1:2], s_[:, 0:1], AF.Ln, [s_], [s_], bias=EPS, scale=1.0 / D)
        _act(P, s_[:, 2:3], s_[:, 1:2], AF.Exp, [s_], [s_], scale=-0.5)
        _stt(P, "vector", tmpf[:], x1[:, i, :], s_[:, 2:3], mt0[:], ALU.mult, ALU.mult, [x1s[i], s_, mt0], [tmpf])
        _tt(P, "vector", out_ap, tmpf[:], mt1[:], ALU.add, [tmpf, mt1], [out_tl])

    h2T = mixT
    nitems = []
    for i in range(NSUBQ):
        P.capture()
        if i == 0:
            norm_tiles(8192, 6144, 1, True)
        if i == 8:
            norm_tiles(8192, 6144, 1, False)
        norm_sub(i, hb[:], hb, hb)
        for half in range(2):
            bk = PS.one()
            for kk in range(8):
                k = half * 8 + kk
                _tr(P, PS.bf(bk)[:, kk * 128:(kk + 1) * 128], hb[:, k * 128:(k + 1) * 128], identb[:],
                    [hb, identb], [bank[bk]])
            _cp(P, "scalar" if half else "vector", h2T[:, half * 8:(half + 1) * 8, i * 128:(i + 1) * 128],
                PS.bf(bk).rearrange("p (k t) -> p k t", k=8), [bank[bk]], [h2T])
        nitems.append(P.end_capture())
    P.replay_pipe(nitems, 3, burst=1)
    P.barrier()
    P.flush()
    XA.close()

    XB = Ctx(nc)
    actT = XB.sb("actT", [128, 8, NQ], BF16)
    sgB = [XB.sb("p3_sgB%d" % i, [128, 512], F32) for i in range(2)]
    rl = [XB.sb("p3_rl%d" % i, [128, 512], F32) for i in range(2)]
    _mod_tiles(P, modd, 10240, mt0, mt1, "p3G")
    ttiles = [(0, 512), (512, 512), (1024, 128)]
    ne = 0
    nr = 0
    for fb in range(8):
        for half in range(2):
            slot, key = wslot()
            c0 = fb * 1024 + half * 512
            _ld(P, "gpsimd", key, slot[:], wup_d[:, c0:c0 + 512].rearrange("(k p) c -> p k c", p=128), w=[slot])
            for sbk in range(4):
                for (t0, nt) in ttiles:
                    bk = PS.one()
                    for k in range(16):
                        _mm(P, PS.f32(bk)[:, 0:nt], slot[:, k, sbk * 128:(sbk + 1) * 128], h2T[:, k, t0:t0 + nt],
                            k == 0, k == 15, [slot, h2T], [bank[bk]])
                    r_ = rl[nr % 2]
                    nr += 1
                    _act(P, r_[:, 0:nt], PS.f32(bk)[:, 0:nt], AF.Relu, [bank[bk]], [r_])
                    _tt(P, "vector", actT[:, half * 4 + sbk, t0:t0 + nt], r_[:, 0:nt], r_[:, 0:nt], ALU.mult,
                        [r_], [actT])
        sd = []
        for half in range(2):
            slot, key = wslot()
            r0 = fb * 1024 + half * 512
            sv = slot[:, :, :].rearrange("p k c -> p (k c)").rearrange("p (s c) -> p s c", s=4)
            _ld(P, "gpsimd", key, sv, wdn_d[r0:r0 + 512, :].rearrange("(s p) c -> p s c", p=128), w=[slot])
            sd.append((slot, sv))
        for i in range(NSUBQ):
            for cb in range(4):
                bk = PS.one()
                for s8 in range(8):
                    slot, sv = sd[s8 // 4]
                    _mm(P, PS.f32(bk)[:, :], actT[:, s8, i * 128:(i + 1) * 128], sv[:, s8 % 4, cb * 512:(cb + 1) * 512],
                        s8 == 0, s8 == 7, [actT, slot], [bank[bk]])
                sg = sgB[ne % 2]
                ne += 1
                _tt(P, "vector", sg[:], PS.f32(bk)[:, :], mtile(i)[:, cb * 512:(cb + 1) * 512], ALU.mult,
                    [bank[bk], mtile(i)], [sg])
                _tt(P, "vector", x1[:, i, cb * 512:(cb + 1) * 512], x1[:, i, cb * 512:(cb + 1) * 512], sg[:], ALU.add,
                    [x1s[i], sg], [x1s[i]])
    P.barrier()
    P.flush()
    XB.close()

    XC = Ctx(nc)
    tmpf = XC.sb("p3c_tmp", [128, D], F32)
    yo = [XC.sb("p3c_y%d" % i, [128, D], F32) for i in range(2)]
    junk = XC.sb("p3c_junk", [128, D], BF16)
    nitems = []
    for i in range(NSUBQ):
        P.capture()
        if i == 0:
            norm_tiles(14336, 12288, 2, True)
        if i == 8:
            norm_tiles(14336, 12288, 2, False)
        y_ = yo[i % 2]
        norm_sub(i, y_[:], y_, junk)
        _ld(P, "sync", "p3y%d" % (i % 2), y_d[i * 128:(i + 1) * 128, :], y_[:], r=[y_])
        nitems.append(P.end_capture())
    P.replay_pipe(nitems, 3, burst=1)
    P.barrier()
    P.flush()
    XC.close()
    X.close()


def kernel(**inputs):
    inp = {k: np.asarray(v) for k, v in inputs.items()}
    B = build(debug=False)
    sh = _shared_inputs(inp)
    in_maps = []
    for c in range(8):
        m = _core_inputs(inp, c, sh)
        in_maps.append({k: v for k, v in m.items() if k in B.ins})
    res = run_bass_kernel_spmd(B.nc, in_maps, core_ids=list(range(8)))
    r = [{k: np.asarray(v) for k, v in rr.items()} for rr in res.results]

    y_prompt = np.empty((4, 2048, D), np.float32)
    y_sample = np.empty((128, LS, D), np.float32)
    pC = np.empty((1, 4, NH, DH, DH), np.float32)
    pn = np.empty((1, 4, NH, DH), np.float32)
    pm = np.empty((1, 4, NH), np.float32)
    pS = np.empty((1, 4, NH, DH, DH), np.float32)
    pconv = np.empty((1, 4, 3, 3072), np.float32)
    sC = np.empty((1, 128, NH, DH, DH), np.float32)
    sn = np.empty((1, 128, NH, DH), np.float32)
    sm = np.empty((1, 128, NH), np.float32)
    sS = np.empty((1, 128, NH, DH, DH), np.float32)
    sconv = np.empty((1, 128, 3, 3072), np.float32)
    for c in range(8):
        b, half = c // 2, c % 2
        sl = slice(c * NSEQ, (c + 1) * NSEQ)
        o = r[c]
        y_prompt[b, half * T1:(half + 1) * T1] = o["y"][:T1]
        y_sample[sl] = o["y"][T1:].reshape(NSEQ, LS, D)
        if half == 1:
            pC[0, b] = o["pC"]
            pn[0, b] = o["pn_t"].T
            pm[0, b] = o["pm"][0]
            pS[0, b] = o["pS"]
            pconv[0, b] = o["pconv_t"].reshape(128, 24, 3).transpose(2, 1, 0).reshape(3, 3072)
        sC[0, sl] = o["oC"]
        sn[0, sl] = o["on_t"].transpose(0, 2, 1)
        sm[0, sl] = o["om"]
        sS[0, sl] = o["oS"]
        sconv[0, sl] = o["oconv_t"].reshape(128, 24, NSEQ, 3).transpose(2, 3, 1, 0).reshape(NSEQ, 3, 3072)
    return (y_prompt, y_sample, pC, pn, pm, pS, pconv, sC, sn, sm, sS, sconv)
```
